# Optimizing a Trainium2 kernel written in Bass

```python
import math
import jax
import jax.numpy as jnp
from jax import lax
import numpy as np

D_MODEL = 1024
BATCH = 8
SEQ = 4096
DEPTH = 4

CTX_LEN = 256
GRID_W = 64
N_DIR = 2
EPS = 1e-6

GLA_HEADS = 4
GLA_VAL = D_MODEL // 2
GLA_KEY = GLA_VAL // 2
GLA_DV = GLA_VAL // GLA_HEADS
GLA_DK = GLA_KEY // GLA_HEADS
GLA_RANK = 16
GLA_TAU = 16.0
GLA_CHUNK = 64

LRU_WIDTH = D_MODEL // 4
LRU_BLOCKS = 4
LRU_BLOCK = LRU_WIDTH // LRU_BLOCKS
LRU_CONV = 4
LRU_C = 8.0

S5_WIDTH = D_MODEL // 4
S5_GROUP = 16
S5_GROUPS = S5_WIDTH // S5_GROUP
S5_STATE = 64

MIX_WIDTH = GLA_VAL + LRU_WIDTH + S5_WIDTH
D_FF = 4 * D_MODEL
IN_COLS = 2 * GLA_KEY + 2 * GLA_VAL + N_DIR * GLA_RANK + 2 * LRU_WIDTH + S5_WIDTH

kernel_name = "hybrid_gla_rglru_s5_prefix_dit"


def rmsnorm(x, g):
    xf = x.astype(jnp.float32)
    y = xf * lax.rsqrt(jnp.mean(xf * xf, axis=-1, keepdims=True) + EPS)
    return (y * g.astype(jnp.float32)).astype(x.dtype)


def flip_seq(t):
    return jnp.flip(t, axis=1)


def to_col_major(t, rows):
    b, l, ch = t.shape
    return t.reshape(b, rows, GRID_W, ch).transpose(0, 2, 1, 3).reshape(b, l, ch)


def from_col_major(t, rows):
    b, l, ch = t.shape
    return t.reshape(b, GRID_W, rows, ch).transpose(0, 2, 1, 3).reshape(b, l, ch)


def split_projection(z):
    sizes = (GLA_KEY, GLA_KEY, GLA_VAL, GLA_VAL, N_DIR * GLA_RANK, LRU_WIDTH, LRU_WIDTH, S5_WIDTH)
    idx = [int(i) for i in np.cumsum(sizes)[:-1]]
    return jnp.split(z, idx, axis=-1)


def linear_scan(a, b):
    def comb(l, r):
        return l[0] * r[0], r[0] * l[1] + r[1]
    return lax.associative_scan(comb, (a, b), axis=1)[1]


def complex_linear_scan(ar, ai, br, bi):
    def comb(l, r):
        ar1, ai1, br1, bi1 = l
        ar2, ai2, br2, bi2 = r
        return (ar2 * ar1 - ai2 * ai1, ar2 * ai1 + ai2 * ar1,
                ar2 * br1 - ai2 * bi1 + br2, ar2 * bi1 + ai2 * br1 + bi2)
    _, _, hr, hi = lax.associative_scan(comb, (ar, ai, br, bi), axis=1)
    return hr, hi


def bidirectional(run, ctx_in, lat_in):
    y_ctx, y_lat = None, None
    for d in range(N_DIR):
        ci, li = ctx_in, lat_in
        if d == 1:
            ci, li = tuple(map(flip_seq, ci)), tuple(map(flip_seq, li))
        yc, h_ctx = run(d, ci, None)
        yl, _ = run(d, li, h_ctx)
        if d == 1:
            yc, yl = flip_seq(yc), flip_seq(yl)
        y_ctx = yc if y_ctx is None else y_ctx + yc
        y_lat = yl if y_lat is None else y_lat + yl
    return y_ctx, y_lat


def gla_chunked(q, k, v, log_a, s0):
    bsz, L, H, _ = q.shape
    DV = v.shape[-1]
    C = GLA_CHUNK
    n = L // C

    def chunk(t):
        return t.astype(jnp.float32).reshape(bsz, n, C, H, t.shape[-1])

    q, k, v, log_a = chunk(q), chunk(k), chunk(v), chunk(log_a)
    b = jnp.cumsum(log_a, axis=2)
    b_last = b[:, :, -1]
    q_dec = q * jnp.exp(b)
    k_inv = k * jnp.exp(-b)
    k_end = k * jnp.exp(b_last[:, :, None] - b)
    lower = jnp.tril(jnp.ones((C, C), dtype=bool))
    scores = jnp.where(lower, jnp.einsum('bnihd,bnjhd->bnhij', q_dec, k_inv), 0.0)
    o_intra = jnp.einsum('bnhij,bnjhe->bnihe', scores, v)
    ds = jnp.einsum('bnjhd,bnjhe->bnhde', k_end, v)

    def step(s, inp):
        decay, inc = inp
        return decay[..., None] * s + inc, s

    s_fin, s_in = lax.scan(step, s0, (jnp.moveaxis(jnp.exp(b_last), 1, 0), jnp.moveaxis(ds, 1, 0)))
    o_inter = jnp.einsum('bnihd,nbhde->bnihe', q_dec, s_in)
    return (o_intra + o_inter).reshape(bsz, L, H, DV), s_fin


def gla_mixer(ctx_in, lat_in, up_w, up_b):
    def run(d, inp, h0):
        q, k, v, lr = inp
        bsz, L, _ = q.shape
        logit = (lr[..., d * GLA_RANK:(d + 1) * GLA_RANK].astype(jnp.float32) @ up_w[d].astype(jnp.float32)
                 + up_b[d].astype(jnp.float32))
        log_a = jax.nn.log_sigmoid(logit) / GLA_TAU
        s0 = jnp.zeros((bsz, GLA_HEADS, GLA_DK, GLA_DV), jnp.float32) if h0 is None else h0

        def heads(t, e):
            return t.reshape(bsz, L, GLA_HEADS, e)

        return gla_chunked(heads(q, GLA_DK) * GLA_DK ** -0.5, heads(k, GLA_DK), heads(v, GLA_DV),
                           heads(log_a, GLA_DK), s0)
    return bidirectional(run, ctx_in, lat_in)


def short_conv(x, w, b):
    K, ch = w.shape
    y = lax.conv_general_dilated(x, w[:, None, :].astype(x.dtype), window_strides=(1,),
                                 padding=[(K - 1, 0)], dimension_numbers=('NWC', 'WIO', 'NWC'),
                                 feature_group_count=ch)
    return y + b.astype(x.dtype)


def rglru_mixer(ctx_in, lat_in, conv_w, conv_b, wa, ba, wx, bx, lam):
    def run(d, inp, h0):
        (xb,) = inp
        xc = short_conv(xb, conv_w[d], conv_b[d]).astype(jnp.float32)
        bsz, L, _ = xc.shape
        xr = xc.reshape(bsz, L, LRU_BLOCKS, LRU_BLOCK)
        r = jax.nn.sigmoid(jnp.einsum('blnj,njk->blnk', xr, wa[d].astype(jnp.float32)).reshape(bsz, L, LRU_WIDTH)
                           + ba[d].astype(jnp.float32))
        i = jax.nn.sigmoid(jnp.einsum('blnj,njk->blnk', xr, wx[d].astype(jnp.float32)).reshape(bsz, L, LRU_WIDTH)
                           + bx[d].astype(jnp.float32))
        log_a = -LRU_C * r * jax.nn.softplus(-lam[d].astype(jnp.float32))
        a = jnp.exp(log_a)
        bt = jnp.sqrt(-jnp.expm1(2.0 * log_a)) * (i * xc)
        if h0 is not None:
            bt = bt.at[:, 0].add(a[:, 0] * h0)
        h = linear_scan(a, bt)
        return h, h[:, -1]
    return bidirectional(run, ctx_in, lat_in)


def s5_mixer(ctx_in, lat_in, lam_re, lam_im, log_dt, b_re, b_im, c_re, c_im):
    def run(d, inp, h0):
        (u,) = inp
        bsz, L, _ = u.shape
        lr_ = lam_re[d].astype(jnp.float32)
        li_ = lam_im[d].astype(jnp.float32)
        dt = jnp.exp(log_dt[d].astype(jnp.float32))[:, None]
        mag = jnp.exp(lr_ * dt)
        ang = li_ * dt
        abar_r, abar_i = mag * jnp.cos(ang), mag * jnp.sin(ang)
        den = lr_ * lr_ + li_ * li_
        num_r = abar_r - 1.0
        coef_r = (num_r * lr_ + abar_i * li_) / den
        coef_i = (abar_i * lr_ - num_r * li_) / den
        br, bi = b_re[d].astype(jnp.float32), b_im[d].astype(jnp.float32)
        bbar_r = coef_r[..., None] * br - coef_i[..., None] * bi
        bbar_i = coef_r[..., None] * bi + coef_i[..., None] * br
        ug = u.astype(jnp.float32).reshape(bsz, L, S5_GROUPS, S5_GROUP)
        bu_r = jnp.einsum('blgh,gph->blgp', ug, bbar_r)
        bu_i = jnp.einsum('blgh,gph->blgp', ug, bbar_i)
        if h0 is not None:
            hr, hi = h0
            bu_r = bu_r.at[:, 0].add(abar_r * hr - abar_i * hi)
            bu_i = bu_i.at[:, 0].add(abar_r * hi + abar_i * hr)
        shape = bu_r.shape
        xr, xi = complex_linear_scan(jnp.broadcast_to(abar_r, shape), jnp.broadcast_to(abar_i, shape), bu_r, bu_i)
        y = (jnp.einsum('ghp,blgp->blgh', c_re[d].astype(jnp.float32), xr)
             - jnp.einsum('ghp,blgp->blgh', c_im[d].astype(jnp.float32), xi))
        return y.reshape(bsz, L, S5_WIDTH), (xr[:, -1], xi[:, -1])
    return bidirectional(run, ctx_in, lat_in)


def merge_groups(gla_o, gla_gate, lru_h, lru_gate, s5_y, s5_u, gla_norm, s5_d, glu_w, glu_b, w_out, dtype):
    bsz, L = gla_o.shape[:2]
    o = gla_o * lax.rsqrt(jnp.mean(gla_o * gla_o, axis=-1, keepdims=True) + EPS)
    o = o.reshape(bsz, L, GLA_VAL) * gla_norm.astype(jnp.float32) * jax.nn.silu(gla_gate.astype(jnp.float32))
    r = lru_h * jax.nn.gelu(lru_gate.astype(jnp.float32))
    s = jax.nn.gelu(s5_y + s5_d.astype(jnp.float32) * s5_u.astype(jnp.float32))
    s = s * jax.nn.sigmoid(s @ glu_w.astype(jnp.float32) + glu_b.astype(jnp.float32))
    cat = jnp.concatenate([o, r, s], axis=-1).astype(dtype)
    return cat @ w_out


def sqrelu_mlp(h, w1, w2):
    return jnp.square(jax.nn.relu(h @ w1)) @ w2


def setup_inputs(seed: int = 0) -> dict:
    key = jax.random.key(seed)
    ks = iter(jax.random.split(key, 48))
    Ld, D = DEPTH, D_MODEL

    def nrm(shape, scale):
        return jax.random.normal(next(ks), shape, jnp.float32) * scale

    x = nrm((BATCH, SEQ, D), 1.0)
    c = nrm((BATCH, D), 1.0)
    ctx = nrm((BATCH, CTX_LEN, D), 1.0)
    c_ctx = nrm((D,), 1.0)
    w_mod = nrm((Ld, D, 6 * D), 0.5 * D ** -0.5)
    b_mod = nrm((Ld, 6 * D), 0.02)
    norm1 = 1.0 + nrm((Ld, D), 0.02)
    norm2 = 1.0 + nrm((Ld, D), 0.02)
    w_in = nrm((Ld, D, IN_COLS), D ** -0.5)
    gla_up_w = nrm((Ld, N_DIR, GLA_RANK, GLA_KEY), GLA_RANK ** -0.5)
    gla_up_b = 2.0 + nrm((Ld, N_DIR, GLA_KEY), 0.1)
    gla_norm = 1.0 + nrm((Ld, GLA_VAL), 0.02)
    lru_conv_w = nrm((Ld, N_DIR, LRU_CONV, LRU_WIDTH), LRU_CONV ** -0.5)
    lru_conv_b = nrm((Ld, N_DIR, LRU_WIDTH), 0.01)
    lru_wa = nrm((Ld, N_DIR, LRU_BLOCKS, LRU_BLOCK, LRU_BLOCK), LRU_BLOCK ** -0.5)
    lru_ba = nrm((Ld, N_DIR, LRU_WIDTH), 0.01)
    lru_wx = nrm((Ld, N_DIR, LRU_BLOCKS, LRU_BLOCK, LRU_BLOCK), LRU_BLOCK ** -0.5)
    lru_bx = nrm((Ld, N_DIR, LRU_WIDTH), 0.01)
    a8 = jax.random.uniform(next(ks), (Ld, N_DIR, LRU_WIDTH), jnp.float32, 0.9, 0.999)
    a = jnp.exp(jnp.log(a8) / LRU_C)
    lru_lambda = jnp.log(a) - jnp.log1p(-a)
    n_idx = jnp.arange(S5_STATE, dtype=jnp.float32)
    s5_lam_re = -0.5 * (1.0 + nrm((Ld, N_DIR, S5_GROUPS, S5_STATE), 0.05))
    s5_lam_im = jnp.pi * n_idx + nrm((Ld, N_DIR, S5_GROUPS, S5_STATE), 0.05)
    s5_log_dt = jax.random.uniform(next(ks), (Ld, N_DIR, S5_GROUPS), jnp.float32,
                                   math.log(1e-3), math.log(1e-1))
    s5_b_re = nrm((Ld, N_DIR, S5_GROUPS, S5_STATE, S5_GROUP), (2 * S5_GROUP) ** -0.5)
    s5_b_im = nrm((Ld, N_DIR, S5_GROUPS, S5_STATE, S5_GROUP), (2 * S5_GROUP) ** -0.5)
    s5_c_re = nrm((Ld, N_DIR, S5_GROUPS, S5_GROUP, S5_STATE), 0.5)
    s5_c_im = nrm((Ld, N_DIR, S5_GROUPS, S5_GROUP, S5_STATE), 0.5)
    s5_d = nrm((Ld, S5_WIDTH), 0.5)
    s5_glu_w = nrm((Ld, S5_WIDTH, S5_WIDTH), S5_WIDTH ** -0.5)
    s5_glu_b = nrm((Ld, S5_WIDTH), 0.01)
    w_out = nrm((Ld, MIX_WIDTH, D), MIX_WIDTH ** -0.5)
    w_ff1 = nrm((Ld, D, D_FF), D ** -0.5)
    w_ff2 = nrm((Ld, D_FF, D), D_FF ** -0.5)
    final_norm = 1.0 + nrm((D,), 0.02)
    return {"x": x, "c": c, "ctx": ctx, "c_ctx": c_ctx, "w_mod": w_mod, "b_mod": b_mod,
            "norm1": norm1, "norm2": norm2, "w_in": w_in, "gla_up_w": gla_up_w, "gla_up_b": gla_up_b,
            "gla_norm": gla_norm, "lru_conv_w": lru_conv_w, "lru_conv_b": lru_conv_b, "lru_wa": lru_wa,
            "lru_ba": lru_ba, "lru_wx": lru_wx, "lru_bx": lru_bx, "lru_lambda": lru_lambda,
            "s5_lam_re": s5_lam_re, "s5_lam_im": s5_lam_im, "s5_log_dt": s5_log_dt,
            "s5_b_re": s5_b_re, "s5_b_im": s5_b_im, "s5_c_re": s5_c_re, "s5_c_im": s5_c_im,
            "s5_d": s5_d, "s5_glu_w": s5_glu_w, "s5_glu_b": s5_glu_b, "w_out": w_out,
            "w_ff1": w_ff1, "w_ff2": w_ff2, "final_norm": final_norm}


def reference(x, c, ctx, c_ctx, w_mod, b_mod, norm1, norm2, w_in, gla_up_w, gla_up_b, gla_norm,
              lru_conv_w, lru_conv_b, lru_wa, lru_ba, lru_wx, lru_bx, lru_lambda,
              s5_lam_re, s5_lam_im, s5_log_dt, s5_b_re, s5_b_im, s5_c_re, s5_c_im,
              s5_d, s5_glu_w, s5_glu_b, w_out, w_ff1, w_ff2, final_norm):
    rows = x.shape[1] // GRID_W
    x_lat, x_ctx = x, ctx
    for l in range(DEPTH):
        last = l == DEPTH - 1
        m_lat = (jax.nn.silu(c) @ w_mod[l] + b_mod[l])[:, None, :]
        m_ctx = (jax.nn.silu(c_ctx) @ w_mod[l] + b_mod[l])[None, None, :]
        sh1, sc1, g1, sh2, sc2, g2 = jnp.split(m_lat, 6, axis=-1)
        csh1, csc1, cg1, csh2, csc2, cg2 = jnp.split(m_ctx, 6, axis=-1)

        h_lat = rmsnorm(x_lat, norm1[l]) * (1.0 + sc1) + sh1
        h_ctx = rmsnorm(x_ctx, norm1[l]) * (1.0 + csc1) + csh1
        q_l, k_l, v_l, gg_l, lr_l, lx_l, lg_l, su_l = split_projection(h_lat @ w_in[l])
        q_c, k_c, v_c, gg_c, lr_c, lx_c, lg_c, su_c = split_projection(h_ctx @ w_in[l])

        gla_c, gla_l = gla_mixer((q_c, k_c, v_c, lr_c), (q_l, k_l, v_l, lr_l), gla_up_w[l], gla_up_b[l])
        lru_c, lru_l = rglru_mixer((lx_c,), (lx_l,), lru_conv_w[l], lru_conv_b[l], lru_wa[l], lru_ba[l],
                                   lru_wx[l], lru_bx[l], lru_lambda[l])
        s5_c, s5_l_cm = s5_mixer((su_c,), (to_col_major(su_l, rows),), s5_lam_re[l], s5_lam_im[l],
                                 s5_log_dt[l], s5_b_re[l], s5_b_im[l], s5_c_re[l], s5_c_im[l])
        s5_l = from_col_major(s5_l_cm, rows)

        mix_lat = merge_groups(gla_l, gg_l, lru_l, lg_l, s5_l, su_l, gla_norm[l], s5_d[l],
                               s5_glu_w[l], s5_glu_b[l], w_out[l], x_lat.dtype)
        x_lat = x_lat + g1 * mix_lat
        h2 = rmsnorm(x_lat, norm2[l]) * (1.0 + sc2) + sh2
        x_lat = x_lat + g2 * sqrelu_mlp(h2, w_ff1[l], w_ff2[l])

        if not last:
            mix_ctx = merge_groups(gla_c, gg_c, lru_c, lg_c, s5_c, su_c, gla_norm[l], s5_d[l],
                                   s5_glu_w[l], s5_glu_b[l], w_out[l], x_ctx.dtype)
            x_ctx = x_ctx + cg1 * mix_ctx
            hc2 = rmsnorm(x_ctx, norm2[l]) * (1.0 + csc2) + csh2
            x_ctx = x_ctx + cg2 * sqrelu_mlp(hc2, w_ff1[l], w_ff2[l])
    return rmsnorm(x_lat, final_norm)
```

```python
import math
import os
from contextlib import ExitStack

import numpy as np
import concourse.bass as bass
import concourse.mybir as mybir
from concourse.bass_utils import run_bass_kernel_spmd

F32 = mybir.dt.float32
BF16 = mybir.dt.bfloat16
ALU = mybir.AluOpType
AF = mybir.ActivationFunctionType

D = 1024
KT = 8
EPS = 1e-6
MAGIC = 12582912.0
TWO_PI = 2.0 * math.pi

COMPUTE = ("pe", "act", "dve", "pool")
QUEUES = ("sp", "poolq")
ENG_OF = {"pe": "pe", "act": "act", "dve": "dve", "pool": "pool", "sp": "sp", "poolq": "pool"}


class Prog:
    def __init__(self, nc, ndma=8):
        import os
        self.nc = nc
        self.es = ExitStack()
        self.streams = {e: [] for e in ("pe", "act", "dve", "pool", "sp")}
        self.sem = {}
        self.nop_eng = {}
        for e in COMPUTE:
            self.sem[e] = self.es.enter_context(nc.semaphore("s_" + e))
            self.nop_eng[e] = 0
        self.dsem, self.dcnt, self.dnext = {}, {}, {}
        for q in QUEUES:
            self.dsem[q] = [self.es.enter_context(nc.semaphore("d_%s%d" % (q, i))) for i in range(ndma)]
            self.dcnt[q] = [0] * ndma
            self.dnext[q] = 0
        self.seen = {e: {} for e in self.streams}
        self.lastw = {}
        self.readers = {}
        self.nops = 0
        self.waited = {e: set() for e in COMPUTE}
        self.limit = int(os.environ["OPLIMIT"]) if os.environ.get("OPLIMIT") else None

    def sbuf(self, name, shape, dtype=F32):
        return self.es.enter_context(self.nc.sbuf_tensor(name, list(shape), dtype))

    def psum(self, name, shape, dtype=F32):
        return self.es.enter_context(self.nc.psum_tensor(name, list(shape), dtype))

    @staticmethod
    def _tkey(tok):
        return ("c", tok[1]) if tok[0] == "c" else ("d", tok[1].name)

    @staticmethod
    def _tval(tok):
        return tok[2]

    def _need(self, stream, tok, waits):
        if tok is None:
            return
        if tok[0] == "c" and tok[1] == "pe" and stream == "pe":
            return
        k = self._tkey(tok)
        if self.seen[stream].get(k, 0) >= self._tval(tok):
            return
        cur = waits.get(k)
        if cur is None or self._tval(cur) < self._tval(tok):
            waits[k] = tok

    def _deps(self, stream, reads, writes, waits):
        for k in reads:
            self._need(stream, self.lastw.get(k), waits)
        for k in writes:
            self._need(stream, self.lastw.get(k), waits)
            for t in self.readers.get(k, ()):
                self._need(stream, t, waits)

    def _commit(self, stream, tok, reads, writes, waits):
        for k, t in waits.items():
            self.seen[stream][k] = self._tval(t)
            if t[0] == "c":
                self.waited[t[1]].add(t[2])
        for k in writes:
            self.lastw[k] = tok
            self.readers[k] = []
        for k in reads:
            if k in writes:
                continue
            lst = self.readers.setdefault(k, [])
            lst.append(tok)
            if len(lst) > 16:
                best = {}
                for t in lst:
                    kk = self._tkey(t)
                    b = best.get(kk)
                    if b is None or self._tval(b) < self._tval(t):
                        best[kk] = t
                self.readers[k] = list(best.values())

    def op(self, eng, fn, reads=(), writes=()):
        if self.limit is not None and self.nops >= self.limit:
            return None
        rec = _Rec()
        fn(rec)
        name, args, kwargs = rec.call
        if os.environ.get("OPTRACE"):
            def _d(a):
                try:
                    return "%s%s" % (tuple(a.shape), "" )
                except Exception:
                    return str(a)[:30]
            print("OP", self.nops, eng, name, [_d(a) for a in args], {k: _d(v) for k, v in kwargs.items()}, flush=True)
        fn = lambda e, name=name, args=args, kwargs=kwargs: getattr(e, name)(*args, **kwargs)
        waits = {}
        self._deps(eng, reads, writes, waits)
        self.nop_eng[eng] += 1
        tok = ("c", eng, self.nop_eng[eng])
        self._commit(eng, tok, reads, writes, waits)
        self.streams[eng].append([list(waits.values()), fn, tok])
        self.nops += 1
        return tok

    def dma(self, q, out, in_, reads=(), writes=(), **kw):
        if self.limit is not None and self.nops >= self.limit:
            return None
        stream = ENG_OF[q]
        waits = {}
        i = self.dnext[q]
        self.dnext[q] = (i + 1) % len(self.dsem[q])
        sem = self.dsem[q][i]
        if self.dcnt[q][i] > 0:
            self._need(stream, ("d", sem, 16 * self.dcnt[q][i], q), waits)
        self._deps(stream, reads, writes, waits)
        self.dcnt[q][i] += 1
        tok = ("d", sem, 16 * self.dcnt[q][i], q)
        self._commit(stream, tok, reads, writes, waits)
        fn = lambda e, out=out, in_=in_, kw=kw: e.dma_start(out=out, in_=in_, **kw)
        self.streams[stream].append([list(waits.values()), fn, tok])
        self.nops += 1
        return tok

    def barrier(self):
        toks = []
        for q in QUEUES:
            for i, sem in enumerate(self.dsem[q]):
                if self.dcnt[q][i]:
                    toks.append(("d", sem, 16 * self.dcnt[q][i], q))
        for e in COMPUTE:
            if self.nop_eng[e]:
                toks.append(("c", e, self.nop_eng[e]))
        for stream in self.streams:
            waits = {}
            for t in toks:
                if t[0] == "c" and t[1] == stream:
                    continue
                self._need(stream, t, waits)
            for k, t in waits.items():
                self.seen[stream][k] = self._tval(t)
                if t[0] == "c":
                    self.waited[t[1]].add(t[2])
            if waits:
                self.streams[stream].append([list(waits.values()), None, None])
        self.lastw = {}
        self.readers = {}

    def emit(self):
        nc = self.nc
        streams = self.streams
        rank = {}
        for e in COMPUTE:
            rank[e] = {idx: r + 1 for r, idx in enumerate(sorted(self.waited[e]))}

        def run(eng_obj, lst):
            for waits, fn, tok in lst:
                for t in waits:
                    if t[0] == "c":
                        eng_obj.wait_ge(self.sem[t[1]], rank[t[1]][t[2]])
                    else:
                        eng_obj.wait_ge(t[1], t[2])
                if fn is not None:
                    ins = fn(eng_obj)
                    if tok[0] == "d":
                        ins.then_inc(tok[1], 16)
                    elif tok[2] in rank[tok[1]]:
                        ins.then_inc(self.sem[tok[1]], 1)

        with nc.Block() as block:
            @block.tensor
            def _(e):
                run(e, streams["pe"])

            @block.scalar
            def _(e):
                run(e, streams["act"])

            @block.vector
            def _(e):
                run(e, streams["dve"])

            @block.gpsimd
            def _(e):
                run(e, streams["pool"])

            @block.sync
            def _(e):
                run(e, streams["sp"])

    def close(self):
        self.es.close()


class _Rec:
    def __init__(self):
        self.call = None

    def __getattr__(self, name):
        def f(*args, **kwargs):
            self.call = (name, args, kwargs)
            return self
        return f


class Arena:
    def __init__(self, P, words):
        self.t = P.sbuf("arena", [128, words], F32)
        self.words = words
        self.off = 0
        self.n = 0

    def reset(self):
        self.off = 0

    def f32(self, n):
        assert self.off + n <= self.words, ("arena overflow", self.off, n, self.words)
        ap = self.t[:, self.off:self.off + n]
        self.off += n
        return ap

    def bf16(self, n):
        w = (n + 1) // 2
        return self.f32(w).bitcast(BF16)[:, 0:n]


class Rot:
    def __init__(self, items):
        self.items = items
        self.i = 0

    def next(self):
        it = self.items[self.i % len(self.items)]
        self.i += 1
        return it


def build(LL, LC, depth, debug=False):
    TT = LL + LC
    NT = TT // 128
    NCH = TT // 64
    NC8 = TT // 8
    LC8 = LC // 8
    ROWS = LL // 64
    RB = ROWS // 8
    nc = bass.Bass("TRN2", target_bir_lowering=False)
    P = Prog(nc)

    def din(name, shape, dt=F32):
        return nc.dram_tensor(name, list(shape), dt, kind="ExternalInput").ap()

    dkind = "ExternalOutput" if debug else "Internal"

    def dscr(name, shape, dt=F32):
        return nc.dram_tensor(name, list(shape), dt, kind=dkind).ap()

    xin = din("xin", [TT, D])
    cvec = din("cvec", [128, KT, 2])
    w_mod = din("w_mod", [depth, D, 6 * D])
    b_mod = din("b_mod", [depth, 6 * D])
    norm1 = din("norm1", [depth, D])
    norm2 = din("norm2", [depth, D])
    w_in = din("w_in", [depth, D, 2336])
    gla_up_w = din("gla_up_w", [depth, 2, 16, 256])
    pp = din("pp", [depth, 128, 64])
    lru_wa = din("lru_wa", [depth, 2, 4, 64, 64])
    lru_wx = din("lru_wx", [depth, 2, 4, 64, 64])
    s5p = din("s5p", [depth, 128, 3, 16])
    s5b = din("s5b", [depth, 128, 2, 16, 16])
    s5c = din("s5c", [depth, 128, 2, 16, 16])
    s5_d = din("s5_d", [depth, 256])
    s5_glu_w = din("s5_glu_w", [depth, 256, 256])
    w_out = din("w_out", [depth, D, D])
    w_ff1 = din("w_ff1", [depth, D, 4 * D])
    w_ff2 = din("w_ff2", [depth, 4 * D, D])
    final_norm = din("final_norm", [1, D])
    tau_in = din("tau", [1, 2 * NC8])

    out_d = nc.dram_tensor("out", [LL, D], F32, kind="ExternalOutput").ap()

    mraw = dscr("mraw", [depth, 2, 6 * D])
    gsc = dscr("gsc", [depth, 2, 2, D])
    qT = dscr("qT", [256, TT]); kT = dscr("kT", [256, TT]); ggT = dscr("ggT", [512, TT])
    lrT = [dscr("lrT0", [16, TT]), dscr("lrT1", [16, TT])]
    lxT = dscr("lxT", [256, TT]); lgT = dscr("lgT", [256, TT])
    v_tok = dscr("v_tok", [TT, 512], BF16)
    su_tok = dscr("su_tok", [TT, 256])
    oT = dscr("oT", [512, TT])
    lruT = dscr("lruT", [256, TT])
    s5y = dscr("s5y", [TT, 256])
    x1 = dscr("x1", [TT, D])
    xres = dscr("xres", [TT, D])
    dbg = {}

    ident_f = P.sbuf("ident_f", [128, 128], F32)
    ident_b = P.sbuf("ident_b", [128, 128], BF16)
    ones_f = P.sbuf("ones_f", [128, 128], F32)
    ones_b = P.sbuf("ones_b", [128, 128], BF16)
    maskF = P.sbuf("maskF", [128, 128], F32)
    maskB = P.sbuf("maskB", [128, 128], F32)
    m8F = P.sbuf("m8F", [128, 512], F32)
    m8B = P.sbuf("m8B", [128, 512], F32)
    jidx = P.sbuf("jidx", [128, 2, 8, 9], F32)
    ppsb = P.sbuf("ppsb", [128, 64], F32)
    AW = 50000
    A = Arena(P, AW)
    PS = [P.psum("ps%d" % i, [128, 512], F32) for i in range(8)]

    def setup_consts():
        P.op("pool", lambda e: e.memset(ident_f[:], 0.0), writes=["ident_f"])
        P.op("pool", lambda e: e.affine_select(out=ident_f[:], in_=ident_f[:], pattern=[[-1, 128]], compare_op=ALU.not_equal,
                                               fill=1.0, base=0, channel_multiplier=1), reads=["ident_f"], writes=["ident_f"])
        P.op("dve", lambda e: e.tensor_copy(out=ident_b[:], in_=ident_f[:]), reads=["ident_f"], writes=["ident_b"])
        P.op("dve", lambda e: e.memset(ones_f[:], 1.0), writes=["ones_f"])
        P.op("dve", lambda e: e.memset(ones_b[:], 1.0), writes=["ones_b"])
        P.op("pool", lambda e: e.affine_select(out=maskF[:], in_=ones_f[:], pattern=[[1, 128]], compare_op=ALU.is_ge,
                                               fill=0.0, base=0, channel_multiplier=-1), reads=["ones_f"], writes=["maskF"])
        P.op("pool", lambda e: e.memset(maskF[0:64, 64:128], 0.0), reads=["maskF"], writes=["maskF"])
        P.op("pool", lambda e: e.affine_select(out=maskB[:], in_=ones_f[:], pattern=[[-1, 128]], compare_op=ALU.is_ge,
                                               fill=0.0, base=0, channel_multiplier=1), reads=["ones_f"], writes=["maskB"])
        P.op("pool", lambda e: e.memset(maskB[64:128, 0:64], 0.0), reads=["maskB"], writes=["maskB"])
        P.op("pool", lambda e: e.memset(m8F[:], 1.0), writes=["m8F"])
        P.op("pool", lambda e: e.memset(m8B[:], 1.0), writes=["m8B"])
        P.op("pool", lambda e: e.affine_select(out=m8F[:].rearrange("p (r j h) -> p r j h", r=4, j=8), in_=m8F[:].rearrange("p (r j h) -> p r j h", r=4, j=8),
                                               pattern=[[0, 4], [16, 8], [0, 16]], compare_op=ALU.is_ge, fill=0.0, base=15, channel_multiplier=-1),
             reads=["m8F"], writes=["m8F"])
        P.op("pool", lambda e: e.affine_select(out=m8B[:].rearrange("p (r j h) -> p r j h", r=4, j=8), in_=m8B[:].rearrange("p (r j h) -> p r j h", r=4, j=8),
                                               pattern=[[0, 4], [-16, 8], [0, 16]], compare_op=ALU.is_ge, fill=0.0, base=0, channel_multiplier=1),
             reads=["m8B"], writes=["m8B"])
        for j in range(9):
            P.op("dve", lambda e, j=j: e.memset(jidx[:, 0, :, j:j + 1], float(j)), writes=["jidx"])
            P.op("dve", lambda e, j=j: e.memset(jidx[:, 1, :, j:j + 1], float(7 - j) if j < 8 else 8.0), writes=["jidx"])

    PP_UPB = 0
    PP_GNORM = 4
    PP_CONVW = 8
    PP_CONVB = 24
    PP_BA = 28
    PP_BX = 32
    PP_LAM = 36
    PP_GLUB = 40
    PP_SGN = 42

    def modulation(l):
        A.reset()
        cs_raw = A.f32(16); cs = A.f32(16)
        msb = A.f32(6 * D)
        bm = A.f32(6 * D)
        n12 = A.f32(2 * D)
        gt = A.f32(2 * D)
        wblk = [A.f32(KT * 512), A.f32(KT * 512)]
        P.dma("sp", cs_raw, cvec.rearrange("p k w -> p (k w)"), writes=["cs_raw"])
        P.op("act", lambda e: e.activation(out=cs, in_=cs_raw, func=AF.Silu), reads=["cs_raw"], writes=["cs"])
        P.dma("sp", bm[0:2, :], b_mod[l:l + 1, :].partition_broadcast(2), writes=["bm"])
        P.dma("sp", n12[0:2, 0:D], norm1[l:l + 1, :].partition_broadcast(2), writes=["n12a"])
        P.dma("sp", n12[0:2, D:2 * D], norm2[l:l + 1, :].partition_broadcast(2), writes=["n12b"])
        wv = w_mod[l].rearrange("(kt p) n -> p kt n", p=128)
        cs3 = cs.rearrange("p (k w) -> p k w", w=2)
        for j in range(12):
            wb = wblk[j % 2]
            wb3 = wb.rearrange("p (k n) -> p k n", k=KT)
            P.dma("sp", wb3, wv[:, :, j * 512:(j + 1) * 512], writes=["wblk%d" % (j % 2)])
            ps = PS[j % 2]
            for kt in range(KT):
                P.op("pe", lambda e, ps=ps, kt=kt, wb3=wb3: e.matmul(ps[0:2, :], lhsT=cs3[:, kt, :], rhs=wb3[:, kt, :], start=(kt == 0), stop=(kt == KT - 1)),
                     reads=["cs", "wblk%d" % (j % 2)], writes=["ps%d" % (j % 2)])
            P.op("dve", lambda e, ps=ps, j=j: e.tensor_tensor(out=msb[0:2, j * 512:(j + 1) * 512], in0=ps[0:2, :], in1=bm[0:2, j * 512:(j + 1) * 512], op=ALU.add),
                 reads=["bm"], writes=["ps%d" % (j % 2), "msb"])
        P.op("dve", lambda e: e.scalar_tensor_tensor(out=gt[0:2, 0:D], in0=msb[0:2, D:2 * D], scalar=1.0, in1=n12[0:2, 0:D], op0=ALU.add, op1=ALU.mult),
             reads=["msb", "n12a"], writes=["gt"])
        P.op("dve", lambda e: e.scalar_tensor_tensor(out=gt[0:2, D:2 * D], in0=msb[0:2, 4 * D:5 * D], scalar=1.0, in1=n12[0:2, D:2 * D], op0=ALU.add, op1=ALU.mult),
             reads=["msb", "n12b", "gt"], writes=["gt"])
        P.dma("sp", mraw[l], msb[0:2, :], reads=["msb"], writes=["mraw"])
        P.dma("sp", gsc[l].rearrange("w g d -> w (g d)"), gt[0:2, :], reads=["gt"], writes=["gsc"])

    def bc_load(dst, src_row):
        return src_row.to_broadcast([128, src_row.shape[-1]])

    def token_blocks(last_skip_ctx=False):
        blks = []
        if not last_skip_ctx:
            t = 0
            while t < LC // 128:
                n = min(4, LC // 128 - t)
                blks.append((t, n, 1))
                t += n
        t = LC // 128
        while t < NT:
            n = min(4, NT - t)
            blks.append((t, n, 0))
            t += n
        return blks

    def rmsnorm_mod(xt_ap, Gbc, Sbc, hb_out, sfx, junk, ss, rs, hf):
        P.op("act", lambda e: e.activation(out=junk, in_=xt_ap, func=AF.Square, accum_out=ss), reads=["xt" + sfx], writes=["junk", "ss" + sfx])
        P.op("act", lambda e: e.activation(out=rs, in_=ss, func=AF.Sqrt, scale=1.0 / D, bias=EPS), reads=["ss" + sfx], writes=["rs" + sfx])
        P.op("dve", lambda e: e.reciprocal(out=rs, in_=rs), reads=["rs" + sfx], writes=["rs" + sfx])
        P.op("dve", lambda e: e.scalar_tensor_tensor(out=hf, in0=xt_ap, scalar=rs, in1=Gbc, op0=ALU.mult, op1=ALU.mult),
             reads=["xt" + sfx, "rs" + sfx, "bc"], writes=["hf"])
        P.op("pool", lambda e: e.tensor_tensor(out=hb_out, in0=hf, in1=Sbc, op=ALU.add), reads=["hf", "bc"], writes=["hb" + sfx])

    def transpose_to(hb, hT3, j, sfx, psrot):
        pi = psrot.next()
        pst = PS[pi][:].bitcast(BF16)
        for kt in range(KT):
            P.op("pe", lambda e, kt=kt, pst=pst: e.transpose(pst[:, kt * 128:(kt + 1) * 128], hb[:, kt * 128:(kt + 1) * 128], ident_b[:]),
                 reads=["hb" + sfx, "ident_b"], writes=["ps%d" % pi])
        P.op("act", lambda e, pst=pst: e.activation(out=hT3[:, :, j * 128:(j + 1) * 128], in_=pst.rearrange("p (k t) -> p k t", k=KT), func=AF.Identity),
             reads=[], writes=["ps%d" % pi, "hT"])

    def phaseA(l):
        A.reset()
        xsrc = xin if l == 0 else xres
        win = A.bf16(KT * 2336).rearrange("p (k n) -> p k n", k=KT)
        wv = w_in[l].rearrange("(kt p) n -> p kt n", p=128)
        for c0 in range(0, 2336, 512):
            c1 = min(2336, c0 + 512)
            P.dma("poolq", win[:, :, c0:c1], wv[:, :, c0:c1], writes=["win"])
        bcs = {}
        for w in (0, 1):
            G = A.f32(D); S = A.f32(D)
            P.dma("sp", G, gsc[l, w, 0:1, :].partition_broadcast(128), reads=["gsc"], writes=["bc"])
            P.dma("sp", S, mraw[l, w:w + 1, 0:D].partition_broadcast(128), reads=["mraw"], writes=["bc"])
            bcs[w] = (G, S)
        xts = [A.f32(D), A.f32(D)]
        hbs = [A.bf16(D), A.bf16(D)]
        junk = A.bf16(D)
        hf = A.f32(D)
        sss = [A.f32(1), A.f32(1)]; rss = [A.f32(1), A.f32(1)]
        hT = A.bf16(KT * 512).rearrange("p (k t) -> p k t", k=KT)
        stg = [A.f32(512) for _ in range(3)]
        vst = [A.bf16(512) for _ in range(2)]
        sst = [A.f32(256) for _ in range(2)]
        psT = Rot([0, 1]); psM = Rot([2, 3, 4, 5, 6, 7])
        stgR = Rot([0, 1, 2]); vR = Rot([0, 1]); sR = Rot([0, 1])
        FM = [("q", 0, qT, 0, 128), ("q", 128, qT, 128, 128), ("k", 256, kT, 0, 128), ("k", 384, kT, 128, 128)]
        for i in range(4):
            FM.append(("gg", 1024 + 128 * i, ggT, 128 * i, 128))
        FM.append(("lr0", 1536, lrT[0], 0, 16)); FM.append(("lr1", 1552, lrT[1], 0, 16))
        for i in range(2):
            FM.append(("lx", 1568 + 128 * i, lxT, 128 * i, 128))
        for i in range(2):
            FM.append(("lg", 1824 + 128 * i, lgT, 128 * i, 128))
        tcount = 0
        for (t0, n, w) in token_blocks():
            G, S = bcs[w]
            ntok = n * 128
            for j in range(n):
                ti = t0 + j
                s = tcount % 2; tcount += 1
                sfx = str(s)
                P.dma("sp", xts[s], xsrc[ti * 128:(ti + 1) * 128, :], reads=["xsrc"], writes=["xt" + sfx])
                rmsnorm_mod(xts[s], G, S, hbs[s], sfx, junk, sss[s], rss[s], hf)
                transpose_to(hbs[s], hT, j, sfx, psT)
            tok0 = t0 * 128
            for (nm, c0, dst, r0, m) in FM:
                pi = psM.next()
                for kt in range(KT):
                    P.op("pe", lambda e, pi=pi, kt=kt, c0=c0, m=m: e.matmul(PS[pi][0:m, 0:ntok], lhsT=win[:, kt, c0:c0 + m], rhs=hT[:, kt, 0:ntok],
                                                                         start=(kt == 0), stop=(kt == KT - 1)),
                         reads=["win", "hT"], writes=["ps%d" % pi])
                si = stgR.next()
                P.op("act", lambda e, pi=pi, si=si, m=m: e.activation(out=stg[si][0:m, 0:ntok], in_=PS[pi][0:m, 0:ntok], func=AF.Identity),
                     reads=[], writes=["ps%d" % pi, "stg%d" % si])
                P.dma("sp", dst[r0:r0 + m, tok0:tok0 + ntok], stg[si][0:m, 0:ntok], reads=["stg%d" % si], writes=["zT_" + nm])
            for j in range(n):
                ti = t0 + j
                pi = psM.next()
                for kt in range(KT):
                    P.op("pe", lambda e, pi=pi, kt=kt, j=j: e.matmul(PS[pi][:, :], lhsT=hT[:, kt, j * 128:(j + 1) * 128], rhs=win[:, kt, 512:1024],
                                                                  start=(kt == 0), stop=(kt == KT - 1)),
                         reads=["win", "hT"], writes=["ps%d" % pi])
                vi = vR.next()
                P.op("dve", lambda e, pi=pi, vi=vi: e.tensor_copy(out=vst[vi], in_=PS[pi][:, :]), reads=[], writes=["ps%d" % pi, "vst%d" % vi])
                P.dma("sp", v_tok[ti * 128:(ti + 1) * 128, :], vst[vi], reads=["vst%d" % vi], writes=["v_tok"])
                pi = psM.next()
                for kt in range(KT):
                    P.op("pe", lambda e, pi=pi, kt=kt, j=j: e.matmul(PS[pi][:, 0:256], lhsT=hT[:, kt, j * 128:(j + 1) * 128], rhs=win[:, kt, 2080:2336],
                                                                  start=(kt == 0), stop=(kt == KT - 1)),
                         reads=["win", "hT"], writes=["ps%d" % pi])
                si = sR.next()
                P.op("dve", lambda e, pi=pi, si=si: e.tensor_copy(out=sst[si], in_=PS[pi][:, 0:256]), reads=[], writes=["ps%d" % pi, "sst%d" % si])
                P.dma("sp", su_tok[ti * 128:(ti + 1) * 128, :], sst[si], reads=["sst%d" % si], writes=["su_tok"])

    def load_pp(l):
        P.dma("sp", ppsb[:], pp[l], writes=["ppsb"])

    def gla(l):
        for hp in range(2):
            A.reset()
            sm0 = A.bf16(TT); sm1 = A.bf16(TT)
            P.op("pool", lambda e: e.memset(sm0, 1.0), writes=["sm"])
            P.op("pool", lambda e: e.memset(sm0.rearrange("p (c j) -> p c j", j=64)[:, :, 0:1], 0.0), reads=["sm"], writes=["sm"])
            P.op("pool", lambda e: e.memset(sm1, 1.0), reads=["sm"], writes=["sm"])
            P.op("pool", lambda e: e.memset(sm1.rearrange("p (c j) -> p c j", j=64)[:, :, 63:64], 0.0), reads=["sm"], writes=["sm"])
            vt = A.bf16(NT * 256).rearrange("p (t c) -> p t c", t=NT)
            vsrc = v_tok.rearrange("(t p) c -> p t c", p=128)
            for t0_ in range(0, NT, 8):
                t1_ = min(NT, t0_ + 8)
                P.dma("sp", vt[:, t0_:t1_, :], vsrc[:, t0_:t1_, hp * 256:(hp + 1) * 256], reads=["v_tok"], writes=["vt"])
            oacc = A.f32(2 * TT).rearrange("p (h t) -> p h t", h=2)
            lrsb = A.f32(TT); Bp = A.f32(TT); Bc = A.f32(TT); qk = A.f32(TT)
            qd = A.bf16(TT); ki = A.bf16(TT)
            kiT = A.bf16(NT * 128).rearrange("p (t c) -> p t c", t=NT)
            upw = A.f32(256); nb = A.f32(1)
            gam = A.f32(NCH)
            S = A.f32(128); Sb = A.bf16(128); tmp = A.f32(128)
            sT = [A.bf16(128), A.bf16(128)]
            for d in range(2):
                mask = maskF if d == 0 else maskB
                sm = sm0 if d == 0 else sm1
                P.dma("sp", lrsb[0:16, :], lrT[d], reads=["zT_lr%d" % d], writes=["lrsb"])
                P.dma("sp", upw[0:16, :], gla_up_w[l, d], writes=["upw"])
                P.op("dve", lambda e, d=d: e.tensor_scalar(out=nb, in0=ppsb[:, PP_UPB + d * 2 + hp:PP_UPB + d * 2 + hp + 1], scalar1=-1.0, scalar2=None, op0=ALU.mult),
                     reads=["ppsb"], writes=["nb"])
                psr = Rot([0, 1])
                for b0 in range(0, TT, 512):
                    n = min(512, TT - b0)
                    pi = psr.next()
                    P.op("pe", lambda e, pi=pi, b0=b0, n=n: e.matmul(PS[pi][:, 0:n], lhsT=upw[0:16, hp * 128:(hp + 1) * 128], rhs=lrsb[0:16, b0:b0 + n], start=True, stop=True),
                         reads=["upw", "lrsb"], writes=["ps%d" % pi])
                    P.op("act", lambda e, pi=pi, b0=b0, n=n: e.activation(out=Bc[:, b0:b0 + n], in_=PS[pi][:, 0:n], func=AF.Exp, scale=-1.0, bias=nb),
                         reads=["nb"], writes=["ps%d" % pi, "Bc"])
                P.op("act", lambda e: e.activation(out=Bp, in_=Bc, func=AF.Ln, bias=1.0, scale=1.0), reads=["Bc"], writes=["Bp"])
                if d == 0:
                    P.op("dve", lambda e, sm=sm: e.tensor_tensor_scan(out=Bc, data0=sm, data1=Bp, initial=0.0, op0=ALU.mult, op1=ALU.add),
                         reads=["Bp", "sm"], writes=["Bc"])
                else:
                    P.op("dve", lambda e, sm=sm: e.tensor_tensor_scan(out=Bc[:, ::-1], data0=sm[:, ::-1], data1=Bp[:, ::-1], initial=0.0, op0=ALU.mult, op1=ALU.add),
                         reads=["Bp", "sm"], writes=["Bc"])
                Bc3 = Bc.rearrange("p (c j) -> p c j", j=64)
                endj = 63 if d == 0 else 0
                P.op("act", lambda e, endj=endj: e.activation(out=gam, in_=Bc3[:, :, endj], func=AF.Exp, scale=-1.0 / 16.0), reads=["Bc"], writes=["gam"])
                P.dma("sp", qk, qT[hp * 128:(hp + 1) * 128, :], reads=["zT_q"], writes=["qk"])
                P.op("act", lambda e: e.activation(out=Bp, in_=Bc, func=AF.Exp, scale=-1.0 / 16.0), reads=["Bc"], writes=["Bp"])
                P.op("dve", lambda e: e.scalar_tensor_tensor(out=qd, in0=qk, scalar=0.125, in1=Bp, op0=ALU.mult, op1=ALU.mult), reads=["qk", "Bp"], writes=["qd"])
                P.dma("sp", qk, kT[hp * 128:(hp + 1) * 128, :], reads=["zT_k"], writes=["qk"])
                P.op("act", lambda e: e.activation(out=Bp, in_=Bc, func=AF.Exp, scale=1.0 / 16.0), reads=["Bc"], writes=["Bp"])
                P.op("dve", lambda e: e.tensor_tensor(out=ki, in0=qk, in1=Bp, op=ALU.mult), reads=["qk", "Bp"], writes=["ki"])
                psr = Rot([0, 1])
                for t in range(NT):
                    pi = psr.next()
                    pst = PS[pi][:].bitcast(BF16)
                    P.op("pe", lambda e, pst=pst, t=t: e.transpose(pst[:, 0:128], ki[:, t * 128:(t + 1) * 128], ident_b[:]), reads=["ki", "ident_b"], writes=["ps%d" % pi])
                    P.op("act", lambda e, pst=pst, t=t: e.activation(out=kiT[:, t, :], in_=pst[:, 0:128], func=AF.Identity), reads=[], writes=["ps%d" % pi, "kiT"])
                P.op("dve", lambda e: e.memset(S, 0.0), writes=["S"])
                P.op("dve", lambda e: e.memset(Sb, 0.0), writes=["Sb"])
                ctx_t = list(range(LC // 128)); lat_t = list(range(LC // 128, NT))
                order = ctx_t + lat_t if d == 0 else ctx_t[::-1] + lat_t[::-1]
                corder = (0, 1) if d == 0 else (1, 0)
                psS = Rot([0, 1]); psO = Rot([(2, 3), (4, 5)]); psD = Rot([6, 7])
                for t in order:
                    po = psO.next()
                    for hh in range(2):
                        pr = slice(hh * 64, (hh + 1) * 64)
                        pi = psS.next()
                        P.op("pe", lambda e, pi=pi, pr=pr, t=t: e.matmul(PS[pi][:, 0:128], lhsT=ki[pr, t * 128:(t + 1) * 128], rhs=qd[pr, t * 128:(t + 1) * 128], start=True, stop=True),
                             reads=["ki", "qd"], writes=["ps%d" % pi])
                        P.op("dve", lambda e, pi=pi, hh=hh, mask=mask: e.tensor_tensor(out=sT[hh], in0=PS[pi][:, 0:128], in1=mask[:], op=ALU.mult),
                             reads=["mask"], writes=["ps%d" % pi, "sT%d" % hh])
                        P.op("pe", lambda e, hh=hh, t=t, po=po: e.matmul(PS[po[hh]][:, 0:128], lhsT=vt[:, t, hh * 128:(hh + 1) * 128], rhs=sT[hh], start=True, stop=True),
                             reads=["vt", "sT%d" % hh], writes=["ps%d" % po[hh]])
                    for ci, cc in enumerate(corder):
                        ch = t * 2 + cc
                        cols = slice(t * 128 + cc * 64, t * 128 + cc * 64 + 64)
                        for hh in range(2):
                            pr = slice(hh * 64, (hh + 1) * 64)
                            P.op("pe", lambda e, hh=hh, pr=pr, cols=cols, cc=cc, po=po, ci=ci: e.matmul(PS[po[hh]][:, cc * 64:(cc + 1) * 64], lhsT=Sb[pr, :], rhs=qd[pr, cols],
                                                                                                  start=False, stop=False, skip_group_check=True),
                                 reads=["Sb", "qd"], writes=["ps%d" % po[hh]])
                        pd = psD.next()
                        jr = slice(cc * 64, (cc + 1) * 64)
                        for hh in range(2):
                            pr = slice(hh * 64, (hh + 1) * 64)
                            P.op("pe", lambda e, hh=hh, pr=pr, jr=jr, t=t, pd=pd: e.matmul(PS[pd][pr, 0:128], lhsT=kiT[jr, t, hh * 64:(hh + 1) * 64], rhs=vt[jr, t, hh * 128:(hh + 1) * 128],
                                                                                         start=True, stop=True),
                                 reads=["kiT", "vt"], writes=["ps%d" % pd])
                        P.op("dve", lambda e, pd=pd: e.tensor_tensor(out=tmp, in0=PS[pd][:, 0:128], in1=S, op=ALU.add), reads=["S"], writes=["ps%d" % pd, "tmp"])
                        P.op("dve", lambda e, ch=ch: e.tensor_scalar(out=S, in0=tmp, scalar1=gam[:, ch:ch + 1], scalar2=None, op0=ALU.mult), reads=["tmp", "gam"], writes=["S"])
                        P.op("act", lambda e, ch=ch: e.activation(out=Sb, in_=tmp, func=AF.Identity, scale=gam[:, ch:ch + 1]), reads=["tmp", "gam"], writes=["Sb"])
                    for hh in range(2):
                        if d == 0:
                            P.op("act", lambda e, hh=hh, t=t, po=po: e.activation(out=oacc[:, hh, t * 128:(t + 1) * 128], in_=PS[po[hh]][:, 0:128], func=AF.Identity),
                                 reads=[], writes=["ps%d" % po[hh], "oacc"])
                        else:
                            P.op("dve", lambda e, hh=hh, t=t, po=po: e.tensor_tensor(out=oacc[:, hh, t * 128:(t + 1) * 128], in0=PS[po[hh]][:, 0:128], in1=oacc[:, hh, t * 128:(t + 1) * 128], op=ALU.add),
                                 reads=[], writes=["ps%d" % po[hh], "oacc"])
            for hh in range(2):
                P.dma("sp", oT[(hp * 2 + hh) * 128:(hp * 2 + hh + 1) * 128, :], oacc[:, hh, :], reads=["oacc"], writes=["oT"])
            P.barrier()

    def lru(l):
        A.reset()
        cst = A.f32(4); cst2 = A.f32(4)
        P.op("act", lambda e: e.activation(out=cst, in_=ppsb[:, PP_LAM:PP_LAM + 4], func=AF.Exp, scale=-1.0), reads=["ppsb"], writes=["cst"])
        P.op("act", lambda e: e.activation(out=cst2, in_=cst, func=AF.Ln, bias=1.0, scale=1.0), reads=["cst"], writes=["cst2"])
        P.op("dve", lambda e: e.tensor_scalar(out=cst, in0=cst2, scalar1=-8.0, scalar2=None, op0=ALU.mult), reads=["cst2"], writes=["cst"])
        x = A.f32(TT); xc = A.f32(TT); r = A.f32(TT); ig = A.f32(TT); a = A.f32(TT); hs = A.f32(TT); th = A.f32(TT)
        Wa = A.f32(128); Wx = A.f32(128)
        segs = [(0, LC), (LC, TT)]
        for ct in range(2):
            for d in range(2):
                col = d * 2 + ct
                P.dma("sp", x, lxT[ct * 128:(ct + 1) * 128, :], reads=["zT_lx"], writes=["x"])
                P.op("pool", lambda e: e.memset(Wa, 0.0), writes=["Wa"])
                P.op("pool", lambda e: e.memset(Wx, 0.0), writes=["Wx"])
                for bb in range(2):
                    P.dma("sp", Wa[bb * 64:(bb + 1) * 64, bb * 64:(bb + 1) * 64], lru_wa[l, d, ct * 2 + bb], writes=["Wa"], reads=["Wa"])
                    P.dma("sp", Wx[bb * 64:(bb + 1) * 64, bb * 64:(bb + 1) * 64], lru_wx[l, d, ct * 2 + bb], writes=["Wx"], reads=["Wx"])
                wcol = lambda k: ppsb[:, PP_CONVW + (d * 4 + k) * 2 + ct:PP_CONVW + (d * 4 + k) * 2 + ct + 1]
                bcol = ppsb[:, PP_CONVB + col:PP_CONVB + col + 1]
                P.op("dve", lambda e: e.tensor_scalar(out=xc, in0=x, scalar1=wcol(3), scalar2=bcol, op0=ALU.mult, op1=ALU.add), reads=["x", "ppsb"], writes=["xc"])
                for (s0, s1) in segs:
                    for sh in (1, 2, 3):
                        k = 3 - sh
                        if d == 0:
                            o_ap, i_ap = xc[:, s0 + sh:s1], x[:, s0:s1 - sh]
                        else:
                            o_ap, i_ap = xc[:, s0:s1 - sh], x[:, s0 + sh:s1]
                        P.op("dve", lambda e, o_ap=o_ap, i_ap=i_ap, k=k: e.scalar_tensor_tensor(out=o_ap, in0=i_ap, scalar=wcol(k), in1=o_ap, op0=ALU.mult, op1=ALU.add),
                             reads=["x", "ppsb", "xc"], writes=["xc"])
                psr = Rot([0, 1, 2, 3])
                for (Wm, dst, bc0, nm) in ((Wa, r, PP_BA, "r"), (Wx, ig, PP_BX, "ig")):
                    for b0 in range(0, TT, 512):
                        n = min(512, TT - b0)
                        pi = psr.next()
                        P.op("pe", lambda e, pi=pi, b0=b0, n=n, Wm=Wm: e.matmul(PS[pi][:, 0:n], lhsT=Wm, rhs=xc[:, b0:b0 + n], start=True, stop=True),
                             reads=["Wa", "Wx", "xc"], writes=["ps%d" % pi])
                        P.op("act", lambda e, pi=pi, b0=b0, n=n, dst=dst, bc0=bc0: e.activation(out=dst[:, b0:b0 + n], in_=PS[pi][:, 0:n], func=AF.Sigmoid,
                                                                                              bias=ppsb[:, bc0 + col:bc0 + col + 1]),
                             reads=["ppsb"], writes=["ps%d" % pi, nm])
                P.op("act", lambda e: e.activation(out=a, in_=r, func=AF.Exp, scale=cst[:, col:col + 1]), reads=["r", "cst"], writes=["a"])
                P.op("pool", lambda e: e.tensor_tensor(out=r, in0=a, in1=a, op=ALU.mult), reads=["a"], writes=["r"])
                P.op("act", lambda e: e.activation(out=r, in_=r, func=AF.Sqrt, scale=-1.0, bias=1.0), reads=["r"], writes=["r"])
                P.op("dve", lambda e: e.tensor_tensor(out=ig, in0=ig, in1=r, op=ALU.mult), reads=["ig", "r"], writes=["ig"])
                P.op("dve", lambda e: e.tensor_tensor(out=xc, in0=xc, in1=ig, op=ALU.mult), reads=["ig", "xc"], writes=["xc"])
                if d == 0:
                    P.op("dve", lambda e: e.tensor_tensor_scan(out=hs, data0=a, data1=xc, initial=0.0, op0=ALU.mult, op1=ALU.add), reads=["a", "xc"], writes=["hs"])
                else:
                    P.op("dve", lambda e: e.tensor_tensor_scan(out=th[:, 0:LC][:, ::-1], data0=a[:, 0:LC][:, ::-1], data1=xc[:, 0:LC][:, ::-1], initial=0.0,
                                                               op0=ALU.mult, op1=ALU.add), reads=["a", "xc"], writes=["th"])
                    P.op("dve", lambda e: e.tensor_tensor_scan(out=th[:, LC:TT][:, ::-1], data0=a[:, LC:TT][:, ::-1], data1=xc[:, LC:TT][:, ::-1], initial=th[:, 0:1],
                                                               op0=ALU.mult, op1=ALU.add), reads=["a", "xc", "th"], writes=["th"])
                    P.op("dve", lambda e: e.tensor_tensor(out=hs, in0=hs, in1=th, op=ALU.add), reads=["hs", "th"], writes=["hs"])
            P.dma("sp", lruT[ct * 128:(ct + 1) * 128, :], hs, reads=["hs"], writes=["lruT"])

    def s5(l):
        A.reset()
        NG = 16
        prm = A.f32(3 * NG).rearrange("p (a g) -> p a g", a=3)
        Bsb = A.f32(2 * NG * 16).rearrange("p (a g h) -> p a g h", a=2, g=NG)
        Csb = A.f32(2 * NG * 16).rearrange("p (a g h) -> p a g h", a=2, g=NG)
        P.dma("sp", prm, s5p[l], writes=["prm"])
        P.dma("sp", Bsb, s5b[l], writes=["Bsb"])
        P.dma("sp", Csb, s5c[l], writes=["Csb"])
        tauf = A.f32(2 * NC8)
        tau = tauf.rearrange("p (d c) -> p d c", d=2)
        P.dma("sp", tauf, tau_in.partition_broadcast(128), writes=["tau"])
        dt = A.f32(NG); lrdt = A.f32(NG); th = A.f32(NG); u8 = A.f32(NG); rho8 = A.f32(NG)
        t1 = A.f32(NG); t2 = A.f32(NG); t3 = A.f32(NG); den = A.f32(NG); cr = A.f32(NG); ci = A.f32(NG)
        J = jidx[:].rearrange("p d g j -> p (d g) j")
        mg = A.f32(NG * 9).rearrange("p (g j) -> p g j", j=9)
        xa = A.f32(NG * 9).rearrange("p (g j) -> p g j", j=9)
        xr = A.f32(NG * 9).rearrange("p (g j) -> p g j", j=9)
        sn = A.f32(NG * 9).rearrange("p (g j) -> p g j", j=9)
        cs_ = A.f32(NG * 9).rearrange("p (g j) -> p g j", j=9)
        ar = A.f32(NG * 9).rearrange("p (g j) -> p g j", j=9)
        ai = A.f32(NG * 9).rearrange("p (g j) -> p g j", j=9)
        mr = A.f32(NG * 8).rearrange("p (g j) -> p g j", j=8)
        mi = A.f32(NG * 8).rearrange("p (g j) -> p g j", j=8)
        br_ = A.f32(NG * 8).rearrange("p (g j) -> p g j", j=8)
        bi_ = A.f32(NG * 8).rearrange("p (g j) -> p g j", j=8)
        w1 = A.f32(NG * 9).rearrange("p (g j) -> p g j", j=9)
        K = ["prm", "s5t"]

        def op(eng, fn):
            P.op(eng, fn, reads=K, writes=["s5t"])

        def bg(v, n):
            return v.unsqueeze(2).to_broadcast([128, NG, n])

        op("act", lambda e: e.activation(out=dt, in_=prm[:, 2, :], func=AF.Exp))
        op("dve", lambda e: e.tensor_tensor(out=lrdt, in0=prm[:, 0, :], in1=dt, op=ALU.mult))
        op("dve", lambda e: e.tensor_tensor(out=th, in0=prm[:, 1, :], in1=dt, op=ALU.mult))
        op("dve", lambda e: e.tensor_scalar(out=th, in0=th, scalar1=1.0 / TWO_PI, scalar2=None, op0=ALU.mult))
        op("dve", lambda e: e.tensor_tensor(out=mg, in0=J, in1=bg(lrdt, 9), op=ALU.mult))
        op("act", lambda e: e.activation(out=mg, in_=mg, func=AF.Exp))
        op("dve", lambda e: e.tensor_tensor(out=xa, in0=J, in1=bg(th, 9), op=ALU.mult))
        op("dve", lambda e: e.tensor_scalar(out=xr, in0=xa, scalar1=MAGIC, scalar2=-MAGIC, op0=ALU.add, op1=ALU.add))
        op("dve", lambda e: e.tensor_tensor(out=w1, in0=xa, in1=xr, op=ALU.subtract))
        op("act", lambda e: e.activation(out=sn, in_=w1, func=AF.Sin, scale=TWO_PI))
        op("dve", lambda e: e.tensor_scalar(out=xa, in0=xa, scalar1=0.25, scalar2=None, op0=ALU.add))
        op("dve", lambda e: e.tensor_scalar(out=xr, in0=xa, scalar1=MAGIC, scalar2=-MAGIC, op0=ALU.add, op1=ALU.add))
        op("dve", lambda e: e.tensor_tensor(out=w1, in0=xa, in1=xr, op=ALU.subtract))
        op("act", lambda e: e.activation(out=cs_, in_=w1, func=AF.Sin, scale=TWO_PI))
        op("dve", lambda e: e.tensor_tensor(out=ar, in0=mg, in1=cs_, op=ALU.mult))
        op("dve", lambda e: e.tensor_tensor(out=ai, in0=mg, in1=sn, op=ALU.mult))
        op("dve", lambda e: e.tensor_tensor(out=w1, in0=mg, in1=mg, op=ALU.mult))
        op("dve", lambda e: e.reciprocal(out=w1, in_=w1))
        op("dve", lambda e: e.tensor_tensor(out=mr, in0=ar[:, :, 0:8], in1=w1[:, :, 0:8], op=ALU.mult))
        op("dve", lambda e: e.scalar_tensor_tensor(out=mi, in0=ai[:, :, 0:8], scalar=-1.0, in1=w1[:, :, 0:8], op0=ALU.mult, op1=ALU.mult))
        a1r = A.f32(NG); a1i = A.f32(NG)
        op("dve", lambda e: e.tensor_copy(out=a1r[:, 0:8], in_=ar[:, 0:8, 1]))
        op("dve", lambda e: e.tensor_copy(out=a1r[:, 8:16], in_=ar[:, 8:16, 6]))
        op("dve", lambda e: e.tensor_copy(out=a1i[:, 0:8], in_=ai[:, 0:8, 1]))
        op("dve", lambda e: e.tensor_copy(out=a1i[:, 8:16], in_=ai[:, 8:16, 6]))
        lr_ = prm[:, 0, :]; li_ = prm[:, 1, :]
        op("dve", lambda e: e.tensor_tensor(out=den, in0=lr_, in1=lr_, op=ALU.mult))
        op("dve", lambda e: e.tensor_tensor(out=t1, in0=li_, in1=li_, op=ALU.mult))
        op("dve", lambda e: e.tensor_tensor(out=den, in0=den, in1=t1, op=ALU.add))
        op("dve", lambda e: e.reciprocal(out=den, in_=den))
        op("dve", lambda e: e.tensor_scalar(out=t1, in0=a1r, scalar1=-1.0, scalar2=None, op0=ALU.add))
        op("dve", lambda e: e.tensor_tensor(out=t2, in0=t1, in1=lr_, op=ALU.mult))
        op("dve", lambda e: e.tensor_tensor(out=t3, in0=a1i, in1=li_, op=ALU.mult))
        op("dve", lambda e: e.tensor_tensor(out=t2, in0=t2, in1=t3, op=ALU.add))
        op("dve", lambda e: e.tensor_tensor(out=cr, in0=t2, in1=den, op=ALU.mult))
        op("dve", lambda e: e.tensor_tensor(out=t2, in0=a1i, in1=lr_, op=ALU.mult))
        op("dve", lambda e: e.tensor_tensor(out=t3, in0=t1, in1=li_, op=ALU.mult))
        op("dve", lambda e: e.tensor_tensor(out=t2, in0=t2, in1=t3, op=ALU.subtract))
        op("dve", lambda e: e.tensor_tensor(out=ci, in0=t2, in1=den, op=ALU.mult))
        w8a = A.f32(NG * 8).rearrange("p (g j) -> p g j", j=8)
        op("dve", lambda e: e.tensor_tensor(out=br_, in0=mr, in1=bg(cr, 8), op=ALU.mult))
        op("dve", lambda e: e.tensor_tensor(out=w8a, in0=mi, in1=bg(ci, 8), op=ALU.mult))
        op("dve", lambda e: e.tensor_tensor(out=br_, in0=br_, in1=w8a, op=ALU.subtract))
        op("dve", lambda e: e.tensor_tensor(out=bi_, in0=mr, in1=bg(ci, 8), op=ALU.mult))
        op("dve", lambda e: e.tensor_tensor(out=w8a, in0=mi, in1=bg(cr, 8), op=ALU.mult))
        op("dve", lambda e: e.tensor_tensor(out=bi_, in0=bi_, in1=w8a, op=ALU.add))
        op("dve", lambda e: e.tensor_copy(out=rho8, in_=mg[:, :, 8]))
        op("dve", lambda e: e.tensor_scalar(out=u8, in0=th, scalar1=8.0, scalar2=None, op0=ALU.mult))
        op("dve", lambda e: e.tensor_scalar(out=t1, in0=u8, scalar1=MAGIC, scalar2=-MAGIC, op0=ALU.add, op1=ALU.add))
        op("dve", lambda e: e.tensor_tensor(out=u8, in0=u8, in1=t1, op=ALU.subtract))
        SZ = NG * 8 * 16
        Btr = A.f32(SZ).rearrange("p (g j h) -> p g j h", g=NG, j=8)
        Bti = A.f32(SZ).rearrange("p (g j h) -> p g j h", g=NG, j=8)
        Ctr = A.f32(SZ).rearrange("p (g j h) -> p g j h", g=NG, j=8)
        Cti = A.f32(SZ).rearrange("p (g j h) -> p g j h", g=NG, j=8)
        wk = A.f32(SZ).rearrange("p (g j h) -> p g j h", g=NG, j=8)

        def bj(v):
            return v.unsqueeze(3).to_broadcast([128, NG, 8, 16])

        def bh(v):
            return v.unsqueeze(2).to_broadcast([128, NG, 8, 16])

        Br, Bi = Bsb[:, 0], Bsb[:, 1]
        Cr, Ci = Csb[:, 0], Csb[:, 1]
        KB = ["s5t", "Bsb", "Csb", "s5m"]

        def opb(fn):
            P.op("dve", fn, reads=KB, writes=["s5m"])

        opb(lambda e: e.tensor_tensor(out=Btr, in0=bj(br_), in1=bh(Br), op=ALU.mult))
        opb(lambda e: e.tensor_tensor(out=wk, in0=bj(bi_), in1=bh(Bi), op=ALU.mult))
        opb(lambda e: e.tensor_tensor(out=Btr, in0=Btr, in1=wk, op=ALU.subtract))
        opb(lambda e: e.tensor_tensor(out=Bti, in0=bj(br_), in1=bh(Bi), op=ALU.mult))
        opb(lambda e: e.tensor_tensor(out=wk, in0=bj(bi_), in1=bh(Br), op=ALU.mult))
        opb(lambda e: e.tensor_tensor(out=Bti, in0=Bti, in1=wk, op=ALU.add))
        opb(lambda e: e.tensor_tensor(out=Ctr, in0=bj(ar[:, :, 0:8]), in1=bh(Cr), op=ALU.mult))
        opb(lambda e: e.tensor_tensor(out=wk, in0=bj(ai[:, :, 0:8]), in1=bh(Ci), op=ALU.mult))
        opb(lambda e: e.tensor_tensor(out=Ctr, in0=Ctr, in1=wk, op=ALU.subtract))
        opb(lambda e: e.tensor_tensor(out=Cti, in0=bj(ai[:, :, 0:8]), in1=bh(Cr), op=ALU.mult))
        opb(lambda e: e.tensor_tensor(out=wk, in0=bj(ar[:, :, 0:8]), in1=bh(Ci), op=ALU.mult))
        opb(lambda e: e.tensor_tensor(out=Cti, in0=Cti, in1=wk, op=ALU.add))
        opb(lambda e: e.tensor_scalar(out=Cti, in0=Cti, scalar1=-1.0, scalar2=None, op0=ALU.mult))
        NCT = (NC8 + 127) // 128
        U8 = A.f32(16 * NC8).rearrange("p (g c) -> p g c", g=16)
        cst_ = [A.f32(8 * 256), A.f32(8 * 256)]
        Ug = A.f32(8 * 256)
        Yst = cst_

        def chunk_tiles():
            tiles = []
            c = 0
            while c < NC8:
                n = min(128, NC8 - c)
                tiles.append((c, n))
                c += n
            return tiles

        def chunk_dram(base, c0, n):
            pieces = []
            c = c0
            while c < c0 + n:
                if c < LC8:
                    m = min(c0 + n, LC8) - c
                    ap = base[c * 8:(c + m) * 8, :].rearrange("(c i) h -> c i h", i=8)
                    pieces.append((c - c0, m, ap))
                    c += m
                else:
                    cl = c - LC8
                    col, rb = cl // RB, cl % RB
                    m = min(RB - rb, c0 + n - c)
                    lat = base[LC:TT, :].rearrange("(rb i w) h -> w rb i h", i=8, w=64)
                    ap = lat[col, rb:rb + m, :, :]
                    pieces.append((c - c0, m, ap))
                    c += m
            return pieces

        psr = Rot([0, 1, 2, 3])
        for ti, (c0, n) in enumerate(chunk_tiles()):
            cs = cst_[ti % 2]
            cs3 = cs.rearrange("p (i h) -> p i h", i=8)
            for (p0, m, ap) in chunk_dram(su_tok, c0, n):
                P.dma("sp", cs3[p0:p0 + m, :, :], ap, reads=["su_tok"], writes=["cst%d" % (ti % 2)])
            P.op("dve", lambda e, n=n, cs=cs: e.tensor_copy(out=Ug[0:n].rearrange("p (g i h) -> p g i h", g=16, i=8),
                                                          in_=cs[0:n].rearrange("p (i g h) -> p g i h", i=8, g=16)),
                 reads=["cst%d" % (ti % 2)], writes=["Ug"])
            for g in range(16):
                pi = psr.next()
                P.op("pe", lambda e, pi=pi, g=g, n=n: e.transpose(PS[pi][:, 0:n], Ug[0:n, g * 128:(g + 1) * 128], ident_f[0:n, 0:n]),
                     reads=["Ug", "ident_f"], writes=["ps%d" % pi])
                P.op("act", lambda e, pi=pi, g=g, n=n, c0=c0: e.activation(out=U8[:, g, c0:c0 + n], in_=PS[pi][:, 0:n], func=AF.Identity),
                     reads=[], writes=["ps%d" % pi, "U8"])
        BtT = A.f32(2 * 128).rearrange("p (a s) -> p a s", a=2)
        M8 = A.f32(2 * 128).rearrange("p (g c) -> p g c", g=2)
        Zr = A.f32(NC8); Zi = A.f32(NC8); Wr = A.f32(NC8); Wi = A.f32(NC8); Or = A.f32(NC8); Oi = A.f32(NC8)
        Xr = A.f32(NC8); Xi = A.f32(NC8); tA = A.f32(NC8); tB = A.f32(NC8)
        Cn = A.f32(NC8); Sn = A.f32(NC8); xx = A.f32(NC8); rr = A.f32(NC8)
        Yall = A.f32(16 * NC8).rearrange("p (g c) -> p g c", g=16)
        halves = [(0, min(512, NC8))] + ([(512, NC8)] if NC8 > 512 else [])
        for d in range(2):
            m8 = m8F if d == 0 else m8B
            for gp in range(8):
                dg = d * 8 + gp
                psr = Rot([0, 1, 2, 3, 4, 5, 6, 7])
                for a_, Bt in enumerate((Btr, Bti)):
                    pi = psr.next()
                    P.op("pe", lambda e, pi=pi, Bt=Bt: e.transpose(PS[pi][:, 0:128], Bt[:, dg].rearrange("p j h -> p (j h)"), ident_f[:]),
                         reads=["s5m", "ident_f"], writes=["ps%d" % pi])
                    P.op("act", lambda e, pi=pi, a_=a_: e.activation(out=BtT[:, a_, :], in_=PS[pi][:, 0:128], func=AF.Identity), reads=[], writes=["ps%d" % pi, "BtT"])
                for gm in range(2):
                    pi = psr.next()
                    pr = slice(gm * 64, (gm + 1) * 64)
                    for a_, (Bt, Ct) in enumerate(((Btr, Ctr), (Bti, Cti))):
                        P.op("pe", lambda e, pi=pi, gm=gm, pr=pr, Bt=Bt, Ct=Ct, a_=a_: e.matmul(PS[pi][:, 0:128], lhsT=Bt[pr, dg].rearrange("p j h -> p (j h)"),
                                                                                             rhs=Ct[pr, dg].rearrange("p j h -> p (j h)"), start=(a_ == 0), stop=(a_ == 1)),
                             reads=["s5m"], writes=["ps%d" % pi])
                    P.op("dve", lambda e, pi=pi, m8=m8, gm=gm: e.tensor_tensor(out=M8[:, gm, :], in0=PS[pi][:, 0:128], in1=m8[:, 0:128], op=ALU.mult),
                         reads=["m8"], writes=["ps%d" % pi, "M8"])
                for (Zt, a_) in ((Zr, 0), (Zi, 1)):
                    for (h0, h1) in halves:
                        pi = psr.next()
                        for gm in range(2):
                            g = gp * 2 + gm
                            P.op("pe", lambda e, pi=pi, gm=gm, g=g, a_=a_, h0=h0, h1=h1: e.matmul(PS[pi][gm * 64:(gm + 1) * 64, 0:h1 - h0], lhsT=BtT[:, a_, gm * 64:(gm + 1) * 64],
                                                                                               rhs=U8[:, g, h0:h1], start=True, stop=True),
                                 reads=["BtT", "U8"], writes=["ps%d" % pi])
                        P.op("act", lambda e, pi=pi, Zt=Zt, h0=h0, h1=h1: e.activation(out=Zt[:, h0:h1], in_=PS[pi][:, 0:h1 - h0], func=AF.Identity),
                             reads=[], writes=["ps%d" % pi, "Z"])
                ucol = u8[:, dg:dg + 1]
                P.op("dve", lambda e, ucol=ucol: e.tensor_scalar(out=xx, in0=tau[:, d, :], scalar1=ucol, scalar2=None, op0=ALU.mult), reads=["tau", "s5t"], writes=["xx"])
                P.op("dve", lambda e: e.tensor_scalar(out=rr, in0=xx, scalar1=MAGIC, scalar2=-MAGIC, op0=ALU.add, op1=ALU.add), reads=["xx"], writes=["rr"])
                P.op("dve", lambda e: e.tensor_tensor(out=rr, in0=xx, in1=rr, op=ALU.subtract), reads=["xx", "rr"], writes=["rr"])
                P.op("act", lambda e: e.activation(out=Sn, in_=rr, func=AF.Sin, scale=TWO_PI), reads=["rr"], writes=["Sn"])
                P.op("dve", lambda e: e.tensor_scalar(out=xx, in0=xx, scalar1=0.25, scalar2=None, op0=ALU.add), reads=["xx"], writes=["xx"])
                P.op("dve", lambda e: e.tensor_scalar(out=rr, in0=xx, scalar1=MAGIC, scalar2=-MAGIC, op0=ALU.add, op1=ALU.add), reads=["xx", "Sn"], writes=["rr"])
                P.op("dve", lambda e: e.tensor_tensor(out=rr, in0=xx, in1=rr, op=ALU.subtract), reads=["xx", "rr"], writes=["rr"])
                P.op("act", lambda e: e.activation(out=Cn, in_=rr, func=AF.Sin, scale=TWO_PI), reads=["rr"], writes=["Cn"])
                P.op("dve", lambda e: e.tensor_tensor(out=Wr, in0=Cn, in1=Zr, op=ALU.mult), reads=["Cn", "Z"], writes=["Wr"])
                P.op("dve", lambda e: e.tensor_tensor(out=tA, in0=Sn, in1=Zi, op=ALU.mult), reads=["Sn", "Z"], writes=["tA"])
                P.op("dve", lambda e: e.tensor_tensor(out=Wr, in0=Wr, in1=tA, op=ALU.add), reads=["Wr", "tA"], writes=["Wr"])
                P.op("dve", lambda e: e.tensor_tensor(out=Wi, in0=Cn, in1=Zi, op=ALU.mult), reads=["Cn", "Z"], writes=["Wi"])
                P.op("dve", lambda e: e.tensor_tensor(out=tB, in0=Sn, in1=Zr, op=ALU.mult), reads=["Sn", "Z"], writes=["tB"])
                P.op("dve", lambda e: e.tensor_tensor(out=Wi, in0=Wi, in1=tB, op=ALU.subtract), reads=["Wi", "tB"], writes=["Wi"])
                rcol = rho8[:, dg:dg + 1]
                for (Wt, Ot, nm) in ((Wr, Or, "Or"), (Wi, Oi, "Oi")):
                    if d == 0:
                        P.op("dve", lambda e, Wt=Wt, Ot=Ot, rcol=rcol: e.tensor_tensor_scan(out=Ot, data0=Wt, data1=rcol.to_broadcast([128, NC8]), initial=0.0, op0=ALU.add, op1=ALU.mult),
                             reads=["Wr", "Wi", "s5t"], writes=[nm])
                    else:
                        P.op("dve", lambda e, Wt=Wt, Ot=Ot, rcol=rcol: e.tensor_tensor_scan(out=Ot[:, 0:LC8][:, ::-1], data0=Wt[:, 0:LC8][:, ::-1], data1=rcol.to_broadcast([128, LC8]),
                                                                                         initial=0.0, op0=ALU.add, op1=ALU.mult),
                             reads=["Wr", "Wi", "s5t"], writes=[nm])
                        P.op("dve", lambda e, Wt=Wt, Ot=Ot, rcol=rcol: e.tensor_tensor_scan(out=Ot[:, LC8:NC8][:, ::-1], data0=Wt[:, LC8:NC8][:, ::-1], data1=rcol.to_broadcast([128, NC8 - LC8]),
                                                                                         initial=Ot[:, 0:1], op0=ALU.add, op1=ALU.mult),
                             reads=["Wr", "Wi", "s5t", nm], writes=[nm])
                if d == 0:
                    sh = [(slice(1, NC8), slice(0, NC8 - 1))]
                    zero_cols = [0]
                    carry = None
                else:
                    sh = [(slice(0, LC8 - 1), slice(1, LC8)), (slice(LC8, NC8 - 1), slice(LC8 + 1, NC8))]
                    zero_cols = [LC8 - 1]
                    carry = (NC8 - 1, 0)
                RK = ["Or", "Oi", "Cn", "Sn", "X", "tA", "tB"]
                for (do, so) in sh:
                    P.op("dve", lambda e, do=do, so=so: e.tensor_tensor(out=Xr[:, do], in0=Cn[:, do], in1=Or[:, so], op=ALU.mult), reads=RK, writes=["X"])
                    P.op("dve", lambda e, do=do, so=so: e.tensor_tensor(out=tA[:, do], in0=Sn[:, do], in1=Oi[:, so], op=ALU.mult), reads=RK, writes=["tA"])
                    P.op("dve", lambda e, do=do, so=so: e.tensor_tensor(out=Xr[:, do], in0=Xr[:, do], in1=tA[:, do], op=ALU.subtract), reads=RK, writes=["X"])
                    P.op("dve", lambda e, do=do, so=so: e.tensor_tensor(out=Xi[:, do], in0=Cn[:, do], in1=Oi[:, so], op=ALU.mult), reads=RK, writes=["X"])
                    P.op("dve", lambda e, do=do, so=so: e.tensor_tensor(out=tB[:, do], in0=Sn[:, do], in1=Or[:, so], op=ALU.mult), reads=RK, writes=["tB"])
                    P.op("dve", lambda e, do=do, so=so: e.tensor_tensor(out=Xi[:, do], in0=Xi[:, do], in1=tB[:, do], op=ALU.add), reads=RK, writes=["X"])
                for zc in zero_cols:
                    P.op("dve", lambda e, zc=zc: e.memset(Xr[:, zc:zc + 1], 0.0), reads=RK, writes=["X"])
                    P.op("dve", lambda e, zc=zc: e.memset(Xi[:, zc:zc + 1], 0.0), reads=RK, writes=["X"])
                if carry is not None:
                    dc, sc = carry
                    do, so = slice(dc, dc + 1), slice(sc, sc + 1)
                    P.op("dve", lambda e, do=do, so=so: e.tensor_tensor(out=Xr[:, do], in0=Cn[:, do], in1=Or[:, so], op=ALU.mult), reads=RK, writes=["X"])
                    P.op("dve", lambda e, do=do, so=so: e.tensor_tensor(out=tA[:, do], in0=Sn[:, do], in1=Oi[:, so], op=ALU.mult), reads=RK, writes=["tA"])
                    P.op("dve", lambda e, do=do, so=so: e.tensor_tensor(out=Xr[:, do], in0=Xr[:, do], in1=tA[:, do], op=ALU.subtract), reads=RK, writes=["X"])
                    P.op("dve", lambda e, do=do, so=so: e.tensor_tensor(out=Xi[:, do], in0=Cn[:, do], in1=Oi[:, so], op=ALU.mult), reads=RK, writes=["X"])
                    P.op("dve", lambda e, do=do, so=so: e.tensor_tensor(out=tB[:, do], in0=Sn[:, do], in1=Or[:, so], op=ALU.mult), reads=RK, writes=["tB"])
                    P.op("dve", lambda e, do=do, so=so: e.tensor_tensor(out=Xi[:, do], in0=Xi[:, do], in1=tB[:, do], op=ALU.add), reads=RK, writes=["X"])
                for gm in range(2):
                    g = gp * 2 + gm
                    pr = slice(gm * 64, (gm + 1) * 64)
                    for (h0, h1) in halves:
                        pi = psr.next()
                        P.op("pe", lambda e, pi=pi, gm=gm, g=g, h0=h0, h1=h1: e.matmul(PS[pi][:, 0:h1 - h0], lhsT=M8[:, gm, :], rhs=U8[:, g, h0:h1], start=True, stop=False),
                             reads=["M8", "U8"], writes=["ps%d" % pi])
                        P.op("pe", lambda e, pi=pi, pr=pr, h0=h0, h1=h1: e.matmul(PS[pi][:, 0:h1 - h0], lhsT=Ctr[pr, dg].rearrange("p j h -> p (j h)"), rhs=Xr[pr, h0:h1], start=False, stop=False),
                             reads=["s5m", "X"], writes=["ps%d" % pi])
                        P.op("pe", lambda e, pi=pi, pr=pr, h0=h0, h1=h1: e.matmul(PS[pi][:, 0:h1 - h0], lhsT=Cti[pr, dg].rearrange("p j h -> p (j h)"), rhs=Xi[pr, h0:h1], start=False, stop=True),
                             reads=["s5m", "X"], writes=["ps%d" % pi])
                        if d == 0:
                            P.op("act", lambda e, pi=pi, g=g, h0=h0, h1=h1: e.activation(out=Yall[:, g, h0:h1], in_=PS[pi][:, 0:h1 - h0], func=AF.Identity),
                                 reads=[], writes=["ps%d" % pi, "Yall"])
                        else:
                            P.op("dve", lambda e, pi=pi, g=g, h0=h0, h1=h1: e.tensor_tensor(out=Yall[:, g, h0:h1], in0=PS[pi][:, 0:h1 - h0], in1=Yall[:, g, h0:h1], op=ALU.add),
                                 reads=[], writes=["ps%d" % pi, "Yall"])
        psr = Rot([0, 1, 2, 3])
        for ti, (c0, n) in enumerate(chunk_tiles()):
            ys = Yst[ti % 2]
            ys3 = ys.rearrange("p (i h) -> p i h", i=8)
            for g in range(16):
                pi = psr.next()
                P.op("pe", lambda e, pi=pi, g=g, n=n, c0=c0: e.transpose(PS[pi][0:n, 0:128], Yall[:, g, c0:c0 + n], ident_f[:]),
                     reads=["Yall", "ident_f"], writes=["ps%d" % pi])
                P.op("act", lambda e, pi=pi, g=g, n=n, ys3=ys3: e.activation(out=ys3[0:n, :, g * 16:(g + 1) * 16], in_=PS[pi][0:n, 0:128].rearrange("p (j h) -> p j h", j=8), func=AF.Identity),
                     reads=[], writes=["ps%d" % pi, "yst%d" % (ti % 2)])
            for (p0, m, ap) in chunk_dram(s5y, c0, n):
                P.dma("sp", ap, ys3[p0:p0 + m, :, :], reads=["yst%d" % (ti % 2)], writes=["s5y"])

    def phaseC1(l, last):
        A.reset()
        xsrc = xin if l == 0 else xres
        wo = A.bf16(KT * D).rearrange("p (k n) -> p k n", k=KT)
        wv = w_out[l].rearrange("(kt p) n -> p kt n", p=128)
        for c0 in range(0, D, 512):
            P.dma("poolq", wo[:, :, c0:c0 + 512], wv[:, :, c0:c0 + 512], writes=["wo"])
        glu = A.bf16(2 * 256).rearrange("p (k n) -> p k n", k=2)
        P.dma("poolq", glu, s5_glu_w[l].rearrange("(kt p) n -> p kt n", p=128), writes=["glu"])
        g1bc = {}
        for w in ((0,) if last else (0, 1)):
            g = A.f32(D)
            P.dma("sp", g, mraw[l, w:w + 1, 2 * D:3 * D].partition_broadcast(128), reads=["mraw"], writes=["bc"])
            g1bc[w] = g
        dbc = A.f32(256)
        P.dma("sp", dbc, s5_d[l:l + 1, :].partition_broadcast(128), writes=["bc"])
        NB = 256
        o4 = A.f32(4 * NB).rearrange("p (h t) -> p h t", h=4)
        g4 = A.f32(4 * NB).rearrange("p (h t) -> p h t", h=4)
        sq = A.bf16(4 * NB).rearrange("p (h t) -> p h t", h=4)
        rn = A.f32(4 * NB).rearrange("p (h t) -> p h t", h=4)
        lh = A.f32(2 * NB).rearrange("p (h t) -> p h t", h=2)
        lg = A.f32(2 * NB).rearrange("p (h t) -> p h t", h=2)
        cat = A.bf16(KT * NB).rearrange("p (k t) -> p k t", k=KT)
        ysb = A.f32(2 * 256).rearrange("p (j c) -> p j c", j=2)
        usb = A.f32(2 * 256).rearrange("p (j c) -> p j c", j=2)
        sb16 = A.bf16(2 * 256).rearrange("p (j c) -> p j c", j=2)
        sTt = A.bf16(2 * NB).rearrange("p (k t) -> p k t", k=2)
        gsig = A.f32(2 * NB).rearrange("p (k t) -> p k t", k=2)
        xt = [A.f32(D), A.f32(D)]
        xo = [A.f32(D), A.f32(D)]
        t0 = 0 if not last else LC
        psr = Rot([0, 1, 2, 3, 4, 5, 6, 7])
        while t0 < TT:
            w = 1 if t0 < LC else 0
            n = min(NB, (LC if w == 1 else TT) - t0)
            tk = slice(t0, t0 + n)
            P.dma("sp", o4[:, :, 0:n], oT.rearrange("(h p) t -> p h t", p=128)[:, :, tk], reads=["oT"], writes=["o4"])
            P.dma("sp", g4[:, :, 0:n], ggT.rearrange("(h p) t -> p h t", p=128)[:, :, tk], reads=["zT_gg"], writes=["g4"])
            P.op("pool", lambda e, n=n: e.tensor_tensor(out=sq[:, :, 0:n], in0=o4[:, :, 0:n], in1=o4[:, :, 0:n], op=ALU.mult), reads=["o4"], writes=["sq"])
            P.op("act", lambda e, n=n: e.activation(out=g4[:, :, 0:n], in_=g4[:, :, 0:n], func=AF.Silu), reads=["g4"], writes=["g4"])
            for h in range(4):
                pi = psr.next()
                P.op("pe", lambda e, pi=pi, h=h, n=n: e.matmul(PS[pi][:, 0:n], lhsT=ones_b[:], rhs=sq[:, h, 0:n], start=True, stop=True), reads=["sq", "ones_b"], writes=["ps%d" % pi])
                P.op("act", lambda e, pi=pi, h=h, n=n: e.activation(out=rn[:, h, 0:n], in_=PS[pi][:, 0:n], func=AF.Sqrt, scale=1.0 / 128.0, bias=EPS), reads=[], writes=["ps%d" % pi, "rn"])
            P.op("dve", lambda e, n=n: e.reciprocal(out=rn[:, :, 0:n], in_=rn[:, :, 0:n]), reads=["rn"], writes=["rn"])
            P.op("dve", lambda e, n=n: e.tensor_tensor(out=o4[:, :, 0:n], in0=o4[:, :, 0:n], in1=rn[:, :, 0:n], op=ALU.mult), reads=["o4", "rn"], writes=["o4"])
            for h in range(4):
                P.op("dve", lambda e, h=h, n=n: e.scalar_tensor_tensor(out=cat[:, h, 0:n], in0=o4[:, h, 0:n], scalar=ppsb[:, PP_GNORM + h:PP_GNORM + h + 1], in1=g4[:, h, 0:n],
                                                                       op0=ALU.mult, op1=ALU.mult), reads=["o4", "g4", "ppsb"], writes=["cat"])
            P.dma("sp", lh[:, :, 0:n], lruT.rearrange("(h p) t -> p h t", p=128)[:, :, tk], reads=["lruT"], writes=["lh"])
            P.dma("sp", lg[:, :, 0:n], lgT.rearrange("(h p) t -> p h t", p=128)[:, :, tk], reads=["zT_lg"], writes=["lg"])
            P.op("act", lambda e, n=n: e.activation(out=lg[:, :, 0:n], in_=lg[:, :, 0:n], func=AF.Gelu), reads=["lg"], writes=["lg"])
            P.op("dve", lambda e, n=n: e.tensor_tensor(out=cat[:, 4:6, 0:n], in0=lh[:, :, 0:n], in1=lg[:, :, 0:n], op=ALU.mult), reads=["lh", "lg"], writes=["cat"])
            nj = n // 128
            P.dma("sp", ysb[:, 0:nj, :], s5y[tk, :].rearrange("(j p) c -> p j c", p=128), reads=["s5y"], writes=["ysb"])
            P.dma("sp", usb[:, 0:nj, :], su_tok[tk, :].rearrange("(j p) c -> p j c", p=128), reads=["su_tok"], writes=["usb"])
            P.op("dve", lambda e, nj=nj: e.tensor_tensor(out=usb[:, 0:nj, :], in0=usb[:, 0:nj, :], in1=dbc.unsqueeze(1).to_broadcast([128, nj, 256]), op=ALU.mult), reads=["usb", "bc"], writes=["usb"])
            P.op("dve", lambda e, nj=nj: e.tensor_tensor(out=ysb[:, 0:nj, :], in0=ysb[:, 0:nj, :], in1=usb[:, 0:nj, :], op=ALU.add), reads=["usb", "ysb"], writes=["ysb"])
            P.op("act", lambda e, nj=nj: e.activation(out=sb16[:, 0:nj, :], in_=ysb[:, 0:nj, :], func=AF.Gelu), reads=["ysb"], writes=["sb16"])
            for j in range(nj):
                pi = psr.next()
                pst = PS[pi][:].bitcast(BF16)
                for k in range(2):
                    P.op("pe", lambda e, pst=pst, j=j, k=k: e.transpose(pst[:, k * 128:(k + 1) * 128], sb16[:, j, k * 128:(k + 1) * 128], ident_b[:]), reads=["sb16", "ident_b"], writes=["ps%d" % pi])
                P.op("act", lambda e, pst=pst, j=j: e.activation(out=sTt[:, :, j * 128:(j + 1) * 128], in_=pst[:, 0:256].rearrange("p (k t) -> p k t", k=2), func=AF.Identity),
                     reads=[], writes=["ps%d" % pi, "sTt"])
            for ko in range(2):
                pi = psr.next()
                for ki_ in range(2):
                    P.op("pe", lambda e, pi=pi, ko=ko, ki_=ki_, n=n: e.matmul(PS[pi][:, 0:n], lhsT=glu[:, ki_, ko * 128:(ko + 1) * 128], rhs=sTt[:, ki_, 0:n], start=(ki_ == 0), stop=(ki_ == 1)),
                         reads=["glu", "sTt"], writes=["ps%d" % pi])
                P.op("act", lambda e, pi=pi, ko=ko, n=n: e.activation(out=gsig[:, ko, 0:n], in_=PS[pi][:, 0:n], func=AF.Sigmoid, bias=ppsb[:, PP_GLUB + ko:PP_GLUB + ko + 1]),
                     reads=["ppsb"], writes=["ps%d" % pi, "gsig"])
            P.op("dve", lambda e, n=n: e.tensor_tensor(out=cat[:, 6:8, 0:n], in0=sTt[:, :, 0:n], in1=gsig[:, :, 0:n], op=ALU.mult), reads=["sTt", "gsig"], writes=["cat"])
            for j in range(nj):
                ti0 = t0 + j * 128
                xs = (ti0 // 128) % 2
                P.dma("sp", xt[xs], xsrc[ti0:ti0 + 128, :], reads=["xsrc"], writes=["xtc%d" % xs])
                for hf_ in range(2):
                    pi = psr.next()
                    for kt in range(KT):
                        P.op("pe", lambda e, pi=pi, kt=kt, j=j, hf_=hf_: e.matmul(PS[pi][:, :], lhsT=cat[:, kt, j * 128:(j + 1) * 128], rhs=wo[:, kt, hf_ * 512:(hf_ + 1) * 512],
                                                                               start=(kt == 0), stop=(kt == KT - 1)),
                             reads=["cat", "wo"], writes=["ps%d" % pi])
                    P.op("dve", lambda e, pi=pi, xs=xs, hf_=hf_, w=w: e.tensor_tensor(out=xo[xs][:, hf_ * 512:(hf_ + 1) * 512], in0=PS[pi][:, :], in1=g1bc[w][:, hf_ * 512:(hf_ + 1) * 512], op=ALU.mult),
                         reads=["bc"], writes=["ps%d" % pi, "xo%d" % xs])
                P.op("pool", lambda e, xs=xs: e.tensor_tensor(out=xo[xs], in0=xo[xs], in1=xt[xs], op=ALU.add), reads=["xtc%d" % xs, "xo%d" % xs], writes=["xo%d" % xs])
                P.dma("sp", x1[ti0:ti0 + 128, :], xo[xs], reads=["xo%d" % xs], writes=["x1"])
            t0 += n

    def phaseC2(l, last):
        A.reset()
        w1 = A.bf16(KT * 4 * D).rearrange("p (k n) -> p k n", k=KT)
        w2 = A.bf16(32 * D).rearrange("p (k n) -> p k n", k=32)
        w1v = w_ff1[l].rearrange("(kt p) n -> p kt n", p=128)
        w2v = w_ff2[l].rearrange("(kt p) n -> p kt n", p=128)
        for c0 in range(0, 4 * D, 512):
            P.dma("poolq", w1[:, :, c0:c0 + 512], w1v[:, :, c0:c0 + 512], writes=["w1"])
        for k0 in range(0, 32, 4):
            for c0 in range(0, D, 512):
                P.dma("poolq", w2[:, k0:k0 + 4, c0:c0 + 512], w2v[:, k0:k0 + 4, c0:c0 + 512], writes=["w2"])
        G = A.f32(D); S = A.f32(D); g2 = A.f32(D)
        cur_w = [None]

        def load_bc(w):
            if cur_w[0] == w:
                return
            cur_w[0] = w
            P.dma("sp", G, gsc[l, w, 1:2, :].partition_broadcast(128), reads=["gsc"], writes=["bc"])
            P.dma("sp", S, mraw[l, w:w + 1, 3 * D:4 * D].partition_broadcast(128), reads=["mraw"], writes=["bc"])
            P.dma("sp", g2, mraw[l, w:w + 1, 5 * D:6 * D].partition_broadcast(128), reads=["mraw"], writes=["bc"])
        if last:
            fn = A.f32(D)
            P.dma("sp", fn, final_norm.partition_broadcast(128), writes=["bcf"])
        NB = 256
        xts = [A.f32(D), A.f32(D)]
        hbs = [A.bf16(D), A.bf16(D)]
        junk = A.bf16(D); hf = A.f32(D)
        sss = [A.f32(1), A.f32(1)]; rss = [A.f32(1), A.f32(1)]
        hT = A.bf16(KT * NB).rearrange("p (k t) -> p k t", k=KT)
        uT = A.bf16(32 * NB).rearrange("p (k t) -> p k t", k=32)
        rl = [A.bf16(NB), A.bf16(NB)]
        yo = [A.f32(D), A.f32(D)]
        psT = Rot([0, 1]); psM = Rot([2, 3, 4, 5, 6, 7]); rlR = Rot([0, 1])
        t0 = 0 if not last else LC
        while t0 < TT:
            w = 1 if t0 < LC else 0
            n = min(NB, (LC if w == 1 else TT) - t0)
            nj = n // 128
            load_bc(w)
            for j in range(nj):
                ti0 = t0 + j * 128
                s = j % 2; sfx = str(s)
                P.dma("sp", xts[s], x1[ti0:ti0 + 128, :], reads=["x1"], writes=["xt" + sfx])
                rmsnorm_mod(xts[s], G, S, hbs[s], sfx, junk, sss[s], rss[s], hf)
                transpose_to(hbs[s], hT, j, sfx, psT)
            for ft in range(32):
                pi = psM.next()
                for kt in range(KT):
                    P.op("pe", lambda e, pi=pi, kt=kt, ft=ft, n=n: e.matmul(PS[pi][:, 0:n], lhsT=w1[:, kt, ft * 128:(ft + 1) * 128], rhs=hT[:, kt, 0:n], start=(kt == 0), stop=(kt == KT - 1)),
                         reads=["w1", "hT"], writes=["ps%d" % pi])
                ri = rlR.next()
                P.op("act", lambda e, pi=pi, ri=ri, n=n: e.activation(out=rl[ri][:, 0:n], in_=PS[pi][:, 0:n], func=AF.Relu), reads=[], writes=["ps%d" % pi, "rl%d" % ri])
                P.op("dve", lambda e, ri=ri, ft=ft, n=n: e.tensor_tensor(out=uT[:, ft, 0:n], in0=rl[ri][:, 0:n], in1=rl[ri][:, 0:n], op=ALU.mult), reads=["rl%d" % ri], writes=["uT"])
            for j in range(nj):
                ti0 = t0 + j * 128
                s = j % 2; sfx = str(s)
                for hf_ in range(2):
                    pi = psM.next()
                    for ft in range(32):
                        P.op("pe", lambda e, pi=pi, ft=ft, j=j, hf_=hf_: e.matmul(PS[pi][:, :], lhsT=uT[:, ft, j * 128:(j + 1) * 128], rhs=w2[:, ft, hf_ * 512:(hf_ + 1) * 512],
                                                                               start=(ft == 0), stop=(ft == 31)),
                             reads=["w2", "uT"], writes=["ps%d" % pi])
                    P.op("dve", lambda e, pi=pi, s=s, hf_=hf_, g2=g2: e.tensor_tensor(out=yo[s][:, hf_ * 512:(hf_ + 1) * 512], in0=PS[pi][:, :], in1=g2[:, hf_ * 512:(hf_ + 1) * 512], op=ALU.mult),
                         reads=["bc"], writes=["ps%d" % pi, "yo%d" % s])
                P.op("pool", lambda e, s=s: e.tensor_tensor(out=yo[s], in0=yo[s], in1=xts[s], op=ALU.add), reads=["xt" + str(s), "yo%d" % s], writes=["yo%d" % s])
                if not last:
                    P.dma("sp", xres[ti0:ti0 + 128, :], yo[s], reads=["yo%d" % s], writes=["xres"])
                else:
                    sfx2 = "f" + str(s)
                    P.op("act", lambda e, s=s: e.activation(out=junk, in_=yo[s], func=AF.Square, accum_out=sss[s]), reads=["yo%d" % s], writes=["junk", "ssf%d" % s])
                    P.op("act", lambda e, s=s: e.activation(out=rss[s], in_=sss[s], func=AF.Sqrt, scale=1.0 / D, bias=EPS), reads=["ssf%d" % s], writes=["rsf%d" % s])
                    P.op("dve", lambda e, s=s: e.reciprocal(out=rss[s], in_=rss[s]), reads=["rsf%d" % s], writes=["rsf%d" % s])
                    P.op("dve", lambda e, s=s: e.scalar_tensor_tensor(out=yo[s], in0=yo[s], scalar=rss[s], in1=fn, op0=ALU.mult, op1=ALU.mult),
                         reads=["yo%d" % s, "rsf%d" % s, "bcf"], writes=["yo%d" % s])
                    P.dma("sp", out_d[ti0 - LC:ti0 - LC + 128, :], yo[s], reads=["yo%d" % s], writes=["out"])
            t0 += n

    setup_consts()
    stages = build.stages if hasattr(build, "stages") else None
    for l in range(depth):
        last = (l == depth - 1)
        if stages == "C":
            break
        modulation(l)
        load_pp(l)
        P.barrier()
        if stages == "M":
            break
        phaseA(l)
        P.barrier()
        if stages is not None and "A" == stages:
            break
        print("nops before gla", P.nops, flush=True)
        gla(l)
        P.barrier()
        print("nops before lru", P.nops, flush=True)
        lru(l)
        P.barrier()
        print("nops before s5", P.nops, flush=True)
        s5(l)
        P.barrier()
        print("nops after s5", P.nops, flush=True)
        if stages is not None and "B" == stages:
            break
        phaseC1(l, last)
        P.barrier()
        phaseC2(l, last)
        P.barrier()
    P.barrier()
    print("nops", P.nops, flush=True)
    P.emit()
    P.close()
    return nc


def prep_inputs(inp, b, LL, LC, depth):
    f = lambda a: np.ascontiguousarray(np.asarray(a, dtype=np.float32))
    TT = LL + LC
    NC8, LC8 = TT // 8, LC // 8
    m = {}
    m["xin"] = f(np.concatenate([inp["ctx"][b], inp["x"][b]], axis=0))
    cv = np.stack([np.asarray(inp["c"][b]).reshape(KT, 128).T, np.asarray(inp["c_ctx"]).reshape(KT, 128).T], axis=-1)
    m["cvec"] = f(cv)
    for k in ("w_mod", "b_mod", "norm1", "norm2", "w_in", "gla_up_w", "lru_wa", "lru_wx", "s5_d", "s5_glu_w", "w_out", "w_ff1", "w_ff2"):
        m[k] = f(inp[k])
    m["final_norm"] = f(np.asarray(inp["final_norm"]).reshape(1, D))
    pp = np.zeros((depth, 128, 64), np.float32)
    for l in range(depth):
        for d in range(2):
            for hp in range(2):
                pp[l, :, 0 + d * 2 + hp] = inp["gla_up_b"][l, d, hp * 128:(hp + 1) * 128]
            for ct in range(2):
                sl = slice(ct * 128, (ct + 1) * 128)
                for k in range(4):
                    pp[l, :, 8 + (d * 4 + k) * 2 + ct] = inp["lru_conv_w"][l, d, k, sl]
                pp[l, :, 24 + d * 2 + ct] = inp["lru_conv_b"][l, d, sl]
                pp[l, :, 28 + d * 2 + ct] = inp["lru_ba"][l, d, sl]
                pp[l, :, 32 + d * 2 + ct] = inp["lru_bx"][l, d, sl]
                pp[l, :, 36 + d * 2 + ct] = inp["lru_lambda"][l, d, sl]
        for h in range(4):
            pp[l, :, 4 + h] = inp["gla_norm"][l, h * 128:(h + 1) * 128]
        for ct in range(2):
            pp[l, :, 40 + ct] = inp["s5_glu_b"][l, ct * 128:(ct + 1) * 128]
    m["pp"] = pp
    s5p = np.zeros((depth, 128, 3, 16), np.float32)
    s5b = np.zeros((depth, 128, 2, 16, 16), np.float32)
    s5c = np.zeros((depth, 128, 2, 16, 16), np.float32)
    for l in range(depth):
        for d in range(2):
            for gp in range(8):
                for gm in range(2):
                    g = gp * 2 + gm
                    ps_ = slice(gm * 64, (gm + 1) * 64)
                    s5p[l, ps_, 0, d * 8 + gp] = inp["s5_lam_re"][l, d, g]
                    s5p[l, ps_, 1, d * 8 + gp] = inp["s5_lam_im"][l, d, g]
                    s5p[l, ps_, 2, d * 8 + gp] = inp["s5_log_dt"][l, d, g]
                    s5b[l, ps_, 0, d * 8 + gp, :] = inp["s5_b_re"][l, d, g]
                    s5b[l, ps_, 1, d * 8 + gp, :] = inp["s5_b_im"][l, d, g]
                    s5c[l, ps_, 0, d * 8 + gp, :] = np.asarray(inp["s5_c_re"][l, d, g]).T
                    s5c[l, ps_, 1, d * 8 + gp, :] = np.asarray(inp["s5_c_im"][l, d, g]).T
    m["s5p"], m["s5b"], m["s5c"] = s5p, s5b, s5c
    tau = np.zeros((2, NC8), np.float32)
    tau[0] = np.arange(NC8)
    tau[1, :LC8] = LC8 - 1 - np.arange(LC8)
    tau[1, LC8:] = LC8 + (NC8 - 1 - np.arange(LC8, NC8))
    m["tau"] = tau.reshape(1, 2 * NC8)
    return m


_CACHE = {}


def kernel(**inputs):
    LL, LC, depth = 4096, 256, 4
    B = inputs["x"].shape[0]
    key = (LL, LC, depth)
    if key not in _CACHE:
        _CACHE[key] = build(LL, LC, depth)
    nc = _CACHE[key]
    in_maps = [prep_inputs(inputs, b, LL, LC, depth) for b in range(B)]
    res = run_bass_kernel_spmd(nc, in_maps, core_ids=list(range(B)))
    return np.stack([np.asarray(r["out"], dtype=np.float32) for r in res.results], axis=0)
```

```python
import math
import os
from contextlib import ExitStack

import numpy as np
import concourse.bass as bass
import concourse.mybir as mybir
from concourse.bass_utils import run_bass_kernel_spmd

F32 = mybir.dt.float32
BF16 = mybir.dt.bfloat16
ALU = mybir.AluOpType
AF = mybir.ActivationFunctionType

D = 1024
KT = 8
EPS = 1e-6
MAGIC = 12582912.0
TWO_PI = 2.0 * math.pi

COMPUTE = ("pe", "act", "dve", "pool")
QUEUES = ("sp", "poolq")
ENG_OF = {"pe": "pe", "act": "act", "dve": "dve", "pool": "pool", "sp": "sp", "poolq": "pool"}


class Prog:
    def __init__(self, nc, ndma=8):
        import os
        self.nc = nc
        self.es = ExitStack()
        self.streams = {e: [] for e in ("pe", "act", "dve", "pool", "sp")}
        self.sem = {}
        self.nop_eng = {}
        for e in COMPUTE:
            self.sem[e] = self.es.enter_context(nc.semaphore("s_" + e))
            self.nop_eng[e] = 0
        self.dsem, self.dcnt, self.dnext = {}, {}, {}
        for q in QUEUES:
            self.dsem[q] = [self.es.enter_context(nc.semaphore("d_%s%d" % (q, i))) for i in range(ndma)]
            self.dcnt[q] = [0] * ndma
            self.dnext[q] = 0
        self.seen = {e: {} for e in self.streams}
        self.lastw = {}
        self.readers = {}
        self.nops = 0
        self.waited = {e: set() for e in COMPUTE}
        self.limit = int(os.environ["OPLIMIT"]) if os.environ.get("OPLIMIT") else None

    def sbuf(self, name, shape, dtype=F32):
        return self.es.enter_context(self.nc.sbuf_tensor(name, list(shape), dtype))

    def psum(self, name, shape, dtype=F32):
        return self.es.enter_context(self.nc.psum_tensor(name, list(shape), dtype))

    @staticmethod
    def _tkey(tok):
        return ("c", tok[1]) if tok[0] == "c" else ("d", tok[1].name)

    @staticmethod
    def _tval(tok):
        return tok[2]

    def _need(self, stream, tok, waits):
        if tok is None:
            return
        if tok[0] == "c" and tok[1] == "pe" and stream == "pe":
            return
        k = self._tkey(tok)
        if self.seen[stream].get(k, 0) >= self._tval(tok):
            return
        cur = waits.get(k)
        if cur is None or self._tval(cur) < self._tval(tok):
            waits[k] = tok

    def _deps(self, stream, reads, writes, waits):
        for k in reads:
            self._need(stream, self.lastw.get(k), waits)
        for k in writes:
            self._need(stream, self.lastw.get(k), waits)
            for t in self.readers.get(k, ()):
                self._need(stream, t, waits)

    def _commit(self, stream, tok, reads, writes, waits):
        for k, t in waits.items():
            self.seen[stream][k] = self._tval(t)
            if t[0] == "c":
                self.waited[t[1]].add(t[2])
        for k in writes:
            self.lastw[k] = tok
            self.readers[k] = []
        for k in reads:
            if k in writes:
                continue
            lst = self.readers.setdefault(k, [])
            lst.append(tok)
            if len(lst) > 16:
                best = {}
                for t in lst:
                    kk = self._tkey(t)
                    b = best.get(kk)
                    if b is None or self._tval(b) < self._tval(t):
                        best[kk] = t
                self.readers[k] = list(best.values())

    def op(self, eng, fn, reads=(), writes=()):
        if self.limit is not None and self.nops >= self.limit:
            return None
        rec = _Rec()
        fn(rec)
        name, args, kwargs = rec.call
        if os.environ.get("OPTRACE"):
            def _d(a):
                try:
                    return "%s%s" % (tuple(a.shape), "" )
                except Exception:
                    return str(a)[:30]
            print("OP", self.nops, eng, name, [_d(a) for a in args], {k: _d(v) for k, v in kwargs.items()}, flush=True)
        fn = lambda e, name=name, args=args, kwargs=kwargs: getattr(e, name)(*args, **kwargs)
        waits = {}
        self._deps(eng, reads, writes, waits)
        self.nop_eng[eng] += 1
        tok = ("c", eng, self.nop_eng[eng])
        self._commit(eng, tok, reads, writes, waits)
        self.streams[eng].append([list(waits.values()), fn, tok])
        self.nops += 1
        return tok

    def dma(self, q, out, in_, reads=(), writes=(), **kw):
        if self.limit is not None and self.nops >= self.limit:
            return None
        stream = ENG_OF[q]
        waits = {}
        i = self.dnext[q]
        self.dnext[q] = (i + 1) % len(self.dsem[q])
        sem = self.dsem[q][i]
        if self.dcnt[q][i] > 0:
            self._need(stream, ("d", sem, 16 * self.dcnt[q][i], q), waits)
        self._deps(stream, reads, writes, waits)
        self.dcnt[q][i] += 1
        tok = ("d", sem, 16 * self.dcnt[q][i], q)
        self._commit(stream, tok, reads, writes, waits)
        fn = lambda e, out=out, in_=in_, kw=kw: e.dma_start(out=out, in_=in_, **kw)
        self.streams[stream].append([list(waits.values()), fn, tok])
        self.nops += 1
        return tok

    def barrier(self):
        toks = []
        for q in QUEUES:
            for i, sem in enumerate(self.dsem[q]):
                if self.dcnt[q][i]:
                    toks.append(("d", sem, 16 * self.dcnt[q][i], q))
        for e in COMPUTE:
            if self.nop_eng[e]:
                toks.append(("c", e, self.nop_eng[e]))
        for stream in self.streams:
            waits = {}
            for t in toks:
                if t[0] == "c" and t[1] == stream:
                    continue
                self._need(stream, t, waits)
            for k, t in waits.items():
                self.seen[stream][k] = self._tval(t)
                if t[0] == "c":
                    self.waited[t[1]].add(t[2])
            if waits:
                self.streams[stream].append([list(waits.values()), None, None])
        self.lastw = {}
        self.readers = {}

    def emit(self):
        nc = self.nc
        streams = self.streams
        rank = {}
        for e in COMPUTE:
            rank[e] = {idx: r + 1 for r, idx in enumerate(sorted(self.waited[e]))}

        def run(eng_obj, lst):
            for waits, fn, tok in lst:
                for t in waits:
                    if t[0] == "c":
                        eng_obj.wait_ge(self.sem[t[1]], rank[t[1]][t[2]])
                    else:
                        eng_obj.wait_ge(t[1], t[2])
                if fn is not None:
                    ins = fn(eng_obj)
                    if tok[0] == "d":
                        ins.then_inc(tok[1], 16)
                    elif tok[2] in rank[tok[1]]:
                        ins.then_inc(self.sem[tok[1]], 1)

        with nc.Block() as block:
            @block.tensor
            def _(e):
                run(e, streams["pe"])

            @block.scalar
            def _(e):
                run(e, streams["act"])

            @block.vector
            def _(e):
                run(e, streams["dve"])

            @block.gpsimd
            def _(e):
                run(e, streams["pool"])

            @block.sync
            def _(e):
                run(e, streams["sp"])

    def close(self):
        self.es.close()


class _Rec:
    def __init__(self):
        self.call = None

    def __getattr__(self, name):
        def f(*args, **kwargs):
            self.call = (name, args, kwargs)
            return self
        return f


class Arena:
    def __init__(self, P, words):
        self.t = P.sbuf("arena", [128, words], F32)
        self.words = words
        self.off = 0
        self.n = 0

    def reset(self):
        self.off = 0

    def f32(self, n):
        assert self.off + n <= self.words, ("arena overflow", self.off, n, self.words)
        ap = self.t[:, self.off:self.off + n]
        self.off += n
        return ap

    def bf16(self, n):
        w = (n + 1) // 2
        return self.f32(w).bitcast(BF16)[:, 0:n]


class Rot:
    def __init__(self, items):
        self.items = items
        self.i = 0

    def next(self):
        it = self.items[self.i % len(self.items)]
        self.i += 1
        return it


def build(LL, LC, depth, debug=False):
    TT = LL + LC
    NT = TT // 128
    NCH = TT // 64
    NC8 = TT // 8
    LC8 = LC // 8
    ROWS = LL // 64
    RB = ROWS // 8
    nc = bass.Bass("TRN2", target_bir_lowering=False)
    P = Prog(nc)

    def din(name, shape, dt=F32):
        return nc.dram_tensor(name, list(shape), dt, kind="ExternalInput").ap()

    dkind = "ExternalOutput" if debug else "Internal"

    def dscr(name, shape, dt=F32):
        return nc.dram_tensor(name, list(shape), dt, kind=dkind).ap()

    xin = din("xin", [TT, D])
    cvec = din("cvec", [128, KT, 2])
    w_mod = din("w_mod", [depth, D, 6 * D])
    b_mod = din("b_mod", [depth, 6 * D])
    norm1 = din("norm1", [depth, D])
    norm2 = din("norm2", [depth, D])
    w_in = din("w_in", [depth, D, 2336])
    gla_up_w = din("gla_up_w", [depth, 2, 16, 256])
    pp = din("pp", [depth, 128, 64])
    lru_wa = din("lru_wa", [depth, 2, 4, 64, 64])
    lru_wx = din("lru_wx", [depth, 2, 4, 64, 64])
    s5p = din("s5p", [depth, 128, 3, 16])
    s5b = din("s5b", [depth, 128, 2, 16, 16])
    s5c = din("s5c", [depth, 128, 2, 16, 16])
    s5_d = din("s5_d", [depth, 256])
    s5_glu_w = din("s5_glu_w", [depth, 256, 256])
    w_out = din("w_out", [depth, D, D])
    w_ff1 = din("w_ff1", [depth, D, 4 * D])
    w_ff2 = din("w_ff2", [depth, 4 * D, D])
    final_norm = din("final_norm", [1, D])
    tau_in = din("tau", [1, 2 * NC8])

    out_d = nc.dram_tensor("out", [LL, D], F32, kind="ExternalOutput").ap()

    mraw = dscr("mraw", [depth, 2, 6 * D])
    gsc = dscr("gsc", [depth, 2, 2, D])
    qT = dscr("qT", [256, TT]); kT = dscr("kT", [256, TT]); ggT = dscr("ggT", [512, TT])
    lrT = [dscr("lrT0", [16, TT]), dscr("lrT1", [16, TT])]
    lxT = dscr("lxT", [256, TT]); lgT = dscr("lgT", [256, TT])
    v_tok = dscr("v_tok", [TT, 512], BF16)
    su_tok = dscr("su_tok", [TT, 256])
    oT = dscr("oT", [512, TT])
    lruT = dscr("lruT", [256, TT])
    s5y = dscr("s5y", [TT, 256])
    x1 = dscr("x1", [TT, D])
    xres = dscr("xres", [TT, D])
    dbg = {}

    ident_f = P.sbuf("ident_f", [128, 128], F32)
    ident_b = P.sbuf("ident_b", [128, 128], BF16)
    ones_f = P.sbuf("ones_f", [128, 128], F32)
    ones_b = P.sbuf("ones_b", [128, 128], BF16)
    maskF = P.sbuf("maskF", [128, 128], F32)
    maskB = P.sbuf("maskB", [128, 128], F32)
    m8F = P.sbuf("m8F", [128, 512], F32)
    m8B = P.sbuf("m8B", [128, 512], F32)
    jidx = P.sbuf("jidx", [128, 2, 8, 9], F32)
    ppsb = P.sbuf("ppsb", [128, 64], F32)
    AW = 50400
    A = Arena(P, AW)
    PS = [P.psum("ps%d" % i, [128, 512], F32) for i in range(8)]

    def setup_consts():
        P.op("pool", lambda e: e.memset(ident_f[:], 0.0), writes=["ident_f"])
        P.op("pool", lambda e: e.affine_select(out=ident_f[:], in_=ident_f[:], pattern=[[-1, 128]], compare_op=ALU.not_equal,
                                               fill=1.0, base=0, channel_multiplier=1), reads=["ident_f"], writes=["ident_f"])
        P.op("dve", lambda e: e.tensor_copy(out=ident_b[:], in_=ident_f[:]), reads=["ident_f"], writes=["ident_b"])
        P.op("dve", lambda e: e.memset(ones_f[:], 1.0), writes=["ones_f"])
        P.op("dve", lambda e: e.memset(ones_b[:], 1.0), writes=["ones_b"])
        P.op("pool", lambda e: e.affine_select(out=maskF[:], in_=ones_f[:], pattern=[[1, 128]], compare_op=ALU.is_ge,
                                               fill=0.0, base=0, channel_multiplier=-1), reads=["ones_f"], writes=["maskF"])
        P.op("pool", lambda e: e.memset(maskF[0:64, 64:128], 0.0), reads=["maskF"], writes=["maskF"])
        P.op("pool", lambda e: e.affine_select(out=maskB[:], in_=ones_f[:], pattern=[[-1, 128]], compare_op=ALU.is_ge,
                                               fill=0.0, base=0, channel_multiplier=1), reads=["ones_f"], writes=["maskB"])
        P.op("pool", lambda e: e.memset(maskB[64:128, 0:64], 0.0), reads=["maskB"], writes=["maskB"])
        P.op("pool", lambda e: e.memset(m8F[:], 1.0), writes=["m8F"])
        P.op("pool", lambda e: e.memset(m8B[:], 1.0), writes=["m8B"])
        P.op("pool", lambda e: e.affine_select(out=m8F[:].rearrange("p (r j h) -> p r j h", r=4, j=8), in_=m8F[:].rearrange("p (r j h) -> p r j h", r=4, j=8),
                                               pattern=[[0, 4], [16, 8], [0, 16]], compare_op=ALU.is_ge, fill=0.0, base=15, channel_multiplier=-1),
             reads=["m8F"], writes=["m8F"])
        P.op("pool", lambda e: e.affine_select(out=m8B[:].rearrange("p (r j h) -> p r j h", r=4, j=8), in_=m8B[:].rearrange("p (r j h) -> p r j h", r=4, j=8),
                                               pattern=[[0, 4], [-16, 8], [0, 16]], compare_op=ALU.is_ge, fill=0.0, base=0, channel_multiplier=1),
             reads=["m8B"], writes=["m8B"])
        for j in range(9):
            P.op("dve", lambda e, j=j: e.memset(jidx[:, 0, :, j:j + 1], float(j)), writes=["jidx"])
            P.op("dve", lambda e, j=j: e.memset(jidx[:, 1, :, j:j + 1], float(7 - j) if j < 8 else 8.0), writes=["jidx"])

    PP_UPB = 0
    PP_GNORM = 4
    PP_CONVW = 8
    PP_CONVB = 24
    PP_BA = 28
    PP_BX = 32
    PP_LAM = 36
    PP_GLUB = 40
    PP_SGN = 42

    def modulation(l):
        A.reset()
        cs_raw = A.f32(16); cs = A.f32(16)
        msb = A.f32(6 * D)
        bm = A.f32(6 * D)
        n12 = A.f32(2 * D)
        gt = A.f32(2 * D)
        wblk = [A.f32(KT * 512), A.f32(KT * 512)]
        P.dma("sp", cs_raw, cvec.rearrange("p k w -> p (k w)"), writes=["cs_raw"])
        P.op("act", lambda e: e.activation(out=cs, in_=cs_raw, func=AF.Silu), reads=["cs_raw"], writes=["cs"])
        P.dma("sp", bm[0:2, :], b_mod[l:l + 1, :].partition_broadcast(2), writes=["bm"])
        P.dma("sp", n12[0:2, 0:D], norm1[l:l + 1, :].partition_broadcast(2), writes=["n12a"])
        P.dma("sp", n12[0:2, D:2 * D], norm2[l:l + 1, :].partition_broadcast(2), writes=["n12b"])
        wv = w_mod[l].rearrange("(kt p) n -> p kt n", p=128)
        cs3 = cs.rearrange("p (k w) -> p k w", w=2)
        for j in range(12):
            wb = wblk[j % 2]
            wb3 = wb.rearrange("p (k n) -> p k n", k=KT)
            P.dma("sp", wb3, wv[:, :, j * 512:(j + 1) * 512], writes=["wblk%d" % (j % 2)])
            ps = PS[j % 2]
            for kt in range(KT):
                P.op("pe", lambda e, ps=ps, kt=kt, wb3=wb3: e.matmul(ps[0:2, :], lhsT=cs3[:, kt, :], rhs=wb3[:, kt, :], start=(kt == 0), stop=(kt == KT - 1)),
                     reads=["cs", "wblk%d" % (j % 2)], writes=["ps%d" % (j % 2)])
            P.op("dve", lambda e, ps=ps, j=j: e.tensor_tensor(out=msb[0:2, j * 512:(j + 1) * 512], in0=ps[0:2, :], in1=bm[0:2, j * 512:(j + 1) * 512], op=ALU.add),
                 reads=["bm"], writes=["ps%d" % (j % 2), "msb"])
        P.op("dve", lambda e: e.scalar_tensor_tensor(out=gt[0:2, 0:D], in0=msb[0:2, D:2 * D], scalar=1.0, in1=n12[0:2, 0:D], op0=ALU.add, op1=ALU.mult),
             reads=["msb", "n12a"], writes=["gt"])
        P.op("dve", lambda e: e.scalar_tensor_tensor(out=gt[0:2, D:2 * D], in0=msb[0:2, 4 * D:5 * D], scalar=1.0, in1=n12[0:2, D:2 * D], op0=ALU.add, op1=ALU.mult),
             reads=["msb", "n12b", "gt"], writes=["gt"])
        P.dma("sp", mraw[l], msb[0:2, :], reads=["msb"], writes=["mraw"])
        P.dma("sp", gsc[l].rearrange("w g d -> w (g d)"), gt[0:2, :], reads=["gt"], writes=["gsc"])

    def bc_load(dst, src_row):
        return src_row.to_broadcast([128, src_row.shape[-1]])

    def token_blocks(last_skip_ctx=False):
        blks = []
        if not last_skip_ctx:
            t = 0
            while t < LC // 128:
                n = min(4, LC // 128 - t)
                blks.append((t, n, 1))
                t += n
        t = LC // 128
        while t < NT:
            n = min(4, NT - t)
            blks.append((t, n, 0))
            t += n
        return blks

    def rmsnorm_mod(xt_ap, Gbc, Sbc, hb_out, sfx, junk, ss, rs, hf):
        P.op("act", lambda e: e.activation(out=junk, in_=xt_ap, func=AF.Square, accum_out=ss), reads=["xt" + sfx], writes=["junk", "ss" + sfx])
        P.op("act", lambda e: e.activation(out=rs, in_=ss, func=AF.Sqrt, scale=1.0 / D, bias=EPS), reads=["ss" + sfx], writes=["rs" + sfx])
        P.op("dve", lambda e: e.reciprocal(out=rs, in_=rs), reads=["rs" + sfx], writes=["rs" + sfx])
        P.op("dve", lambda e: e.scalar_tensor_tensor(out=hf, in0=xt_ap, scalar=rs, in1=Gbc, op0=ALU.mult, op1=ALU.mult),
             reads=["xt" + sfx, "rs" + sfx, "bc"], writes=["hf"])
        P.op("pool", lambda e: e.tensor_tensor(out=hb_out, in0=hf, in1=Sbc, op=ALU.add), reads=["hf", "bc"], writes=["hb" + sfx])

    def transpose_to(hb, hT3, j, sfx, psrot):
        pi = psrot.next()
        pst = PS[pi][:].bitcast(BF16)
        for kt in range(KT):
            P.op("pe", lambda e, kt=kt, pst=pst: e.transpose(pst[:, kt * 128:(kt + 1) * 128], hb[:, kt * 128:(kt + 1) * 128], ident_b[:]),
                 reads=["hb" + sfx, "ident_b"], writes=["ps%d" % pi])
        P.op("act", lambda e, pst=pst: e.activation(out=hT3[:, :, j * 128:(j + 1) * 128], in_=pst.rearrange("p (k t) -> p k t", k=KT), func=AF.Identity),
             reads=[], writes=["ps%d" % pi, "hT"])

    def phaseA(l):
        A.reset()
        xsrc = xin if l == 0 else xres
        win = A.bf16(KT * 2336).rearrange("p (k n) -> p k n", k=KT)
        wv = w_in[l].rearrange("(kt p) n -> p kt n", p=128)
        for c0 in range(0, 2336, 512):
            c1 = min(2336, c0 + 512)
            P.dma("poolq", win[:, :, c0:c1], wv[:, :, c0:c1], writes=["win"])
        bcs = {}
        for w in (0, 1):
            G = A.f32(D); S = A.f32(D)
            P.dma("sp", G, gsc[l, w, 0:1, :].partition_broadcast(128), reads=["gsc"], writes=["bc"])
            P.dma("sp", S, mraw[l, w:w + 1, 0:D].partition_broadcast(128), reads=["mraw"], writes=["bc"])
            bcs[w] = (G, S)
        NS = 8
        xts = [A.f32(D) for _ in range(2)]
        hbs = [A.bf16(D) for _ in range(NS)]
        hfs = [A.f32(D), A.f32(D)]
        sss = [A.f32(1) for _ in range(NS)]; rss = [A.f32(1) for _ in range(NS)]
        hTs = [A.bf16(KT * 512).rearrange("p (k t) -> p k t", k=KT) for _ in range(2)]
        stg = [A.f32(512) for _ in range(3)]
        vst = [A.bf16(512) for _ in range(2)]
        sst = [A.f32(256) for _ in range(2)]
        psT = Rot([0, 1]); psM = Rot([2, 3, 4, 5, 6, 7])
        stgR = Rot([0, 1, 2]); vR = Rot([0, 1]); sR = Rot([0, 1])
        FM = [("q", 0, qT, 0, 128), ("q", 128, qT, 128, 128), ("k", 256, kT, 0, 128), ("k", 384, kT, 128, 128)]
        for i in range(4):
            FM.append(("gg", 1024 + 128 * i, ggT, 128 * i, 128))
        FM.append(("lr0", 1536, lrT[0], 0, 16)); FM.append(("lr1", 1552, lrT[1], 0, 16))
        for i in range(2):
            FM.append(("lx", 1568 + 128 * i, lxT, 128 * i, 128))
        for i in range(2):
            FM.append(("lg", 1824 + 128 * i, lgT, 128 * i, 128))
        blocks = token_blocks()
        slot_ctr = [0]
        blk_slots = {}

        def norms(bi):
            (t0, n, w) = blocks[bi]
            G, S = bcs[w]
            sl = []
            for j in range(n):
                ti = t0 + j
                s = slot_ctr[0] % NS; slot_ctr[0] += 1
                xs = s % 2
                sl.append(s)
                kx, kh = "xt%d" % xs, "hb%d" % s
                P.dma("sp", xts[xs], xsrc[ti * 128:(ti + 1) * 128, :], reads=["xsrc"], writes=[kx])
                P.op("act", lambda e, xs=xs, s=s: e.activation(out=hbs[s], in_=xts[xs], func=AF.Square, accum_out=sss[s]), reads=[kx], writes=[kh, "ss%d" % s])
                P.op("act", lambda e, s=s: e.activation(out=rss[s], in_=sss[s], func=AF.Sqrt, scale=1.0 / D, bias=EPS), reads=["ss%d" % s], writes=["rs%d" % s])
                P.op("dve", lambda e, s=s: e.reciprocal(out=rss[s], in_=rss[s]), reads=["rs%d" % s], writes=["rs%d" % s])
                P.op("dve", lambda e, xs=xs, s=s, G=G: e.scalar_tensor_tensor(out=hfs[xs], in0=xts[xs], scalar=rss[s], in1=G, op0=ALU.mult, op1=ALU.mult),
                     reads=[kx, "rs%d" % s, "bc"], writes=["hf%d" % xs])
                P.op("pool", lambda e, xs=xs, s=s, S=S: e.tensor_tensor(out=hbs[s], in0=hfs[xs], in1=S, op=ALU.add), reads=["hf%d" % xs, "bc"], writes=[kh])
            blk_slots[bi] = sl

        def transposes(bi):
            hT = hTs[bi % 2]
            for j, s in enumerate(blk_slots[bi]):
                pi = psT.next()
                pst = PS[pi][:].bitcast(BF16)
                for kt in range(KT):
                    P.op("pe", lambda e, kt=kt, pst=pst, s=s: e.transpose(pst[:, kt * 128:(kt + 1) * 128], hbs[s][:, kt * 128:(kt + 1) * 128], ident_b[:]),
                         reads=["hb%d" % s, "ident_b"], writes=["ps%d" % pi])
                P.op("act", lambda e, pst=pst, j=j, hT=hT: e.activation(out=hT[:, :, j * 128:(j + 1) * 128], in_=pst.rearrange("p (k t) -> p k t", k=KT), func=AF.Identity),
                     reads=[], writes=["ps%d" % pi, "hT%d" % (bi % 2)])

        def fm_part(bi):
            (t0, n, w) = blocks[bi]
            hT = hTs[bi % 2]; kT_ = "hT%d" % (bi % 2)
            ntok = n * 128; tok0 = t0 * 128
            for (nm, c0, dst, r0, m) in FM:
                pi = psM.next()
                for kt in range(KT):
                    P.op("pe", lambda e, pi=pi, kt=kt, c0=c0, m=m, hT=hT, ntok=ntok: e.matmul(PS[pi][0:m, 0:ntok], lhsT=win[:, kt, c0:c0 + m], rhs=hT[:, kt, 0:ntok],
                                                                                           start=(kt == 0), stop=(kt == KT - 1)),
                         reads=["win", kT_], writes=["ps%d" % pi])
                si = stgR.next()
                P.op("act", lambda e, pi=pi, si=si, m=m, ntok=ntok: e.activation(out=stg[si][0:m, 0:ntok], in_=PS[pi][0:m, 0:ntok], func=AF.Identity),
                     reads=[], writes=["ps%d" % pi, "stg%d" % si])
                P.dma("sp", dst[r0:r0 + m, tok0:tok0 + ntok], stg[si][0:m, 0:ntok], reads=["stg%d" % si], writes=["zT_%s_%d" % (nm, r0)])

        def tm_part(bi):
            (t0, n, w) = blocks[bi]
            hT = hTs[bi % 2]; kT_ = "hT%d" % (bi % 2)
            for j in range(n):
                ti = t0 + j
                pi = psM.next()
                for kt in range(KT):
                    P.op("pe", lambda e, pi=pi, kt=kt, j=j, hT=hT: e.matmul(PS[pi][:, :], lhsT=hT[:, kt, j * 128:(j + 1) * 128], rhs=win[:, kt, 512:1024],
                                                                         start=(kt == 0), stop=(kt == KT - 1)),
                         reads=["win", kT_], writes=["ps%d" % pi])
                vi = vR.next()
                P.op("dve", lambda e, pi=pi, vi=vi: e.tensor_copy(out=vst[vi], in_=PS[pi][:, :]), reads=[], writes=["ps%d" % pi, "vst%d" % vi])
                P.dma("sp", v_tok[ti * 128:(ti + 1) * 128, :], vst[vi], reads=["vst%d" % vi], writes=["v_tok%d" % ti])
                pi = psM.next()
                for kt in range(KT):
                    P.op("pe", lambda e, pi=pi, kt=kt, j=j, hT=hT: e.matmul(PS[pi][:, 0:256], lhsT=hT[:, kt, j * 128:(j + 1) * 128], rhs=win[:, kt, 2080:2336],
                                                                         start=(kt == 0), stop=(kt == KT - 1)),
                         reads=["win", kT_], writes=["ps%d" % pi])
                si = sR.next()
                P.op("dve", lambda e, pi=pi, si=si: e.tensor_copy(out=sst[si], in_=PS[pi][:, 0:256]), reads=[], writes=["ps%d" % pi, "sst%d" % si])
                P.dma("sp", su_tok[ti * 128:(ti + 1) * 128, :], sst[si], reads=["sst%d" % si], writes=["su_tok%d" % ti])

        norms(0)
        transposes(0)
        for bi in range(len(blocks)):
            if bi + 1 < len(blocks):
                norms(bi + 1)
            fm_part(bi)
            if bi + 1 < len(blocks):
                transposes(bi + 1)
            tm_part(bi)

    def load_pp(l):
        P.dma("sp", ppsb[:], pp[l], writes=["ppsb"])

    def gla(l):
        for hp in range(2):
            A.reset()
            sm0 = A.bf16(TT); sm1 = A.bf16(TT)
            P.op("pool", lambda e: e.memset(sm0, 1.0), writes=["sm"])
            P.op("pool", lambda e: e.memset(sm0.rearrange("p (c j) -> p c j", j=64)[:, :, 0:1], 0.0), reads=["sm"], writes=["sm"])
            P.op("pool", lambda e: e.memset(sm1, 1.0), reads=["sm"], writes=["sm"])
            P.op("pool", lambda e: e.memset(sm1.rearrange("p (c j) -> p c j", j=64)[:, :, 63:64], 0.0), reads=["sm"], writes=["sm"])
            vt = A.bf16(NT * 256).rearrange("p (t c) -> p t c", t=NT)
            vsrc = v_tok.rearrange("(t p) c -> p t c", p=128)
            for t0_ in range(0, NT, 8):
                t1_ = min(NT, t0_ + 8)
                P.dma("sp", vt[:, t0_:t1_, :], vsrc[:, t0_:t1_, hp * 256:(hp + 1) * 256], reads=["v_tok"], writes=["vt"])
            oacc = A.f32(2 * TT).rearrange("p (h t) -> p h t", h=2)
            lrsb = A.f32(TT); Bp = A.f32(TT); Bc = A.f32(TT); qk = A.f32(TT)
            qd = A.bf16(TT); ki = A.bf16(TT)
            kiT = A.bf16(NT * 128).rearrange("p (t c) -> p t c", t=NT)
            upw = A.f32(256); nb = A.f32(1)
            gam = A.f32(NCH)
            S = A.f32(128); Sb = A.bf16(128); tmp = A.f32(128)
            sT = [A.bf16(128), A.bf16(128)]
            for d in range(2):
                mask = maskF if d == 0 else maskB
                sm = sm0 if d == 0 else sm1
                P.dma("sp", lrsb[0:16, :], lrT[d], reads=["zT_lr%d" % d], writes=["lrsb"])
                P.dma("sp", upw[0:16, :], gla_up_w[l, d], writes=["upw"])
                P.op("dve", lambda e, d=d: e.tensor_scalar(out=nb, in0=ppsb[:, PP_UPB + d * 2 + hp:PP_UPB + d * 2 + hp + 1], scalar1=-1.0, scalar2=None, op0=ALU.mult),
                     reads=["ppsb"], writes=["nb"])
                psr = Rot([0, 1])
                for b0 in range(0, TT, 512):
                    n = min(512, TT - b0)
                    pi = psr.next()
                    P.op("pe", lambda e, pi=pi, b0=b0, n=n: e.matmul(PS[pi][:, 0:n], lhsT=upw[0:16, hp * 128:(hp + 1) * 128], rhs=lrsb[0:16, b0:b0 + n], start=True, stop=True),
                         reads=["upw", "lrsb"], writes=["ps%d" % pi])
                    P.op("act", lambda e, pi=pi, b0=b0, n=n: e.activation(out=Bc[:, b0:b0 + n], in_=PS[pi][:, 0:n], func=AF.Exp, scale=-1.0, bias=nb),
                         reads=["nb"], writes=["ps%d" % pi, "Bc"])
                P.op("act", lambda e: e.activation(out=Bp, in_=Bc, func=AF.Ln, bias=1.0, scale=1.0), reads=["Bc"], writes=["Bp"])
                if d == 0:
                    P.op("dve", lambda e, sm=sm: e.tensor_tensor_scan(out=Bc, data0=sm, data1=Bp, initial=0.0, op0=ALU.mult, op1=ALU.add),
                         reads=["Bp", "sm"], writes=["Bc"])
                else:
                    P.op("dve", lambda e, sm=sm: e.tensor_tensor_scan(out=Bc[:, ::-1], data0=sm[:, ::-1], data1=Bp[:, ::-1], initial=0.0, op0=ALU.mult, op1=ALU.add),
                         reads=["Bp", "sm"], writes=["Bc"])
                Bc3 = Bc.rearrange("p (c j) -> p c j", j=64)
                endj = 63 if d == 0 else 0
                P.op("act", lambda e, endj=endj: e.activation(out=gam, in_=Bc3[:, :, endj], func=AF.Exp, scale=-1.0 / 16.0), reads=["Bc"], writes=["gam"])
                P.dma("sp", qk, qT[hp * 128:(hp + 1) * 128, :], reads=["zT_q"], writes=["qk"])
                P.op("act", lambda e: e.activation(out=Bp, in_=Bc, func=AF.Exp, scale=-1.0 / 16.0), reads=["Bc"], writes=["Bp"])
                P.op("dve", lambda e: e.scalar_tensor_tensor(out=qd, in0=qk, scalar=0.125, in1=Bp, op0=ALU.mult, op1=ALU.mult), reads=["qk", "Bp"], writes=["qd"])
                P.dma("sp", qk, kT[hp * 128:(hp + 1) * 128, :], reads=["zT_k"], writes=["qk"])
                P.op("act", lambda e: e.activation(out=Bp, in_=Bc, func=AF.Exp, scale=1.0 / 16.0), reads=["Bc"], writes=["Bp"])
                P.op("dve", lambda e: e.tensor_tensor(out=ki, in0=qk, in1=Bp, op=ALU.mult), reads=["qk", "Bp"], writes=["ki"])
                psr = Rot([0, 1])
                for t in range(NT):
                    pi = psr.next()
                    pst = PS[pi][:].bitcast(BF16)
                    P.op("pe", lambda e, pst=pst, t=t: e.transpose(pst[:, 0:128], ki[:, t * 128:(t + 1) * 128], ident_b[:]), reads=["ki", "ident_b"], writes=["ps%d" % pi])
                    P.op("act", lambda e, pst=pst, t=t: e.activation(out=kiT[:, t, :], in_=pst[:, 0:128], func=AF.Identity), reads=[], writes=["ps%d" % pi, "kiT"])
                P.op("dve", lambda e: e.memset(S, 0.0), writes=["S"])
                P.op("dve", lambda e: e.memset(Sb, 0.0), writes=["Sb"])
                ctx_t = list(range(LC // 128)); lat_t = list(range(LC // 128, NT))
                order = ctx_t + lat_t if d == 0 else ctx_t[::-1] + lat_t[::-1]
                corder = (0, 1) if d == 0 else (1, 0)
                psS = Rot([0, 1]); psO = Rot([(2, 3), (4, 5)]); psD = Rot([6, 7])
                for t in order:
                    po = psO.next()
                    for hh in range(2):
                        pr = slice(hh * 64, (hh + 1) * 64)
                        pi = psS.next()
                        P.op("pe", lambda e, pi=pi, pr=pr, t=t: e.matmul(PS[pi][:, 0:128], lhsT=ki[pr, t * 128:(t + 1) * 128], rhs=qd[pr, t * 128:(t + 1) * 128], start=True, stop=True),
                             reads=["ki", "qd"], writes=["ps%d" % pi])
                        P.op("dve", lambda e, pi=pi, hh=hh, mask=mask: e.tensor_tensor(out=sT[hh], in0=PS[pi][:, 0:128], in1=mask[:], op=ALU.mult),
                             reads=["mask"], writes=["ps%d" % pi, "sT%d" % hh])
                        P.op("pe", lambda e, hh=hh, t=t, po=po: e.matmul(PS[po[hh]][:, 0:128], lhsT=vt[:, t, hh * 128:(hh + 1) * 128], rhs=sT[hh], start=True, stop=True),
                             reads=["vt", "sT%d" % hh], writes=["ps%d" % po[hh]])
                    for ci, cc in enumerate(corder):
                        ch = t * 2 + cc
                        cols = slice(t * 128 + cc * 64, t * 128 + cc * 64 + 64)
                        for hh in range(2):
                            pr = slice(hh * 64, (hh + 1) * 64)
                            P.op("pe", lambda e, hh=hh, pr=pr, cols=cols, cc=cc, po=po, ci=ci: e.matmul(PS[po[hh]][:, cc * 64:(cc + 1) * 64], lhsT=Sb[pr, :], rhs=qd[pr, cols],
                                                                                                  start=False, stop=False, skip_group_check=True),
                                 reads=["Sb", "qd"], writes=["ps%d" % po[hh]])
                        pd = psD.next()
                        jr = slice(cc * 64, (cc + 1) * 64)
                        for hh in range(2):
                            pr = slice(hh * 64, (hh + 1) * 64)
                            P.op("pe", lambda e, hh=hh, pr=pr, jr=jr, t=t, pd=pd: e.matmul(PS[pd][pr, 0:128], lhsT=kiT[jr, t, hh * 64:(hh + 1) * 64], rhs=vt[jr, t, hh * 128:(hh + 1) * 128],
                                                                                         start=True, stop=True),
                                 reads=["kiT", "vt"], writes=["ps%d" % pd])
                        P.op("dve", lambda e, pd=pd: e.tensor_tensor(out=tmp, in0=PS[pd][:, 0:128], in1=S, op=ALU.add), reads=["S"], writes=["ps%d" % pd, "tmp"])
                        P.op("dve", lambda e, ch=ch: e.tensor_scalar(out=S, in0=tmp, scalar1=gam[:, ch:ch + 1], scalar2=None, op0=ALU.mult), reads=["tmp", "gam"], writes=["S"])
                        P.op("act", lambda e, ch=ch: e.activation(out=Sb, in_=tmp, func=AF.Identity, scale=gam[:, ch:ch + 1]), reads=["tmp", "gam"], writes=["Sb"])
                    for hh in range(2):
                        if d == 0:
                            P.op("act", lambda e, hh=hh, t=t, po=po: e.activation(out=oacc[:, hh, t * 128:(t + 1) * 128], in_=PS[po[hh]][:, 0:128], func=AF.Identity),
                                 reads=[], writes=["ps%d" % po[hh], "oacc"])
                        else:
                            P.op("dve", lambda e, hh=hh, t=t, po=po: e.tensor_tensor(out=oacc[:, hh, t * 128:(t + 1) * 128], in0=PS[po[hh]][:, 0:128], in1=oacc[:, hh, t * 128:(t + 1) * 128], op=ALU.add),
                                 reads=[], writes=["ps%d" % po[hh], "oacc"])
            for hh in range(2):
                P.dma("sp", oT[(hp * 2 + hh) * 128:(hp * 2 + hh + 1) * 128, :], oacc[:, hh, :], reads=["oacc"], writes=["oT"])
            P.barrier()

    def lru(l):
        A.reset()
        cst = A.f32(4); cst2 = A.f32(4)
        P.op("act", lambda e: e.activation(out=cst, in_=ppsb[:, PP_LAM:PP_LAM + 4], func=AF.Exp, scale=-1.0), reads=["ppsb"], writes=["cst"])
        P.op("act", lambda e: e.activation(out=cst2, in_=cst, func=AF.Ln, bias=1.0, scale=1.0), reads=["cst"], writes=["cst2"])
        P.op("dve", lambda e: e.tensor_scalar(out=cst, in0=cst2, scalar1=-8.0, scalar2=None, op0=ALU.mult), reads=["cst2"], writes=["cst"])
        x = A.f32(TT); xc = A.f32(TT); r = A.f32(TT); ig = A.f32(TT); a = A.f32(TT); hs = A.f32(TT); th = A.f32(TT)
        Wa = A.f32(128); Wx = A.f32(128)
        segs = [(0, LC), (LC, TT)]
        for ct in range(2):
            for d in range(2):
                col = d * 2 + ct
                P.dma("sp", x, lxT[ct * 128:(ct + 1) * 128, :], reads=["zT_lx"], writes=["x"])
                P.op("pool", lambda e: e.memset(Wa, 0.0), writes=["Wa"])
                P.op("pool", lambda e: e.memset(Wx, 0.0), writes=["Wx"])
                for bb in range(2):
                    P.dma("sp", Wa[bb * 64:(bb + 1) * 64, bb * 64:(bb + 1) * 64], lru_wa[l, d, ct * 2 + bb], writes=["Wa"], reads=["Wa"])
                    P.dma("sp", Wx[bb * 64:(bb + 1) * 64, bb * 64:(bb + 1) * 64], lru_wx[l, d, ct * 2 + bb], writes=["Wx"], reads=["Wx"])
                wcol = lambda k: ppsb[:, PP_CONVW + (d * 4 + k) * 2 + ct:PP_CONVW + (d * 4 + k) * 2 + ct + 1]
                bcol = ppsb[:, PP_CONVB + col:PP_CONVB + col + 1]
                P.op("dve", lambda e: e.tensor_scalar(out=xc, in0=x, scalar1=wcol(3), scalar2=bcol, op0=ALU.mult, op1=ALU.add), reads=["x", "ppsb"], writes=["xc"])
                for (s0, s1) in segs:
                    for sh in (1, 2, 3):
                        k = 3 - sh
                        if d == 0:
                            o_ap, i_ap = xc[:, s0 + sh:s1], x[:, s0:s1 - sh]
                        else:
                            o_ap, i_ap = xc[:, s0:s1 - sh], x[:, s0 + sh:s1]
                        P.op("dve", lambda e, o_ap=o_ap, i_ap=i_ap, k=k: e.scalar_tensor_tensor(out=o_ap, in0=i_ap, scalar=wcol(k), in1=o_ap, op0=ALU.mult, op1=ALU.add),
                             reads=["x", "ppsb", "xc"], writes=["xc"])
                psr = Rot([0, 1, 2, 3])
                for (Wm, dst, bc0, nm) in ((Wa, r, PP_BA, "r"), (Wx, ig, PP_BX, "ig")):
                    for b0 in range(0, TT, 512):
                        n = min(512, TT - b0)
                        pi = psr.next()
                        P.op("pe", lambda e, pi=pi, b0=b0, n=n, Wm=Wm: e.matmul(PS[pi][:, 0:n], lhsT=Wm, rhs=xc[:, b0:b0 + n], start=True, stop=True),
                             reads=["Wa", "Wx", "xc"], writes=["ps%d" % pi])
                        P.op("act", lambda e, pi=pi, b0=b0, n=n, dst=dst, bc0=bc0: e.activation(out=dst[:, b0:b0 + n], in_=PS[pi][:, 0:n], func=AF.Sigmoid,
                                                                                              bias=ppsb[:, bc0 + col:bc0 + col + 1]),
                             reads=["ppsb"], writes=["ps%d" % pi, nm])
                P.op("act", lambda e: e.activation(out=a, in_=r, func=AF.Exp, scale=cst[:, col:col + 1]), reads=["r", "cst"], writes=["a"])
                P.op("pool", lambda e: e.tensor_tensor(out=r, in0=a, in1=a, op=ALU.mult), reads=["a"], writes=["r"])
                P.op("act", lambda e: e.activation(out=r, in_=r, func=AF.Sqrt, scale=-1.0, bias=1.0), reads=["r"], writes=["r"])
                P.op("dve", lambda e: e.tensor_tensor(out=ig, in0=ig, in1=r, op=ALU.mult), reads=["ig", "r"], writes=["ig"])
                P.op("dve", lambda e: e.tensor_tensor(out=xc, in0=xc, in1=ig, op=ALU.mult), reads=["ig", "xc"], writes=["xc"])
                if d == 0:
                    P.op("dve", lambda e: e.tensor_tensor_scan(out=hs, data0=a, data1=xc, initial=0.0, op0=ALU.mult, op1=ALU.add), reads=["a", "xc"], writes=["hs"])
                else:
                    P.op("dve", lambda e: e.tensor_tensor_scan(out=th[:, 0:LC][:, ::-1], data0=a[:, 0:LC][:, ::-1], data1=xc[:, 0:LC][:, ::-1], initial=0.0,
                                                               op0=ALU.mult, op1=ALU.add), reads=["a", "xc"], writes=["th"])
                    P.op("dve", lambda e: e.tensor_tensor_scan(out=th[:, LC:TT][:, ::-1], data0=a[:, LC:TT][:, ::-1], data1=xc[:, LC:TT][:, ::-1], initial=th[:, 0:1],
                                                               op0=ALU.mult, op1=ALU.add), reads=["a", "xc", "th"], writes=["th"])
                    P.op("dve", lambda e: e.tensor_tensor(out=hs, in0=hs, in1=th, op=ALU.add), reads=["hs", "th"], writes=["hs"])
            P.dma("sp", lruT[ct * 128:(ct + 1) * 128, :], hs, reads=["hs"], writes=["lruT"])

    def s5(l):
        A.reset()
        NG = 16
        prm = A.f32(3 * NG).rearrange("p (a g) -> p a g", a=3)
        Bsb = A.f32(2 * NG * 16).rearrange("p (a g h) -> p a g h", a=2, g=NG)
        Csb = A.f32(2 * NG * 16).rearrange("p (a g h) -> p a g h", a=2, g=NG)
        P.dma("sp", prm, s5p[l], writes=["prm"])
        P.dma("sp", Bsb, s5b[l], writes=["Bsb"])
        P.dma("sp", Csb, s5c[l], writes=["Csb"])
        tauf = A.f32(2 * NC8)
        tau = tauf.rearrange("p (d c) -> p d c", d=2)
        P.dma("sp", tauf, tau_in.partition_broadcast(128), writes=["tau"])
        dt = A.f32(NG); lrdt = A.f32(NG); th = A.f32(NG); u8 = A.f32(NG); rho8 = A.f32(NG)
        t1 = A.f32(NG); t2 = A.f32(NG); t3 = A.f32(NG); den = A.f32(NG); cr = A.f32(NG); ci = A.f32(NG)
        J = jidx[:].rearrange("p d g j -> p (d g) j")
        mg = A.f32(NG * 9).rearrange("p (g j) -> p g j", j=9)
        xa = A.f32(NG * 9).rearrange("p (g j) -> p g j", j=9)
        xr = A.f32(NG * 9).rearrange("p (g j) -> p g j", j=9)
        sn = A.f32(NG * 9).rearrange("p (g j) -> p g j", j=9)
        cs_ = A.f32(NG * 9).rearrange("p (g j) -> p g j", j=9)
        ar = A.f32(NG * 9).rearrange("p (g j) -> p g j", j=9)
        ai = A.f32(NG * 9).rearrange("p (g j) -> p g j", j=9)
        mr = A.f32(NG * 8).rearrange("p (g j) -> p g j", j=8)
        mi = A.f32(NG * 8).rearrange("p (g j) -> p g j", j=8)
        br_ = A.f32(NG * 8).rearrange("p (g j) -> p g j", j=8)
        bi_ = A.f32(NG * 8).rearrange("p (g j) -> p g j", j=8)
        w1 = A.f32(NG * 9).rearrange("p (g j) -> p g j", j=9)
        K = ["prm", "s5t"]

        def op(eng, fn):
            P.op(eng, fn, reads=K, writes=["s5t"])

        def bg(v, n):
            return v.unsqueeze(2).to_broadcast([128, NG, n])

        op("act", lambda e: e.activation(out=dt, in_=prm[:, 2, :], func=AF.Exp))
        op("dve", lambda e: e.tensor_tensor(out=lrdt, in0=prm[:, 0, :], in1=dt, op=ALU.mult))
        op("dve", lambda e: e.tensor_tensor(out=th, in0=prm[:, 1, :], in1=dt, op=ALU.mult))
        op("dve", lambda e: e.tensor_scalar(out=th, in0=th, scalar1=1.0 / TWO_PI, scalar2=None, op0=ALU.mult))
        op("dve", lambda e: e.tensor_tensor(out=mg, in0=J, in1=bg(lrdt, 9), op=ALU.mult))
        op("act", lambda e: e.activation(out=mg, in_=mg, func=AF.Exp))
        op("dve", lambda e: e.tensor_tensor(out=xa, in0=J, in1=bg(th, 9), op=ALU.mult))
        op("dve", lambda e: e.tensor_scalar(out=xr, in0=xa, scalar1=MAGIC, scalar2=-MAGIC, op0=ALU.add, op1=ALU.add))
        op("dve", lambda e: e.tensor_tensor(out=w1, in0=xa, in1=xr, op=ALU.subtract))
        op("act", lambda e: e.activation(out=sn, in_=w1, func=AF.Sin, scale=TWO_PI))
        op("dve", lambda e: e.tensor_scalar(out=xa, in0=xa, scalar1=0.25, scalar2=None, op0=ALU.add))
        op("dve", lambda e: e.tensor_scalar(out=xr, in0=xa, scalar1=MAGIC, scalar2=-MAGIC, op0=ALU.add, op1=ALU.add))
        op("dve", lambda e: e.tensor_tensor(out=w1, in0=xa, in1=xr, op=ALU.subtract))
        op("act", lambda e: e.activation(out=cs_, in_=w1, func=AF.Sin, scale=TWO_PI))
        op("dve", lambda e: e.tensor_tensor(out=ar, in0=mg, in1=cs_, op=ALU.mult))
        op("dve", lambda e: e.tensor_tensor(out=ai, in0=mg, in1=sn, op=ALU.mult))
        op("dve", lambda e: e.tensor_tensor(out=w1, in0=mg, in1=mg, op=ALU.mult))
        op("dve", lambda e: e.reciprocal(out=w1, in_=w1))
        op("dve", lambda e: e.tensor_tensor(out=mr, in0=ar[:, :, 0:8], in1=w1[:, :, 0:8], op=ALU.mult))
        op("dve", lambda e: e.scalar_tensor_tensor(out=mi, in0=ai[:, :, 0:8], scalar=-1.0, in1=w1[:, :, 0:8], op0=ALU.mult, op1=ALU.mult))
        a1r = A.f32(NG); a1i = A.f32(NG)
        op("dve", lambda e: e.tensor_copy(out=a1r[:, 0:8], in_=ar[:, 0:8, 1]))
        op("dve", lambda e: e.tensor_copy(out=a1r[:, 8:16], in_=ar[:, 8:16, 6]))
        op("dve", lambda e: e.tensor_copy(out=a1i[:, 0:8], in_=ai[:, 0:8, 1]))
        op("dve", lambda e: e.tensor_copy(out=a1i[:, 8:16], in_=ai[:, 8:16, 6]))
        lr_ = prm[:, 0, :]; li_ = prm[:, 1, :]
        op("dve", lambda e: e.tensor_tensor(out=den, in0=lr_, in1=lr_, op=ALU.mult))
        op("dve", lambda e: e.tensor_tensor(out=t1, in0=li_, in1=li_, op=ALU.mult))
        op("dve", lambda e: e.tensor_tensor(out=den, in0=den, in1=t1, op=ALU.add))
        op("dve", lambda e: e.reciprocal(out=den, in_=den))
        op("dve", lambda e: e.tensor_scalar(out=t1, in0=a1r, scalar1=-1.0, scalar2=None, op0=ALU.add))
        op("dve", lambda e: e.tensor_tensor(out=t2, in0=t1, in1=lr_, op=ALU.mult))
        op("dve", lambda e: e.tensor_tensor(out=t3, in0=a1i, in1=li_, op=ALU.mult))
        op("dve", lambda e: e.tensor_tensor(out=t2, in0=t2, in1=t3, op=ALU.add))
        op("dve", lambda e: e.tensor_tensor(out=cr, in0=t2, in1=den, op=ALU.mult))
        op("dve", lambda e: e.tensor_tensor(out=t2, in0=a1i, in1=lr_, op=ALU.mult))
        op("dve", lambda e: e.tensor_tensor(out=t3, in0=t1, in1=li_, op=ALU.mult))
        op("dve", lambda e: e.tensor_tensor(out=t2, in0=t2, in1=t3, op=ALU.subtract))
        op("dve", lambda e: e.tensor_tensor(out=ci, in0=t2, in1=den, op=ALU.mult))
        w8a = A.f32(NG * 8).rearrange("p (g j) -> p g j", j=8)
        op("dve", lambda e: e.tensor_tensor(out=br_, in0=mr, in1=bg(cr, 8), op=ALU.mult))
        op("dve", lambda e: e.tensor_tensor(out=w8a, in0=mi, in1=bg(ci, 8), op=ALU.mult))
        op("dve", lambda e: e.tensor_tensor(out=br_, in0=br_, in1=w8a, op=ALU.subtract))
        op("dve", lambda e: e.tensor_tensor(out=bi_, in0=mr, in1=bg(ci, 8), op=ALU.mult))
        op("dve", lambda e: e.tensor_tensor(out=w8a, in0=mi, in1=bg(cr, 8), op=ALU.mult))
        op("dve", lambda e: e.tensor_tensor(out=bi_, in0=bi_, in1=w8a, op=ALU.add))
        op("dve", lambda e: e.tensor_copy(out=rho8, in_=mg[:, :, 8]))
        op("dve", lambda e: e.tensor_scalar(out=u8, in0=th, scalar1=8.0, scalar2=None, op0=ALU.mult))
        op("dve", lambda e: e.tensor_scalar(out=t1, in0=u8, scalar1=MAGIC, scalar2=-MAGIC, op0=ALU.add, op1=ALU.add))
        op("dve", lambda e: e.tensor_tensor(out=u8, in0=u8, in1=t1, op=ALU.subtract))
        SZ = NG * 8 * 16
        Btr = A.f32(SZ).rearrange("p (g j h) -> p g j h", g=NG, j=8)
        Bti = A.f32(SZ).rearrange("p (g j h) -> p g j h", g=NG, j=8)
        Ctr = A.f32(SZ).rearrange("p (g j h) -> p g j h", g=NG, j=8)
        Cti = A.f32(SZ).rearrange("p (g j h) -> p g j h", g=NG, j=8)
        regB = A.f32(4 * 2048)
        wk = regB[:, 0:2048].rearrange("p (g j h) -> p g j h", g=NG, j=8)

        def bj(v):
            return v.unsqueeze(3).to_broadcast([128, NG, 8, 16])

        def bh(v):
            return v.unsqueeze(2).to_broadcast([128, NG, 8, 16])

        Br, Bi = Bsb[:, 0], Bsb[:, 1]
        Cr, Ci = Csb[:, 0], Csb[:, 1]
        KB = ["s5t", "Bsb", "Csb", "s5m"]

        def opb(fn):
            P.op("dve", fn, reads=KB, writes=["s5m"])

        opb(lambda e: e.tensor_tensor(out=Btr, in0=bj(br_), in1=bh(Br), op=ALU.mult))
        opb(lambda e: e.tensor_tensor(out=wk, in0=bj(bi_), in1=bh(Bi), op=ALU.mult))
        opb(lambda e: e.tensor_tensor(out=Btr, in0=Btr, in1=wk, op=ALU.subtract))
        opb(lambda e: e.tensor_tensor(out=Bti, in0=bj(br_), in1=bh(Bi), op=ALU.mult))
        opb(lambda e: e.tensor_tensor(out=wk, in0=bj(bi_), in1=bh(Br), op=ALU.mult))
        opb(lambda e: e.tensor_tensor(out=Bti, in0=Bti, in1=wk, op=ALU.add))
        opb(lambda e: e.tensor_tensor(out=Ctr, in0=bj(ar[:, :, 0:8]), in1=bh(Cr), op=ALU.mult))
        opb(lambda e: e.tensor_tensor(out=wk, in0=bj(ai[:, :, 0:8]), in1=bh(Ci), op=ALU.mult))
        opb(lambda e: e.tensor_tensor(out=Ctr, in0=Ctr, in1=wk, op=ALU.subtract))
        opb(lambda e: e.tensor_tensor(out=Cti, in0=bj(ai[:, :, 0:8]), in1=bh(Cr), op=ALU.mult))
        opb(lambda e: e.tensor_tensor(out=wk, in0=bj(ar[:, :, 0:8]), in1=bh(Ci), op=ALU.mult))
        opb(lambda e: e.tensor_tensor(out=Cti, in0=Cti, in1=wk, op=ALU.add))
        opb(lambda e: e.tensor_scalar(out=Cti, in0=Cti, scalar1=-1.0, scalar2=None, op0=ALU.mult))
        NCT = (NC8 + 127) // 128
        U8 = A.f32(16 * NC8).rearrange("p (g c) -> p g c", g=16)
        cst_ = [regB[:, 2048:4096], regB[:, 4096:6144]]
        Ug = regB[:, 6144:8192]
        Yst = cst_

        def chunk_tiles():
            tiles = []
            c = 0
            while c < NC8:
                n = min(128, NC8 - c)
                tiles.append((c, n))
                c += n
            return tiles

        def chunk_dram(base, c0, n):
            pieces = []
            c = c0
            while c < c0 + n:
                if c < LC8:
                    m = min(c0 + n, LC8) - c
                    ap = base[c * 8:(c + m) * 8, :].rearrange("(c i) h -> c i h", i=8)
                    pieces.append((c - c0, m, ap))
                    c += m
                else:
                    cl = c - LC8
                    col, rb = cl // RB, cl % RB
                    m = min(RB - rb, c0 + n - c)
                    lat = base[LC:TT, :].rearrange("(rb i w) h -> w rb i h", i=8, w=64)
                    ap = lat[col, rb:rb + m, :, :]
                    pieces.append((c - c0, m, ap))
                    c += m
            return pieces

        psr = Rot([0, 1, 2, 3])
        for ti, (c0, n) in enumerate(chunk_tiles()):
            cs = cst_[ti % 2]
            cs3 = cs.rearrange("p (i h) -> p i h", i=8)
            for (p0, m, ap) in chunk_dram(su_tok, c0, n):
                P.dma("sp", cs3[p0:p0 + m, :, :], ap, reads=["su_tok"], writes=["cst%d" % (ti % 2)])
            P.op("dve", lambda e, n=n, cs=cs: e.tensor_copy(out=Ug[0:n].rearrange("p (g i h) -> p g i h", g=16, i=8),
                                                          in_=cs[0:n].rearrange("p (i g h) -> p g i h", i=8, g=16)),
                 reads=["cst%d" % (ti % 2)], writes=["Ug"])
            for g in range(16):
                pi = psr.next()
                P.op("pe", lambda e, pi=pi, g=g, n=n: e.transpose(PS[pi][:, 0:n], Ug[0:n, g * 128:(g + 1) * 128], ident_f[0:n, 0:n]),
                     reads=["Ug", "ident_f"], writes=["ps%d" % pi])
                P.op("act", lambda e, pi=pi, g=g, n=n, c0=c0: e.activation(out=U8[:, g, c0:c0 + n], in_=PS[pi][:, 0:n], func=AF.Identity),
                     reads=[], writes=["ps%d" % pi, "U8"])
        P.barrier()
        halves = [(0, min(512, NC8))] + ([(512, NC8)] if NC8 > 512 else [])

        def carve(base):
            o = [0]

            def take(n):
                ap = base[:, o[0]:o[0] + n]
                o[0] += n
                return ap
            d_ = {}
            d_["BtT"] = take(256).rearrange("p (a s) -> p a s", a=2)
            d_["M8"] = take(256).rearrange("p (g c) -> p g c", g=2)
            for nm in ("Zr", "Zi", "Wr", "Wi", "Or", "Oi", "Xr", "Xi", "tA", "tB", "Cn", "Sn"):
                d_[nm] = take(NC8)
            return d_

        SETW = 512 + 12 * NC8
        sets = [carve(A.f32(SETW)), carve(regB)]
        Yall = A.f32(16 * NC8).rearrange("p (g c) -> p g c", g=16)
        psr = Rot([0, 1, 2, 3, 4, 5, 6, 7])

        def front(it):
            d, gp = divmod(it, 8)
            dg = it
            par = str(it % 2)
            S_ = sets[it % 2]
            BtT, M8, Zr, Zi, Cn, Sn = S_["BtT"], S_["M8"], S_["Zr"], S_["Zi"], S_["Cn"], S_["Sn"]
            xx, rr = S_["Wr"], S_["Wi"]
            m8 = m8F if d == 0 else m8B
            for a_, Bt in enumerate((Btr, Bti)):
                pi = psr.next()
                P.op("pe", lambda e, pi=pi, Bt=Bt: e.transpose(PS[pi][:, 0:128], Bt[:, dg].rearrange("p j h -> p (j h)"), ident_f[:]),
                     reads=["s5m", "ident_f"], writes=["ps%d" % pi])
                P.op("act", lambda e, pi=pi, a_=a_: e.activation(out=BtT[:, a_, :], in_=PS[pi][:, 0:128], func=AF.Identity), reads=[], writes=["ps%d" % pi, "BtT" + par])
            for gm in range(2):
                pi = psr.next()
                pr = slice(gm * 64, (gm + 1) * 64)
                for a_, (Bt, Ct) in enumerate(((Btr, Ctr), (Bti, Cti))):
                    P.op("pe", lambda e, pi=pi, gm=gm, pr=pr, Bt=Bt, Ct=Ct, a_=a_: e.matmul(PS[pi][:, 0:128], lhsT=Bt[pr, dg].rearrange("p j h -> p (j h)"),
                                                                                         rhs=Ct[pr, dg].rearrange("p j h -> p (j h)"), start=(a_ == 0), stop=(a_ == 1)),
                         reads=["s5m"], writes=["ps%d" % pi])
                P.op("dve", lambda e, pi=pi, m8=m8, gm=gm: e.tensor_tensor(out=M8[:, gm, :], in0=PS[pi][:, 0:128], in1=m8[:, 0:128], op=ALU.mult),
                     reads=["m8"], writes=["ps%d" % pi, "M8" + par])
            for (Zt, a_) in ((Zr, 0), (Zi, 1)):
                for (h0, h1) in halves:
                    pi = psr.next()
                    for gm in range(2):
                        g = gp * 2 + gm
                        P.op("pe", lambda e, pi=pi, gm=gm, g=g, a_=a_, h0=h0, h1=h1: e.matmul(PS[pi][gm * 64:(gm + 1) * 64, 0:h1 - h0], lhsT=BtT[:, a_, gm * 64:(gm + 1) * 64],
                                                                                           rhs=U8[:, g, h0:h1], start=True, stop=True),
                             reads=["BtT" + par, "U8"], writes=["ps%d" % pi])
                    P.op("act", lambda e, pi=pi, Zt=Zt, h0=h0, h1=h1: e.activation(out=Zt[:, h0:h1], in_=PS[pi][:, 0:h1 - h0], func=AF.Identity),
                         reads=[], writes=["ps%d" % pi, "Z" + par])
            ucol = u8[:, dg:dg + 1]
            kx, kr = "Wr" + par, "Wi" + par
            P.op("dve", lambda e, ucol=ucol: e.tensor_scalar(out=xx, in0=tau[:, d, :], scalar1=ucol, scalar2=None, op0=ALU.mult), reads=["tau", "s5t"], writes=[kx])
            P.op("dve", lambda e: e.tensor_scalar(out=rr, in0=xx, scalar1=MAGIC, scalar2=-MAGIC, op0=ALU.add, op1=ALU.add), reads=[kx], writes=[kr])
            P.op("dve", lambda e: e.tensor_tensor(out=rr, in0=xx, in1=rr, op=ALU.subtract), reads=[kx, kr], writes=[kr])
            P.op("act", lambda e: e.activation(out=Sn, in_=rr, func=AF.Sin, scale=TWO_PI), reads=[kr], writes=["Sn" + par])
            P.op("dve", lambda e: e.tensor_scalar(out=xx, in0=xx, scalar1=0.25, scalar2=None, op0=ALU.add), reads=[kx], writes=[kx])
            P.op("dve", lambda e: e.tensor_scalar(out=rr, in0=xx, scalar1=MAGIC, scalar2=-MAGIC, op0=ALU.add, op1=ALU.add), reads=[kx, "Sn" + par], writes=[kr])
            P.op("dve", lambda e: e.tensor_tensor(out=rr, in0=xx, in1=rr, op=ALU.subtract), reads=[kx, kr], writes=[kr])
            P.op("act", lambda e: e.activation(out=Cn, in_=rr, func=AF.Sin, scale=TWO_PI), reads=[kr], writes=["Cn" + par])

        def back(it):
            d, gp = divmod(it, 8)
            dg = it
            par = str(it % 2)
            S_ = sets[it % 2]
            M8, Zr, Zi, Wr, Wi, Or, Oi = S_["M8"], S_["Zr"], S_["Zi"], S_["Wr"], S_["Wi"], S_["Or"], S_["Oi"]
            Xr, Xi, tA, tB, Cn, Sn = S_["Xr"], S_["Xi"], S_["tA"], S_["tB"], S_["Cn"], S_["Sn"]
            kC, kS, kZ, kWr, kWi, kA, kB, kX = "Cn" + par, "Sn" + par, "Z" + par, "Wr" + par, "Wi" + par, "tA" + par, "tB" + par, "X" + par
            P.op("dve", lambda e: e.tensor_tensor(out=Wr, in0=Cn, in1=Zr, op=ALU.mult), reads=[kC, kZ], writes=[kWr])
            P.op("dve", lambda e: e.tensor_tensor(out=tA, in0=Sn, in1=Zi, op=ALU.mult), reads=[kS, kZ], writes=[kA])
            P.op("dve", lambda e: e.tensor_tensor(out=Wr, in0=Wr, in1=tA, op=ALU.add), reads=[kWr, kA], writes=[kWr])
            P.op("dve", lambda e: e.tensor_tensor(out=Wi, in0=Cn, in1=Zi, op=ALU.mult), reads=[kC, kZ], writes=[kWi])
            P.op("dve", lambda e: e.tensor_tensor(out=tB, in0=Sn, in1=Zr, op=ALU.mult), reads=[kS, kZ], writes=[kB])
            P.op("dve", lambda e: e.tensor_tensor(out=Wi, in0=Wi, in1=tB, op=ALU.subtract), reads=[kWi, kB], writes=[kWi])
            rcol = rho8[:, dg:dg + 1]
            for (Wt, Ot, nm) in ((Wr, Or, "Or" + par), (Wi, Oi, "Oi" + par)):
                if d == 0:
                    P.op("dve", lambda e, Wt=Wt, Ot=Ot, rcol=rcol: e.tensor_tensor_scan(out=Ot, data0=Wt, data1=rcol.to_broadcast([128, NC8]), initial=0.0, op0=ALU.add, op1=ALU.mult),
                         reads=[kWr, kWi, "s5t"], writes=[nm])
                else:
                    P.op("dve", lambda e, Wt=Wt, Ot=Ot, rcol=rcol: e.tensor_tensor_scan(out=Ot[:, 0:LC8][:, ::-1], data0=Wt[:, 0:LC8][:, ::-1], data1=rcol.to_broadcast([128, LC8]),
                                                                                     initial=0.0, op0=ALU.add, op1=ALU.mult),
                         reads=[kWr, kWi, "s5t"], writes=[nm])
                    P.op("dve", lambda e, Wt=Wt, Ot=Ot, rcol=rcol: e.tensor_tensor_scan(out=Ot[:, LC8:NC8][:, ::-1], data0=Wt[:, LC8:NC8][:, ::-1], data1=rcol.to_broadcast([128, NC8 - LC8]),
                                                                                     initial=Ot[:, 0:1], op0=ALU.add, op1=ALU.mult),
                         reads=[kWr, kWi, "s5t", nm], writes=[nm])
            if d == 0:
                sh = [(slice(1, NC8), slice(0, NC8 - 1))]
                zero_cols = [0]
                carry = None
            else:
                sh = [(slice(0, LC8 - 1), slice(1, LC8)), (slice(LC8, NC8 - 1), slice(LC8 + 1, NC8))]
                zero_cols = [LC8 - 1]
                carry = (NC8 - 1, 0)
            RK = ["Or" + par, "Oi" + par, kC, kS, kX, kA, kB]
            pairs = list(sh)
            if carry is not None:
                dc, sc = carry
                pairs.append((slice(dc, dc + 1), slice(sc, sc + 1)))
            for (do, so) in pairs:
                P.op("dve", lambda e, do=do, so=so: e.tensor_tensor(out=Xr[:, do], in0=Cn[:, do], in1=Or[:, so], op=ALU.mult), reads=RK, writes=[kX])
                P.op("dve", lambda e, do=do, so=so: e.tensor_tensor(out=tA[:, do], in0=Sn[:, do], in1=Oi[:, so], op=ALU.mult), reads=RK, writes=[kA])
                P.op("dve", lambda e, do=do, so=so: e.tensor_tensor(out=Xr[:, do], in0=Xr[:, do], in1=tA[:, do], op=ALU.subtract), reads=RK, writes=[kX])
                P.op("dve", lambda e, do=do, so=so: e.tensor_tensor(out=Xi[:, do], in0=Cn[:, do], in1=Oi[:, so], op=ALU.mult), reads=RK, writes=[kX])
                P.op("dve", lambda e, do=do, so=so: e.tensor_tensor(out=tB[:, do], in0=Sn[:, do], in1=Or[:, so], op=ALU.mult), reads=RK, writes=[kB])
                P.op("dve", lambda e, do=do, so=so: e.tensor_tensor(out=Xi[:, do], in0=Xi[:, do], in1=tB[:, do], op=ALU.add), reads=RK, writes=[kX])
            for zc in zero_cols:
                P.op("dve", lambda e, zc=zc: e.memset(Xr[:, zc:zc + 1], 0.0), reads=RK, writes=[kX])
                P.op("dve", lambda e, zc=zc: e.memset(Xi[:, zc:zc + 1], 0.0), reads=RK, writes=[kX])
            for gm in range(2):
                g = gp * 2 + gm
                pr = slice(gm * 64, (gm + 1) * 64)
                for (h0, h1) in halves:
                    pi = psr.next()
                    P.op("pe", lambda e, pi=pi, gm=gm, g=g, h0=h0, h1=h1: e.matmul(PS[pi][:, 0:h1 - h0], lhsT=M8[:, gm, :], rhs=U8[:, g, h0:h1], start=True, stop=False),
                         reads=["M8" + par, "U8"], writes=["ps%d" % pi])
                    P.op("pe", lambda e, pi=pi, pr=pr, h0=h0, h1=h1: e.matmul(PS[pi][:, 0:h1 - h0], lhsT=Ctr[pr, dg].rearrange("p j h -> p (j h)"), rhs=Xr[pr, h0:h1], start=False, stop=False),
                         reads=["s5m", kX], writes=["ps%d" % pi])
                    P.op("pe", lambda e, pi=pi, pr=pr, h0=h0, h1=h1: e.matmul(PS[pi][:, 0:h1 - h0], lhsT=Cti[pr, dg].rearrange("p j h -> p (j h)"), rhs=Xi[pr, h0:h1], start=False, stop=True),
                         reads=["s5m", kX], writes=["ps%d" % pi])
                    if d == 0:
                        P.op("act", lambda e, pi=pi, g=g, h0=h0, h1=h1: e.activation(out=Yall[:, g, h0:h1], in_=PS[pi][:, 0:h1 - h0], func=AF.Identity),
                             reads=[], writes=["ps%d" % pi, "Yall"])
                    else:
                        P.op("dve", lambda e, pi=pi, g=g, h0=h0, h1=h1: e.tensor_tensor(out=Yall[:, g, h0:h1], in0=PS[pi][:, 0:h1 - h0], in1=Yall[:, g, h0:h1], op=ALU.add),
                             reads=[], writes=["ps%d" % pi, "Yall"])

        front(0)
        for it in range(16):
            if it + 1 < 16:
                front(it + 1)
            back(it)
        P.barrier()
        psr = Rot([0, 1, 2, 3])
        for ti, (c0, n) in enumerate(chunk_tiles()):
            ys = Yst[ti % 2]
            ys3 = ys.rearrange("p (i h) -> p i h", i=8)
            for g in range(16):
                pi = psr.next()
                P.op("pe", lambda e, pi=pi, g=g, n=n, c0=c0: e.transpose(PS[pi][0:n, 0:128], Yall[:, g, c0:c0 + n], ident_f[:]),
                     reads=["Yall", "ident_f"], writes=["ps%d" % pi])
                P.op("act", lambda e, pi=pi, g=g, n=n, ys3=ys3: e.activation(out=ys3[0:n, :, g * 16:(g + 1) * 16], in_=PS[pi][0:n, 0:128].rearrange("p (j h) -> p j h", j=8), func=AF.Identity),
                     reads=[], writes=["ps%d" % pi, "yst%d" % (ti % 2)])
            for (p0, m, ap) in chunk_dram(s5y, c0, n):
                P.dma("sp", ap, ys3[p0:p0 + m, :, :], reads=["yst%d" % (ti % 2)], writes=["s5y"])

    def phaseC1(l, last):
        A.reset()
        xsrc = xin if l == 0 else xres
        w1p = A.bf16(KT * 4 * D).rearrange("p (k n) -> p k n", k=KT)
        w2p = A.bf16(32 * D).rearrange("p (k n) -> p k n", k=32)
        w1v = w_ff1[l].rearrange("(kt p) n -> p kt n", p=128)
        w2v = w_ff2[l].rearrange("(kt p) n -> p kt n", p=128)
        pre = []
        for c0 in range(0, 4 * D, 512):
            pre.append((w1p[:, :, c0:c0 + 512], w1v[:, :, c0:c0 + 512], "w1"))
        for k0 in range(0, 32, 4):
            for c0 in range(0, D, 512):
                pre.append((w2p[:, k0:k0 + 4, c0:c0 + 512], w2v[:, k0:k0 + 4, c0:c0 + 512], "w2"))
        wo = A.bf16(KT * D).rearrange("p (k n) -> p k n", k=KT)
        wv = w_out[l].rearrange("(kt p) n -> p kt n", p=128)
        for c0 in range(0, D, 512):
            P.dma("poolq", wo[:, :, c0:c0 + 512], wv[:, :, c0:c0 + 512], writes=["wo"])
        glu = A.bf16(2 * 256).rearrange("p (k n) -> p k n", k=2)
        P.dma("poolq", glu, s5_glu_w[l].rearrange("(kt p) n -> p kt n", p=128), writes=["glu"])
        g1t = A.f32(D)
        g1bc = {0: g1t, 1: g1t}
        cur_w = [None]

        def load_g1(w):
            if cur_w[0] == w:
                return
            cur_w[0] = w
            P.dma("sp", g1t, mraw[l, w:w + 1, 2 * D:3 * D].partition_broadcast(128), reads=["mraw"], writes=["bc"])

        dbc = A.f32(256)
        P.dma("sp", dbc, s5_d[l:l + 1, :].partition_broadcast(128), writes=["bcd"])
        NB = 256
        o4 = A.f32(4 * NB).rearrange("p (h t) -> p h t", h=4)
        g4 = A.f32(4 * NB).rearrange("p (h t) -> p h t", h=4)
        sq = A.bf16(4 * NB).rearrange("p (h t) -> p h t", h=4)
        rn = A.f32(4 * NB).rearrange("p (h t) -> p h t", h=4)
        lh = A.f32(2 * NB).rearrange("p (h t) -> p h t", h=2)
        lg = A.f32(2 * NB).rearrange("p (h t) -> p h t", h=2)
        cat = A.bf16(KT * NB).rearrange("p (k t) -> p k t", k=KT)
        ysb = A.f32(2 * 256).rearrange("p (j c) -> p j c", j=2)
        usb = A.f32(2 * 256).rearrange("p (j c) -> p j c", j=2)
        sb16 = A.bf16(2 * 256).rearrange("p (j c) -> p j c", j=2)
        sTt = A.bf16(2 * NB).rearrange("p (k t) -> p k t", k=2)
        gsig = A.f32(2 * NB).rearrange("p (k t) -> p k t", k=2)
        xt = [A.f32(D), A.f32(D)]
        xo = [A.f32(D), A.f32(D)]
        t0 = 0 if not last else LC
        psr = Rot([0, 1, 2, 3, 4, 5, 6, 7])
        while t0 < TT:
            w = 1 if t0 < LC else 0
            n = min(NB, (LC if w == 1 else TT) - t0)
            tk = slice(t0, t0 + n)
            load_g1(w)
            for _ in range(3):
                if pre:
                    o_, i_, k_ = pre.pop(0)
                    P.dma("poolq", o_, i_, writes=[k_])
            P.dma("sp", o4[:, :, 0:n], oT.rearrange("(h p) t -> p h t", p=128)[:, :, tk], reads=["oT"], writes=["o4"])
            P.dma("sp", g4[:, :, 0:n], ggT.rearrange("(h p) t -> p h t", p=128)[:, :, tk], reads=["zT_gg"], writes=["g4"])
            P.op("pool", lambda e, n=n: e.tensor_tensor(out=sq[:, :, 0:n], in0=o4[:, :, 0:n], in1=o4[:, :, 0:n], op=ALU.mult), reads=["o4"], writes=["sq"])
            P.op("act", lambda e, n=n: e.activation(out=g4[:, :, 0:n], in_=g4[:, :, 0:n], func=AF.Silu), reads=["g4"], writes=["g4"])
            for h in range(4):
                pi = psr.next()
                P.op("pe", lambda e, pi=pi, h=h, n=n: e.matmul(PS[pi][:, 0:n], lhsT=ones_b[:], rhs=sq[:, h, 0:n], start=True, stop=True), reads=["sq", "ones_b"], writes=["ps%d" % pi])
                P.op("act", lambda e, pi=pi, h=h, n=n: e.activation(out=rn[:, h, 0:n], in_=PS[pi][:, 0:n], func=AF.Sqrt, scale=1.0 / 128.0, bias=EPS), reads=[], writes=["ps%d" % pi, "rn"])
            P.op("dve", lambda e, n=n: e.reciprocal(out=rn[:, :, 0:n], in_=rn[:, :, 0:n]), reads=["rn"], writes=["rn"])
            P.op("dve", lambda e, n=n: e.tensor_tensor(out=o4[:, :, 0:n], in0=o4[:, :, 0:n], in1=rn[:, :, 0:n], op=ALU.mult), reads=["o4", "rn"], writes=["o4"])
            for h in range(4):
                P.op("dve", lambda e, h=h, n=n: e.scalar_tensor_tensor(out=cat[:, h, 0:n], in0=o4[:, h, 0:n], scalar=ppsb[:, PP_GNORM + h:PP_GNORM + h + 1], in1=g4[:, h, 0:n],
                                                                       op0=ALU.mult, op1=ALU.mult), reads=["o4", "g4", "ppsb"], writes=["cat"])
            P.dma("sp", lh[:, :, 0:n], lruT.rearrange("(h p) t -> p h t", p=128)[:, :, tk], reads=["lruT"], writes=["lh"])
            P.dma("sp", lg[:, :, 0:n], lgT.rearrange("(h p) t -> p h t", p=128)[:, :, tk], reads=["zT_lg"], writes=["lg"])
            P.op("act", lambda e, n=n: e.activation(out=lg[:, :, 0:n], in_=lg[:, :, 0:n], func=AF.Gelu), reads=["lg"], writes=["lg"])
            P.op("dve", lambda e, n=n: e.tensor_tensor(out=cat[:, 4:6, 0:n], in0=lh[:, :, 0:n], in1=lg[:, :, 0:n], op=ALU.mult), reads=["lh", "lg"], writes=["cat"])
            nj = n // 128
            P.dma("sp", ysb[:, 0:nj, :], s5y[tk, :].rearrange("(j p) c -> p j c", p=128), reads=["s5y"], writes=["ysb"])
            P.dma("sp", usb[:, 0:nj, :], su_tok[tk, :].rearrange("(j p) c -> p j c", p=128), reads=["su_tok"], writes=["usb"])
            P.op("dve", lambda e, nj=nj: e.tensor_tensor(out=usb[:, 0:nj, :], in0=usb[:, 0:nj, :], in1=dbc.unsqueeze(1).to_broadcast([128, nj, 256]), op=ALU.mult), reads=["usb", "bcd"], writes=["usb"])
            P.op("dve", lambda e, nj=nj: e.tensor_tensor(out=ysb[:, 0:nj, :], in0=ysb[:, 0:nj, :], in1=usb[:, 0:nj, :], op=ALU.add), reads=["usb", "ysb"], writes=["ysb"])
            P.op("act", lambda e, nj=nj: e.activation(out=sb16[:, 0:nj, :], in_=ysb[:, 0:nj, :], func=AF.Gelu), reads=["ysb"], writes=["sb16"])
            for j in range(nj):
                pi = psr.next()
                pst = PS[pi][:].bitcast(BF16)
                for k in range(2):
                    P.op("pe", lambda e, pst=pst, j=j, k=k: e.transpose(pst[:, k * 128:(k + 1) * 128], sb16[:, j, k * 128:(k + 1) * 128], ident_b[:]), reads=["sb16", "ident_b"], writes=["ps%d" % pi])
                P.op("act", lambda e, pst=pst, j=j: e.activation(out=sTt[:, :, j * 128:(j + 1) * 128], in_=pst[:, 0:256].rearrange("p (k t) -> p k t", k=2), func=AF.Identity),
                     reads=[], writes=["ps%d" % pi, "sTt"])
            for ko in range(2):
                pi = psr.next()
                for ki_ in range(2):
                    P.op("pe", lambda e, pi=pi, ko=ko, ki_=ki_, n=n: e.matmul(PS[pi][:, 0:n], lhsT=glu[:, ki_, ko * 128:(ko + 1) * 128], rhs=sTt[:, ki_, 0:n], start=(ki_ == 0), stop=(ki_ == 1)),
                         reads=["glu", "sTt"], writes=["ps%d" % pi])
                P.op("act", lambda e, pi=pi, ko=ko, n=n: e.activation(out=gsig[:, ko, 0:n], in_=PS[pi][:, 0:n], func=AF.Sigmoid, bias=ppsb[:, PP_GLUB + ko:PP_GLUB + ko + 1]),
                     reads=["ppsb"], writes=["ps%d" % pi, "gsig"])
            P.op("dve", lambda e, n=n: e.tensor_tensor(out=cat[:, 6:8, 0:n], in0=sTt[:, :, 0:n], in1=gsig[:, :, 0:n], op=ALU.mult), reads=["sTt", "gsig"], writes=["cat"])
            for j in range(nj):
                ti0 = t0 + j * 128
                xs = (ti0 // 128) % 2
                P.dma("sp", xt[xs], xsrc[ti0:ti0 + 128, :], reads=["xsrc"], writes=["xtc%d" % xs])
                for hf_ in range(2):
                    pi = psr.next()
                    for kt in range(KT):
                        P.op("pe", lambda e, pi=pi, kt=kt, j=j, hf_=hf_: e.matmul(PS[pi][:, :], lhsT=cat[:, kt, j * 128:(j + 1) * 128], rhs=wo[:, kt, hf_ * 512:(hf_ + 1) * 512],
                                                                               start=(kt == 0), stop=(kt == KT - 1)),
                             reads=["cat", "wo"], writes=["ps%d" % pi])
                    P.op("dve", lambda e, pi=pi, xs=xs, hf_=hf_, w=w: e.tensor_tensor(out=xo[xs][:, hf_ * 512:(hf_ + 1) * 512], in0=PS[pi][:, :], in1=g1bc[w][:, hf_ * 512:(hf_ + 1) * 512], op=ALU.mult),
                         reads=["bc"], writes=["ps%d" % pi, "xo%d" % xs])
                P.op("pool", lambda e, xs=xs: e.tensor_tensor(out=xo[xs], in0=xo[xs], in1=xt[xs], op=ALU.add), reads=["xtc%d" % xs, "xo%d" % xs], writes=["xo%d" % xs])
                P.dma("sp", x1[ti0:ti0 + 128, :], xo[xs], reads=["xo%d" % xs], writes=["x1"])
            t0 += n
        while pre:
            o_, i_, k_ = pre.pop(0)
            P.dma("poolq", o_, i_, writes=[k_])

    def phaseC2(l, last):
        A.reset()
        w1 = A.bf16(KT * 4 * D).rearrange("p (k n) -> p k n", k=KT)
        w2 = A.bf16(32 * D).rearrange("p (k n) -> p k n", k=32)
        G = A.f32(D); S = A.f32(D); g2 = A.f32(D)
        cur_w = [None]

        def load_bc(w):
            if cur_w[0] == w:
                return
            cur_w[0] = w
            P.dma("sp", G, gsc[l, w, 1:2, :].partition_broadcast(128), reads=["gsc"], writes=["bc"])
            P.dma("sp", S, mraw[l, w:w + 1, 3 * D:4 * D].partition_broadcast(128), reads=["mraw"], writes=["bc"])
            P.dma("sp", g2, mraw[l, w:w + 1, 5 * D:6 * D].partition_broadcast(128), reads=["mraw"], writes=["bc"])
        if last:
            fn = A.f32(D)
            P.dma("sp", fn, final_norm.partition_broadcast(128), writes=["bcf"])
        NB = 256
        xts = [A.f32(D), A.f32(D)]
        hbs = [A.bf16(D), A.bf16(D)]
        junk = A.bf16(D); hf = A.f32(D)
        sss = [A.f32(1), A.f32(1)]; rss = [A.f32(1), A.f32(1)]
        hT = A.bf16(KT * NB).rearrange("p (k t) -> p k t", k=KT)
        uT = A.bf16(32 * NB).rearrange("p (k t) -> p k t", k=32)
        rl = [A.bf16(NB), A.bf16(NB)]
        yo = [A.f32(D), A.f32(D)]
        psT = Rot([0, 1]); psM = Rot([2, 3, 4, 5, 6, 7]); rlR = Rot([0, 1])
        t0 = 0 if not last else LC
        while t0 < TT:
            w = 1 if t0 < LC else 0
            n = min(NB, (LC if w == 1 else TT) - t0)
            nj = n // 128
            load_bc(w)
            for j in range(nj):
                ti0 = t0 + j * 128
                s = j % 2; sfx = str(s)
                P.dma("sp", xts[s], x1[ti0:ti0 + 128, :], reads=["x1"], writes=["xt" + sfx])
                rmsnorm_mod(xts[s], G, S, hbs[s], sfx, junk, sss[s], rss[s], hf)
                transpose_to(hbs[s], hT, j, sfx, psT)
            for ft in range(32):
                pi = psM.next()
                for kt in range(KT):
                    P.op("pe", lambda e, pi=pi, kt=kt, ft=ft, n=n: e.matmul(PS[pi][:, 0:n], lhsT=w1[:, kt, ft * 128:(ft + 1) * 128], rhs=hT[:, kt, 0:n], start=(kt == 0), stop=(kt == KT - 1)),
                         reads=["w1", "hT"], writes=["ps%d" % pi])
                ri = rlR.next()
                P.op("act", lambda e, pi=pi, ri=ri, n=n: e.activation(out=rl[ri][:, 0:n], in_=PS[pi][:, 0:n], func=AF.Relu), reads=[], writes=["ps%d" % pi, "rl%d" % ri])
                P.op("dve", lambda e, ri=ri, ft=ft, n=n: e.tensor_tensor(out=uT[:, ft, 0:n], in0=rl[ri][:, 0:n], in1=rl[ri][:, 0:n], op=ALU.mult), reads=["rl%d" % ri], writes=["uT"])
            for j in range(nj):
                ti0 = t0 + j * 128
                s = j % 2; sfx = str(s)
                for hf_ in range(2):
                    pi = psM.next()
                    for ft in range(32):
                        P.op("pe", lambda e, pi=pi, ft=ft, j=j, hf_=hf_: e.matmul(PS[pi][:, :], lhsT=uT[:, ft, j * 128:(j + 1) * 128], rhs=w2[:, ft, hf_ * 512:(hf_ + 1) * 512],
                                                                               start=(ft == 0), stop=(ft == 31)),
                             reads=["w2", "uT"], writes=["ps%d" % pi])
                    P.op("dve", lambda e, pi=pi, s=s, hf_=hf_, g2=g2: e.tensor_tensor(out=yo[s][:, hf_ * 512:(hf_ + 1) * 512], in0=PS[pi][:, :], in1=g2[:, hf_ * 512:(hf_ + 1) * 512], op=ALU.mult),
                         reads=["bc"], writes=["ps%d" % pi, "yo%d" % s])
                P.op("pool", lambda e, s=s: e.tensor_tensor(out=yo[s], in0=yo[s], in1=xts[s], op=ALU.add), reads=["xt" + str(s), "yo%d" % s], writes=["yo%d" % s])
                if not last:
                    P.dma("sp", xres[ti0:ti0 + 128, :], yo[s], reads=["yo%d" % s], writes=["xres"])
                else:
                    sfx2 = "f" + str(s)
                    P.op("act", lambda e, s=s: e.activation(out=junk, in_=yo[s], func=AF.Square, accum_out=sss[s]), reads=["yo%d" % s], writes=["junk", "ssf%d" % s])
                    P.op("act", lambda e, s=s: e.activation(out=rss[s], in_=sss[s], func=AF.Sqrt, scale=1.0 / D, bias=EPS), reads=["ssf%d" % s], writes=["rsf%d" % s])
                    P.op("dve", lambda e, s=s: e.reciprocal(out=rss[s], in_=rss[s]), reads=["rsf%d" % s], writes=["rsf%d" % s])
                    P.op("dve", lambda e, s=s: e.scalar_tensor_tensor(out=yo[s], in0=yo[s], scalar=rss[s], in1=fn, op0=ALU.mult, op1=ALU.mult),
                         reads=["yo%d" % s, "rsf%d" % s, "bcf"], writes=["yo%d" % s])
                    P.dma("sp", out_d[ti0 - LC:ti0 - LC + 128, :], yo[s], reads=["yo%d" % s], writes=["out"])
            t0 += n

    setup_consts()
    stages = build.stages if hasattr(build, "stages") else None
    for l in range(depth):
        last = (l == depth - 1)
        if stages == "C":
            break
        modulation(l)
        load_pp(l)
        P.barrier()
        if stages == "M":
            break
        phaseA(l)
        P.barrier()
        if stages is not None and "A" == stages:
            break
        print("nops before gla", P.nops, flush=True)
        gla(l)
        P.barrier()
        print("nops before lru", P.nops, flush=True)
        lru(l)
        P.barrier()
        print("nops before s5", P.nops, flush=True)
        s5(l)
        P.barrier()
        print("nops after s5", P.nops, flush=True)
        if stages is not None and "B" == stages:
            break
        phaseC1(l, last)
        P.barrier()
        phaseC2(l, last)
        P.barrier()
    P.barrier()
    print("nops", P.nops, flush=True)
    P.emit()
    P.close()
    return nc


def prep_inputs(inp, b, LL, LC, depth):
    f = lambda a: np.ascontiguousarray(np.asarray(a, dtype=np.float32))
    TT = LL + LC
    NC8, LC8 = TT // 8, LC // 8
    m = {}
    m["xin"] = f(np.concatenate([inp["ctx"][b], inp["x"][b]], axis=0))
    cv = np.stack([np.asarray(inp["c"][b]).reshape(KT, 128).T, np.asarray(inp["c_ctx"]).reshape(KT, 128).T], axis=-1)
    m["cvec"] = f(cv)
    for k in ("w_mod", "b_mod", "norm1", "norm2", "w_in", "gla_up_w", "lru_wa", "lru_wx", "s5_d", "s5_glu_w", "w_out", "w_ff1", "w_ff2"):
        m[k] = f(inp[k])
    m["final_norm"] = f(np.asarray(inp["final_norm"]).reshape(1, D))
    pp = np.zeros((depth, 128, 64), np.float32)
    for l in range(depth):
        for d in range(2):
            for hp in range(2):
                pp[l, :, 0 + d * 2 + hp] = inp["gla_up_b"][l, d, hp * 128:(hp + 1) * 128]
            for ct in range(2):
                sl = slice(ct * 128, (ct + 1) * 128)
                for k in range(4):
                    pp[l, :, 8 + (d * 4 + k) * 2 + ct] = inp["lru_conv_w"][l, d, k, sl]
                pp[l, :, 24 + d * 2 + ct] = inp["lru_conv_b"][l, d, sl]
                pp[l, :, 28 + d * 2 + ct] = inp["lru_ba"][l, d, sl]
                pp[l, :, 32 + d * 2 + ct] = inp["lru_bx"][l, d, sl]
                pp[l, :, 36 + d * 2 + ct] = inp["lru_lambda"][l, d, sl]
        for h in range(4):
            pp[l, :, 4 + h] = inp["gla_norm"][l, h * 128:(h + 1) * 128]
        for ct in range(2):
            pp[l, :, 40 + ct] = inp["s5_glu_b"][l, ct * 128:(ct + 1) * 128]
    m["pp"] = pp
    s5p = np.zeros((depth, 128, 3, 16), np.float32)
    s5b = np.zeros((depth, 128, 2, 16, 16), np.float32)
    s5c = np.zeros((depth, 128, 2, 16, 16), np.float32)
    for l in range(depth):
        for d in range(2):
            for gp in range(8):
                for gm in range(2):
                    g = gp * 2 + gm
                    ps_ = slice(gm * 64, (gm + 1) * 64)
                    s5p[l, ps_, 0, d * 8 + gp] = inp["s5_lam_re"][l, d, g]
                    s5p[l, ps_, 1, d * 8 + gp] = inp["s5_lam_im"][l, d, g]
                    s5p[l, ps_, 2, d * 8 + gp] = inp["s5_log_dt"][l, d, g]
                    s5b[l, ps_, 0, d * 8 + gp, :] = inp["s5_b_re"][l, d, g]
                    s5b[l, ps_, 1, d * 8 + gp, :] = inp["s5_b_im"][l, d, g]
                    s5c[l, ps_, 0, d * 8 + gp, :] = np.asarray(inp["s5_c_re"][l, d, g]).T
                    s5c[l, ps_, 1, d * 8 + gp, :] = np.asarray(inp["s5_c_im"][l, d, g]).T
    m["s5p"], m["s5b"], m["s5c"] = s5p, s5b, s5c
    tau = np.zeros((2, NC8), np.float32)
    tau[0] = np.arange(NC8)
    tau[1, :LC8] = LC8 - 1 - np.arange(LC8)
    tau[1, LC8:] = LC8 + (NC8 - 1 - np.arange(LC8, NC8))
    m["tau"] = tau.reshape(1, 2 * NC8)
    return m


_CACHE = {}


def kernel(**inputs):
    LL, LC, depth = 4096, 256, 4
    B = inputs["x"].shape[0]
    key = (LL, LC, depth)
    if key not in _CACHE:
        _CACHE[key] = build(LL, LC, depth)
    nc = _CACHE[key]
    in_maps = [prep_inputs(inputs, b, LL, LC, depth) for b in range(B)]
    res = run_bass_kernel_spmd(nc, in_maps, core_ids=list(range(B)))
    return np.stack([np.asarray(r["out"], dtype=np.float32) for r in res.results], axis=0)
```

```python
import math
import os
from contextlib import ExitStack

import numpy as np
import concourse.bass as bass
import concourse.mybir as mybir
from concourse.bass_utils import run_bass_kernel_spmd

F32 = mybir.dt.float32
BF16 = mybir.dt.bfloat16
ALU = mybir.AluOpType
AF = mybir.ActivationFunctionType

D = 1024
KT = 8
EPS = 1e-6
MAGIC = 12582912.0
TWO_PI = 2.0 * math.pi

COMPUTE = ("pe", "act", "dve", "pool")
QUEUES = ("sp", "poolq")
ENG_OF = {"pe": "pe", "act": "act", "dve": "dve", "pool": "pool", "sp": "sp", "poolq": "pool"}


class Prog:
    def __init__(self, nc, ndma=8):
        import os
        self.nc = nc
        self.es = ExitStack()
        self.streams = {e: [] for e in ("pe", "act", "dve", "pool", "sp")}
        self.sem = {}
        self.nop_eng = {}
        for e in COMPUTE:
            self.sem[e] = self.es.enter_context(nc.semaphore("s_" + e))
            self.nop_eng[e] = 0
        self.dsem, self.dcnt, self.dnext = {}, {}, {}
        for q in QUEUES:
            self.dsem[q] = [self.es.enter_context(nc.semaphore("d_%s%d" % (q, i))) for i in range(ndma)]
            self.dcnt[q] = [0] * ndma
            self.dnext[q] = 0
        self.seen = {e: {} for e in self.streams}
        self.lastw = {}
        self.readers = {}
        self.nops = 0
        self.waited = {e: set() for e in COMPUTE}
        self._cap = None
        self.limit = int(os.environ["OPLIMIT"]) if os.environ.get("OPLIMIT") else None

    def sbuf(self, name, shape, dtype=F32):
        return self.es.enter_context(self.nc.sbuf_tensor(name, list(shape), dtype))

    def psum(self, name, shape, dtype=F32):
        return self.es.enter_context(self.nc.psum_tensor(name, list(shape), dtype))

    @staticmethod
    def _tkey(tok):
        return ("c", tok[1]) if tok[0] == "c" else ("d", tok[1].name)

    @staticmethod
    def _tval(tok):
        return tok[2]

    def _need(self, stream, tok, waits):
        if tok is None:
            return
        if tok[0] == "c" and tok[1] == "pe" and stream == "pe":
            return
        k = self._tkey(tok)
        if self.seen[stream].get(k, 0) >= self._tval(tok):
            return
        cur = waits.get(k)
        if cur is None or self._tval(cur) < self._tval(tok):
            waits[k] = tok

    def _deps(self, stream, reads, writes, waits):
        for k in reads:
            self._need(stream, self.lastw.get(k), waits)
        for k in writes:
            self._need(stream, self.lastw.get(k), waits)
            for t in self.readers.get(k, ()):
                self._need(stream, t, waits)

    def _commit(self, stream, tok, reads, writes, waits):
        for k, t in waits.items():
            self.seen[stream][k] = self._tval(t)
            if t[0] == "c":
                self.waited[t[1]].add(t[2])
        for k in writes:
            self.lastw[k] = tok
            self.readers[k] = []
        for k in reads:
            if k in writes:
                continue
            lst = self.readers.setdefault(k, [])
            lst.append(tok)
            if len(lst) > 16:
                best = {}
                for t in lst:
                    kk = self._tkey(t)
                    b = best.get(kk)
                    if b is None or self._tval(b) < self._tval(t):
                        best[kk] = t
                self.readers[k] = list(best.values())

    def capture(self, f):
        prev = self._cap
        self._cap = []
        f()
        lst = self._cap
        self._cap = prev
        return lst

    def interleave(self, lists):
        lists = [list(l) for l in lists if l]
        idx = [0] * len(lists)
        while True:
            done = True
            for k, l in enumerate(lists):
                if idx[k] < len(l):
                    done = False
                    kind, a = l[idx[k]]
                    idx[k] += 1
                    if kind == "op":
                        self._op2(*a)
                    else:
                        self._dma2(*a)
            if done:
                break

    def op(self, eng, fn, reads=(), writes=()):
        if self.limit is not None and self.nops >= self.limit:
            return None
        rec = _Rec()
        fn(rec)
        name, args, kwargs = rec.call
        if self._cap is not None:
            self._cap.append(("op", (eng, name, args, kwargs, tuple(reads), tuple(writes))))
            return None
        return self._op2(eng, name, args, kwargs, reads, writes)

    def _op2(self, eng, name, args, kwargs, reads, writes):
        if os.environ.get("OPTRACE"):
            def _d(a):
                try:
                    return "%s%s" % (tuple(a.shape), "" )
                except Exception:
                    return str(a)[:30]
            print("OP", self.nops, eng, name, [_d(a) for a in args], {k: _d(v) for k, v in kwargs.items()}, flush=True)
        fn = lambda e, name=name, args=args, kwargs=kwargs: getattr(e, name)(*args, **kwargs)
        waits = {}
        self._deps(eng, reads, writes, waits)
        self.nop_eng[eng] += 1
        tok = ("c", eng, self.nop_eng[eng])
        self._commit(eng, tok, reads, writes, waits)
        self.streams[eng].append([list(waits.values()), fn, tok])
        self.nops += 1
        return tok

    def dma(self, q, out, in_, reads=(), writes=(), **kw):
        if self.limit is not None and self.nops >= self.limit:
            return None
        if self._cap is not None:
            self._cap.append(("dma", (q, out, in_, tuple(reads), tuple(writes), kw)))
            return None
        return self._dma2(q, out, in_, reads, writes, kw)

    def _dma2(self, q, out, in_, reads, writes, kw):
        stream = ENG_OF[q]
        waits = {}
        i = self.dnext[q]
        self.dnext[q] = (i + 1) % len(self.dsem[q])
        sem = self.dsem[q][i]
        if self.dcnt[q][i] > 0:
            self._need(stream, ("d", sem, 16 * self.dcnt[q][i], q), waits)
        self._deps(stream, reads, writes, waits)
        self.dcnt[q][i] += 1
        tok = ("d", sem, 16 * self.dcnt[q][i], q)
        self._commit(stream, tok, reads, writes, waits)
        fn = lambda e, out=out, in_=in_, kw=kw: e.dma_start(out=out, in_=in_, **kw)
        self.streams[stream].append([list(waits.values()), fn, tok])
        self.nops += 1
        return tok

    def barrier(self):
        toks = []
        for q in QUEUES:
            for i, sem in enumerate(self.dsem[q]):
                if self.dcnt[q][i]:
                    toks.append(("d", sem, 16 * self.dcnt[q][i], q))
        for e in COMPUTE:
            if self.nop_eng[e]:
                toks.append(("c", e, self.nop_eng[e]))
        for stream in self.streams:
            waits = {}
            for t in toks:
                if t[0] == "c" and t[1] == stream:
                    continue
                self._need(stream, t, waits)
            for k, t in waits.items():
                self.seen[stream][k] = self._tval(t)
                if t[0] == "c":
                    self.waited[t[1]].add(t[2])
            if waits:
                self.streams[stream].append([list(waits.values()), None, None])
        self.lastw = {}
        self.readers = {}

    def emit(self):
        nc = self.nc
        streams = self.streams
        rank = {}
        for e in COMPUTE:
            rank[e] = {idx: r + 1 for r, idx in enumerate(sorted(self.waited[e]))}

        def run(eng_obj, lst):
            for waits, fn, tok in lst:
                for t in waits:
                    if t[0] == "c":
                        eng_obj.wait_ge(self.sem[t[1]], rank[t[1]][t[2]])
                    else:
                        eng_obj.wait_ge(t[1], t[2])
                if fn is not None:
                    ins = fn(eng_obj)
                    if tok[0] == "d":
                        ins.then_inc(tok[1], 16)
                    elif tok[2] in rank[tok[1]]:
                        ins.then_inc(self.sem[tok[1]], 1)

        with nc.Block() as block:
            @block.tensor
            def _(e):
                run(e, streams["pe"])

            @block.scalar
            def _(e):
                run(e, streams["act"])

            @block.vector
            def _(e):
                run(e, streams["dve"])

            @block.gpsimd
            def _(e):
                run(e, streams["pool"])

            @block.sync
            def _(e):
                run(e, streams["sp"])

    def close(self):
        self.es.close()


class _Rec:
    def __init__(self):
        self.call = None

    def __getattr__(self, name):
        def f(*args, **kwargs):
            self.call = (name, args, kwargs)
            return self
        return f


class Arena:
    def __init__(self, P, words):
        self.t = P.sbuf("arena", [128, words], F32)
        self.words = words
        self.off = 0
        self.n = 0

    def reset(self):
        self.off = 0

    def f32(self, n):
        assert self.off + n <= self.words, ("arena overflow", self.off, n, self.words)
        ap = self.t[:, self.off:self.off + n]
        self.off += n
        return ap

    def bf16(self, n):
        w = (n + 1) // 2
        return self.f32(w).bitcast(BF16)[:, 0:n]


class Rot:
    def __init__(self, items):
        self.items = items
        self.i = 0

    def next(self):
        it = self.items[self.i % len(self.items)]
        self.i += 1
        return it


def build(LL, LC, depth, debug=False):
    TT = LL + LC
    NT = TT // 128
    NCH = TT // 64
    NC8 = TT // 8
    LC8 = LC // 8
    ROWS = LL // 64
    RB = ROWS // 8
    nc = bass.Bass("TRN2", target_bir_lowering=False)
    P = Prog(nc)

    def din(name, shape, dt=F32):
        return nc.dram_tensor(name, list(shape), dt, kind="ExternalInput").ap()

    dkind = "ExternalOutput" if debug else "Internal"

    def dscr(name, shape, dt=F32):
        return nc.dram_tensor(name, list(shape), dt, kind=dkind).ap()

    xin = din("xin", [TT, D])
    cvec = din("cvec", [128, KT, 2])
    w_mod = din("w_mod", [depth, D, 6 * D])
    b_mod = din("b_mod", [depth, 6 * D])
    norm1 = din("norm1", [depth, D])
    norm2 = din("norm2", [depth, D])
    w_in = din("w_in", [depth, D, 2336])
    gla_up_w = din("gla_up_w", [depth, 2, 16, 256])
    pp = din("pp", [depth, 128, 64])
    lru_wa = din("lru_wa", [depth, 2, 4, 64, 64])
    lru_wx = din("lru_wx", [depth, 2, 4, 64, 64])
    s5p = din("s5p", [depth, 128, 3, 16])
    s5b = din("s5b", [depth, 128, 2, 16, 16])
    s5c = din("s5c", [depth, 128, 2, 16, 16])
    s5_d = din("s5_d", [depth, 256])
    s5_glu_w = din("s5_glu_w", [depth, 256, 256])
    w_out = din("w_out", [depth, D, D])
    w_ff1 = din("w_ff1", [depth, D, 4 * D])
    w_ff2 = din("w_ff2", [depth, 4 * D, D])
    final_norm = din("final_norm", [1, D])
    tau_in = din("tau", [1, 2 * NC8])

    out_d = nc.dram_tensor("out", [LL, D], F32, kind="ExternalOutput").ap()

    mraw = dscr("mraw", [depth, 2, 6 * D])
    gsc = dscr("gsc", [depth, 2, 2, D])
    qT = dscr("qT", [256, TT]); kT = dscr("kT", [256, TT]); ggT = dscr("ggT", [512, TT])
    lrT = [dscr("lrT0", [16, TT]), dscr("lrT1", [16, TT])]
    lxT = dscr("lxT", [256, TT]); lgT = dscr("lgT", [256, TT])
    v_tok = dscr("v_tok", [TT, 512], BF16)
    su_tok = dscr("su_tok", [TT, 256])
    oT = dscr("oT", [512, TT])
    lruT = dscr("lruT", [256, TT])
    s5y = dscr("s5y", [TT, 256])
    x1 = dscr("x1", [TT, D])
    xres = dscr("xres", [TT, D])
    dbg = {}

    ident_f = P.sbuf("ident_f", [128, 128], F32)
    ident_b = P.sbuf("ident_b", [128, 128], BF16)
    ones_f = P.sbuf("ones_f", [128, 128], F32)
    ones_b = P.sbuf("ones_b", [128, 128], BF16)
    maskF = P.sbuf("maskF", [128, 128], F32)
    maskB = P.sbuf("maskB", [128, 128], F32)
    m8F = P.sbuf("m8F", [128, 512], F32)
    m8B = P.sbuf("m8B", [128, 512], F32)
    jidx = P.sbuf("jidx", [128, 2, 8, 9], F32)
    ppsb = P.sbuf("ppsb", [128, 64], F32)
    AW = 50400
    A = Arena(P, AW)
    PS = [P.psum("ps%d" % i, [128, 512], F32) for i in range(8)]

    def setup_consts():
        P.op("pool", lambda e: e.memset(ident_f[:], 0.0), writes=["ident_f"])
        P.op("pool", lambda e: e.affine_select(out=ident_f[:], in_=ident_f[:], pattern=[[-1, 128]], compare_op=ALU.not_equal,
                                               fill=1.0, base=0, channel_multiplier=1), reads=["ident_f"], writes=["ident_f"])
        P.op("dve", lambda e: e.tensor_copy(out=ident_b[:], in_=ident_f[:]), reads=["ident_f"], writes=["ident_b"])
        P.op("dve", lambda e: e.memset(ones_f[:], 1.0), writes=["ones_f"])
        P.op("dve", lambda e: e.memset(ones_b[:], 1.0), writes=["ones_b"])
        P.op("pool", lambda e: e.affine_select(out=maskF[:], in_=ones_f[:], pattern=[[1, 128]], compare_op=ALU.is_ge,
                                               fill=0.0, base=0, channel_multiplier=-1), reads=["ones_f"], writes=["maskF"])
        P.op("pool", lambda e: e.memset(maskF[0:64, 64:128], 0.0), reads=["maskF"], writes=["maskF"])
        P.op("pool", lambda e: e.affine_select(out=maskB[:], in_=ones_f[:], pattern=[[-1, 128]], compare_op=ALU.is_ge,
                                               fill=0.0, base=0, channel_multiplier=1), reads=["ones_f"], writes=["maskB"])
        P.op("pool", lambda e: e.memset(maskB[64:128, 0:64], 0.0), reads=["maskB"], writes=["maskB"])
        P.op("pool", lambda e: e.memset(m8F[:], 1.0), writes=["m8F"])
        P.op("pool", lambda e: e.memset(m8B[:], 1.0), writes=["m8B"])
        P.op("pool", lambda e: e.affine_select(out=m8F[:].rearrange("p (r j h) -> p r j h", r=4, j=8), in_=m8F[:].rearrange("p (r j h) -> p r j h", r=4, j=8),
                                               pattern=[[0, 4], [16, 8], [0, 16]], compare_op=ALU.is_ge, fill=0.0, base=15, channel_multiplier=-1),
             reads=["m8F"], writes=["m8F"])
        P.op("pool", lambda e: e.affine_select(out=m8B[:].rearrange("p (r j h) -> p r j h", r=4, j=8), in_=m8B[:].rearrange("p (r j h) -> p r j h", r=4, j=8),
                                               pattern=[[0, 4], [-16, 8], [0, 16]], compare_op=ALU.is_ge, fill=0.0, base=0, channel_multiplier=1),
             reads=["m8B"], writes=["m8B"])
        for j in range(9):
            P.op("dve", lambda e, j=j: e.memset(jidx[:, 0, :, j:j + 1], float(j)), writes=["jidx"])
            P.op("dve", lambda e, j=j: e.memset(jidx[:, 1, :, j:j + 1], float(7 - j) if j < 8 else 8.0), writes=["jidx"])

    PP_UPB = 0
    PP_GNORM = 4
    PP_CONVW = 8
    PP_CONVB = 24
    PP_BA = 28
    PP_BX = 32
    PP_LAM = 36
    PP_GLUB = 40
    PP_SGN = 42

    def modulation(l):
        A.reset()
        cs_raw = A.f32(16); cs = A.f32(16)
        msb = A.f32(6 * D)
        bm = A.f32(6 * D)
        n12 = A.f32(2 * D)
        gt = A.f32(2 * D)
        wblk = [A.f32(KT * 512), A.f32(KT * 512)]
        P.dma("sp", cs_raw, cvec.rearrange("p k w -> p (k w)"), writes=["cs_raw"])
        P.op("act", lambda e: e.activation(out=cs, in_=cs_raw, func=AF.Silu), reads=["cs_raw"], writes=["cs"])
        P.dma("sp", bm[0:2, :], b_mod[l:l + 1, :].partition_broadcast(2), writes=["bm"])
        P.dma("sp", n12[0:2, 0:D], norm1[l:l + 1, :].partition_broadcast(2), writes=["n12a"])
        P.dma("sp", n12[0:2, D:2 * D], norm2[l:l + 1, :].partition_broadcast(2), writes=["n12b"])
        wv = w_mod[l].rearrange("(kt p) n -> p kt n", p=128)
        cs3 = cs.rearrange("p (k w) -> p k w", w=2)
        for j in range(12):
            wb = wblk[j % 2]
            wb3 = wb.rearrange("p (k n) -> p k n", k=KT)
            P.dma("sp", wb3, wv[:, :, j * 512:(j + 1) * 512], writes=["wblk%d" % (j % 2)])
            ps = PS[j % 2]
            for kt in range(KT):
                P.op("pe", lambda e, ps=ps, kt=kt, wb3=wb3: e.matmul(ps[0:2, :], lhsT=cs3[:, kt, :], rhs=wb3[:, kt, :], start=(kt == 0), stop=(kt == KT - 1)),
                     reads=["cs", "wblk%d" % (j % 2)], writes=["ps%d" % (j % 2)])
            P.op("dve", lambda e, ps=ps, j=j: e.tensor_tensor(out=msb[0:2, j * 512:(j + 1) * 512], in0=ps[0:2, :], in1=bm[0:2, j * 512:(j + 1) * 512], op=ALU.add),
                 reads=["bm"], writes=["ps%d" % (j % 2), "msb"])
        P.op("dve", lambda e: e.scalar_tensor_tensor(out=gt[0:2, 0:D], in0=msb[0:2, D:2 * D], scalar=1.0, in1=n12[0:2, 0:D], op0=ALU.add, op1=ALU.mult),
             reads=["msb", "n12a"], writes=["gt"])
        P.op("dve", lambda e: e.scalar_tensor_tensor(out=gt[0:2, D:2 * D], in0=msb[0:2, 4 * D:5 * D], scalar=1.0, in1=n12[0:2, D:2 * D], op0=ALU.add, op1=ALU.mult),
             reads=["msb", "n12b", "gt"], writes=["gt"])
        P.dma("sp", mraw[l], msb[0:2, :], reads=["msb"], writes=["mraw"])
        P.dma("sp", gsc[l].rearrange("w g d -> w (g d)"), gt[0:2, :], reads=["gt"], writes=["gsc"])

    def bc_load(dst, src_row):
        return src_row.to_broadcast([128, src_row.shape[-1]])

    def token_blocks(last_skip_ctx=False):
        blks = []
        if not last_skip_ctx:
            t = 0
            while t < LC // 128:
                n = min(4, LC // 128 - t)
                blks.append((t, n, 1))
                t += n
        t = LC // 128
        while t < NT:
            n = min(4, NT - t)
            blks.append((t, n, 0))
            t += n
        return blks

    def rmsnorm_mod(xt_ap, Gbc, Sbc, hb_out, sfx, junk, ss, rs, hf):
        P.op("act", lambda e: e.activation(out=junk, in_=xt_ap, func=AF.Square, accum_out=ss), reads=["xt" + sfx], writes=["junk", "ss" + sfx])
        P.op("act", lambda e: e.activation(out=rs, in_=ss, func=AF.Sqrt, scale=1.0 / D, bias=EPS), reads=["ss" + sfx], writes=["rs" + sfx])
        P.op("dve", lambda e: e.reciprocal(out=rs, in_=rs), reads=["rs" + sfx], writes=["rs" + sfx])
        P.op("dve", lambda e: e.scalar_tensor_tensor(out=hf, in0=xt_ap, scalar=rs, in1=Gbc, op0=ALU.mult, op1=ALU.mult),
             reads=["xt" + sfx, "rs" + sfx, "bc"], writes=["hf"])
        P.op("pool", lambda e: e.tensor_tensor(out=hb_out, in0=hf, in1=Sbc, op=ALU.add), reads=["hf", "bc"], writes=["hb" + sfx])

    def transpose_to(hb, hT3, j, sfx, psrot):
        pi = psrot.next()
        pst = PS[pi][:].bitcast(BF16)
        for kt in range(KT):
            P.op("pe", lambda e, kt=kt, pst=pst: e.transpose(pst[:, kt * 128:(kt + 1) * 128], hb[:, kt * 128:(kt + 1) * 128], ident_b[:]),
                 reads=["hb" + sfx, "ident_b"], writes=["ps%d" % pi])
        P.op("act", lambda e, pst=pst: e.activation(out=hT3[:, :, j * 128:(j + 1) * 128], in_=pst.rearrange("p (k t) -> p k t", k=KT), func=AF.Identity),
             reads=[], writes=["ps%d" % pi, "hT"])

    def phaseA(l):
        A.reset()
        xsrc = xin if l == 0 else xres
        win = A.bf16(KT * 2336).rearrange("p (k n) -> p k n", k=KT)
        wv = w_in[l].rearrange("(kt p) n -> p kt n", p=128)
        for c0 in range(0, 2336, 512):
            c1 = min(2336, c0 + 512)
            P.dma("poolq", win[:, :, c0:c1], wv[:, :, c0:c1], writes=["win"])
        bcs = {}
        for w in (0, 1):
            G = A.f32(D); S = A.f32(D)
            P.dma("sp", G, gsc[l, w, 0:1, :].partition_broadcast(128), reads=["gsc"], writes=["bc"])
            P.dma("sp", S, mraw[l, w:w + 1, 0:D].partition_broadcast(128), reads=["mraw"], writes=["bc"])
            bcs[w] = (G, S)
        NS = 8
        xts = [A.f32(D) for _ in range(2)]
        hbs = [A.bf16(D) for _ in range(NS)]
        hfs = [A.f32(D), A.f32(D)]
        sss = [A.f32(1) for _ in range(NS)]; rss = [A.f32(1) for _ in range(NS)]
        hTs = [A.bf16(KT * 512).rearrange("p (k t) -> p k t", k=KT) for _ in range(2)]
        stg = [A.f32(512) for _ in range(3)]
        vst = [A.bf16(512) for _ in range(2)]
        sst = [A.f32(256) for _ in range(2)]
        psT = Rot([0, 1]); psM = Rot([2, 3, 4, 5, 6, 7])
        stgR = Rot([0, 1, 2]); vR = Rot([0, 1]); sR = Rot([0, 1])
        FM = [("q", 0, qT, 0, 128), ("q", 128, qT, 128, 128), ("k", 256, kT, 0, 128), ("k", 384, kT, 128, 128)]
        for i in range(4):
            FM.append(("gg", 1024 + 128 * i, ggT, 128 * i, 128))
        FM.append(("lr0", 1536, lrT[0], 0, 16)); FM.append(("lr1", 1552, lrT[1], 0, 16))
        for i in range(2):
            FM.append(("lx", 1568 + 128 * i, lxT, 128 * i, 128))
        for i in range(2):
            FM.append(("lg", 1824 + 128 * i, lgT, 128 * i, 128))
        blocks = token_blocks()
        slot_ctr = [0]
        blk_slots = {}

        def norms(bi):
            (t0, n, w) = blocks[bi]
            G, S = bcs[w]
            sl = []
            for j in range(n):
                ti = t0 + j
                s = slot_ctr[0] % NS; slot_ctr[0] += 1
                xs = s % 2
                sl.append(s)
                kx, kh = "xt%d" % xs, "hb%d" % s
                P.dma("sp", xts[xs], xsrc[ti * 128:(ti + 1) * 128, :], reads=["xsrc"], writes=[kx])
                P.op("act", lambda e, xs=xs, s=s: e.activation(out=hbs[s], in_=xts[xs], func=AF.Square, accum_out=sss[s]), reads=[kx], writes=[kh, "ss%d" % s])
                P.op("act", lambda e, s=s: e.activation(out=rss[s], in_=sss[s], func=AF.Sqrt, scale=1.0 / D, bias=EPS), reads=["ss%d" % s], writes=["rs%d" % s])
                P.op("dve", lambda e, s=s: e.reciprocal(out=rss[s], in_=rss[s]), reads=["rs%d" % s], writes=["rs%d" % s])
                P.op("dve", lambda e, xs=xs, s=s, G=G: e.scalar_tensor_tensor(out=hfs[xs], in0=xts[xs], scalar=rss[s], in1=G, op0=ALU.mult, op1=ALU.mult),
                     reads=[kx, "rs%d" % s, "bc"], writes=["hf%d" % xs])
                P.op("pool", lambda e, xs=xs, s=s, S=S: e.tensor_tensor(out=hbs[s], in0=hfs[xs], in1=S, op=ALU.add), reads=["hf%d" % xs, "bc"], writes=[kh])
            blk_slots[bi] = sl

        def transposes(bi):
            hT = hTs[bi % 2]
            for j, s in enumerate(blk_slots[bi]):
                pi = psT.next()
                pst = PS[pi][:].bitcast(BF16)
                for kt in range(KT):
                    P.op("pe", lambda e, kt=kt, pst=pst, s=s: e.transpose(pst[:, kt * 128:(kt + 1) * 128], hbs[s][:, kt * 128:(kt + 1) * 128], ident_b[:]),
                         reads=["hb%d" % s, "ident_b"], writes=["ps%d" % pi])
                P.op("act", lambda e, pst=pst, j=j, hT=hT: e.activation(out=hT[:, :, j * 128:(j + 1) * 128], in_=pst.rearrange("p (k t) -> p k t", k=KT), func=AF.Identity),
                     reads=[], writes=["ps%d" % pi, "hT%d" % (bi % 2)])

        def fm_part(bi):
            (t0, n, w) = blocks[bi]
            hT = hTs[bi % 2]; kT_ = "hT%d" % (bi % 2)
            ntok = n * 128; tok0 = t0 * 128
            for (nm, c0, dst, r0, m) in FM:
                pi = psM.next()
                for kt in range(KT):
                    P.op("pe", lambda e, pi=pi, kt=kt, c0=c0, m=m, hT=hT, ntok=ntok: e.matmul(PS[pi][0:m, 0:ntok], lhsT=win[:, kt, c0:c0 + m], rhs=hT[:, kt, 0:ntok],
                                                                                           start=(kt == 0), stop=(kt == KT - 1)),
                         reads=["win", kT_], writes=["ps%d" % pi])
                si = stgR.next()
                P.op("act", lambda e, pi=pi, si=si, m=m, ntok=ntok: e.activation(out=stg[si][0:m, 0:ntok], in_=PS[pi][0:m, 0:ntok], func=AF.Identity),
                     reads=[], writes=["ps%d" % pi, "stg%d" % si])
                P.dma("sp", dst[r0:r0 + m, tok0:tok0 + ntok], stg[si][0:m, 0:ntok], reads=["stg%d" % si], writes=["zT_%s_%d" % (nm, r0)])

        def tm_part(bi):
            (t0, n, w) = blocks[bi]
            hT = hTs[bi % 2]; kT_ = "hT%d" % (bi % 2)
            for j in range(n):
                ti = t0 + j
                pi = psM.next()
                for kt in range(KT):
                    P.op("pe", lambda e, pi=pi, kt=kt, j=j, hT=hT: e.matmul(PS[pi][:, :], lhsT=hT[:, kt, j * 128:(j + 1) * 128], rhs=win[:, kt, 512:1024],
                                                                         start=(kt == 0), stop=(kt == KT - 1)),
                         reads=["win", kT_], writes=["ps%d" % pi])
                vi = vR.next()
                P.op("dve", lambda e, pi=pi, vi=vi: e.tensor_copy(out=vst[vi], in_=PS[pi][:, :]), reads=[], writes=["ps%d" % pi, "vst%d" % vi])
                P.dma("sp", v_tok[ti * 128:(ti + 1) * 128, :], vst[vi], reads=["vst%d" % vi], writes=["v_tok%d" % ti])
                pi = psM.next()
                for kt in range(KT):
                    P.op("pe", lambda e, pi=pi, kt=kt, j=j, hT=hT: e.matmul(PS[pi][:, 0:256], lhsT=hT[:, kt, j * 128:(j + 1) * 128], rhs=win[:, kt, 2080:2336],
                                                                         start=(kt == 0), stop=(kt == KT - 1)),
                         reads=["win", kT_], writes=["ps%d" % pi])
                si = sR.next()
                P.op("dve", lambda e, pi=pi, si=si: e.tensor_copy(out=sst[si], in_=PS[pi][:, 0:256]), reads=[], writes=["ps%d" % pi, "sst%d" % si])
                P.dma("sp", su_tok[ti * 128:(ti + 1) * 128, :], sst[si], reads=["sst%d" % si], writes=["su_tok%d" % ti])

        norms(0)
        transposes(0)
        for bi in range(len(blocks)):
            if bi + 1 < len(blocks):
                norms(bi + 1)
            fm_part(bi)
            if bi + 1 < len(blocks):
                transposes(bi + 1)
            tm_part(bi)

    def load_pp(l):
        P.dma("sp", ppsb[:], pp[l], writes=["ppsb"])

    def gla(l):
        for hp in range(2):
            A.reset()
            sm0 = A.bf16(TT); sm1 = A.bf16(TT)
            P.op("pool", lambda e: e.memset(sm0, 1.0), writes=["sm"])
            P.op("pool", lambda e: e.memset(sm0.rearrange("p (c j) -> p c j", j=64)[:, :, 0:1], 0.0), reads=["sm"], writes=["sm"])
            P.op("pool", lambda e: e.memset(sm1, 1.0), reads=["sm"], writes=["sm"])
            P.op("pool", lambda e: e.memset(sm1.rearrange("p (c j) -> p c j", j=64)[:, :, 63:64], 0.0), reads=["sm"], writes=["sm"])
            vt = A.bf16(NT * 256).rearrange("p (t c) -> p t c", t=NT)
            vsrc = v_tok.rearrange("(t p) c -> p t c", p=128)
            for t0_ in range(0, NT, 8):
                t1_ = min(NT, t0_ + 8)
                P.dma("sp", vt[:, t0_:t1_, :], vsrc[:, t0_:t1_, hp * 256:(hp + 1) * 256], reads=["v_tok"], writes=["vt"])
            oacc = A.f32(2 * TT).rearrange("p (h t) -> p h t", h=2)
            P.op("pool", lambda e: e.memset(oacc, 0.0), writes=["oacc%d_%d" % (hh, t) for hh in range(2) for t in range(NT)])
            lrsb = A.f32(TT); Bp = A.f32(TT); Bc = A.f32(TT); qk = A.f32(TT)
            qd = [A.bf16(TT), A.bf16(TT)]; ki = [A.bf16(TT), A.bf16(TT)]
            kiT = [A.bf16(NT * 128).rearrange("p (t c) -> p t c", t=NT) for _ in range(2)]
            upw = A.f32(256); nb = A.f32(1)
            gam = [A.f32(NCH), A.f32(NCH)]
            S = [A.f32(128), A.f32(128)]; Sb = [A.bf16(128), A.bf16(128)]; tmp = [A.f32(128), A.f32(128)]
            sT = [[A.bf16(128), A.bf16(128)], [A.bf16(128), A.bf16(128)]]
            for d in range(2):
                ds_ = str(d)
                sm = sm0 if d == 0 else sm1
                P.dma("sp", lrsb[0:16, :], lrT[d], reads=["zT_lr%d" % d], writes=["lrsb"])
                P.dma("sp", upw[0:16, :], gla_up_w[l, d], writes=["upw"])
                P.op("dve", lambda e, d=d: e.tensor_scalar(out=nb, in0=ppsb[:, PP_UPB + d * 2 + hp:PP_UPB + d * 2 + hp + 1], scalar1=-1.0, scalar2=None, op0=ALU.mult),
                     reads=["ppsb"], writes=["nb"])
                psr = Rot([0, 1])
                for b0 in range(0, TT, 512):
                    n = min(512, TT - b0)
                    pi = psr.next()
                    P.op("pe", lambda e, pi=pi, b0=b0, n=n: e.matmul(PS[pi][:, 0:n], lhsT=upw[0:16, hp * 128:(hp + 1) * 128], rhs=lrsb[0:16, b0:b0 + n], start=True, stop=True),
                         reads=["upw", "lrsb"], writes=["ps%d" % pi])
                    P.op("act", lambda e, pi=pi, b0=b0, n=n: e.activation(out=Bc[:, b0:b0 + n], in_=PS[pi][:, 0:n], func=AF.Exp, scale=-1.0, bias=nb),
                         reads=["nb"], writes=["ps%d" % pi, "Bc"])
                P.op("act", lambda e: e.activation(out=Bp, in_=Bc, func=AF.Ln, bias=1.0, scale=1.0), reads=["Bc"], writes=["Bp"])
                if d == 0:
                    P.op("dve", lambda e, sm=sm: e.tensor_tensor_scan(out=Bc, data0=sm, data1=Bp, initial=0.0, op0=ALU.mult, op1=ALU.add),
                         reads=["Bp", "sm"], writes=["Bc"])
                else:
                    P.op("dve", lambda e, sm=sm: e.tensor_tensor_scan(out=Bc[:, ::-1], data0=sm[:, ::-1], data1=Bp[:, ::-1], initial=0.0, op0=ALU.mult, op1=ALU.add),
                         reads=["Bp", "sm"], writes=["Bc"])
                Bc3 = Bc.rearrange("p (c j) -> p c j", j=64)
                endj = 63 if d == 0 else 0
                P.op("act", lambda e, endj=endj, d=d: e.activation(out=gam[d], in_=Bc3[:, :, endj], func=AF.Exp, scale=-1.0 / 16.0), reads=["Bc"], writes=["gam" + ds_])
                P.dma("sp", qk, qT[hp * 128:(hp + 1) * 128, :], reads=["zT_q"], writes=["qk"])
                P.op("act", lambda e: e.activation(out=Bp, in_=Bc, func=AF.Exp, scale=-1.0 / 16.0), reads=["Bc"], writes=["Bp"])
                P.op("dve", lambda e, d=d: e.scalar_tensor_tensor(out=qd[d], in0=qk, scalar=0.125, in1=Bp, op0=ALU.mult, op1=ALU.mult), reads=["qk", "Bp"], writes=["qd" + ds_])
                P.dma("sp", qk, kT[hp * 128:(hp + 1) * 128, :], reads=["zT_k"], writes=["qk"])
                P.op("act", lambda e: e.activation(out=Bp, in_=Bc, func=AF.Exp, scale=1.0 / 16.0), reads=["Bc"], writes=["Bp"])
                P.op("dve", lambda e, d=d: e.tensor_tensor(out=ki[d], in0=qk, in1=Bp, op=ALU.mult), reads=["qk", "Bp"], writes=["ki" + ds_])
                psr = Rot([0, 1])
                for t in range(NT):
                    pi = psr.next()
                    pst = PS[pi][:].bitcast(BF16)
                    P.op("pe", lambda e, pst=pst, t=t, d=d: e.transpose(pst[:, 0:128], ki[d][:, t * 128:(t + 1) * 128], ident_b[:]), reads=["ki" + ds_, "ident_b"], writes=["ps%d" % pi])
                    P.op("act", lambda e, pst=pst, t=t, d=d: e.activation(out=kiT[d][:, t, :], in_=pst[:, 0:128], func=AF.Identity), reads=[], writes=["ps%d" % pi, "kiT" + ds_])
                P.op("dve", lambda e, d=d: e.memset(S[d], 0.0), writes=["S" + ds_])
                P.op("dve", lambda e, d=d: e.memset(Sb[d], 0.0), writes=["Sb" + ds_])

            def chunk_loop(d):
                ds_ = str(d)
                mask = maskF if d == 0 else maskB
                ctx_t = list(range(LC // 128)); lat_t = list(range(LC // 128, NT))
                order = ctx_t + lat_t if d == 0 else ctx_t[::-1] + lat_t[::-1]
                corder = (0, 1) if d == 0 else (1, 0)
                pS = d * 4 + 0; pO = (d * 4 + 1, d * 4 + 2); pD = d * 4 + 3
                for t in order:
                    for hh in range(2):
                        pr = slice(hh * 64, (hh + 1) * 64)
                        P.op("pe", lambda e, pr=pr, t=t: e.matmul(PS[pS][:, 0:128], lhsT=ki[d][pr, t * 128:(t + 1) * 128], rhs=qd[d][pr, t * 128:(t + 1) * 128], start=True, stop=True),
                             reads=["ki" + ds_, "qd" + ds_], writes=["ps%d" % pS])
                        P.op("dve", lambda e, hh=hh: e.tensor_tensor(out=sT[d][hh], in0=PS[pS][:, 0:128], in1=mask[:], op=ALU.mult),
                             reads=["mask"], writes=["ps%d" % pS, "sT%s_%d" % (ds_, hh)])
                        P.op("pe", lambda e, hh=hh, t=t: e.matmul(PS[pO[hh]][:, 0:128], lhsT=vt[:, t, hh * 128:(hh + 1) * 128], rhs=sT[d][hh], start=True, stop=True),
                             reads=["vt", "sT%s_%d" % (ds_, hh)], writes=["ps%d" % pO[hh]])
                    for ci, cc in enumerate(corder):
                        ch = t * 2 + cc
                        cols = slice(t * 128 + cc * 64, t * 128 + cc * 64 + 64)
                        for hh in range(2):
                            pr = slice(hh * 64, (hh + 1) * 64)
                            P.op("pe", lambda e, hh=hh, pr=pr, cols=cols, cc=cc: e.matmul(PS[pO[hh]][:, cc * 64:(cc + 1) * 64], lhsT=Sb[d][pr, :], rhs=qd[d][pr, cols],
                                                                                      start=False, stop=False, skip_group_check=True),
                                 reads=["Sb" + ds_, "qd" + ds_], writes=["ps%d" % pO[hh]])
                        jr = slice(cc * 64, (cc + 1) * 64)
                        for hh in range(2):
                            pr = slice(hh * 64, (hh + 1) * 64)
                            P.op("pe", lambda e, hh=hh, pr=pr, jr=jr, t=t: e.matmul(PS[pD][pr, 0:128], lhsT=kiT[d][jr, t, hh * 64:(hh + 1) * 64], rhs=vt[jr, t, hh * 128:(hh + 1) * 128],
                                                                                 start=True, stop=True),
                                 reads=["kiT" + ds_, "vt"], writes=["ps%d" % pD])
                        P.op("dve", lambda e: e.tensor_tensor(out=tmp[d], in0=PS[pD][:, 0:128], in1=S[d], op=ALU.add), reads=["S" + ds_], writes=["ps%d" % pD, "tmp" + ds_])
                        P.op("dve", lambda e, ch=ch: e.tensor_scalar(out=S[d], in0=tmp[d], scalar1=gam[d][:, ch:ch + 1], scalar2=None, op0=ALU.mult), reads=["tmp" + ds_, "gam" + ds_], writes=["S" + ds_])
                        P.op("act", lambda e, ch=ch: e.activation(out=Sb[d], in_=tmp[d], func=AF.Identity, scale=gam[d][:, ch:ch + 1]), reads=["tmp" + ds_, "gam" + ds_], writes=["Sb" + ds_])
                    for hh in range(2):
                        ko = "oacc%d_%d" % (hh, t)
                        P.op("dve", lambda e, hh=hh, t=t: e.tensor_tensor(out=oacc[:, hh, t * 128:(t + 1) * 128], in0=PS[pO[hh]][:, 0:128], in1=oacc[:, hh, t * 128:(t + 1) * 128], op=ALU.add),
                             reads=[], writes=["ps%d" % pO[hh], ko])

            P.interleave([P.capture(lambda: chunk_loop(0)), P.capture(lambda: chunk_loop(1))])
            for hh in range(2):
                P.dma("sp", oT[(hp * 2 + hh) * 128:(hp * 2 + hh + 1) * 128, :], oacc[:, hh, :],
                      reads=["oacc%d_%d" % (hh, t) for t in range(NT)], writes=["oT"])
            P.barrier()

    def lru(l):
        A.reset()
        cst = A.f32(4); cst2 = A.f32(4)
        P.op("act", lambda e: e.activation(out=cst, in_=ppsb[:, PP_LAM:PP_LAM + 4], func=AF.Exp, scale=-1.0), reads=["ppsb"], writes=["cst"])
        P.op("act", lambda e: e.activation(out=cst2, in_=cst, func=AF.Ln, bias=1.0, scale=1.0), reads=["cst"], writes=["cst2"])
        P.op("dve", lambda e: e.tensor_scalar(out=cst, in0=cst2, scalar1=-8.0, scalar2=None, op0=ALU.mult), reads=["cst2"], writes=["cst"])
        x = A.f32(TT); hs = A.f32(TT); th = A.f32(TT)
        xc = [A.f32(TT), A.f32(TT)]; r = [A.f32(TT), A.f32(TT)]; ig = [A.f32(TT), A.f32(TT)]; a = [A.f32(TT), A.f32(TT)]
        Wa = [A.f32(128), A.f32(128)]; Wx = [A.f32(128), A.f32(128)]
        segs = [(0, LC), (LC, TT)]
        for ct in range(2):
            P.dma("sp", x, lxT[ct * 128:(ct + 1) * 128, :], reads=["zT_lx"], writes=["x"])

            def body(d):
                ds_ = str(d)
                col = d * 2 + ct
                kxc, kr, kig, ka, kWa, kWx = "xc" + ds_, "r" + ds_, "ig" + ds_, "a" + ds_, "Wa" + ds_, "Wx" + ds_
                P.op("pool", lambda e: e.memset(Wa[d], 0.0), writes=[kWa])
                P.op("pool", lambda e: e.memset(Wx[d], 0.0), writes=[kWx])
                for bb in range(2):
                    P.dma("sp", Wa[d][bb * 64:(bb + 1) * 64, bb * 64:(bb + 1) * 64], lru_wa[l, d, ct * 2 + bb], writes=[kWa], reads=[kWa])
                    P.dma("sp", Wx[d][bb * 64:(bb + 1) * 64, bb * 64:(bb + 1) * 64], lru_wx[l, d, ct * 2 + bb], writes=[kWx], reads=[kWx])
                wcol = lambda k: ppsb[:, PP_CONVW + (d * 4 + k) * 2 + ct:PP_CONVW + (d * 4 + k) * 2 + ct + 1]
                bcol = ppsb[:, PP_CONVB + col:PP_CONVB + col + 1]
                P.op("dve", lambda e: e.tensor_scalar(out=xc[d], in0=x, scalar1=wcol(3), scalar2=bcol, op0=ALU.mult, op1=ALU.add), reads=["x", "ppsb"], writes=[kxc])
                for (s0, s1) in segs:
                    for sh in (1, 2, 3):
                        k = 3 - sh
                        if d == 0:
                            o_ap, i_ap = xc[d][:, s0 + sh:s1], x[:, s0:s1 - sh]
                        else:
                            o_ap, i_ap = xc[d][:, s0:s1 - sh], x[:, s0 + sh:s1]
                        P.op("dve", lambda e, o_ap=o_ap, i_ap=i_ap, k=k: e.scalar_tensor_tensor(out=o_ap, in0=i_ap, scalar=wcol(k), in1=o_ap, op0=ALU.mult, op1=ALU.add),
                             reads=["x", "ppsb", kxc], writes=[kxc])
                psr = Rot([d * 4 + 0, d * 4 + 1, d * 4 + 2, d * 4 + 3])
                for (Wm, dst, bc0, nm, kW) in ((Wa[d], r[d], PP_BA, kr, kWa), (Wx[d], ig[d], PP_BX, kig, kWx)):
                    for b0 in range(0, TT, 512):
                        n = min(512, TT - b0)
                        pi = psr.next()
                        P.op("pe", lambda e, pi=pi, b0=b0, n=n, Wm=Wm: e.matmul(PS[pi][:, 0:n], lhsT=Wm, rhs=xc[d][:, b0:b0 + n], start=True, stop=True),
                             reads=[kW, kxc], writes=["ps%d" % pi])
                        P.op("act", lambda e, pi=pi, b0=b0, n=n, dst=dst, bc0=bc0: e.activation(out=dst[:, b0:b0 + n], in_=PS[pi][:, 0:n], func=AF.Sigmoid,
                                                                                              bias=ppsb[:, bc0 + col:bc0 + col + 1]),
                             reads=["ppsb"], writes=["ps%d" % pi, nm])
                P.op("act", lambda e: e.activation(out=a[d], in_=r[d], func=AF.Exp, scale=cst[:, col:col + 1]), reads=[kr, "cst"], writes=[ka])
                P.op("pool", lambda e: e.tensor_tensor(out=r[d], in0=a[d], in1=a[d], op=ALU.mult), reads=[ka], writes=[kr])
                P.op("act", lambda e: e.activation(out=r[d], in_=r[d], func=AF.Sqrt, scale=-1.0, bias=1.0), reads=[kr], writes=[kr])
                P.op("dve", lambda e: e.tensor_tensor(out=ig[d], in0=ig[d], in1=r[d], op=ALU.mult), reads=[kig, kr], writes=[kig])
                P.op("dve", lambda e: e.tensor_tensor(out=xc[d], in0=xc[d], in1=ig[d], op=ALU.mult), reads=[kig, kxc], writes=[kxc])
                if d == 0:
                    P.op("dve", lambda e: e.tensor_tensor_scan(out=hs, data0=a[d], data1=xc[d], initial=0.0, op0=ALU.mult, op1=ALU.add), reads=[ka, kxc], writes=["hs"])
                else:
                    P.op("dve", lambda e: e.tensor_tensor_scan(out=th[:, 0:LC][:, ::-1], data0=a[d][:, 0:LC][:, ::-1], data1=xc[d][:, 0:LC][:, ::-1], initial=0.0,
                                                               op0=ALU.mult, op1=ALU.add), reads=[ka, kxc], writes=["th"])
                    P.op("dve", lambda e: e.tensor_tensor_scan(out=th[:, LC:TT][:, ::-1], data0=a[d][:, LC:TT][:, ::-1], data1=xc[d][:, LC:TT][:, ::-1], initial=th[:, 0:1],
                                                               op0=ALU.mult, op1=ALU.add), reads=[ka, kxc, "th"], writes=["th"])

            P.interleave([P.capture(lambda: body(0)), P.capture(lambda: body(1))])
            P.op("dve", lambda e: e.tensor_tensor(out=hs, in0=hs, in1=th, op=ALU.add), reads=["hs", "th"], writes=["hs"])
            P.dma("sp", lruT[ct * 128:(ct + 1) * 128, :], hs, reads=["hs"], writes=["lruT"])

    def s5(l):
        A.reset()
        NG = 16
        prm = A.f32(3 * NG).rearrange("p (a g) -> p a g", a=3)
        Bsb = A.f32(2 * NG * 16).rearrange("p (a g h) -> p a g h", a=2, g=NG)
        Csb = A.f32(2 * NG * 16).rearrange("p (a g h) -> p a g h", a=2, g=NG)
        P.dma("sp", prm, s5p[l], writes=["prm"])
        P.dma("sp", Bsb, s5b[l], writes=["Bsb"])
        P.dma("sp", Csb, s5c[l], writes=["Csb"])
        tauf = A.f32(2 * NC8)
        tau = tauf.rearrange("p (d c) -> p d c", d=2)
        P.dma("sp", tauf, tau_in.partition_broadcast(128), writes=["tau"])
        dt = A.f32(NG); lrdt = A.f32(NG); th = A.f32(NG); u8 = A.f32(NG); rho8 = A.f32(NG)
        t1 = A.f32(NG); t2 = A.f32(NG); t3 = A.f32(NG); den = A.f32(NG); cr = A.f32(NG); ci = A.f32(NG)
        J = jidx[:].rearrange("p d g j -> p (d g) j")
        mg = A.f32(NG * 9).rearrange("p (g j) -> p g j", j=9)
        xa = A.f32(NG * 9).rearrange("p (g j) -> p g j", j=9)
        xr = A.f32(NG * 9).rearrange("p (g j) -> p g j", j=9)
        sn = A.f32(NG * 9).rearrange("p (g j) -> p g j", j=9)
        cs_ = A.f32(NG * 9).rearrange("p (g j) -> p g j", j=9)
        ar = A.f32(NG * 9).rearrange("p (g j) -> p g j", j=9)
        ai = A.f32(NG * 9).rearrange("p (g j) -> p g j", j=9)
        mr = A.f32(NG * 8).rearrange("p (g j) -> p g j", j=8)
        mi = A.f32(NG * 8).rearrange("p (g j) -> p g j", j=8)
        br_ = A.f32(NG * 8).rearrange("p (g j) -> p g j", j=8)
        bi_ = A.f32(NG * 8).rearrange("p (g j) -> p g j", j=8)
        w1 = A.f32(NG * 9).rearrange("p (g j) -> p g j", j=9)
        K = ["prm", "s5t"]

        def op(eng, fn):
            P.op(eng, fn, reads=K, writes=["s5t"])

        def bg(v, n):
            return v.unsqueeze(2).to_broadcast([128, NG, n])

        op("act", lambda e: e.activation(out=dt, in_=prm[:, 2, :], func=AF.Exp))
        op("dve", lambda e: e.tensor_tensor(out=lrdt, in0=prm[:, 0, :], in1=dt, op=ALU.mult))
        op("dve", lambda e: e.tensor_tensor(out=th, in0=prm[:, 1, :], in1=dt, op=ALU.mult))
        op("dve", lambda e: e.tensor_scalar(out=th, in0=th, scalar1=1.0 / TWO_PI, scalar2=None, op0=ALU.mult))
        op("dve", lambda e: e.tensor_tensor(out=mg, in0=J, in1=bg(lrdt, 9), op=ALU.mult))
        op("act", lambda e: e.activation(out=mg, in_=mg, func=AF.Exp))
        op("dve", lambda e: e.tensor_tensor(out=xa, in0=J, in1=bg(th, 9), op=ALU.mult))
        op("dve", lambda e: e.tensor_scalar(out=xr, in0=xa, scalar1=MAGIC, scalar2=-MAGIC, op0=ALU.add, op1=ALU.add))
        op("dve", lambda e: e.tensor_tensor(out=w1, in0=xa, in1=xr, op=ALU.subtract))
        op("act", lambda e: e.activation(out=sn, in_=w1, func=AF.Sin, scale=TWO_PI))
        op("dve", lambda e: e.tensor_scalar(out=xa, in0=xa, scalar1=0.25, scalar2=None, op0=ALU.add))
        op("dve", lambda e: e.tensor_scalar(out=xr, in0=xa, scalar1=MAGIC, scalar2=-MAGIC, op0=ALU.add, op1=ALU.add))
        op("dve", lambda e: e.tensor_tensor(out=w1, in0=xa, in1=xr, op=ALU.subtract))
        op("act", lambda e: e.activation(out=cs_, in_=w1, func=AF.Sin, scale=TWO_PI))
        op("dve", lambda e: e.tensor_tensor(out=ar, in0=mg, in1=cs_, op=ALU.mult))
        op("dve", lambda e: e.tensor_tensor(out=ai, in0=mg, in1=sn, op=ALU.mult))
        op("dve", lambda e: e.tensor_tensor(out=w1, in0=mg, in1=mg, op=ALU.mult))
        op("dve", lambda e: e.reciprocal(out=w1, in_=w1))
        op("dve", lambda e: e.tensor_tensor(out=mr, in0=ar[:, :, 0:8], in1=w1[:, :, 0:8], op=ALU.mult))
        op("dve", lambda e: e.scalar_tensor_tensor(out=mi, in0=ai[:, :, 0:8], scalar=-1.0, in1=w1[:, :, 0:8], op0=ALU.mult, op1=ALU.mult))
        a1r = A.f32(NG); a1i = A.f32(NG)
        op("dve", lambda e: e.tensor_copy(out=a1r[:, 0:8], in_=ar[:, 0:8, 1]))
        op("dve", lambda e: e.tensor_copy(out=a1r[:, 8:16], in_=ar[:, 8:16, 6]))
        op("dve", lambda e: e.tensor_copy(out=a1i[:, 0:8], in_=ai[:, 0:8, 1]))
        op("dve", lambda e: e.tensor_copy(out=a1i[:, 8:16], in_=ai[:, 8:16, 6]))
        lr_ = prm[:, 0, :]; li_ = prm[:, 1, :]
        op("dve", lambda e: e.tensor_tensor(out=den, in0=lr_, in1=lr_, op=ALU.mult))
        op("dve", lambda e: e.tensor_tensor(out=t1, in0=li_, in1=li_, op=ALU.mult))
        op("dve", lambda e: e.tensor_tensor(out=den, in0=den, in1=t1, op=ALU.add))
        op("dve", lambda e: e.reciprocal(out=den, in_=den))
        op("dve", lambda e: e.tensor_scalar(out=t1, in0=a1r, scalar1=-1.0, scalar2=None, op0=ALU.add))
        op("dve", lambda e: e.tensor_tensor(out=t2, in0=t1, in1=lr_, op=ALU.mult))
        op("dve", lambda e: e.tensor_tensor(out=t3, in0=a1i, in1=li_, op=ALU.mult))
        op("dve", lambda e: e.tensor_tensor(out=t2, in0=t2, in1=t3, op=ALU.add))
        op("dve", lambda e: e.tensor_tensor(out=cr, in0=t2, in1=den, op=ALU.mult))
        op("dve", lambda e: e.tensor_tensor(out=t2, in0=a1i, in1=lr_, op=ALU.mult))
        op("dve", lambda e: e.tensor_tensor(out=t3, in0=t1, in1=li_, op=ALU.mult))
        op("dve", lambda e: e.tensor_tensor(out=t2, in0=t2, in1=t3, op=ALU.subtract))
        op("dve", lambda e: e.tensor_tensor(out=ci, in0=t2, in1=den, op=ALU.mult))
        w8a = A.f32(NG * 8).rearrange("p (g j) -> p g j", j=8)
        op("dve", lambda e: e.tensor_tensor(out=br_, in0=mr, in1=bg(cr, 8), op=ALU.mult))
        op("dve", lambda e: e.tensor_tensor(out=w8a, in0=mi, in1=bg(ci, 8), op=ALU.mult))
        op("dve", lambda e: e.tensor_tensor(out=br_, in0=br_, in1=w8a, op=ALU.subtract))
        op("dve", lambda e: e.tensor_tensor(out=bi_, in0=mr, in1=bg(ci, 8), op=ALU.mult))
        op("dve", lambda e: e.tensor_tensor(out=w8a, in0=mi, in1=bg(cr, 8), op=ALU.mult))
        op("dve", lambda e: e.tensor_tensor(out=bi_, in0=bi_, in1=w8a, op=ALU.add))
        op("dve", lambda e: e.tensor_copy(out=rho8, in_=mg[:, :, 8]))
        op("dve", lambda e: e.tensor_scalar(out=u8, in0=th, scalar1=8.0, scalar2=None, op0=ALU.mult))
        op("dve", lambda e: e.tensor_scalar(out=t1, in0=u8, scalar1=MAGIC, scalar2=-MAGIC, op0=ALU.add, op1=ALU.add))
        op("dve", lambda e: e.tensor_tensor(out=u8, in0=u8, in1=t1, op=ALU.subtract))
        SZ = NG * 8 * 16
        Btr = A.f32(SZ).rearrange("p (g j h) -> p g j h", g=NG, j=8)
        Bti = A.f32(SZ).rearrange("p (g j h) -> p g j h", g=NG, j=8)
        Ctr = A.f32(SZ).rearrange("p (g j h) -> p g j h", g=NG, j=8)
        Cti = A.f32(SZ).rearrange("p (g j h) -> p g j h", g=NG, j=8)
        regB = A.f32(4 * 2048)
        wk = regB[:, 0:2048].rearrange("p (g j h) -> p g j h", g=NG, j=8)

        def bj(v):
            return v.unsqueeze(3).to_broadcast([128, NG, 8, 16])

        def bh(v):
            return v.unsqueeze(2).to_broadcast([128, NG, 8, 16])

        Br, Bi = Bsb[:, 0], Bsb[:, 1]
        Cr, Ci = Csb[:, 0], Csb[:, 1]
        KB = ["s5t", "Bsb", "Csb", "s5m"]

        def opb(fn):
            P.op("dve", fn, reads=KB, writes=["s5m"])

        opb(lambda e: e.tensor_tensor(out=Btr, in0=bj(br_), in1=bh(Br), op=ALU.mult))
        opb(lambda e: e.tensor_tensor(out=wk, in0=bj(bi_), in1=bh(Bi), op=ALU.mult))
        opb(lambda e: e.tensor_tensor(out=Btr, in0=Btr, in1=wk, op=ALU.subtract))
        opb(lambda e: e.tensor_tensor(out=Bti, in0=bj(br_), in1=bh(Bi), op=ALU.mult))
        opb(lambda e: e.tensor_tensor(out=wk, in0=bj(bi_), in1=bh(Br), op=ALU.mult))
        opb(lambda e: e.tensor_tensor(out=Bti, in0=Bti, in1=wk, op=ALU.add))
        opb(lambda e: e.tensor_tensor(out=Ctr, in0=bj(ar[:, :, 0:8]), in1=bh(Cr), op=ALU.mult))
        opb(lambda e: e.tensor_tensor(out=wk, in0=bj(ai[:, :, 0:8]), in1=bh(Ci), op=ALU.mult))
        opb(lambda e: e.tensor_tensor(out=Ctr, in0=Ctr, in1=wk, op=ALU.subtract))
        opb(lambda e: e.tensor_tensor(out=Cti, in0=bj(ai[:, :, 0:8]), in1=bh(Cr), op=ALU.mult))
        opb(lambda e: e.tensor_tensor(out=wk, in0=bj(ar[:, :, 0:8]), in1=bh(Ci), op=ALU.mult))
        opb(lambda e: e.tensor_tensor(out=Cti, in0=Cti, in1=wk, op=ALU.add))
        opb(lambda e: e.tensor_scalar(out=Cti, in0=Cti, scalar1=-1.0, scalar2=None, op0=ALU.mult))
        NCT = (NC8 + 127) // 128
        U8 = A.f32(16 * NC8).rearrange("p (g c) -> p g c", g=16)
        cst_ = [regB[:, 2048:4096], regB[:, 4096:6144]]
        Ug = regB[:, 6144:8192]
        Yst = cst_

        def chunk_tiles():
            tiles = []
            c = 0
            while c < NC8:
                n = min(128, NC8 - c)
                tiles.append((c, n))
                c += n
            return tiles

        def chunk_dram(base, c0, n):
            pieces = []
            c = c0
            while c < c0 + n:
                if c < LC8:
                    m = min(c0 + n, LC8) - c
                    ap = base[c * 8:(c + m) * 8, :].rearrange("(c i) h -> c i h", i=8)
                    pieces.append((c - c0, m, ap))
                    c += m
                else:
                    cl = c - LC8
                    col, rb = cl // RB, cl % RB
                    m = min(RB - rb, c0 + n - c)
                    lat = base[LC:TT, :].rearrange("(rb i w) h -> w rb i h", i=8, w=64)
                    ap = lat[col, rb:rb + m, :, :]
                    pieces.append((c - c0, m, ap))
                    c += m
            return pieces

        psr = Rot([0, 1, 2, 3])
        for ti, (c0, n) in enumerate(chunk_tiles()):
            cs = cst_[ti % 2]
            cs3 = cs.rearrange("p (i h) -> p i h", i=8)
            for (p0, m, ap) in chunk_dram(su_tok, c0, n):
                P.dma("sp", cs3[p0:p0 + m, :, :], ap, reads=["su_tok"], writes=["cst%d" % (ti % 2)])
            P.op("dve", lambda e, n=n, cs=cs: e.tensor_copy(out=Ug[0:n].rearrange("p (g i h) -> p g i h", g=16, i=8),
                                                          in_=cs[0:n].rearrange("p (i g h) -> p g i h", i=8, g=16)),
                 reads=["cst%d" % (ti % 2)], writes=["Ug"])
            for g in range(16):
                pi = psr.next()
                P.op("pe", lambda e, pi=pi, g=g, n=n: e.transpose(PS[pi][:, 0:n], Ug[0:n, g * 128:(g + 1) * 128], ident_f[0:n, 0:n]),
                     reads=["Ug", "ident_f"], writes=["ps%d" % pi])
                P.op("act", lambda e, pi=pi, g=g, n=n, c0=c0: e.activation(out=U8[:, g, c0:c0 + n], in_=PS[pi][:, 0:n], func=AF.Identity),
                     reads=[], writes=["ps%d" % pi, "U8"])
        P.barrier()
        halves = [(0, min(512, NC8))] + ([(512, NC8)] if NC8 > 512 else [])

        def carve(base):
            o = [0]

            def take(n):
                ap = base[:, o[0]:o[0] + n]
                o[0] += n
                return ap
            d_ = {}
            d_["BtT"] = take(256).rearrange("p (a s) -> p a s", a=2)
            d_["M8"] = take(256).rearrange("p (g c) -> p g c", g=2)
            for nm in ("Zr", "Zi", "Wr", "Wi", "Or", "Oi", "Xr", "Xi", "tA", "tB", "Cn", "Sn"):
                d_[nm] = take(NC8)
            return d_

        SETW = 512 + 12 * NC8
        sets = [carve(A.f32(SETW)), carve(regB)]
        Yall = A.f32(16 * NC8).rearrange("p (g c) -> p g c", g=16)
        psr = Rot([0, 1, 2, 3, 4, 5, 6, 7])

        def front(it):
            d, gp = divmod(it, 8)
            dg = it
            par = str(it % 2)
            S_ = sets[it % 2]
            BtT, M8, Zr, Zi, Cn, Sn = S_["BtT"], S_["M8"], S_["Zr"], S_["Zi"], S_["Cn"], S_["Sn"]
            xx, rr = S_["Wr"], S_["Wi"]
            m8 = m8F if d == 0 else m8B
            for a_, Bt in enumerate((Btr, Bti)):
                pi = psr.next()
                P.op("pe", lambda e, pi=pi, Bt=Bt: e.transpose(PS[pi][:, 0:128], Bt[:, dg].rearrange("p j h -> p (j h)"), ident_f[:]),
                     reads=["s5m", "ident_f"], writes=["ps%d" % pi])
                P.op("act", lambda e, pi=pi, a_=a_: e.activation(out=BtT[:, a_, :], in_=PS[pi][:, 0:128], func=AF.Identity), reads=[], writes=["ps%d" % pi, "BtT" + par])
            for gm in range(2):
                pi = psr.next()
                pr = slice(gm * 64, (gm + 1) * 64)
                for a_, (Bt, Ct) in enumerate(((Btr, Ctr), (Bti, Cti))):
                    P.op("pe", lambda e, pi=pi, gm=gm, pr=pr, Bt=Bt, Ct=Ct, a_=a_: e.matmul(PS[pi][:, 0:128], lhsT=Bt[pr, dg].rearrange("p j h -> p (j h)"),
                                                                                         rhs=Ct[pr, dg].rearrange("p j h -> p (j h)"), start=(a_ == 0), stop=(a_ == 1)),
                         reads=["s5m"], writes=["ps%d" % pi])
                P.op("dve", lambda e, pi=pi, m8=m8, gm=gm: e.tensor_tensor(out=M8[:, gm, :], in0=PS[pi][:, 0:128], in1=m8[:, 0:128], op=ALU.mult),
                     reads=["m8"], writes=["ps%d" % pi, "M8" + par])
            for (Zt, a_) in ((Zr, 0), (Zi, 1)):
                for (h0, h1) in halves:
                    pi = psr.next()
                    for gm in range(2):
                        g = gp * 2 + gm
                        P.op("pe", lambda e, pi=pi, gm=gm, g=g, a_=a_, h0=h0, h1=h1: e.matmul(PS[pi][gm * 64:(gm + 1) * 64, 0:h1 - h0], lhsT=BtT[:, a_, gm * 64:(gm + 1) * 64],
                                                                                           rhs=U8[:, g, h0:h1], start=True, stop=True),
                             reads=["BtT" + par, "U8"], writes=["ps%d" % pi])
                    P.op("act", lambda e, pi=pi, Zt=Zt, h0=h0, h1=h1: e.activation(out=Zt[:, h0:h1], in_=PS[pi][:, 0:h1 - h0], func=AF.Identity),
                         reads=[], writes=["ps%d" % pi, "Z" + par])
            ucol = u8[:, dg:dg + 1]
            kx, kr = "Wr" + par, "Wi" + par
            P.op("dve", lambda e, ucol=ucol: e.tensor_scalar(out=xx, in0=tau[:, d, :], scalar1=ucol, scalar2=None, op0=ALU.mult), reads=["tau", "s5t"], writes=[kx])
            P.op("dve", lambda e: e.tensor_scalar(out=rr, in0=xx, scalar1=MAGIC, scalar2=-MAGIC, op0=ALU.add, op1=ALU.add), reads=[kx], writes=[kr])
            P.op("dve", lambda e: e.tensor_tensor(out=rr, in0=xx, in1=rr, op=ALU.subtract), reads=[kx, kr], writes=[kr])
            P.op("act", lambda e: e.activation(out=Sn, in_=rr, func=AF.Sin, scale=TWO_PI), reads=[kr], writes=["Sn" + par])
            P.op("dve", lambda e: e.tensor_scalar(out=xx, in0=xx, scalar1=0.25, scalar2=None, op0=ALU.add), reads=[kx], writes=[kx])
            P.op("dve", lambda e: e.tensor_scalar(out=rr, in0=xx, scalar1=MAGIC, scalar2=-MAGIC, op0=ALU.add, op1=ALU.add), reads=[kx, "Sn" + par], writes=[kr])
            P.op("dve", lambda e: e.tensor_tensor(out=rr, in0=xx, in1=rr, op=ALU.subtract), reads=[kx, kr], writes=[kr])
            P.op("act", lambda e: e.activation(out=Cn, in_=rr, func=AF.Sin, scale=TWO_PI), reads=[kr], writes=["Cn" + par])

        def back(it):
            d, gp = divmod(it, 8)
            dg = it
            par = str(it % 2)
            S_ = sets[it % 2]
            M8, Zr, Zi, Wr, Wi, Or, Oi = S_["M8"], S_["Zr"], S_["Zi"], S_["Wr"], S_["Wi"], S_["Or"], S_["Oi"]
            Xr, Xi, tA, tB, Cn, Sn = S_["Xr"], S_["Xi"], S_["tA"], S_["tB"], S_["Cn"], S_["Sn"]
            kC, kS, kZ, kWr, kWi, kA, kB, kX = "Cn" + par, "Sn" + par, "Z" + par, "Wr" + par, "Wi" + par, "tA" + par, "tB" + par, "X" + par
            P.op("dve", lambda e: e.tensor_tensor(out=Wr, in0=Cn, in1=Zr, op=ALU.mult), reads=[kC, kZ], writes=[kWr])
            P.op("dve", lambda e: e.tensor_tensor(out=tA, in0=Sn, in1=Zi, op=ALU.mult), reads=[kS, kZ], writes=[kA])
            P.op("dve", lambda e: e.tensor_tensor(out=Wr, in0=Wr, in1=tA, op=ALU.add), reads=[kWr, kA], writes=[kWr])
            P.op("dve", lambda e: e.tensor_tensor(out=Wi, in0=Cn, in1=Zi, op=ALU.mult), reads=[kC, kZ], writes=[kWi])
            P.op("dve", lambda e: e.tensor_tensor(out=tB, in0=Sn, in1=Zr, op=ALU.mult), reads=[kS, kZ], writes=[kB])
            P.op("dve", lambda e: e.tensor_tensor(out=Wi, in0=Wi, in1=tB, op=ALU.subtract), reads=[kWi, kB], writes=[kWi])
            rcol = rho8[:, dg:dg + 1]
            for (Wt, Ot, nm) in ((Wr, Or, "Or" + par), (Wi, Oi, "Oi" + par)):
                if d == 0:
                    P.op("dve", lambda e, Wt=Wt, Ot=Ot, rcol=rcol: e.tensor_tensor_scan(out=Ot, data0=Wt, data1=rcol.to_broadcast([128, NC8]), initial=0.0, op0=ALU.add, op1=ALU.mult),
                         reads=[kWr, kWi, "s5t"], writes=[nm])
                else:
                    P.op("dve", lambda e, Wt=Wt, Ot=Ot, rcol=rcol: e.tensor_tensor_scan(out=Ot[:, 0:LC8][:, ::-1], data0=Wt[:, 0:LC8][:, ::-1], data1=rcol.to_broadcast([128, LC8]),
                                                                                     initial=0.0, op0=ALU.add, op1=ALU.mult),
                         reads=[kWr, kWi, "s5t"], writes=[nm])
                    P.op("dve", lambda e, Wt=Wt, Ot=Ot, rcol=rcol: e.tensor_tensor_scan(out=Ot[:, LC8:NC8][:, ::-1], data0=Wt[:, LC8:NC8][:, ::-1], data1=rcol.to_broadcast([128, NC8 - LC8]),
                                                                                     initial=Ot[:, 0:1], op0=ALU.add, op1=ALU.mult),
                         reads=[kWr, kWi, "s5t", nm], writes=[nm])
            if d == 0:
                sh = [(slice(1, NC8), slice(0, NC8 - 1))]
                zero_cols = [0]
                carry = None
            else:
                sh = [(slice(0, LC8 - 1), slice(1, LC8)), (slice(LC8, NC8 - 1), slice(LC8 + 1, NC8))]
                zero_cols = [LC8 - 1]
                carry = (NC8 - 1, 0)
            RK = ["Or" + par, "Oi" + par, kC, kS, kX, kA, kB]
            pairs = list(sh)
            if carry is not None:
                dc, sc = carry
                pairs.append((slice(dc, dc + 1), slice(sc, sc + 1)))
            for (do, so) in pairs:
                P.op("dve", lambda e, do=do, so=so: e.tensor_tensor(out=Xr[:, do], in0=Cn[:, do], in1=Or[:, so], op=ALU.mult), reads=RK, writes=[kX])
                P.op("dve", lambda e, do=do, so=so: e.tensor_tensor(out=tA[:, do], in0=Sn[:, do], in1=Oi[:, so], op=ALU.mult), reads=RK, writes=[kA])
                P.op("dve", lambda e, do=do, so=so: e.tensor_tensor(out=Xr[:, do], in0=Xr[:, do], in1=tA[:, do], op=ALU.subtract), reads=RK, writes=[kX])
                P.op("dve", lambda e, do=do, so=so: e.tensor_tensor(out=Xi[:, do], in0=Cn[:, do], in1=Oi[:, so], op=ALU.mult), reads=RK, writes=[kX])
                P.op("dve", lambda e, do=do, so=so: e.tensor_tensor(out=tB[:, do], in0=Sn[:, do], in1=Or[:, so], op=ALU.mult), reads=RK, writes=[kB])
                P.op("dve", lambda e, do=do, so=so: e.tensor_tensor(out=Xi[:, do], in0=Xi[:, do], in1=tB[:, do], op=ALU.add), reads=RK, writes=[kX])
            for zc in zero_cols:
                P.op("dve", lambda e, zc=zc: e.memset(Xr[:, zc:zc + 1], 0.0), reads=RK, writes=[kX])
                P.op("dve", lambda e, zc=zc: e.memset(Xi[:, zc:zc + 1], 0.0), reads=RK, writes=[kX])
            for gm in range(2):
                g = gp * 2 + gm
                pr = slice(gm * 64, (gm + 1) * 64)
                for (h0, h1) in halves:
                    pi = psr.next()
                    P.op("pe", lambda e, pi=pi, gm=gm, g=g, h0=h0, h1=h1: e.matmul(PS[pi][:, 0:h1 - h0], lhsT=M8[:, gm, :], rhs=U8[:, g, h0:h1], start=True, stop=False),
                         reads=["M8" + par, "U8"], writes=["ps%d" % pi])
                    P.op("pe", lambda e, pi=pi, pr=pr, h0=h0, h1=h1: e.matmul(PS[pi][:, 0:h1 - h0], lhsT=Ctr[pr, dg].rearrange("p j h -> p (j h)"), rhs=Xr[pr, h0:h1], start=False, stop=False),
                         reads=["s5m", kX], writes=["ps%d" % pi])
                    P.op("pe", lambda e, pi=pi, pr=pr, h0=h0, h1=h1: e.matmul(PS[pi][:, 0:h1 - h0], lhsT=Cti[pr, dg].rearrange("p j h -> p (j h)"), rhs=Xi[pr, h0:h1], start=False, stop=True),
                         reads=["s5m", kX], writes=["ps%d" % pi])
                    if d == 0:
                        P.op("act", lambda e, pi=pi, g=g, h0=h0, h1=h1: e.activation(out=Yall[:, g, h0:h1], in_=PS[pi][:, 0:h1 - h0], func=AF.Identity),
                             reads=[], writes=["ps%d" % pi, "Yall"])
                    else:
                        P.op("dve", lambda e, pi=pi, g=g, h0=h0, h1=h1: e.tensor_tensor(out=Yall[:, g, h0:h1], in0=PS[pi][:, 0:h1 - h0], in1=Yall[:, g, h0:h1], op=ALU.add),
                             reads=[], writes=["ps%d" % pi, "Yall"])

        front(0)
        for it in range(16):
            if it + 1 < 16:
                P.interleave([P.capture(lambda: back(it)), P.capture(lambda: front(it + 1))])
            else:
                back(it)
        P.barrier()
        psr = Rot([0, 1, 2, 3])
        for ti, (c0, n) in enumerate(chunk_tiles()):
            ys = Yst[ti % 2]
            ys3 = ys.rearrange("p (i h) -> p i h", i=8)
            for g in range(16):
                pi = psr.next()
                P.op("pe", lambda e, pi=pi, g=g, n=n, c0=c0: e.transpose(PS[pi][0:n, 0:128], Yall[:, g, c0:c0 + n], ident_f[:]),
                     reads=["Yall", "ident_f"], writes=["ps%d" % pi])
                P.op("act", lambda e, pi=pi, g=g, n=n, ys3=ys3: e.activation(out=ys3[0:n, :, g * 16:(g + 1) * 16], in_=PS[pi][0:n, 0:128].rearrange("p (j h) -> p j h", j=8), func=AF.Identity),
                     reads=[], writes=["ps%d" % pi, "yst%d" % (ti % 2)])
            for (p0, m, ap) in chunk_dram(s5y, c0, n):
                P.dma("sp", ap, ys3[p0:p0 + m, :, :], reads=["yst%d" % (ti % 2)], writes=["s5y"])

    def phaseC1(l, last):
        A.reset()
        xsrc = xin if l == 0 else xres
        w1p = A.bf16(KT * 4 * D).rearrange("p (k n) -> p k n", k=KT)
        w2p = A.bf16(32 * D).rearrange("p (k n) -> p k n", k=32)
        w1v = w_ff1[l].rearrange("(kt p) n -> p kt n", p=128)
        w2v = w_ff2[l].rearrange("(kt p) n -> p kt n", p=128)
        pre = []
        for c0 in range(0, 4 * D, 512):
            pre.append((w1p[:, :, c0:c0 + 512], w1v[:, :, c0:c0 + 512], "w1"))
        for k0 in range(0, 32, 4):
            for c0 in range(0, D, 512):
                pre.append((w2p[:, k0:k0 + 4, c0:c0 + 512], w2v[:, k0:k0 + 4, c0:c0 + 512], "w2"))
        wo = A.bf16(KT * D).rearrange("p (k n) -> p k n", k=KT)
        wv = w_out[l].rearrange("(kt p) n -> p kt n", p=128)
        for c0 in range(0, D, 512):
            P.dma("poolq", wo[:, :, c0:c0 + 512], wv[:, :, c0:c0 + 512], writes=["wo"])
        glu = A.bf16(2 * 256).rearrange("p (k n) -> p k n", k=2)
        P.dma("poolq", glu, s5_glu_w[l].rearrange("(kt p) n -> p kt n", p=128), writes=["glu"])
        g1t = A.f32(D)
        g1bc = {0: g1t, 1: g1t}
        cur_w = [None]

        def load_g1(w):
            if cur_w[0] == w:
                return
            cur_w[0] = w
            P.dma("sp", g1t, mraw[l, w:w + 1, 2 * D:3 * D].partition_broadcast(128), reads=["mraw"], writes=["bc"])

        dbc = A.f32(256)
        P.dma("sp", dbc, s5_d[l:l + 1, :].partition_broadcast(128), writes=["bcd"])
        NB = 256
        o4 = A.f32(4 * NB).rearrange("p (h t) -> p h t", h=4)
        g4 = A.f32(4 * NB).rearrange("p (h t) -> p h t", h=4)
        sq = A.bf16(4 * NB).rearrange("p (h t) -> p h t", h=4)
        rn = A.f32(4 * NB).rearrange("p (h t) -> p h t", h=4)
        lh = A.f32(2 * NB).rearrange("p (h t) -> p h t", h=2)
        lg = A.f32(2 * NB).rearrange("p (h t) -> p h t", h=2)
        cat = A.bf16(KT * NB).rearrange("p (k t) -> p k t", k=KT)
        ysb = A.f32(2 * 256).rearrange("p (j c) -> p j c", j=2)
        usb = A.f32(2 * 256).rearrange("p (j c) -> p j c", j=2)
        sb16 = A.bf16(2 * 256).rearrange("p (j c) -> p j c", j=2)
        sTt = A.bf16(2 * NB).rearrange("p (k t) -> p k t", k=2)
        gsig = A.f32(2 * NB).rearrange("p (k t) -> p k t", k=2)
        xt = [A.f32(D), A.f32(D)]
        xo = [A.f32(D), A.f32(D)]
        t0 = 0 if not last else LC
        psr = Rot([0, 1, 2, 3, 4, 5, 6, 7])
        while t0 < TT:
            w = 1 if t0 < LC else 0
            n = min(NB, (LC if w == 1 else TT) - t0)
            tk = slice(t0, t0 + n)
            load_g1(w)
            for _ in range(3):
                if pre:
                    o_, i_, k_ = pre.pop(0)
                    P.dma("poolq", o_, i_, writes=[k_])
            P.dma("sp", o4[:, :, 0:n], oT.rearrange("(h p) t -> p h t", p=128)[:, :, tk], reads=["oT"], writes=["o4"])
            P.dma("sp", g4[:, :, 0:n], ggT.rearrange("(h p) t -> p h t", p=128)[:, :, tk], reads=["zT_gg"], writes=["g4"])
            P.op("pool", lambda e, n=n: e.tensor_tensor(out=sq[:, :, 0:n], in0=o4[:, :, 0:n], in1=o4[:, :, 0:n], op=ALU.mult), reads=["o4"], writes=["sq"])
            P.op("act", lambda e, n=n: e.activation(out=g4[:, :, 0:n], in_=g4[:, :, 0:n], func=AF.Silu), reads=["g4"], writes=["g4"])
            for h in range(4):
                pi = psr.next()
                P.op("pe", lambda e, pi=pi, h=h, n=n: e.matmul(PS[pi][:, 0:n], lhsT=ones_b[:], rhs=sq[:, h, 0:n], start=True, stop=True), reads=["sq", "ones_b"], writes=["ps%d" % pi])
                P.op("act", lambda e, pi=pi, h=h, n=n: e.activation(out=rn[:, h, 0:n], in_=PS[pi][:, 0:n], func=AF.Sqrt, scale=1.0 / 128.0, bias=EPS), reads=[], writes=["ps%d" % pi, "rn"])
            P.op("dve", lambda e, n=n: e.reciprocal(out=rn[:, :, 0:n], in_=rn[:, :, 0:n]), reads=["rn"], writes=["rn"])
            P.op("dve", lambda e, n=n: e.tensor_tensor(out=o4[:, :, 0:n], in0=o4[:, :, 0:n], in1=rn[:, :, 0:n], op=ALU.mult), reads=["o4", "rn"], writes=["o4"])
            for h in range(4):
                P.op("dve", lambda e, h=h, n=n: e.scalar_tensor_tensor(out=cat[:, h, 0:n], in0=o4[:, h, 0:n], scalar=ppsb[:, PP_GNORM + h:PP_GNORM + h + 1], in1=g4[:, h, 0:n],
                                                                       op0=ALU.mult, op1=ALU.mult), reads=["o4", "g4", "ppsb"], writes=["cat"])
            P.dma("sp", lh[:, :, 0:n], lruT.rearrange("(h p) t -> p h t", p=128)[:, :, tk], reads=["lruT"], writes=["lh"])
            P.dma("sp", lg[:, :, 0:n], lgT.rearrange("(h p) t -> p h t", p=128)[:, :, tk], reads=["zT_lg"], writes=["lg"])
            P.op("act", lambda e, n=n: e.activation(out=lg[:, :, 0:n], in_=lg[:, :, 0:n], func=AF.Gelu), reads=["lg"], writes=["lg"])
            P.op("dve", lambda e, n=n: e.tensor_tensor(out=cat[:, 4:6, 0:n], in0=lh[:, :, 0:n], in1=lg[:, :, 0:n], op=ALU.mult), reads=["lh", "lg"], writes=["cat"])
            nj = n // 128
            P.dma("sp", ysb[:, 0:nj, :], s5y[tk, :].rearrange("(j p) c -> p j c", p=128), reads=["s5y"], writes=["ysb"])
            P.dma("sp", usb[:, 0:nj, :], su_tok[tk, :].rearrange("(j p) c -> p j c", p=128), reads=["su_tok"], writes=["usb"])
            P.op("dve", lambda e, nj=nj: e.tensor_tensor(out=usb[:, 0:nj, :], in0=usb[:, 0:nj, :], in1=dbc.unsqueeze(1).to_broadcast([128, nj, 256]), op=ALU.mult), reads=["usb", "bcd"], writes=["usb"])
            P.op("dve", lambda e, nj=nj: e.tensor_tensor(out=ysb[:, 0:nj, :], in0=ysb[:, 0:nj, :], in1=usb[:, 0:nj, :], op=ALU.add), reads=["usb", "ysb"], writes=["ysb"])
            P.op("act", lambda e, nj=nj: e.activation(out=sb16[:, 0:nj, :], in_=ysb[:, 0:nj, :], func=AF.Gelu), reads=["ysb"], writes=["sb16"])
            for j in range(nj):
                pi = psr.next()
                pst = PS[pi][:].bitcast(BF16)
                for k in range(2):
                    P.op("pe", lambda e, pst=pst, j=j, k=k: e.transpose(pst[:, k * 128:(k + 1) * 128], sb16[:, j, k * 128:(k + 1) * 128], ident_b[:]), reads=["sb16", "ident_b"], writes=["ps%d" % pi])
                P.op("act", lambda e, pst=pst, j=j: e.activation(out=sTt[:, :, j * 128:(j + 1) * 128], in_=pst[:, 0:256].rearrange("p (k t) -> p k t", k=2), func=AF.Identity),
                     reads=[], writes=["ps%d" % pi, "sTt"])
            for ko in range(2):
                pi = psr.next()
                for ki_ in range(2):
                    P.op("pe", lambda e, pi=pi, ko=ko, ki_=ki_, n=n: e.matmul(PS[pi][:, 0:n], lhsT=glu[:, ki_, ko * 128:(ko + 1) * 128], rhs=sTt[:, ki_, 0:n], start=(ki_ == 0), stop=(ki_ == 1)),
                         reads=["glu", "sTt"], writes=["ps%d" % pi])
                P.op("act", lambda e, pi=pi, ko=ko, n=n: e.activation(out=gsig[:, ko, 0:n], in_=PS[pi][:, 0:n], func=AF.Sigmoid, bias=ppsb[:, PP_GLUB + ko:PP_GLUB + ko + 1]),
                     reads=["ppsb"], writes=["ps%d" % pi, "gsig"])
            P.op("dve", lambda e, n=n: e.tensor_tensor(out=cat[:, 6:8, 0:n], in0=sTt[:, :, 0:n], in1=gsig[:, :, 0:n], op=ALU.mult), reads=["sTt", "gsig"], writes=["cat"])
            for j in range(nj):
                ti0 = t0 + j * 128
                xs = (ti0 // 128) % 2
                P.dma("sp", xt[xs], xsrc[ti0:ti0 + 128, :], reads=["xsrc"], writes=["xtc%d" % xs])
                for hf_ in range(2):
                    pi = psr.next()
                    for kt in range(KT):
                        P.op("pe", lambda e, pi=pi, kt=kt, j=j, hf_=hf_: e.matmul(PS[pi][:, :], lhsT=cat[:, kt, j * 128:(j + 1) * 128], rhs=wo[:, kt, hf_ * 512:(hf_ + 1) * 512],
                                                                               start=(kt == 0), stop=(kt == KT - 1)),
                             reads=["cat", "wo"], writes=["ps%d" % pi])
                    P.op("dve", lambda e, pi=pi, xs=xs, hf_=hf_, w=w: e.tensor_tensor(out=xo[xs][:, hf_ * 512:(hf_ + 1) * 512], in0=PS[pi][:, :], in1=g1bc[w][:, hf_ * 512:(hf_ + 1) * 512], op=ALU.mult),
                         reads=["bc"], writes=["ps%d" % pi, "xo%d" % xs])
                P.op("pool", lambda e, xs=xs: e.tensor_tensor(out=xo[xs], in0=xo[xs], in1=xt[xs], op=ALU.add), reads=["xtc%d" % xs, "xo%d" % xs], writes=["xo%d" % xs])
                P.dma("sp", x1[ti0:ti0 + 128, :], xo[xs], reads=["xo%d" % xs], writes=["x1"])
            t0 += n
        while pre:
            o_, i_, k_ = pre.pop(0)
            P.dma("poolq", o_, i_, writes=[k_])

    def phaseC2(l, last):
        A.reset()
        w1 = A.bf16(KT * 4 * D).rearrange("p (k n) -> p k n", k=KT)
        w2 = A.bf16(32 * D).rearrange("p (k n) -> p k n", k=32)
        G = A.f32(D); S = A.f32(D); g2 = A.f32(D)
        cur_w = [None]

        def load_bc(w):
            if cur_w[0] == w:
                return
            cur_w[0] = w
            P.dma("sp", G, gsc[l, w, 1:2, :].partition_broadcast(128), reads=["gsc"], writes=["bc"])
            P.dma("sp", S, mraw[l, w:w + 1, 3 * D:4 * D].partition_broadcast(128), reads=["mraw"], writes=["bc"])
            P.dma("sp", g2, mraw[l, w:w + 1, 5 * D:6 * D].partition_broadcast(128), reads=["mraw"], writes=["bc"])
        if last:
            fn = A.f32(D)
            P.dma("sp", fn, final_norm.partition_broadcast(128), writes=["bcf"])
        NB = 256
        xts = [A.f32(D), A.f32(D)]
        hbs = [A.bf16(D), A.bf16(D)]
        junk = A.bf16(D); hf = A.f32(D)
        sss = [A.f32(1), A.f32(1)]; rss = [A.f32(1), A.f32(1)]
        hT = A.bf16(KT * NB).rearrange("p (k t) -> p k t", k=KT)
        uT = A.bf16(32 * NB).rearrange("p (k t) -> p k t", k=32)
        rl = [A.bf16(NB), A.bf16(NB)]
        yo = [A.f32(D), A.f32(D)]
        psT = Rot([0, 1]); psM = Rot([2, 3, 4, 5, 6, 7]); rlR = Rot([0, 1])
        t0 = 0 if not last else LC
        while t0 < TT:
            w = 1 if t0 < LC else 0
            n = min(NB, (LC if w == 1 else TT) - t0)
            nj = n // 128
            load_bc(w)
            for j in range(nj):
                ti0 = t0 + j * 128
                s = j % 2; sfx = str(s)
                P.dma("sp", xts[s], x1[ti0:ti0 + 128, :], reads=["x1"], writes=["xt" + sfx])
                rmsnorm_mod(xts[s], G, S, hbs[s], sfx, junk, sss[s], rss[s], hf)
                transpose_to(hbs[s], hT, j, sfx, psT)
            for ft in range(32):
                pi = psM.next()
                for kt in range(KT):
                    P.op("pe", lambda e, pi=pi, kt=kt, ft=ft, n=n: e.matmul(PS[pi][:, 0:n], lhsT=w1[:, kt, ft * 128:(ft + 1) * 128], rhs=hT[:, kt, 0:n], start=(kt == 0), stop=(kt == KT - 1)),
                         reads=["w1", "hT"], writes=["ps%d" % pi])
                ri = rlR.next()
                P.op("act", lambda e, pi=pi, ri=ri, n=n: e.activation(out=rl[ri][:, 0:n], in_=PS[pi][:, 0:n], func=AF.Relu), reads=[], writes=["ps%d" % pi, "rl%d" % ri])
                P.op("dve", lambda e, ri=ri, ft=ft, n=n: e.tensor_tensor(out=uT[:, ft, 0:n], in0=rl[ri][:, 0:n], in1=rl[ri][:, 0:n], op=ALU.mult), reads=["rl%d" % ri], writes=["uT"])
            for j in range(nj):
                ti0 = t0 + j * 128
                s = j % 2; sfx = str(s)
                for hf_ in range(2):
                    pi = psM.next()
                    for ft in range(32):
                        P.op("pe", lambda e, pi=pi, ft=ft, j=j, hf_=hf_: e.matmul(PS[pi][:, :], lhsT=uT[:, ft, j * 128:(j + 1) * 128], rhs=w2[:, ft, hf_ * 512:(hf_ + 1) * 512],
                                                                               start=(ft == 0), stop=(ft == 31)),
                             reads=["w2", "uT"], writes=["ps%d" % pi])
                    P.op("dve", lambda e, pi=pi, s=s, hf_=hf_, g2=g2: e.tensor_tensor(out=yo[s][:, hf_ * 512:(hf_ + 1) * 512], in0=PS[pi][:, :], in1=g2[:, hf_ * 512:(hf_ + 1) * 512], op=ALU.mult),
                         reads=["bc"], writes=["ps%d" % pi, "yo%d" % s])
                P.op("pool", lambda e, s=s: e.tensor_tensor(out=yo[s], in0=yo[s], in1=xts[s], op=ALU.add), reads=["xt" + str(s), "yo%d" % s], writes=["yo%d" % s])
                if not last:
                    P.dma("sp", xres[ti0:ti0 + 128, :], yo[s], reads=["yo%d" % s], writes=["xres"])
                else:
                    sfx2 = "f" + str(s)
                    P.op("act", lambda e, s=s: e.activation(out=junk, in_=yo[s], func=AF.Square, accum_out=sss[s]), reads=["yo%d" % s], writes=["junk", "ssf%d" % s])
                    P.op("act", lambda e, s=s: e.activation(out=rss[s], in_=sss[s], func=AF.Sqrt, scale=1.0 / D, bias=EPS), reads=["ssf%d" % s], writes=["rsf%d" % s])
                    P.op("dve", lambda e, s=s: e.reciprocal(out=rss[s], in_=rss[s]), reads=["rsf%d" % s], writes=["rsf%d" % s])
                    P.op("dve", lambda e, s=s: e.scalar_tensor_tensor(out=yo[s], in0=yo[s], scalar=rss[s], in1=fn, op0=ALU.mult, op1=ALU.mult),
                         reads=["yo%d" % s, "rsf%d" % s, "bcf"], writes=["yo%d" % s])
                    P.dma("sp", out_d[ti0 - LC:ti0 - LC + 128, :], yo[s], reads=["yo%d" % s], writes=["out"])
            t0 += n

    setup_consts()
    stages = build.stages if hasattr(build, "stages") else None
    for l in range(depth):
        last = (l == depth - 1)
        if stages == "C":
            break
        modulation(l)
        load_pp(l)
        P.barrier()
        if stages == "M":
            break
        phaseA(l)
        P.barrier()
        if stages is not None and "A" == stages:
            break
        print("nops before gla", P.nops, flush=True)
        gla(l)
        P.barrier()
        print("nops before lru", P.nops, flush=True)
        lru(l)
        P.barrier()
        print("nops before s5", P.nops, flush=True)
        s5(l)
        P.barrier()
        print("nops after s5", P.nops, flush=True)
        if stages is not None and "B" == stages:
            break
        phaseC1(l, last)
        P.barrier()
        phaseC2(l, last)
        P.barrier()
    P.barrier()
    print("nops", P.nops, flush=True)
    P.emit()
    P.close()
    return nc


def prep_inputs(inp, b, LL, LC, depth):
    f = lambda a: np.ascontiguousarray(np.asarray(a, dtype=np.float32))
    TT = LL + LC
    NC8, LC8 = TT // 8, LC // 8
    m = {}
    m["xin"] = f(np.concatenate([inp["ctx"][b], inp["x"][b]], axis=0))
    cv = np.stack([np.asarray(inp["c"][b]).reshape(KT, 128).T, np.asarray(inp["c_ctx"]).reshape(KT, 128).T], axis=-1)
    m["cvec"] = f(cv)
    for k in ("w_mod", "b_mod", "norm1", "norm2", "w_in", "gla_up_w", "lru_wa", "lru_wx", "s5_d", "s5_glu_w", "w_out", "w_ff1", "w_ff2"):
        m[k] = f(inp[k])
    m["final_norm"] = f(np.asarray(inp["final_norm"]).reshape(1, D))
    pp = np.zeros((depth, 128, 64), np.float32)
    for l in range(depth):
        for d in range(2):
            for hp in range(2):
                pp[l, :, 0 + d * 2 + hp] = inp["gla_up_b"][l, d, hp * 128:(hp + 1) * 128]
            for ct in range(2):
                sl = slice(ct * 128, (ct + 1) * 128)
                for k in range(4):
                    pp[l, :, 8 + (d * 4 + k) * 2 + ct] = inp["lru_conv_w"][l, d, k, sl]
                pp[l, :, 24 + d * 2 + ct] = inp["lru_conv_b"][l, d, sl]
                pp[l, :, 28 + d * 2 + ct] = inp["lru_ba"][l, d, sl]
                pp[l, :, 32 + d * 2 + ct] = inp["lru_bx"][l, d, sl]
                pp[l, :, 36 + d * 2 + ct] = inp["lru_lambda"][l, d, sl]
        for h in range(4):
            pp[l, :, 4 + h] = inp["gla_norm"][l, h * 128:(h + 1) * 128]
        for ct in range(2):
            pp[l, :, 40 + ct] = inp["s5_glu_b"][l, ct * 128:(ct + 1) * 128]
    m["pp"] = pp
    s5p = np.zeros((depth, 128, 3, 16), np.float32)
    s5b = np.zeros((depth, 128, 2, 16, 16), np.float32)
    s5c = np.zeros((depth, 128, 2, 16, 16), np.float32)
    for l in range(depth):
        for d in range(2):
            for gp in range(8):
                for gm in range(2):
                    g = gp * 2 + gm
                    ps_ = slice(gm * 64, (gm + 1) * 64)
                    s5p[l, ps_, 0, d * 8 + gp] = inp["s5_lam_re"][l, d, g]
                    s5p[l, ps_, 1, d * 8 + gp] = inp["s5_lam_im"][l, d, g]
                    s5p[l, ps_, 2, d * 8 + gp] = inp["s5_log_dt"][l, d, g]
                    s5b[l, ps_, 0, d * 8 + gp, :] = inp["s5_b_re"][l, d, g]
                    s5b[l, ps_, 1, d * 8 + gp, :] = inp["s5_b_im"][l, d, g]
                    s5c[l, ps_, 0, d * 8 + gp, :] = np.asarray(inp["s5_c_re"][l, d, g]).T
                    s5c[l, ps_, 1, d * 8 + gp, :] = np.asarray(inp["s5_c_im"][l, d, g]).T
    m["s5p"], m["s5b"], m["s5c"] = s5p, s5b, s5c
    tau = np.zeros((2, NC8), np.float32)
    tau[0] = np.arange(NC8)
    tau[1, :LC8] = LC8 - 1 - np.arange(LC8)
    tau[1, LC8:] = LC8 + (NC8 - 1 - np.arange(LC8, NC8))
    m["tau"] = tau.reshape(1, 2 * NC8)
    return m


_CACHE = {}


def kernel(**inputs):
    LL, LC, depth = 4096, 256, 4
    B = inputs["x"].shape[0]
    key = (LL, LC, depth)
    if key not in _CACHE:
        _CACHE[key] = build(LL, LC, depth)
    nc = _CACHE[key]
    in_maps = [prep_inputs(inputs, b, LL, LC, depth) for b in range(B)]
    res = run_bass_kernel_spmd(nc, in_maps, core_ids=list(range(B)))
    return np.stack([np.asarray(r["out"], dtype=np.float32) for r in res.results], axis=0)
```

```python
import math
import os
from contextlib import ExitStack

import numpy as np
import concourse.bass as bass
import concourse.mybir as mybir
from concourse.bass_utils import run_bass_kernel_spmd

F32 = mybir.dt.float32
BF16 = mybir.dt.bfloat16
ALU = mybir.AluOpType
AF = mybir.ActivationFunctionType

D = 1024
KT = 8
EPS = 1e-6
MAGIC = 12582912.0
TWO_PI = 2.0 * math.pi

COMPUTE = ("pe", "act", "dve", "pool")
QUEUES = ("sp", "poolq")
ENG_OF = {"pe": "pe", "act": "act", "dve": "dve", "pool": "pool", "sp": "sp", "poolq": "pool"}


class Prog:
    def __init__(self, nc, ndma=8):
        import os
        self.nc = nc
        self.es = ExitStack()
        self.streams = {e: [] for e in ("pe", "act", "dve", "pool", "sp")}
        self.sem = {}
        self.nop_eng = {}
        for e in COMPUTE:
            self.sem[e] = self.es.enter_context(nc.semaphore("s_" + e))
            self.nop_eng[e] = 0
        self.dsem, self.dcnt, self.dnext = {}, {}, {}
        for q in QUEUES:
            self.dsem[q] = [self.es.enter_context(nc.semaphore("d_%s%d" % (q, i))) for i in range(ndma)]
            self.dcnt[q] = [0] * ndma
            self.dnext[q] = 0
        self.seen = {e: {} for e in self.streams}
        self.lastw = {}
        self.readers = {}
        self.nops = 0
        self.waited = {e: set() for e in COMPUTE}
        self._cap = None
        self.limit = int(os.environ["OPLIMIT"]) if os.environ.get("OPLIMIT") else None

    def sbuf(self, name, shape, dtype=F32):
        return self.es.enter_context(self.nc.sbuf_tensor(name, list(shape), dtype))

    def psum(self, name, shape, dtype=F32):
        return self.es.enter_context(self.nc.psum_tensor(name, list(shape), dtype))

    @staticmethod
    def _tkey(tok):
        return ("c", tok[1]) if tok[0] == "c" else ("d", tok[1].name)

    @staticmethod
    def _tval(tok):
        return tok[2]

    def _need(self, stream, tok, waits):
        if tok is None:
            return
        if tok[0] == "c" and tok[1] == "pe" and stream == "pe":
            return
        k = self._tkey(tok)
        if self.seen[stream].get(k, 0) >= self._tval(tok):
            return
        cur = waits.get(k)
        if cur is None or self._tval(cur) < self._tval(tok):
            waits[k] = tok

    def _deps(self, stream, reads, writes, waits, is_dma=False):
        for k in reads:
            self._need(stream, self.lastw.get(k), waits)
        for k in writes:
            t = self.lastw.get(k)
            if is_dma or not (t is not None and t[0] == "c" and t[1] == stream):
                self._need(stream, t, waits)
            for t in self.readers.get(k, ()):
                if is_dma or not (t[0] == "c" and t[1] == stream):
                    self._need(stream, t, waits)

    def _commit(self, stream, tok, reads, writes, waits):
        for k, t in waits.items():
            self.seen[stream][k] = self._tval(t)
            if t[0] == "c":
                self.waited[t[1]].add(t[2])
        for k in writes:
            self.lastw[k] = tok
            self.readers[k] = []
        for k in reads:
            if k in writes:
                continue
            lst = self.readers.setdefault(k, [])
            lst.append(tok)
            if len(lst) > 16:
                best = {}
                for t in lst:
                    kk = self._tkey(t)
                    b = best.get(kk)
                    if b is None or self._tval(b) < self._tval(t):
                        best[kk] = t
                self.readers[k] = list(best.values())

    def capture(self, f):
        prev = self._cap
        self._cap = []
        f()
        lst = self._cap
        self._cap = prev
        return lst

    def interleave(self, lists):
        lists = [list(l) for l in lists if l]
        idx = [0] * len(lists)
        while True:
            done = True
            for k, l in enumerate(lists):
                if idx[k] < len(l):
                    done = False
                    kind, a = l[idx[k]]
                    idx[k] += 1
                    if kind == "op":
                        self._op2(*a)
                    else:
                        self._dma2(*a)
            if done:
                break

    def op(self, eng, fn, reads=(), writes=()):
        if self.limit is not None and self.nops >= self.limit:
            return None
        rec = _Rec()
        fn(rec)
        name, args, kwargs = rec.call
        if self._cap is not None:
            self._cap.append(("op", (eng, name, args, kwargs, tuple(reads), tuple(writes))))
            return None
        return self._op2(eng, name, args, kwargs, reads, writes)

    def _op2(self, eng, name, args, kwargs, reads, writes):
        if os.environ.get("OPTRACE"):
            def _d(a):
                try:
                    return "%s%s" % (tuple(a.shape), "" )
                except Exception:
                    return str(a)[:30]
            print("OP", self.nops, eng, name, [_d(a) for a in args], {k: _d(v) for k, v in kwargs.items()}, flush=True)
        fn = lambda e, name=name, args=args, kwargs=kwargs: getattr(e, name)(*args, **kwargs)
        waits = {}
        self._deps(eng, reads, writes, waits)
        self.nop_eng[eng] += 1
        tok = ("c", eng, self.nop_eng[eng])
        self._commit(eng, tok, reads, writes, waits)
        self.streams[eng].append([list(waits.values()), fn, tok])
        self.nops += 1
        return tok

    def dma(self, q, out, in_, reads=(), writes=(), **kw):
        if self.limit is not None and self.nops >= self.limit:
            return None
        if self._cap is not None:
            self._cap.append(("dma", (q, out, in_, tuple(reads), tuple(writes), kw)))
            return None
        return self._dma2(q, out, in_, reads, writes, kw)

    def _dma2(self, q, out, in_, reads, writes, kw):
        stream = ENG_OF[q]
        waits = {}
        i = self.dnext[q]
        self.dnext[q] = (i + 1) % len(self.dsem[q])
        sem = self.dsem[q][i]
        if self.dcnt[q][i] > 0:
            self._need(stream, ("d", sem, 16 * self.dcnt[q][i], q), waits)
        self._deps(stream, reads, writes, waits, is_dma=True)
        self.dcnt[q][i] += 1
        tok = ("d", sem, 16 * self.dcnt[q][i], q)
        self._commit(stream, tok, reads, writes, waits)
        fn = lambda e, out=out, in_=in_, kw=kw: e.dma_start(out=out, in_=in_, **kw)
        self.streams[stream].append([list(waits.values()), fn, tok])
        self.nops += 1
        return tok

    def barrier(self):
        toks = []
        for q in QUEUES:
            for i, sem in enumerate(self.dsem[q]):
                if self.dcnt[q][i]:
                    toks.append(("d", sem, 16 * self.dcnt[q][i], q))
        for e in COMPUTE:
            if self.nop_eng[e]:
                toks.append(("c", e, self.nop_eng[e]))
        for stream in self.streams:
            waits = {}
            for t in toks:
                if t[0] == "c" and t[1] == stream:
                    continue
                self._need(stream, t, waits)
            for k, t in waits.items():
                self.seen[stream][k] = self._tval(t)
                if t[0] == "c":
                    self.waited[t[1]].add(t[2])
            if waits:
                self.streams[stream].append([list(waits.values()), None, None])
        self.lastw = {}
        self.readers = {}

    def emit(self):
        nc = self.nc
        streams = self.streams
        rank = {}
        for e in COMPUTE:
            rank[e] = {idx: r + 1 for r, idx in enumerate(sorted(self.waited[e]))}

        def run(eng_obj, lst):
            for waits, fn, tok in lst:
                for t in waits:
                    if t[0] == "c":
                        eng_obj.wait_ge(self.sem[t[1]], rank[t[1]][t[2]])
                    else:
                        eng_obj.wait_ge(t[1], t[2])
                if fn is not None:
                    ins = fn(eng_obj)
                    if tok[0] == "d":
                        ins.then_inc(tok[1], 16)
                    elif tok[2] in rank[tok[1]]:
                        ins.then_inc(self.sem[tok[1]], 1)

        with nc.Block() as block:
            @block.tensor
            def _(e):
                run(e, streams["pe"])

            @block.scalar
            def _(e):
                run(e, streams["act"])

            @block.vector
            def _(e):
                run(e, streams["dve"])

            @block.gpsimd
            def _(e):
                run(e, streams["pool"])

            @block.sync
            def _(e):
                run(e, streams["sp"])

    def close(self):
        self.es.close()


class _Rec:
    def __init__(self):
        self.call = None

    def __getattr__(self, name):
        def f(*args, **kwargs):
            self.call = (name, args, kwargs)
            return self
        return f


class Arena:
    def __init__(self, P, words):
        self.t = P.sbuf("arena", [128, words], F32)
        self.words = words
        self.off = 0
        self.n = 0

    def reset(self):
        self.off = 0

    def f32(self, n):
        assert self.off + n <= self.words, ("arena overflow", self.off, n, self.words)
        ap = self.t[:, self.off:self.off + n]
        self.off += n
        return ap

    def bf16(self, n):
        w = (n + 1) // 2
        return self.f32(w).bitcast(BF16)[:, 0:n]


class Rot:
    def __init__(self, items):
        self.items = items
        self.i = 0

    def next(self):
        it = self.items[self.i % len(self.items)]
        self.i += 1
        return it


def build(LL, LC, depth, debug=False):
    TT = LL + LC
    NT = TT // 128
    NCH = TT // 64
    NC8 = TT // 8
    LC8 = LC // 8
    ROWS = LL // 64
    RB = ROWS // 8
    nc = bass.Bass("TRN2", target_bir_lowering=False)
    P = Prog(nc)

    def din(name, shape, dt=F32):
        return nc.dram_tensor(name, list(shape), dt, kind="ExternalInput").ap()

    dkind = "ExternalOutput" if debug else "Internal"

    def dscr(name, shape, dt=F32):
        return nc.dram_tensor(name, list(shape), dt, kind=dkind).ap()

    xin = din("xin", [TT, D])
    cvec = din("cvec", [128, KT, 2])
    w_mod = din("w_mod", [depth, D, 6 * D])
    b_mod = din("b_mod", [depth, 6 * D])
    norm1 = din("norm1", [depth, D])
    norm2 = din("norm2", [depth, D])
    w_in = din("w_in", [depth, D, 2336])
    gla_up_w = din("gla_up_w", [depth, 2, 16, 256])
    pp = din("pp", [depth, 128, 64])
    lru_wa = din("lru_wa", [depth, 2, 4, 64, 64])
    lru_wx = din("lru_wx", [depth, 2, 4, 64, 64])
    s5p = din("s5p", [depth, 128, 3, 16])
    s5b = din("s5b", [depth, 128, 2, 16, 16])
    s5c = din("s5c", [depth, 128, 2, 16, 16])
    s5_d = din("s5_d", [depth, 256])
    s5_glu_w = din("s5_glu_w", [depth, 256, 256])
    w_out = din("w_out", [depth, D, D])
    w_ff1 = din("w_ff1", [depth, D, 4 * D])
    w_ff2 = din("w_ff2", [depth, 4 * D, D])
    final_norm = din("final_norm", [1, D])
    tau_in = din("tau", [1, 2 * NC8])

    out_d = nc.dram_tensor("out", [LL, D], F32, kind="ExternalOutput").ap()

    mraw = dscr("mraw", [depth, 2, 6 * D])
    gsc = dscr("gsc", [depth, 2, 2, D])
    qT = dscr("qT", [256, TT]); kT = dscr("kT", [256, TT]); ggT = dscr("ggT", [512, TT])
    lrT = [dscr("lrT0", [16, TT]), dscr("lrT1", [16, TT])]
    lxT = dscr("lxT", [256, TT]); lgT = dscr("lgT", [256, TT])
    v_tok = dscr("v_tok", [TT, 512], BF16)
    su_tok = dscr("su_tok", [TT, 256])
    oT = dscr("oT", [512, TT])
    lruT = dscr("lruT", [256, TT])
    s5y = dscr("s5y", [TT, 256])
    x1 = dscr("x1", [TT, D])
    xres = dscr("xres", [TT, D])
    dbg = {}

    ident_f = P.sbuf("ident_f", [128, 128], F32)
    ident_b = P.sbuf("ident_b", [128, 128], BF16)
    ones_f = P.sbuf("ones_f", [128, 128], F32)
    ones_b = P.sbuf("ones_b", [128, 128], BF16)
    maskF = P.sbuf("maskF", [128, 128], F32)
    maskB = P.sbuf("maskB", [128, 128], F32)
    m8F = P.sbuf("m8F", [128, 512], F32)
    m8B = P.sbuf("m8B", [128, 512], F32)
    jidx = P.sbuf("jidx", [128, 2, 8, 9], F32)
    ppsb = P.sbuf("ppsb", [128, 64], F32)
    AW = 50400
    A = Arena(P, AW)
    PS = [P.psum("ps%d" % i, [128, 512], F32) for i in range(8)]

    def setup_consts():
        P.op("pool", lambda e: e.memset(ident_f[:], 0.0), writes=["ident_f"])
        P.op("pool", lambda e: e.affine_select(out=ident_f[:], in_=ident_f[:], pattern=[[-1, 128]], compare_op=ALU.not_equal,
                                               fill=1.0, base=0, channel_multiplier=1), reads=["ident_f"], writes=["ident_f"])
        P.op("dve", lambda e: e.tensor_copy(out=ident_b[:], in_=ident_f[:]), reads=["ident_f"], writes=["ident_b"])
        P.op("dve", lambda e: e.memset(ones_f[:], 1.0), writes=["ones_f"])
        P.op("dve", lambda e: e.memset(ones_b[:], 1.0), writes=["ones_b"])
        P.op("pool", lambda e: e.affine_select(out=maskF[:], in_=ones_f[:], pattern=[[1, 128]], compare_op=ALU.is_ge,
                                               fill=0.0, base=0, channel_multiplier=-1), reads=["ones_f"], writes=["maskF"])
        P.op("pool", lambda e: e.memset(maskF[0:64, 64:128], 0.0), reads=["maskF"], writes=["maskF"])
        P.op("pool", lambda e: e.affine_select(out=maskB[:], in_=ones_f[:], pattern=[[-1, 128]], compare_op=ALU.is_ge,
                                               fill=0.0, base=0, channel_multiplier=1), reads=["ones_f"], writes=["maskB"])
        P.op("pool", lambda e: e.memset(maskB[64:128, 0:64], 0.0), reads=["maskB"], writes=["maskB"])
        P.op("pool", lambda e: e.memset(m8F[:], 1.0), writes=["m8F"])
        P.op("pool", lambda e: e.memset(m8B[:], 1.0), writes=["m8B"])
        P.op("pool", lambda e: e.affine_select(out=m8F[:].rearrange("p (r j h) -> p r j h", r=4, j=8), in_=m8F[:].rearrange("p (r j h) -> p r j h", r=4, j=8),
                                               pattern=[[0, 4], [16, 8], [0, 16]], compare_op=ALU.is_ge, fill=0.0, base=15, channel_multiplier=-1),
             reads=["m8F"], writes=["m8F"])
        P.op("pool", lambda e: e.affine_select(out=m8B[:].rearrange("p (r j h) -> p r j h", r=4, j=8), in_=m8B[:].rearrange("p (r j h) -> p r j h", r=4, j=8),
                                               pattern=[[0, 4], [-16, 8], [0, 16]], compare_op=ALU.is_ge, fill=0.0, base=0, channel_multiplier=1),
             reads=["m8B"], writes=["m8B"])
        for j in range(9):
            P.op("dve", lambda e, j=j: e.memset(jidx[:, 0, :, j:j + 1], float(j)), writes=["jidx"])
            P.op("dve", lambda e, j=j: e.memset(jidx[:, 1, :, j:j + 1], float(7 - j) if j < 8 else 8.0), writes=["jidx"])

    PP_UPB = 0
    PP_GNORM = 4
    PP_CONVW = 8
    PP_CONVB = 24
    PP_BA = 28
    PP_BX = 32
    PP_LAM = 36
    PP_GLUB = 40
    PP_SGN = 42

    def modulation(l):
        A.reset()
        cs_raw = A.f32(16); cs = A.f32(16)
        msb = A.f32(6 * D)
        bm = A.f32(6 * D)
        n12 = A.f32(2 * D)
        gt = A.f32(2 * D)
        wblk = [A.f32(KT * 512), A.f32(KT * 512)]
        P.dma("sp", cs_raw, cvec.rearrange("p k w -> p (k w)"), writes=["cs_raw"])
        P.op("act", lambda e: e.activation(out=cs, in_=cs_raw, func=AF.Silu), reads=["cs_raw"], writes=["cs"])
        P.dma("sp", bm[0:2, :], b_mod[l:l + 1, :].partition_broadcast(2), writes=["bm"])
        P.dma("sp", n12[0:2, 0:D], norm1[l:l + 1, :].partition_broadcast(2), writes=["n12a"])
        P.dma("sp", n12[0:2, D:2 * D], norm2[l:l + 1, :].partition_broadcast(2), writes=["n12b"])
        wv = w_mod[l].rearrange("(kt p) n -> p kt n", p=128)
        cs3 = cs.rearrange("p (k w) -> p k w", w=2)
        for j in range(12):
            wb = wblk[j % 2]
            wb3 = wb.rearrange("p (k n) -> p k n", k=KT)
            P.dma("sp", wb3, wv[:, :, j * 512:(j + 1) * 512], writes=["wblk%d" % (j % 2)])
            ps = PS[j % 2]
            for kt in range(KT):
                P.op("pe", lambda e, ps=ps, kt=kt, wb3=wb3: e.matmul(ps[0:2, :], lhsT=cs3[:, kt, :], rhs=wb3[:, kt, :], start=(kt == 0), stop=(kt == KT - 1)),
                     reads=["cs", "wblk%d" % (j % 2)], writes=["ps%d" % (j % 2)])
            P.op("dve", lambda e, ps=ps, j=j: e.tensor_tensor(out=msb[0:2, j * 512:(j + 1) * 512], in0=ps[0:2, :], in1=bm[0:2, j * 512:(j + 1) * 512], op=ALU.add),
                 reads=["bm"], writes=["ps%d" % (j % 2), "msb"])
        P.op("dve", lambda e: e.scalar_tensor_tensor(out=gt[0:2, 0:D], in0=msb[0:2, D:2 * D], scalar=1.0, in1=n12[0:2, 0:D], op0=ALU.add, op1=ALU.mult),
             reads=["msb", "n12a"], writes=["gt"])
        P.op("dve", lambda e: e.scalar_tensor_tensor(out=gt[0:2, D:2 * D], in0=msb[0:2, 4 * D:5 * D], scalar=1.0, in1=n12[0:2, D:2 * D], op0=ALU.add, op1=ALU.mult),
             reads=["msb", "n12b", "gt"], writes=["gt"])
        P.dma("sp", mraw[l], msb[0:2, :], reads=["msb"], writes=["mraw"])
        P.dma("sp", gsc[l].rearrange("w g d -> w (g d)"), gt[0:2, :], reads=["gt"], writes=["gsc"])

    def bc_load(dst, src_row):
        return src_row.to_broadcast([128, src_row.shape[-1]])

    def token_blocks(last_skip_ctx=False):
        blks = []
        if not last_skip_ctx:
            t = 0
            while t < LC // 128:
                n = min(4, LC // 128 - t)
                blks.append((t, n, 1))
                t += n
        t = LC // 128
        while t < NT:
            n = min(4, NT - t)
            blks.append((t, n, 0))
            t += n
        return blks

    def rmsnorm_mod(xt_ap, Gbc, Sbc, hb_out, sfx, junk, ss, rs, hf):
        P.op("act", lambda e: e.activation(out=junk, in_=xt_ap, func=AF.Square, accum_out=ss), reads=["xt" + sfx], writes=["junk", "ss" + sfx])
        P.op("act", lambda e: e.activation(out=rs, in_=ss, func=AF.Sqrt, scale=1.0 / D, bias=EPS), reads=["ss" + sfx], writes=["rs" + sfx])
        P.op("dve", lambda e: e.reciprocal(out=rs, in_=rs), reads=["rs" + sfx], writes=["rs" + sfx])
        P.op("dve", lambda e: e.scalar_tensor_tensor(out=hf, in0=xt_ap, scalar=rs, in1=Gbc, op0=ALU.mult, op1=ALU.mult),
             reads=["xt" + sfx, "rs" + sfx, "bc"], writes=["hf"])
        P.op("pool", lambda e: e.tensor_tensor(out=hb_out, in0=hf, in1=Sbc, op=ALU.add), reads=["hf", "bc"], writes=["hb" + sfx])

    def transpose_to(hb, hT3, j, sfx, psrot):
        pi = psrot.next()
        pst = PS[pi][:].bitcast(BF16)
        for kt in range(KT):
            P.op("pe", lambda e, kt=kt, pst=pst: e.transpose(pst[:, kt * 128:(kt + 1) * 128], hb[:, kt * 128:(kt + 1) * 128], ident_b[:]),
                 reads=["hb" + sfx, "ident_b"], writes=["ps%d" % pi])
        P.op("act", lambda e, pst=pst: e.activation(out=hT3[:, :, j * 128:(j + 1) * 128], in_=pst.rearrange("p (k t) -> p k t", k=KT), func=AF.Identity),
             reads=[], writes=["ps%d" % pi, "hT"])

    def phaseA(l):
        A.reset()
        xsrc = xin if l == 0 else xres
        win = A.bf16(KT * 2336).rearrange("p (k n) -> p k n", k=KT)
        wv = w_in[l].rearrange("(kt p) n -> p kt n", p=128)
        for c0 in range(0, 2336, 512):
            c1 = min(2336, c0 + 512)
            P.dma("poolq", win[:, :, c0:c1], wv[:, :, c0:c1], writes=["win"])
        bcs = {}
        for w in (0, 1):
            G = A.f32(D); S = A.f32(D)
            P.dma("sp", G, gsc[l, w, 0:1, :].partition_broadcast(128), reads=["gsc"], writes=["bc"])
            P.dma("sp", S, mraw[l, w:w + 1, 0:D].partition_broadcast(128), reads=["mraw"], writes=["bc"])
            bcs[w] = (G, S)
        NS = 8
        xts = [A.f32(D) for _ in range(2)]
        hbs = [A.bf16(D) for _ in range(NS)]
        hfs = [A.f32(D), A.f32(D)]
        sss = [A.f32(1) for _ in range(NS)]; rss = [A.f32(1) for _ in range(NS)]
        hTs = [A.bf16(KT * 512).rearrange("p (k t) -> p k t", k=KT) for _ in range(2)]
        stg = [A.f32(512) for _ in range(3)]
        vst = [A.bf16(512) for _ in range(2)]
        sst = [A.f32(256) for _ in range(2)]
        psT = Rot([0, 1]); psM = Rot([2, 3, 4, 5, 6, 7])
        stgR = Rot([0, 1, 2]); vR = Rot([0, 1]); sR = Rot([0, 1])
        FM = [("q", 0, qT, 0, 128), ("q", 128, qT, 128, 128), ("k", 256, kT, 0, 128), ("k", 384, kT, 128, 128)]
        for i in range(4):
            FM.append(("gg", 1024 + 128 * i, ggT, 128 * i, 128))
        FM.append(("lr0", 1536, lrT[0], 0, 16)); FM.append(("lr1", 1552, lrT[1], 0, 16))
        for i in range(2):
            FM.append(("lx", 1568 + 128 * i, lxT, 128 * i, 128))
        for i in range(2):
            FM.append(("lg", 1824 + 128 * i, lgT, 128 * i, 128))
        blocks = token_blocks()
        slot_ctr = [0]
        blk_slots = {}

        def norms(bi):
            (t0, n, w) = blocks[bi]
            G, S = bcs[w]
            sl = []
            for j in range(n):
                ti = t0 + j
                s = slot_ctr[0] % NS; slot_ctr[0] += 1
                xs = s % 2
                sl.append(s)
                kx, kh = "xt%d" % xs, "hb%d" % s
                P.dma("sp", xts[xs], xsrc[ti * 128:(ti + 1) * 128, :], reads=["xsrc"], writes=[kx])
                P.op("act", lambda e, xs=xs, s=s: e.activation(out=hbs[s], in_=xts[xs], func=AF.Square, accum_out=sss[s]), reads=[kx], writes=[kh, "ss%d" % s])
                P.op("act", lambda e, s=s: e.activation(out=rss[s], in_=sss[s], func=AF.Sqrt, scale=1.0 / D, bias=EPS), reads=["ss%d" % s], writes=["rs%d" % s])
                P.op("dve", lambda e, s=s: e.reciprocal(out=rss[s], in_=rss[s]), reads=["rs%d" % s], writes=["rs%d" % s])
                P.op("dve", lambda e, xs=xs, s=s, G=G: e.scalar_tensor_tensor(out=hfs[xs], in0=xts[xs], scalar=rss[s], in1=G, op0=ALU.mult, op1=ALU.mult),
                     reads=[kx, "rs%d" % s, "bc"], writes=["hf%d" % xs])
                P.op("pool", lambda e, xs=xs, s=s, S=S: e.tensor_tensor(out=hbs[s], in0=hfs[xs], in1=S, op=ALU.add), reads=["hf%d" % xs, "bc"], writes=[kh])
            blk_slots[bi] = sl

        def transposes(bi):
            hT = hTs[bi % 2]
            for j, s in enumerate(blk_slots[bi]):
                pi = psT.next()
                pst = PS[pi][:].bitcast(BF16)
                for kt in range(KT):
                    P.op("pe", lambda e, kt=kt, pst=pst, s=s: e.transpose(pst[:, kt * 128:(kt + 1) * 128], hbs[s][:, kt * 128:(kt + 1) * 128], ident_b[:]),
                         reads=["hb%d" % s, "ident_b"], writes=["ps%d" % pi])
                P.op("act", lambda e, pst=pst, j=j, hT=hT: e.activation(out=hT[:, :, j * 128:(j + 1) * 128], in_=pst.rearrange("p (k t) -> p k t", k=KT), func=AF.Identity),
                     reads=[], writes=["ps%d" % pi, "hT%d" % (bi % 2)])

        def fm_part(bi):
            (t0, n, w) = blocks[bi]
            hT = hTs[bi % 2]; kT_ = "hT%d" % (bi % 2)
            ntok = n * 128; tok0 = t0 * 128
            for (nm, c0, dst, r0, m) in FM:
                pi = psM.next()
                for kt in range(KT):
                    P.op("pe", lambda e, pi=pi, kt=kt, c0=c0, m=m, hT=hT, ntok=ntok: e.matmul(PS[pi][0:m, 0:ntok], lhsT=win[:, kt, c0:c0 + m], rhs=hT[:, kt, 0:ntok],
                                                                                           start=(kt == 0), stop=(kt == KT - 1)),
                         reads=["win", kT_], writes=["ps%d" % pi])
                si = stgR.next()
                P.op("act", lambda e, pi=pi, si=si, m=m, ntok=ntok: e.activation(out=stg[si][0:m, 0:ntok], in_=PS[pi][0:m, 0:ntok], func=AF.Identity),
                     reads=[], writes=["ps%d" % pi, "stg%d" % si])
                P.dma("sp", dst[r0:r0 + m, tok0:tok0 + ntok], stg[si][0:m, 0:ntok], reads=["stg%d" % si], writes=["zT_%s_%d" % (nm, r0)])

        def tm_part(bi):
            (t0, n, w) = blocks[bi]
            hT = hTs[bi % 2]; kT_ = "hT%d" % (bi % 2)
            for j in range(n):
                ti = t0 + j
                pi = psM.next()
                for kt in range(KT):
                    P.op("pe", lambda e, pi=pi, kt=kt, j=j, hT=hT: e.matmul(PS[pi][:, :], lhsT=hT[:, kt, j * 128:(j + 1) * 128], rhs=win[:, kt, 512:1024],
                                                                         start=(kt == 0), stop=(kt == KT - 1)),
                         reads=["win", kT_], writes=["ps%d" % pi])
                vi = vR.next()
                P.op("dve", lambda e, pi=pi, vi=vi: e.tensor_copy(out=vst[vi], in_=PS[pi][:, :]), reads=[], writes=["ps%d" % pi, "vst%d" % vi])
                P.dma("sp", v_tok[ti * 128:(ti + 1) * 128, :], vst[vi], reads=["vst%d" % vi], writes=["v_tok%d" % ti])
                pi = psM.next()
                for kt in range(KT):
                    P.op("pe", lambda e, pi=pi, kt=kt, j=j, hT=hT: e.matmul(PS[pi][:, 0:256], lhsT=hT[:, kt, j * 128:(j + 1) * 128], rhs=win[:, kt, 2080:2336],
                                                                         start=(kt == 0), stop=(kt == KT - 1)),
                         reads=["win", kT_], writes=["ps%d" % pi])
                si = sR.next()
                P.op("dve", lambda e, pi=pi, si=si: e.tensor_copy(out=sst[si], in_=PS[pi][:, 0:256]), reads=[], writes=["ps%d" % pi, "sst%d" % si])
                P.dma("sp", su_tok[ti * 128:(ti + 1) * 128, :], sst[si], reads=["sst%d" % si], writes=["su_tok%d" % ti])

        norms(0)
        transposes(0)
        for bi in range(len(blocks)):
            if bi + 1 < len(blocks):
                norms(bi + 1)
            fm_part(bi)
            if bi + 1 < len(blocks):
                transposes(bi + 1)
            tm_part(bi)

    def load_pp(l):
        P.dma("sp", ppsb[:], pp[l], writes=["ppsb"])

    def gla(l):
        for hp in range(2):
            A.reset()
            sm0 = A.bf16(TT); sm1 = A.bf16(TT)
            P.op("pool", lambda e: e.memset(sm0, 1.0), writes=["sm"])
            P.op("pool", lambda e: e.memset(sm0.rearrange("p (c j) -> p c j", j=64)[:, :, 0:1], 0.0), reads=["sm"], writes=["sm"])
            P.op("pool", lambda e: e.memset(sm1, 1.0), reads=["sm"], writes=["sm"])
            P.op("pool", lambda e: e.memset(sm1.rearrange("p (c j) -> p c j", j=64)[:, :, 63:64], 0.0), reads=["sm"], writes=["sm"])
            vt = A.bf16(NT * 256).rearrange("p (t c) -> p t c", t=NT)
            vsrc = v_tok.rearrange("(t p) c -> p t c", p=128)
            for t0_ in range(0, NT, 8):
                t1_ = min(NT, t0_ + 8)
                P.dma("sp", vt[:, t0_:t1_, :], vsrc[:, t0_:t1_, hp * 256:(hp + 1) * 256], reads=["v_tok"], writes=["vt"])
            oacc = A.f32(2 * TT).rearrange("p (h t) -> p h t", h=2)
            P.op("pool", lambda e: e.memset(oacc, 0.0), writes=["oacc%d_%d" % (hh, t) for hh in range(2) for t in range(NT)])
            lrsb = A.f32(TT); Bp = A.f32(TT); Bc = A.f32(TT); qk = A.f32(TT)
            qd = [A.bf16(TT), A.bf16(TT)]; ki = [A.bf16(TT), A.bf16(TT)]
            kiT = [A.bf16(NT * 128).rearrange("p (t c) -> p t c", t=NT) for _ in range(2)]
            upw = A.f32(256); nb = A.f32(1)
            gam = [A.f32(NCH), A.f32(NCH)]
            S = [A.f32(128), A.f32(128)]; Sb = [A.bf16(128), A.bf16(128)]; tmp = [A.f32(128), A.f32(128)]
            sT = [[A.bf16(128), A.bf16(128)], [A.bf16(128), A.bf16(128)]]
            for d in range(2):
                ds_ = str(d)
                sm = sm0 if d == 0 else sm1
                P.dma("sp", lrsb[0:16, :], lrT[d], reads=["zT_lr%d" % d], writes=["lrsb"])
                P.dma("sp", upw[0:16, :], gla_up_w[l, d], writes=["upw"])
                P.op("dve", lambda e, d=d: e.tensor_scalar(out=nb, in0=ppsb[:, PP_UPB + d * 2 + hp:PP_UPB + d * 2 + hp + 1], scalar1=-1.0, scalar2=None, op0=ALU.mult),
                     reads=["ppsb"], writes=["nb"])
                psr = Rot([0, 1])
                for b0 in range(0, TT, 512):
                    n = min(512, TT - b0)
                    pi = psr.next()
                    P.op("pe", lambda e, pi=pi, b0=b0, n=n: e.matmul(PS[pi][:, 0:n], lhsT=upw[0:16, hp * 128:(hp + 1) * 128], rhs=lrsb[0:16, b0:b0 + n], start=True, stop=True),
                         reads=["upw", "lrsb"], writes=["ps%d" % pi])
                    P.op("act", lambda e, pi=pi, b0=b0, n=n: e.activation(out=Bc[:, b0:b0 + n], in_=PS[pi][:, 0:n], func=AF.Exp, scale=-1.0, bias=nb),
                         reads=["nb"], writes=["ps%d" % pi, "Bc"])
                P.op("act", lambda e: e.activation(out=Bp, in_=Bc, func=AF.Ln, bias=1.0, scale=1.0), reads=["Bc"], writes=["Bp"])
                if d == 0:
                    P.op("dve", lambda e, sm=sm: e.tensor_tensor_scan(out=Bc, data0=sm, data1=Bp, initial=0.0, op0=ALU.mult, op1=ALU.add),
                         reads=["Bp", "sm"], writes=["Bc"])
                else:
                    P.op("dve", lambda e, sm=sm: e.tensor_tensor_scan(out=Bc[:, ::-1], data0=sm[:, ::-1], data1=Bp[:, ::-1], initial=0.0, op0=ALU.mult, op1=ALU.add),
                         reads=["Bp", "sm"], writes=["Bc"])
                Bc3 = Bc.rearrange("p (c j) -> p c j", j=64)
                endj = 63 if d == 0 else 0
                P.op("act", lambda e, endj=endj, d=d: e.activation(out=gam[d], in_=Bc3[:, :, endj], func=AF.Exp, scale=-1.0 / 16.0), reads=["Bc"], writes=["gam" + ds_])
                P.dma("sp", qk, qT[hp * 128:(hp + 1) * 128, :], reads=["zT_q"], writes=["qk"])
                P.op("act", lambda e: e.activation(out=Bp, in_=Bc, func=AF.Exp, scale=-1.0 / 16.0), reads=["Bc"], writes=["Bp"])
                P.op("dve", lambda e, d=d: e.scalar_tensor_tensor(out=qd[d], in0=qk, scalar=0.125, in1=Bp, op0=ALU.mult, op1=ALU.mult), reads=["qk", "Bp"], writes=["qd" + ds_])
                P.dma("sp", qk, kT[hp * 128:(hp + 1) * 128, :], reads=["zT_k"], writes=["qk"])
                P.op("act", lambda e: e.activation(out=Bp, in_=Bc, func=AF.Exp, scale=1.0 / 16.0), reads=["Bc"], writes=["Bp"])
                P.op("dve", lambda e, d=d: e.tensor_tensor(out=ki[d], in0=qk, in1=Bp, op=ALU.mult), reads=["qk", "Bp"], writes=["ki" + ds_])
                psr = Rot([0, 1])
                for t in range(NT):
                    pi = psr.next()
                    pst = PS[pi][:].bitcast(BF16)
                    P.op("pe", lambda e, pst=pst, t=t, d=d: e.transpose(pst[:, 0:128], ki[d][:, t * 128:(t + 1) * 128], ident_b[:]), reads=["ki" + ds_, "ident_b"], writes=["ps%d" % pi])
                    P.op("act", lambda e, pst=pst, t=t, d=d: e.activation(out=kiT[d][:, t, :], in_=pst[:, 0:128], func=AF.Identity), reads=[], writes=["ps%d" % pi, "kiT" + ds_])
                P.op("dve", lambda e, d=d: e.memset(S[d], 0.0), writes=["S" + ds_])
                P.op("dve", lambda e, d=d: e.memset(Sb[d], 0.0), writes=["Sb" + ds_])

            def chunk_loop(d):
                ds_ = str(d)
                mask = maskF if d == 0 else maskB
                ctx_t = list(range(LC // 128)); lat_t = list(range(LC // 128, NT))
                order = ctx_t + lat_t if d == 0 else ctx_t[::-1] + lat_t[::-1]
                corder = (0, 1) if d == 0 else (1, 0)
                pS = d * 4 + 0; pO = (d * 4 + 1, d * 4 + 2); pD = d * 4 + 3
                for t in order:
                    for hh in range(2):
                        pr = slice(hh * 64, (hh + 1) * 64)
                        P.op("pe", lambda e, pr=pr, t=t: e.matmul(PS[pS][:, 0:128], lhsT=ki[d][pr, t * 128:(t + 1) * 128], rhs=qd[d][pr, t * 128:(t + 1) * 128], start=True, stop=True),
                             reads=["ki" + ds_, "qd" + ds_], writes=["ps%d" % pS])
                        P.op("dve", lambda e, hh=hh: e.tensor_tensor(out=sT[d][hh], in0=PS[pS][:, 0:128], in1=mask[:], op=ALU.mult),
                             reads=["mask"], writes=["ps%d" % pS, "sT%s_%d" % (ds_, hh)])
                        P.op("pe", lambda e, hh=hh, t=t: e.matmul(PS[pO[hh]][:, 0:128], lhsT=vt[:, t, hh * 128:(hh + 1) * 128], rhs=sT[d][hh], start=True, stop=True),
                             reads=["vt", "sT%s_%d" % (ds_, hh)], writes=["ps%d" % pO[hh]])
                    for ci, cc in enumerate(corder):
                        ch = t * 2 + cc
                        cols = slice(t * 128 + cc * 64, t * 128 + cc * 64 + 64)
                        for hh in range(2):
                            pr = slice(hh * 64, (hh + 1) * 64)
                            P.op("pe", lambda e, hh=hh, pr=pr, cols=cols, cc=cc: e.matmul(PS[pO[hh]][:, cc * 64:(cc + 1) * 64], lhsT=Sb[d][pr, :], rhs=qd[d][pr, cols],
                                                                                      start=False, stop=False, skip_group_check=True),
                                 reads=["Sb" + ds_, "qd" + ds_], writes=["ps%d" % pO[hh]])
                        jr = slice(cc * 64, (cc + 1) * 64)
                        for hh in range(2):
                            pr = slice(hh * 64, (hh + 1) * 64)
                            P.op("pe", lambda e, hh=hh, pr=pr, jr=jr, t=t: e.matmul(PS[pD][pr, 0:128], lhsT=kiT[d][jr, t, hh * 64:(hh + 1) * 64], rhs=vt[jr, t, hh * 128:(hh + 1) * 128],
                                                                                 start=True, stop=True),
                                 reads=["kiT" + ds_, "vt"], writes=["ps%d" % pD])
                        P.op("dve", lambda e: e.tensor_tensor(out=tmp[d], in0=PS[pD][:, 0:128], in1=S[d], op=ALU.add), reads=["S" + ds_], writes=["ps%d" % pD, "tmp" + ds_])
                        P.op("dve", lambda e, ch=ch: e.tensor_scalar(out=S[d], in0=tmp[d], scalar1=gam[d][:, ch:ch + 1], scalar2=None, op0=ALU.mult), reads=["tmp" + ds_, "gam" + ds_], writes=["S" + ds_])
                        P.op("act", lambda e, ch=ch: e.activation(out=Sb[d], in_=tmp[d], func=AF.Identity, scale=gam[d][:, ch:ch + 1]), reads=["tmp" + ds_, "gam" + ds_], writes=["Sb" + ds_])
                    for hh in range(2):
                        ko = "oacc%d_%d" % (hh, t)
                        P.op("dve", lambda e, hh=hh, t=t: e.tensor_tensor(out=oacc[:, hh, t * 128:(t + 1) * 128], in0=PS[pO[hh]][:, 0:128], in1=oacc[:, hh, t * 128:(t + 1) * 128], op=ALU.add),
                             reads=[ko], writes=["ps%d" % pO[hh], ko])

            P.interleave([P.capture(lambda: chunk_loop(0)), P.capture(lambda: chunk_loop(1))])
            for hh in range(2):
                P.dma("sp", oT[(hp * 2 + hh) * 128:(hp * 2 + hh + 1) * 128, :], oacc[:, hh, :],
                      reads=["oacc%d_%d" % (hh, t) for t in range(NT)], writes=["oT"])
            P.barrier()

    def lru(l):
        A.reset()
        cst = A.f32(4); cst2 = A.f32(4)
        P.op("act", lambda e: e.activation(out=cst, in_=ppsb[:, PP_LAM:PP_LAM + 4], func=AF.Exp, scale=-1.0), reads=["ppsb"], writes=["cst"])
        P.op("act", lambda e: e.activation(out=cst2, in_=cst, func=AF.Ln, bias=1.0, scale=1.0), reads=["cst"], writes=["cst2"])
        P.op("dve", lambda e: e.tensor_scalar(out=cst, in0=cst2, scalar1=-8.0, scalar2=None, op0=ALU.mult), reads=["cst2"], writes=["cst"])
        x = A.f32(TT); hs = A.f32(TT); th = A.f32(TT)
        xc = [A.f32(TT), A.f32(TT)]; r = [A.f32(TT), A.f32(TT)]; ig = [A.f32(TT), A.f32(TT)]; a = [A.f32(TT), A.f32(TT)]
        Wa = [A.f32(128), A.f32(128)]; Wx = [A.f32(128), A.f32(128)]
        segs = [(0, LC), (LC, TT)]
        for ct in range(2):
            P.dma("sp", x, lxT[ct * 128:(ct + 1) * 128, :], reads=["zT_lx"], writes=["x"])

            def body(d):
                ds_ = str(d)
                col = d * 2 + ct
                kxc, kr, kig, ka, kWa, kWx = "xc" + ds_, "r" + ds_, "ig" + ds_, "a" + ds_, "Wa" + ds_, "Wx" + ds_
                P.op("pool", lambda e: e.memset(Wa[d], 0.0), writes=[kWa])
                P.op("pool", lambda e: e.memset(Wx[d], 0.0), writes=[kWx])
                for bb in range(2):
                    P.dma("sp", Wa[d][bb * 64:(bb + 1) * 64, bb * 64:(bb + 1) * 64], lru_wa[l, d, ct * 2 + bb], writes=[kWa], reads=[kWa])
                    P.dma("sp", Wx[d][bb * 64:(bb + 1) * 64, bb * 64:(bb + 1) * 64], lru_wx[l, d, ct * 2 + bb], writes=[kWx], reads=[kWx])
                wcol = lambda k: ppsb[:, PP_CONVW + (d * 4 + k) * 2 + ct:PP_CONVW + (d * 4 + k) * 2 + ct + 1]
                bcol = ppsb[:, PP_CONVB + col:PP_CONVB + col + 1]
                P.op("dve", lambda e: e.tensor_scalar(out=xc[d], in0=x, scalar1=wcol(3), scalar2=bcol, op0=ALU.mult, op1=ALU.add), reads=["x", "ppsb"], writes=[kxc])
                for (s0, s1) in segs:
                    for sh in (1, 2, 3):
                        k = 3 - sh
                        if d == 0:
                            o_ap, i_ap = xc[d][:, s0 + sh:s1], x[:, s0:s1 - sh]
                        else:
                            o_ap, i_ap = xc[d][:, s0:s1 - sh], x[:, s0 + sh:s1]
                        P.op("dve", lambda e, o_ap=o_ap, i_ap=i_ap, k=k: e.scalar_tensor_tensor(out=o_ap, in0=i_ap, scalar=wcol(k), in1=o_ap, op0=ALU.mult, op1=ALU.add),
                             reads=["x", "ppsb", kxc], writes=[kxc])
                psr = Rot([d * 4 + 0, d * 4 + 1, d * 4 + 2, d * 4 + 3])
                for (Wm, dst, bc0, nm, kW) in ((Wa[d], r[d], PP_BA, kr, kWa), (Wx[d], ig[d], PP_BX, kig, kWx)):
                    for b0 in range(0, TT, 512):
                        n = min(512, TT - b0)
                        pi = psr.next()
                        P.op("pe", lambda e, pi=pi, b0=b0, n=n, Wm=Wm: e.matmul(PS[pi][:, 0:n], lhsT=Wm, rhs=xc[d][:, b0:b0 + n], start=True, stop=True),
                             reads=[kW, kxc], writes=["ps%d" % pi])
                        P.op("act", lambda e, pi=pi, b0=b0, n=n, dst=dst, bc0=bc0: e.activation(out=dst[:, b0:b0 + n], in_=PS[pi][:, 0:n], func=AF.Sigmoid,
                                                                                              bias=ppsb[:, bc0 + col:bc0 + col + 1]),
                             reads=["ppsb"], writes=["ps%d" % pi, nm])
                P.op("act", lambda e: e.activation(out=a[d], in_=r[d], func=AF.Exp, scale=cst[:, col:col + 1]), reads=[kr, "cst"], writes=[ka])
                P.op("pool", lambda e: e.tensor_tensor(out=r[d], in0=a[d], in1=a[d], op=ALU.mult), reads=[ka], writes=[kr])
                P.op("act", lambda e: e.activation(out=r[d], in_=r[d], func=AF.Sqrt, scale=-1.0, bias=1.0), reads=[kr], writes=[kr])
                P.op("dve", lambda e: e.tensor_tensor(out=ig[d], in0=ig[d], in1=r[d], op=ALU.mult), reads=[kig, kr], writes=[kig])
                P.op("dve", lambda e: e.tensor_tensor(out=xc[d], in0=xc[d], in1=ig[d], op=ALU.mult), reads=[kig, kxc], writes=[kxc])
                if d == 0:
                    P.op("dve", lambda e: e.tensor_tensor_scan(out=hs, data0=a[d], data1=xc[d], initial=0.0, op0=ALU.mult, op1=ALU.add), reads=[ka, kxc], writes=["hs"])
                else:
                    P.op("dve", lambda e: e.tensor_tensor_scan(out=th[:, 0:LC][:, ::-1], data0=a[d][:, 0:LC][:, ::-1], data1=xc[d][:, 0:LC][:, ::-1], initial=0.0,
                                                               op0=ALU.mult, op1=ALU.add), reads=[ka, kxc], writes=["th"])
                    P.op("dve", lambda e: e.tensor_tensor_scan(out=th[:, LC:TT][:, ::-1], data0=a[d][:, LC:TT][:, ::-1], data1=xc[d][:, LC:TT][:, ::-1], initial=th[:, 0:1],
                                                               op0=ALU.mult, op1=ALU.add), reads=[ka, kxc, "th"], writes=["th"])

            P.interleave([P.capture(lambda: body(0)), P.capture(lambda: body(1))])
            P.op("dve", lambda e: e.tensor_tensor(out=hs, in0=hs, in1=th, op=ALU.add), reads=["hs", "th"], writes=["hs"])
            P.dma("sp", lruT[ct * 128:(ct + 1) * 128, :], hs, reads=["hs"], writes=["lruT"])

    def s5(l):
        A.reset()
        NG = 16
        prm = A.f32(3 * NG).rearrange("p (a g) -> p a g", a=3)
        Bsb = A.f32(2 * NG * 16).rearrange("p (a g h) -> p a g h", a=2, g=NG)
        Csb = A.f32(2 * NG * 16).rearrange("p (a g h) -> p a g h", a=2, g=NG)
        P.dma("sp", prm, s5p[l], writes=["prm"])
        P.dma("sp", Bsb, s5b[l], writes=["Bsb"])
        P.dma("sp", Csb, s5c[l], writes=["Csb"])
        tauf = A.f32(2 * NC8)
        tau = tauf.rearrange("p (d c) -> p d c", d=2)
        P.dma("sp", tauf, tau_in.partition_broadcast(128), writes=["tau"])
        dt = A.f32(NG); lrdt = A.f32(NG); th = A.f32(NG); u8 = A.f32(NG); rho8 = A.f32(NG)
        t1 = A.f32(NG); t2 = A.f32(NG); t3 = A.f32(NG); den = A.f32(NG); cr = A.f32(NG); ci = A.f32(NG)
        J = jidx[:].rearrange("p d g j -> p (d g) j")
        mg = A.f32(NG * 9).rearrange("p (g j) -> p g j", j=9)
        xa = A.f32(NG * 9).rearrange("p (g j) -> p g j", j=9)
        xr = A.f32(NG * 9).rearrange("p (g j) -> p g j", j=9)
        sn = A.f32(NG * 9).rearrange("p (g j) -> p g j", j=9)
        cs_ = A.f32(NG * 9).rearrange("p (g j) -> p g j", j=9)
        ar = A.f32(NG * 9).rearrange("p (g j) -> p g j", j=9)
        ai = A.f32(NG * 9).rearrange("p (g j) -> p g j", j=9)
        mr = A.f32(NG * 8).rearrange("p (g j) -> p g j", j=8)
        mi = A.f32(NG * 8).rearrange("p (g j) -> p g j", j=8)
        br_ = A.f32(NG * 8).rearrange("p (g j) -> p g j", j=8)
        bi_ = A.f32(NG * 8).rearrange("p (g j) -> p g j", j=8)
        w1 = A.f32(NG * 9).rearrange("p (g j) -> p g j", j=9)
        K = ["prm", "s5t"]

        def op(eng, fn):
            P.op(eng, fn, reads=K, writes=["s5t"])

        def bg(v, n):
            return v.unsqueeze(2).to_broadcast([128, NG, n])

        P._cap = []
        op("act", lambda e: e.activation(out=dt, in_=prm[:, 2, :], func=AF.Exp))
        op("dve", lambda e: e.tensor_tensor(out=lrdt, in0=prm[:, 0, :], in1=dt, op=ALU.mult))
        op("dve", lambda e: e.tensor_tensor(out=th, in0=prm[:, 1, :], in1=dt, op=ALU.mult))
        op("dve", lambda e: e.tensor_scalar(out=th, in0=th, scalar1=1.0 / TWO_PI, scalar2=None, op0=ALU.mult))
        op("dve", lambda e: e.tensor_tensor(out=mg, in0=J, in1=bg(lrdt, 9), op=ALU.mult))
        op("act", lambda e: e.activation(out=mg, in_=mg, func=AF.Exp))
        op("dve", lambda e: e.tensor_tensor(out=xa, in0=J, in1=bg(th, 9), op=ALU.mult))
        op("dve", lambda e: e.tensor_scalar(out=xr, in0=xa, scalar1=MAGIC, scalar2=-MAGIC, op0=ALU.add, op1=ALU.add))
        op("dve", lambda e: e.tensor_tensor(out=w1, in0=xa, in1=xr, op=ALU.subtract))
        op("act", lambda e: e.activation(out=sn, in_=w1, func=AF.Sin, scale=TWO_PI))
        op("dve", lambda e: e.tensor_scalar(out=xa, in0=xa, scalar1=0.25, scalar2=None, op0=ALU.add))
        op("dve", lambda e: e.tensor_scalar(out=xr, in0=xa, scalar1=MAGIC, scalar2=-MAGIC, op0=ALU.add, op1=ALU.add))
        op("dve", lambda e: e.tensor_tensor(out=w1, in0=xa, in1=xr, op=ALU.subtract))
        op("act", lambda e: e.activation(out=cs_, in_=w1, func=AF.Sin, scale=TWO_PI))
        op("dve", lambda e: e.tensor_tensor(out=ar, in0=mg, in1=cs_, op=ALU.mult))
        op("dve", lambda e: e.tensor_tensor(out=ai, in0=mg, in1=sn, op=ALU.mult))
        op("dve", lambda e: e.tensor_tensor(out=w1, in0=mg, in1=mg, op=ALU.mult))
        op("dve", lambda e: e.reciprocal(out=w1, in_=w1))
        op("dve", lambda e: e.tensor_tensor(out=mr, in0=ar[:, :, 0:8], in1=w1[:, :, 0:8], op=ALU.mult))
        op("dve", lambda e: e.scalar_tensor_tensor(out=mi, in0=ai[:, :, 0:8], scalar=-1.0, in1=w1[:, :, 0:8], op0=ALU.mult, op1=ALU.mult))
        a1r = A.f32(NG); a1i = A.f32(NG)
        op("dve", lambda e: e.tensor_copy(out=a1r[:, 0:8], in_=ar[:, 0:8, 1]))
        op("dve", lambda e: e.tensor_copy(out=a1r[:, 8:16], in_=ar[:, 8:16, 6]))
        op("dve", lambda e: e.tensor_copy(out=a1i[:, 0:8], in_=ai[:, 0:8, 1]))
        op("dve", lambda e: e.tensor_copy(out=a1i[:, 8:16], in_=ai[:, 8:16, 6]))
        lr_ = prm[:, 0, :]; li_ = prm[:, 1, :]
        op("dve", lambda e: e.tensor_tensor(out=den, in0=lr_, in1=lr_, op=ALU.mult))
        op("dve", lambda e: e.tensor_tensor(out=t1, in0=li_, in1=li_, op=ALU.mult))
        op("dve", lambda e: e.tensor_tensor(out=den, in0=den, in1=t1, op=ALU.add))
        op("dve", lambda e: e.reciprocal(out=den, in_=den))
        op("dve", lambda e: e.tensor_scalar(out=t1, in0=a1r, scalar1=-1.0, scalar2=None, op0=ALU.add))
        op("dve", lambda e: e.tensor_tensor(out=t2, in0=t1, in1=lr_, op=ALU.mult))
        op("dve", lambda e: e.tensor_tensor(out=t3, in0=a1i, in1=li_, op=ALU.mult))
        op("dve", lambda e: e.tensor_tensor(out=t2, in0=t2, in1=t3, op=ALU.add))
        op("dve", lambda e: e.tensor_tensor(out=cr, in0=t2, in1=den, op=ALU.mult))
        op("dve", lambda e: e.tensor_tensor(out=t2, in0=a1i, in1=lr_, op=ALU.mult))
        op("dve", lambda e: e.tensor_tensor(out=t3, in0=t1, in1=li_, op=ALU.mult))
        op("dve", lambda e: e.tensor_tensor(out=t2, in0=t2, in1=t3, op=ALU.subtract))
        op("dve", lambda e: e.tensor_tensor(out=ci, in0=t2, in1=den, op=ALU.mult))
        w8a = A.f32(NG * 8).rearrange("p (g j) -> p g j", j=8)
        op("dve", lambda e: e.tensor_tensor(out=br_, in0=mr, in1=bg(cr, 8), op=ALU.mult))
        op("dve", lambda e: e.tensor_tensor(out=w8a, in0=mi, in1=bg(ci, 8), op=ALU.mult))
        op("dve", lambda e: e.tensor_tensor(out=br_, in0=br_, in1=w8a, op=ALU.subtract))
        op("dve", lambda e: e.tensor_tensor(out=bi_, in0=mr, in1=bg(ci, 8), op=ALU.mult))
        op("dve", lambda e: e.tensor_tensor(out=w8a, in0=mi, in1=bg(cr, 8), op=ALU.mult))
        op("dve", lambda e: e.tensor_tensor(out=bi_, in0=bi_, in1=w8a, op=ALU.add))
        op("dve", lambda e: e.tensor_copy(out=rho8, in_=mg[:, :, 8]))
        op("dve", lambda e: e.tensor_scalar(out=u8, in0=th, scalar1=8.0, scalar2=None, op0=ALU.mult))
        op("dve", lambda e: e.tensor_scalar(out=t1, in0=u8, scalar1=MAGIC, scalar2=-MAGIC, op0=ALU.add, op1=ALU.add))
        op("dve", lambda e: e.tensor_tensor(out=u8, in0=u8, in1=t1, op=ALU.subtract))
        SZ = NG * 8 * 16
        Btr = A.f32(SZ).rearrange("p (g j h) -> p g j h", g=NG, j=8)
        Bti = A.f32(SZ).rearrange("p (g j h) -> p g j h", g=NG, j=8)
        Ctr = A.f32(SZ).rearrange("p (g j h) -> p g j h", g=NG, j=8)
        Cti = A.f32(SZ).rearrange("p (g j h) -> p g j h", g=NG, j=8)
        regB = A.f32(4 * 2048)
        wk = regB[:, 0:2048].rearrange("p (g j h) -> p g j h", g=NG, j=8)

        def bj(v):
            return v.unsqueeze(3).to_broadcast([128, NG, 8, 16])

        def bh(v):
            return v.unsqueeze(2).to_broadcast([128, NG, 8, 16])

        Br, Bi = Bsb[:, 0], Bsb[:, 1]
        Cr, Ci = Csb[:, 0], Csb[:, 1]
        KB = ["s5t", "Bsb", "Csb", "s5m"]

        def opb(fn):
            P.op("dve", fn, reads=KB, writes=["s5m"])

        opb(lambda e: e.tensor_tensor(out=Btr, in0=bj(br_), in1=bh(Br), op=ALU.mult))
        opb(lambda e: e.tensor_tensor(out=wk, in0=bj(bi_), in1=bh(Bi), op=ALU.mult))
        opb(lambda e: e.tensor_tensor(out=Btr, in0=Btr, in1=wk, op=ALU.subtract))
        opb(lambda e: e.tensor_tensor(out=Bti, in0=bj(br_), in1=bh(Bi), op=ALU.mult))
        opb(lambda e: e.tensor_tensor(out=wk, in0=bj(bi_), in1=bh(Br), op=ALU.mult))
        opb(lambda e: e.tensor_tensor(out=Bti, in0=Bti, in1=wk, op=ALU.add))
        opb(lambda e: e.tensor_tensor(out=Ctr, in0=bj(ar[:, :, 0:8]), in1=bh(Cr), op=ALU.mult))
        opb(lambda e: e.tensor_tensor(out=wk, in0=bj(ai[:, :, 0:8]), in1=bh(Ci), op=ALU.mult))
        opb(lambda e: e.tensor_tensor(out=Ctr, in0=Ctr, in1=wk, op=ALU.subtract))
        opb(lambda e: e.tensor_tensor(out=Cti, in0=bj(ai[:, :, 0:8]), in1=bh(Cr), op=ALU.mult))
        opb(lambda e: e.tensor_tensor(out=wk, in0=bj(ar[:, :, 0:8]), in1=bh(Ci), op=ALU.mult))
        opb(lambda e: e.tensor_tensor(out=Cti, in0=Cti, in1=wk, op=ALU.add))
        opb(lambda e: e.tensor_scalar(out=Cti, in0=Cti, scalar1=-1.0, scalar2=None, op0=ALU.mult))
        NCT = (NC8 + 127) // 128
        U8 = A.f32(16 * NC8).rearrange("p (g c) -> p g c", g=16)
        cst_ = [regB[:, 2048:4096], regB[:, 4096:6144]]
        Ug = regB[:, 6144:8192]
        Yst = cst_

        def chunk_tiles():
            tiles = []
            c = 0
            while c < NC8:
                n = min(128, NC8 - c)
                tiles.append((c, n))
                c += n
            return tiles

        def chunk_dram(base, c0, n):
            pieces = []
            c = c0
            while c < c0 + n:
                if c < LC8:
                    m = min(c0 + n, LC8) - c
                    ap = base[c * 8:(c + m) * 8, :].rearrange("(c i) h -> c i h", i=8)
                    pieces.append((c - c0, m, ap))
                    c += m
                else:
                    cl = c - LC8
                    col, rb = cl // RB, cl % RB
                    m = min(RB - rb, c0 + n - c)
                    lat = base[LC:TT, :].rearrange("(rb i w) h -> w rb i h", i=8, w=64)
                    ap = lat[col, rb:rb + m, :, :]
                    pieces.append((c - c0, m, ap))
                    c += m
            return pieces

        cap_prep = P._cap
        P._cap = []
        cap_u8 = P._cap
        psr = Rot([0, 1, 2, 3])
        for ti, (c0, n) in enumerate(chunk_tiles()):
            cs = cst_[ti % 2]
            cs3 = cs.rearrange("p (i h) -> p i h", i=8)
            for (p0, m, ap) in chunk_dram(su_tok, c0, n):
                P.dma("sp", cs3[p0:p0 + m, :, :], ap, reads=["su_tok"], writes=["cst%d" % (ti % 2)])
            P.op("dve", lambda e, n=n, cs=cs: e.tensor_copy(out=Ug[0:n].rearrange("p (g i h) -> p g i h", g=16, i=8),
                                                          in_=cs[0:n].rearrange("p (i g h) -> p g i h", i=8, g=16)),
                 reads=["cst%d" % (ti % 2)], writes=["Ug"])
            for g in range(16):
                pi = psr.next()
                P.op("pe", lambda e, pi=pi, g=g, n=n: e.transpose(PS[pi][:, 0:n], Ug[0:n, g * 128:(g + 1) * 128], ident_f[0:n, 0:n]),
                     reads=["Ug", "ident_f"], writes=["ps%d" % pi])
                P.op("act", lambda e, pi=pi, g=g, n=n, c0=c0: e.activation(out=U8[:, g, c0:c0 + n], in_=PS[pi][:, 0:n], func=AF.Identity),
                     reads=[], writes=["ps%d" % pi, "U8"])
        P._cap = None
        P.interleave([cap_prep, cap_u8])
        P.barrier()
        halves = [(0, min(512, NC8))] + ([(512, NC8)] if NC8 > 512 else [])

        def carve(base):
            o = [0]

            def take(n):
                ap = base[:, o[0]:o[0] + n]
                o[0] += n
                return ap
            d_ = {}
            d_["BtT"] = take(256).rearrange("p (a s) -> p a s", a=2)
            d_["M8"] = take(256).rearrange("p (g c) -> p g c", g=2)
            for nm in ("Zr", "Zi", "Wr", "Wi", "Or", "Oi", "Xr", "Xi", "tA", "tB", "Cn", "Sn"):
                d_[nm] = take(NC8)
            return d_

        SETW = 512 + 12 * NC8
        sets = [carve(A.f32(SETW)), carve(regB)]
        Yall = A.f32(16 * NC8).rearrange("p (g c) -> p g c", g=16)
        psr = Rot([0, 1, 2, 3, 4, 5, 6, 7])

        def front(it):
            d, gp = divmod(it, 8)
            dg = it
            par = str(it % 2)
            S_ = sets[it % 2]
            BtT, M8, Zr, Zi, Cn, Sn = S_["BtT"], S_["M8"], S_["Zr"], S_["Zi"], S_["Cn"], S_["Sn"]
            xx, rr = S_["Wr"], S_["Wi"]
            m8 = m8F if d == 0 else m8B
            for a_, Bt in enumerate((Btr, Bti)):
                pi = psr.next()
                P.op("pe", lambda e, pi=pi, Bt=Bt: e.transpose(PS[pi][:, 0:128], Bt[:, dg].rearrange("p j h -> p (j h)"), ident_f[:]),
                     reads=["s5m", "ident_f"], writes=["ps%d" % pi])
                P.op("act", lambda e, pi=pi, a_=a_: e.activation(out=BtT[:, a_, :], in_=PS[pi][:, 0:128], func=AF.Identity), reads=[], writes=["ps%d" % pi, "BtT" + par])
            for gm in range(2):
                pi = psr.next()
                pr = slice(gm * 64, (gm + 1) * 64)
                for a_, (Bt, Ct) in enumerate(((Btr, Ctr), (Bti, Cti))):
                    P.op("pe", lambda e, pi=pi, gm=gm, pr=pr, Bt=Bt, Ct=Ct, a_=a_: e.matmul(PS[pi][:, 0:128], lhsT=Bt[pr, dg].rearrange("p j h -> p (j h)"),
                                                                                         rhs=Ct[pr, dg].rearrange("p j h -> p (j h)"), start=(a_ == 0), stop=(a_ == 1)),
                         reads=["s5m"], writes=["ps%d" % pi])
                P.op("dve", lambda e, pi=pi, m8=m8, gm=gm: e.tensor_tensor(out=M8[:, gm, :], in0=PS[pi][:, 0:128], in1=m8[:, 0:128], op=ALU.mult),
                     reads=["m8"], writes=["ps%d" % pi, "M8" + par])
            for (Zt, a_) in ((Zr, 0), (Zi, 1)):
                for (h0, h1) in halves:
                    pi = psr.next()
                    for gm in range(2):
                        g = gp * 2 + gm
                        P.op("pe", lambda e, pi=pi, gm=gm, g=g, a_=a_, h0=h0, h1=h1: e.matmul(PS[pi][gm * 64:(gm + 1) * 64, 0:h1 - h0], lhsT=BtT[:, a_, gm * 64:(gm + 1) * 64],
                                                                                           rhs=U8[:, g, h0:h1], start=True, stop=True),
                             reads=["BtT" + par, "U8"], writes=["ps%d" % pi])
                    P.op("act", lambda e, pi=pi, Zt=Zt, h0=h0, h1=h1: e.activation(out=Zt[:, h0:h1], in_=PS[pi][:, 0:h1 - h0], func=AF.Identity),
                         reads=[], writes=["ps%d" % pi, "Z" + par])
            ucol = u8[:, dg:dg + 1]
            kx, kr = "Wr" + par, "Wi" + par
            P.op("dve", lambda e, ucol=ucol: e.tensor_scalar(out=xx, in0=tau[:, d, :], scalar1=ucol, scalar2=None, op0=ALU.mult), reads=["tau", "s5t"], writes=[kx])
            P.op("dve", lambda e: e.tensor_scalar(out=rr, in0=xx, scalar1=MAGIC, scalar2=-MAGIC, op0=ALU.add, op1=ALU.add), reads=[kx], writes=[kr])
            P.op("dve", lambda e: e.tensor_tensor(out=rr, in0=xx, in1=rr, op=ALU.subtract), reads=[kx, kr], writes=[kr])
            P.op("act", lambda e: e.activation(out=Sn, in_=rr, func=AF.Sin, scale=TWO_PI), reads=[kr], writes=["Sn" + par])
            P.op("dve", lambda e: e.tensor_scalar(out=xx, in0=xx, scalar1=0.25, scalar2=None, op0=ALU.add), reads=[kx], writes=[kx])
            P.op("dve", lambda e: e.tensor_scalar(out=rr, in0=xx, scalar1=MAGIC, scalar2=-MAGIC, op0=ALU.add, op1=ALU.add), reads=[kx, "Sn" + par], writes=[kr])
            P.op("dve", lambda e: e.tensor_tensor(out=rr, in0=xx, in1=rr, op=ALU.subtract), reads=[kx, kr], writes=[kr])
            P.op("act", lambda e: e.activation(out=Cn, in_=rr, func=AF.Sin, scale=TWO_PI), reads=[kr], writes=["Cn" + par])

        def back(it):
            d, gp = divmod(it, 8)
            dg = it
            par = str(it % 2)
            S_ = sets[it % 2]
            M8, Zr, Zi, Wr, Wi, Or, Oi = S_["M8"], S_["Zr"], S_["Zi"], S_["Wr"], S_["Wi"], S_["Or"], S_["Oi"]
            Xr, Xi, tA, tB, Cn, Sn = S_["Xr"], S_["Xi"], S_["tA"], S_["tB"], S_["Cn"], S_["Sn"]
            kC, kS, kZ, kWr, kWi, kA, kB, kX = "Cn" + par, "Sn" + par, "Z" + par, "Wr" + par, "Wi" + par, "tA" + par, "tB" + par, "X" + par
            P.op("dve", lambda e: e.tensor_tensor(out=Wr, in0=Cn, in1=Zr, op=ALU.mult), reads=[kC, kZ], writes=[kWr])
            P.op("dve", lambda e: e.tensor_tensor(out=Wi, in0=Cn, in1=Zi, op=ALU.mult), reads=[kC, kZ], writes=[kWi])
            P.op("dve", lambda e: e.tensor_tensor(out=tA, in0=Sn, in1=Zi, op=ALU.mult), reads=[kS, kZ], writes=[kA])
            P.op("dve", lambda e: e.tensor_tensor(out=tB, in0=Sn, in1=Zr, op=ALU.mult), reads=[kS, kZ], writes=[kB])
            P.op("dve", lambda e: e.tensor_tensor(out=Wr, in0=Wr, in1=tA, op=ALU.add), reads=[kWr, kA], writes=[kWr])
            P.op("dve", lambda e: e.tensor_tensor(out=Wi, in0=Wi, in1=tB, op=ALU.subtract), reads=[kWi, kB], writes=[kWi])
            rcol = rho8[:, dg:dg + 1]
            for (Wt, Ot, nm) in ((Wr, Or, "Or" + par), (Wi, Oi, "Oi" + par)):
                if d == 0:
                    P.op("dve", lambda e, Wt=Wt, Ot=Ot, rcol=rcol: e.tensor_tensor_scan(out=Ot, data0=Wt, data1=rcol.to_broadcast([128, NC8]), initial=0.0, op0=ALU.add, op1=ALU.mult),
                         reads=[kWr, kWi, "s5t"], writes=[nm])
                else:
                    P.op("dve", lambda e, Wt=Wt, Ot=Ot, rcol=rcol: e.tensor_tensor_scan(out=Ot[:, 0:LC8][:, ::-1], data0=Wt[:, 0:LC8][:, ::-1], data1=rcol.to_broadcast([128, LC8]),
                                                                                     initial=0.0, op0=ALU.add, op1=ALU.mult),
                         reads=[kWr, kWi, "s5t"], writes=[nm])
                    P.op("dve", lambda e, Wt=Wt, Ot=Ot, rcol=rcol: e.tensor_tensor_scan(out=Ot[:, LC8:NC8][:, ::-1], data0=Wt[:, LC8:NC8][:, ::-1], data1=rcol.to_broadcast([128, NC8 - LC8]),
                                                                                     initial=Ot[:, 0:1], op0=ALU.add, op1=ALU.mult),
                         reads=[kWr, kWi, "s5t", nm], writes=[nm])
            if d == 0:
                sh = [(slice(1, NC8), slice(0, NC8 - 1))]
                zero_cols = [0]
                carry = None
            else:
                sh = [(slice(0, LC8 - 1), slice(1, LC8)), (slice(LC8, NC8 - 1), slice(LC8 + 1, NC8))]
                zero_cols = [LC8 - 1]
                carry = (NC8 - 1, 0)
            RK = ["Or" + par, "Oi" + par, kC, kS, kX, kA, kB]
            pairs = list(sh)
            if carry is not None:
                dc, sc = carry
                pairs.append((slice(dc, dc + 1), slice(sc, sc + 1)))
            kXr, kXi, kOr, kOi = "Xr" + par, "Xi" + par, "Or" + par, "Oi" + par
            for (do, so) in pairs:
                P.op("dve", lambda e, do=do, so=so: e.tensor_tensor(out=Xr[:, do], in0=Cn[:, do], in1=Or[:, so], op=ALU.mult), reads=[kC, kOr, kX], writes=[kXr])
                P.op("dve", lambda e, do=do, so=so: e.tensor_tensor(out=Xi[:, do], in0=Cn[:, do], in1=Oi[:, so], op=ALU.mult), reads=[kC, kOi, kX], writes=[kXi])
                P.op("dve", lambda e, do=do, so=so: e.tensor_tensor(out=tA[:, do], in0=Sn[:, do], in1=Oi[:, so], op=ALU.mult), reads=[kS, kOi], writes=[kA])
                P.op("dve", lambda e, do=do, so=so: e.tensor_tensor(out=tB[:, do], in0=Sn[:, do], in1=Or[:, so], op=ALU.mult), reads=[kS, kOr], writes=[kB])
                P.op("dve", lambda e, do=do, so=so: e.tensor_tensor(out=Xr[:, do], in0=Xr[:, do], in1=tA[:, do], op=ALU.subtract), reads=[kXr, kA], writes=[kXr])
                P.op("dve", lambda e, do=do, so=so: e.tensor_tensor(out=Xi[:, do], in0=Xi[:, do], in1=tB[:, do], op=ALU.add), reads=[kXi, kB], writes=[kXi])
            for zc in zero_cols:
                P.op("dve", lambda e, zc=zc: e.memset(Xr[:, zc:zc + 1], 0.0), reads=[kX], writes=[kXr])
                P.op("dve", lambda e, zc=zc: e.memset(Xi[:, zc:zc + 1], 0.0), reads=[kX], writes=[kXi])
            for gm in range(2):
                g = gp * 2 + gm
                pr = slice(gm * 64, (gm + 1) * 64)
                for (h0, h1) in halves:
                    pi = psr.next()
                    P.op("pe", lambda e, pi=pi, gm=gm, g=g, h0=h0, h1=h1: e.matmul(PS[pi][:, 0:h1 - h0], lhsT=M8[:, gm, :], rhs=U8[:, g, h0:h1], start=True, stop=False),
                         reads=["M8" + par, "U8"], writes=["ps%d" % pi])
                    P.op("pe", lambda e, pi=pi, pr=pr, h0=h0, h1=h1: e.matmul(PS[pi][:, 0:h1 - h0], lhsT=Ctr[pr, dg].rearrange("p j h -> p (j h)"), rhs=Xr[pr, h0:h1], start=False, stop=False),
                         reads=["s5m", kXr], writes=["ps%d" % pi, kX])
                    P.op("pe", lambda e, pi=pi, pr=pr, h0=h0, h1=h1: e.matmul(PS[pi][:, 0:h1 - h0], lhsT=Cti[pr, dg].rearrange("p j h -> p (j h)"), rhs=Xi[pr, h0:h1], start=False, stop=True),
                         reads=["s5m", kXi], writes=["ps%d" % pi, kX])
                    if d == 0:
                        P.op("act", lambda e, pi=pi, g=g, h0=h0, h1=h1: e.activation(out=Yall[:, g, h0:h1], in_=PS[pi][:, 0:h1 - h0], func=AF.Identity),
                             reads=[], writes=["ps%d" % pi, "Yall"])
                    else:
                        P.op("dve", lambda e, pi=pi, g=g, h0=h0, h1=h1: e.tensor_tensor(out=Yall[:, g, h0:h1], in0=PS[pi][:, 0:h1 - h0], in1=Yall[:, g, h0:h1], op=ALU.add),
                             reads=["Yall"], writes=["ps%d" % pi, "Yall"])

        front(0)
        for it in range(16):
            if it + 1 < 16:
                P.interleave([P.capture(lambda: back(it)), P.capture(lambda: front(it + 1))])
            else:
                back(it)
        P.barrier()
        psr = Rot([0, 1, 2, 3])
        for ti, (c0, n) in enumerate(chunk_tiles()):
            ys = Yst[ti % 2]
            ys3 = ys.rearrange("p (i h) -> p i h", i=8)
            for g in range(16):
                pi = psr.next()
                P.op("pe", lambda e, pi=pi, g=g, n=n, c0=c0: e.transpose(PS[pi][0:n, 0:128], Yall[:, g, c0:c0 + n], ident_f[:]),
                     reads=["Yall", "ident_f"], writes=["ps%d" % pi])
                P.op("act", lambda e, pi=pi, g=g, n=n, ys3=ys3: e.activation(out=ys3[0:n, :, g * 16:(g + 1) * 16], in_=PS[pi][0:n, 0:128].rearrange("p (j h) -> p j h", j=8), func=AF.Identity),
                     reads=[], writes=["ps%d" % pi, "yst%d" % (ti % 2)])
            for (p0, m, ap) in chunk_dram(s5y, c0, n):
                P.dma("sp", ap, ys3[p0:p0 + m, :, :], reads=["yst%d" % (ti % 2)], writes=["s5y"])

    def phaseC1(l, last):
        A.reset()
        xsrc = xin if l == 0 else xres
        w1p = A.bf16(KT * 4 * D).rearrange("p (k n) -> p k n", k=KT)
        w2p = A.bf16(32 * D).rearrange("p (k n) -> p k n", k=32)
        w1v = w_ff1[l].rearrange("(kt p) n -> p kt n", p=128)
        w2v = w_ff2[l].rearrange("(kt p) n -> p kt n", p=128)
        pre = []
        for c0 in range(0, 4 * D, 512):
            pre.append((w1p[:, :, c0:c0 + 512], w1v[:, :, c0:c0 + 512], "w1"))
        for k0 in range(0, 32, 4):
            for c0 in range(0, D, 512):
                pre.append((w2p[:, k0:k0 + 4, c0:c0 + 512], w2v[:, k0:k0 + 4, c0:c0 + 512], "w2"))
        wo = A.bf16(KT * D).rearrange("p (k n) -> p k n", k=KT)
        wv = w_out[l].rearrange("(kt p) n -> p kt n", p=128)
        for c0 in range(0, D, 512):
            P.dma("poolq", wo[:, :, c0:c0 + 512], wv[:, :, c0:c0 + 512], writes=["wo"])
        glu = A.bf16(2 * 256).rearrange("p (k n) -> p k n", k=2)
        P.dma("poolq", glu, s5_glu_w[l].rearrange("(kt p) n -> p kt n", p=128), writes=["glu"])
        g1t = A.f32(D)
        g1bc = {0: g1t, 1: g1t}
        cur_w = [None]

        def load_g1(w):
            if cur_w[0] == w:
                return
            cur_w[0] = w
            P.dma("sp", g1t, mraw[l, w:w + 1, 2 * D:3 * D].partition_broadcast(128), reads=["mraw"], writes=["bc"])

        dbc = A.f32(256)
        P.dma("sp", dbc, s5_d[l:l + 1, :].partition_broadcast(128), writes=["bcd"])
        NB = 256
        o4 = A.f32(4 * NB).rearrange("p (h t) -> p h t", h=4)
        g4 = A.f32(4 * NB).rearrange("p (h t) -> p h t", h=4)
        sq = A.bf16(4 * NB).rearrange("p (h t) -> p h t", h=4)
        rn = A.f32(4 * NB).rearrange("p (h t) -> p h t", h=4)
        lh = A.f32(2 * NB).rearrange("p (h t) -> p h t", h=2)
        lg = A.f32(2 * NB).rearrange("p (h t) -> p h t", h=2)
        cat = A.bf16(KT * NB).rearrange("p (k t) -> p k t", k=KT)
        ysb = A.f32(2 * 256).rearrange("p (j c) -> p j c", j=2)
        usb = A.f32(2 * 256).rearrange("p (j c) -> p j c", j=2)
        sb16 = A.bf16(2 * 256).rearrange("p (j c) -> p j c", j=2)
        sTt = A.bf16(2 * NB).rearrange("p (k t) -> p k t", k=2)
        gsig = A.f32(2 * NB).rearrange("p (k t) -> p k t", k=2)
        xt = [A.f32(D), A.f32(D)]
        xo = [A.f32(D), A.f32(D)]
        t0 = 0 if not last else LC
        psr = Rot([0, 1, 2, 3, 4, 5, 6, 7])
        while t0 < TT:
            w = 1 if t0 < LC else 0
            n = min(NB, (LC if w == 1 else TT) - t0)
            tk = slice(t0, t0 + n)
            load_g1(w)
            for _ in range(3):
                if pre:
                    o_, i_, k_ = pre.pop(0)
                    P.dma("poolq", o_, i_, writes=[k_])
            P.dma("sp", o4[:, :, 0:n], oT.rearrange("(h p) t -> p h t", p=128)[:, :, tk], reads=["oT"], writes=["o4"])
            P.dma("sp", g4[:, :, 0:n], ggT.rearrange("(h p) t -> p h t", p=128)[:, :, tk], reads=["zT_gg"], writes=["g4"])
            P.op("pool", lambda e, n=n: e.tensor_tensor(out=sq[:, :, 0:n], in0=o4[:, :, 0:n], in1=o4[:, :, 0:n], op=ALU.mult), reads=["o4"], writes=["sq"])
            P.op("act", lambda e, n=n: e.activation(out=g4[:, :, 0:n], in_=g4[:, :, 0:n], func=AF.Silu), reads=["g4"], writes=["g4"])
            for h in range(4):
                pi = psr.next()
                P.op("pe", lambda e, pi=pi, h=h, n=n: e.matmul(PS[pi][:, 0:n], lhsT=ones_b[:], rhs=sq[:, h, 0:n], start=True, stop=True), reads=["sq", "ones_b"], writes=["ps%d" % pi])
                P.op("act", lambda e, pi=pi, h=h, n=n: e.activation(out=rn[:, h, 0:n], in_=PS[pi][:, 0:n], func=AF.Sqrt, scale=1.0 / 128.0, bias=EPS), reads=[], writes=["ps%d" % pi, "rn"])
            P.op("dve", lambda e, n=n: e.reciprocal(out=rn[:, :, 0:n], in_=rn[:, :, 0:n]), reads=["rn"], writes=["rn"])
            P.op("dve", lambda e, n=n: e.tensor_tensor(out=o4[:, :, 0:n], in0=o4[:, :, 0:n], in1=rn[:, :, 0:n], op=ALU.mult), reads=["o4", "rn"], writes=["o4"])
            for h in range(4):
                P.op("dve", lambda e, h=h, n=n: e.scalar_tensor_tensor(out=cat[:, h, 0:n], in0=o4[:, h, 0:n], scalar=ppsb[:, PP_GNORM + h:PP_GNORM + h + 1], in1=g4[:, h, 0:n],
                                                                       op0=ALU.mult, op1=ALU.mult), reads=["o4", "g4", "ppsb"], writes=["cat"])
            P.dma("sp", lh[:, :, 0:n], lruT.rearrange("(h p) t -> p h t", p=128)[:, :, tk], reads=["lruT"], writes=["lh"])
            P.dma("sp", lg[:, :, 0:n], lgT.rearrange("(h p) t -> p h t", p=128)[:, :, tk], reads=["zT_lg"], writes=["lg"])
            P.op("act", lambda e, n=n: e.activation(out=lg[:, :, 0:n], in_=lg[:, :, 0:n], func=AF.Gelu), reads=["lg"], writes=["lg"])
            P.op("dve", lambda e, n=n: e.tensor_tensor(out=cat[:, 4:6, 0:n], in0=lh[:, :, 0:n], in1=lg[:, :, 0:n], op=ALU.mult), reads=["lh", "lg"], writes=["cat"])
            nj = n // 128
            P.dma("sp", ysb[:, 0:nj, :], s5y[tk, :].rearrange("(j p) c -> p j c", p=128), reads=["s5y"], writes=["ysb"])
            P.dma("sp", usb[:, 0:nj, :], su_tok[tk, :].rearrange("(j p) c -> p j c", p=128), reads=["su_tok"], writes=["usb"])
            P.op("dve", lambda e, nj=nj: e.tensor_tensor(out=usb[:, 0:nj, :], in0=usb[:, 0:nj, :], in1=dbc.unsqueeze(1).to_broadcast([128, nj, 256]), op=ALU.mult), reads=["usb", "bcd"], writes=["usb"])
            P.op("dve", lambda e, nj=nj: e.tensor_tensor(out=ysb[:, 0:nj, :], in0=ysb[:, 0:nj, :], in1=usb[:, 0:nj, :], op=ALU.add), reads=["usb", "ysb"], writes=["ysb"])
            P.op("act", lambda e, nj=nj: e.activation(out=sb16[:, 0:nj, :], in_=ysb[:, 0:nj, :], func=AF.Gelu), reads=["ysb"], writes=["sb16"])
            for j in range(nj):
                pi = psr.next()
                pst = PS[pi][:].bitcast(BF16)
                for k in range(2):
                    P.op("pe", lambda e, pst=pst, j=j, k=k: e.transpose(pst[:, k * 128:(k + 1) * 128], sb16[:, j, k * 128:(k + 1) * 128], ident_b[:]), reads=["sb16", "ident_b"], writes=["ps%d" % pi])
                P.op("act", lambda e, pst=pst, j=j: e.activation(out=sTt[:, :, j * 128:(j + 1) * 128], in_=pst[:, 0:256].rearrange("p (k t) -> p k t", k=2), func=AF.Identity),
                     reads=[], writes=["ps%d" % pi, "sTt"])
            for ko in range(2):
                pi = psr.next()
                for ki_ in range(2):
                    P.op("pe", lambda e, pi=pi, ko=ko, ki_=ki_, n=n: e.matmul(PS[pi][:, 0:n], lhsT=glu[:, ki_, ko * 128:(ko + 1) * 128], rhs=sTt[:, ki_, 0:n], start=(ki_ == 0), stop=(ki_ == 1)),
                         reads=["glu", "sTt"], writes=["ps%d" % pi])
                P.op("act", lambda e, pi=pi, ko=ko, n=n: e.activation(out=gsig[:, ko, 0:n], in_=PS[pi][:, 0:n], func=AF.Sigmoid, bias=ppsb[:, PP_GLUB + ko:PP_GLUB + ko + 1]),
                     reads=["ppsb"], writes=["ps%d" % pi, "gsig"])
            P.op("dve", lambda e, n=n: e.tensor_tensor(out=cat[:, 6:8, 0:n], in0=sTt[:, :, 0:n], in1=gsig[:, :, 0:n], op=ALU.mult), reads=["sTt", "gsig"], writes=["cat"])
            for j in range(nj):
                ti0 = t0 + j * 128
                xs = (ti0 // 128) % 2
                P.dma("sp", xt[xs], xsrc[ti0:ti0 + 128, :], reads=["xsrc"], writes=["xtc%d" % xs])
                for hf_ in range(2):
                    pi = psr.next()
                    for kt in range(KT):
                        P.op("pe", lambda e, pi=pi, kt=kt, j=j, hf_=hf_: e.matmul(PS[pi][:, :], lhsT=cat[:, kt, j * 128:(j + 1) * 128], rhs=wo[:, kt, hf_ * 512:(hf_ + 1) * 512],
                                                                               start=(kt == 0), stop=(kt == KT - 1)),
                             reads=["cat", "wo"], writes=["ps%d" % pi])
                    P.op("dve", lambda e, pi=pi, xs=xs, hf_=hf_, w=w: e.tensor_tensor(out=xo[xs][:, hf_ * 512:(hf_ + 1) * 512], in0=PS[pi][:, :], in1=g1bc[w][:, hf_ * 512:(hf_ + 1) * 512], op=ALU.mult),
                         reads=["bc"], writes=["ps%d" % pi, "xo%d" % xs])
                P.op("pool", lambda e, xs=xs: e.tensor_tensor(out=xo[xs], in0=xo[xs], in1=xt[xs], op=ALU.add), reads=["xtc%d" % xs, "xo%d" % xs], writes=["xo%d" % xs])
                P.dma("sp", x1[ti0:ti0 + 128, :], xo[xs], reads=["xo%d" % xs], writes=["x1"])
            t0 += n
        while pre:
            o_, i_, k_ = pre.pop(0)
            P.dma("poolq", o_, i_, writes=[k_])

    def phaseC2(l, last):
        A.reset()
        w1 = A.bf16(KT * 4 * D).rearrange("p (k n) -> p k n", k=KT)
        w2 = A.bf16(32 * D).rearrange("p (k n) -> p k n", k=32)
        G = A.f32(D); S = A.f32(D); g2 = A.f32(D)
        cur_w = [None]

        def load_bc(w):
            if cur_w[0] == w:
                return
            cur_w[0] = w
            P.dma("sp", G, gsc[l, w, 1:2, :].partition_broadcast(128), reads=["gsc"], writes=["bc"])
            P.dma("sp", S, mraw[l, w:w + 1, 3 * D:4 * D].partition_broadcast(128), reads=["mraw"], writes=["bc"])
            P.dma("sp", g2, mraw[l, w:w + 1, 5 * D:6 * D].partition_broadcast(128), reads=["mraw"], writes=["bc"])
        if last:
            fn = A.f32(D)
            P.dma("sp", fn, final_norm.partition_broadcast(128), writes=["bcf"])
        NB = 256
        xts = [A.f32(D), A.f32(D)]
        hbs = [A.bf16(D), A.bf16(D)]
        junk = A.bf16(D); hf = A.f32(D)
        sss = [A.f32(1), A.f32(1)]; rss = [A.f32(1), A.f32(1)]
        hT = A.bf16(KT * NB).rearrange("p (k t) -> p k t", k=KT)
        uT = A.bf16(32 * NB).rearrange("p (k t) -> p k t", k=32)
        rl = [A.bf16(NB), A.bf16(NB)]
        yo = [A.f32(D), A.f32(D)]
        psT = Rot([0, 1]); psM = Rot([2, 3, 4, 5, 6, 7]); rlR = Rot([0, 1])
        t0 = 0 if not last else LC
        while t0 < TT:
            w = 1 if t0 < LC else 0
            n = min(NB, (LC if w == 1 else TT) - t0)
            nj = n // 128
            load_bc(w)
            for j in range(nj):
                ti0 = t0 + j * 128
                s = j % 2; sfx = str(s)
                P.dma("sp", xts[s], x1[ti0:ti0 + 128, :], reads=["x1"], writes=["xt" + sfx])
                rmsnorm_mod(xts[s], G, S, hbs[s], sfx, junk, sss[s], rss[s], hf)
                transpose_to(hbs[s], hT, j, sfx, psT)
            for ft in range(32):
                pi = psM.next()
                for kt in range(KT):
                    P.op("pe", lambda e, pi=pi, kt=kt, ft=ft, n=n: e.matmul(PS[pi][:, 0:n], lhsT=w1[:, kt, ft * 128:(ft + 1) * 128], rhs=hT[:, kt, 0:n], start=(kt == 0), stop=(kt == KT - 1)),
                         reads=["w1", "hT"], writes=["ps%d" % pi])
                ri = rlR.next()
                P.op("act", lambda e, pi=pi, ri=ri, n=n: e.activation(out=rl[ri][:, 0:n], in_=PS[pi][:, 0:n], func=AF.Relu), reads=[], writes=["ps%d" % pi, "rl%d" % ri])
                P.op("dve", lambda e, ri=ri, ft=ft, n=n: e.tensor_tensor(out=uT[:, ft, 0:n], in0=rl[ri][:, 0:n], in1=rl[ri][:, 0:n], op=ALU.mult), reads=["rl%d" % ri], writes=["uT"])
            for j in range(nj):
                ti0 = t0 + j * 128
                s = j % 2; sfx = str(s)
                for hf_ in range(2):
                    pi = psM.next()
                    for ft in range(32):
                        P.op("pe", lambda e, pi=pi, ft=ft, j=j, hf_=hf_: e.matmul(PS[pi][:, :], lhsT=uT[:, ft, j * 128:(j + 1) * 128], rhs=w2[:, ft, hf_ * 512:(hf_ + 1) * 512],
                                                                               start=(ft == 0), stop=(ft == 31)),
                             reads=["w2", "uT"], writes=["ps%d" % pi])
                    P.op("dve", lambda e, pi=pi, s=s, hf_=hf_, g2=g2: e.tensor_tensor(out=yo[s][:, hf_ * 512:(hf_ + 1) * 512], in0=PS[pi][:, :], in1=g2[:, hf_ * 512:(hf_ + 1) * 512], op=ALU.mult),
                         reads=["bc"], writes=["ps%d" % pi, "yo%d" % s])
                P.op("pool", lambda e, s=s: e.tensor_tensor(out=yo[s], in0=yo[s], in1=xts[s], op=ALU.add), reads=["xt" + str(s), "yo%d" % s], writes=["yo%d" % s])
                if not last:
                    P.dma("sp", xres[ti0:ti0 + 128, :], yo[s], reads=["yo%d" % s], writes=["xres"])
                else:
                    sfx2 = "f" + str(s)
                    P.op("act", lambda e, s=s: e.activation(out=junk, in_=yo[s], func=AF.Square, accum_out=sss[s]), reads=["yo%d" % s], writes=["junk", "ssf%d" % s])
                    P.op("act", lambda e, s=s: e.activation(out=rss[s], in_=sss[s], func=AF.Sqrt, scale=1.0 / D, bias=EPS), reads=["ssf%d" % s], writes=["rsf%d" % s])
                    P.op("dve", lambda e, s=s: e.reciprocal(out=rss[s], in_=rss[s]), reads=["rsf%d" % s], writes=["rsf%d" % s])
                    P.op("dve", lambda e, s=s: e.scalar_tensor_tensor(out=yo[s], in0=yo[s], scalar=rss[s], in1=fn, op0=ALU.mult, op1=ALU.mult),
                         reads=["yo%d" % s, "rsf%d" % s, "bcf"], writes=["yo%d" % s])
                    P.dma("sp", out_d[ti0 - LC:ti0 - LC + 128, :], yo[s], reads=["yo%d" % s], writes=["out"])
            t0 += n

    setup_consts()
    stages = build.stages if hasattr(build, "stages") else None
    for l in range(depth):
        last = (l == depth - 1)
        if stages == "C":
            break
        modulation(l)
        load_pp(l)
        P.barrier()
        if stages == "M":
            break
        phaseA(l)
        P.barrier()
        if stages is not None and "A" == stages:
            break
        print("nops before gla", P.nops, flush=True)
        gla(l)
        P.barrier()
        print("nops before lru", P.nops, flush=True)
        lru(l)
        P.barrier()
        print("nops before s5", P.nops, flush=True)
        s5(l)
        P.barrier()
        print("nops after s5", P.nops, flush=True)
        if stages is not None and "B" == stages:
            break
        phaseC1(l, last)
        P.barrier()
        phaseC2(l, last)
        P.barrier()
    P.barrier()
    print("nops", P.nops, flush=True)
    P.emit()
    P.close()
    return nc


def prep_inputs(inp, b, LL, LC, depth):
    f = lambda a: np.ascontiguousarray(np.asarray(a, dtype=np.float32))
    TT = LL + LC
    NC8, LC8 = TT // 8, LC // 8
    m = {}
    m["xin"] = f(np.concatenate([inp["ctx"][b], inp["x"][b]], axis=0))
    cv = np.stack([np.asarray(inp["c"][b]).reshape(KT, 128).T, np.asarray(inp["c_ctx"]).reshape(KT, 128).T], axis=-1)
    m["cvec"] = f(cv)
    for k in ("w_mod", "b_mod", "norm1", "norm2", "w_in", "gla_up_w", "lru_wa", "lru_wx", "s5_d", "s5_glu_w", "w_out", "w_ff1", "w_ff2"):
        m[k] = f(inp[k])
    m["final_norm"] = f(np.asarray(inp["final_norm"]).reshape(1, D))
    pp = np.zeros((depth, 128, 64), np.float32)
    for l in range(depth):
        for d in range(2):
            for hp in range(2):
                pp[l, :, 0 + d * 2 + hp] = inp["gla_up_b"][l, d, hp * 128:(hp + 1) * 128]
            for ct in range(2):
                sl = slice(ct * 128, (ct + 1) * 128)
                for k in range(4):
                    pp[l, :, 8 + (d * 4 + k) * 2 + ct] = inp["lru_conv_w"][l, d, k, sl]
                pp[l, :, 24 + d * 2 + ct] = inp["lru_conv_b"][l, d, sl]
                pp[l, :, 28 + d * 2 + ct] = inp["lru_ba"][l, d, sl]
                pp[l, :, 32 + d * 2 + ct] = inp["lru_bx"][l, d, sl]
                pp[l, :, 36 + d * 2 + ct] = inp["lru_lambda"][l, d, sl]
        for h in range(4):
            pp[l, :, 4 + h] = inp["gla_norm"][l, h * 128:(h + 1) * 128]
        for ct in range(2):
            pp[l, :, 40 + ct] = inp["s5_glu_b"][l, ct * 128:(ct + 1) * 128]
    m["pp"] = pp
    s5p = np.zeros((depth, 128, 3, 16), np.float32)
    s5b = np.zeros((depth, 128, 2, 16, 16), np.float32)
    s5c = np.zeros((depth, 128, 2, 16, 16), np.float32)
    for l in range(depth):
        for d in range(2):
            for gp in range(8):
                for gm in range(2):
                    g = gp * 2 + gm
                    ps_ = slice(gm * 64, (gm + 1) * 64)
                    s5p[l, ps_, 0, d * 8 + gp] = inp["s5_lam_re"][l, d, g]
                    s5p[l, ps_, 1, d * 8 + gp] = inp["s5_lam_im"][l, d, g]
                    s5p[l, ps_, 2, d * 8 + gp] = inp["s5_log_dt"][l, d, g]
                    s5b[l, ps_, 0, d * 8 + gp, :] = inp["s5_b_re"][l, d, g]
                    s5b[l, ps_, 1, d * 8 + gp, :] = inp["s5_b_im"][l, d, g]
                    s5c[l, ps_, 0, d * 8 + gp, :] = np.asarray(inp["s5_c_re"][l, d, g]).T
                    s5c[l, ps_, 1, d * 8 + gp, :] = np.asarray(inp["s5_c_im"][l, d, g]).T
    m["s5p"], m["s5b"], m["s5c"] = s5p, s5b, s5c
    tau = np.zeros((2, NC8), np.float32)
    tau[0] = np.arange(NC8)
    tau[1, :LC8] = LC8 - 1 - np.arange(LC8)
    tau[1, LC8:] = LC8 + (NC8 - 1 - np.arange(LC8, NC8))
    m["tau"] = tau.reshape(1, 2 * NC8)
    return m


_CACHE = {}


def kernel(**inputs):
    LL, LC, depth = 4096, 256, 4
    B = inputs["x"].shape[0]
    key = (LL, LC, depth)
    if key not in _CACHE:
        _CACHE[key] = build(LL, LC, depth)
    nc = _CACHE[key]
    in_maps = [prep_inputs(inputs, b, LL, LC, depth) for b in range(B)]
    res = run_bass_kernel_spmd(nc, in_maps, core_ids=list(range(B)))
    return np.stack([np.asarray(r["out"], dtype=np.float32) for r in res.results], axis=0)
```

```python
import math
import os
from contextlib import ExitStack

import numpy as np
import concourse.bass as bass
import concourse.mybir as mybir
from concourse.bass_utils import run_bass_kernel_spmd

F32 = mybir.dt.float32
BF16 = mybir.dt.bfloat16
ALU = mybir.AluOpType
AF = mybir.ActivationFunctionType

D = 1024
KT = 8
EPS = 1e-6
MAGIC = 12582912.0
TWO_PI = 2.0 * math.pi

COMPUTE = ("pe", "act", "dve", "pool")
QUEUES = ("sp", "poolq")
ENG_OF = {"pe": "pe", "act": "act", "dve": "dve", "pool": "pool", "sp": "sp", "poolq": "pool"}


class Prog:
    def __init__(self, nc, ndma=8):
        import os
        self.nc = nc
        self.es = ExitStack()
        self.streams = {e: [] for e in ("pe", "act", "dve", "pool", "sp")}
        self.sem = {}
        self.nop_eng = {}
        for e in COMPUTE:
            self.sem[e] = self.es.enter_context(nc.semaphore("s_" + e))
            self.nop_eng[e] = 0
        self.dsem, self.dcnt, self.dnext = {}, {}, {}
        for q in QUEUES:
            self.dsem[q] = [self.es.enter_context(nc.semaphore("d_%s%d" % (q, i))) for i in range(ndma)]
            self.dcnt[q] = [0] * ndma
            self.dnext[q] = 0
        self.seen = {e: {} for e in self.streams}
        self.lastw = {}
        self.readers = {}
        self.nops = 0
        self.waited = {e: set() for e in COMPUTE}
        self._cap = None
        self.limit = int(os.environ["OPLIMIT"]) if os.environ.get("OPLIMIT") else None

    def sbuf(self, name, shape, dtype=F32):
        return self.es.enter_context(self.nc.sbuf_tensor(name, list(shape), dtype))

    def psum(self, name, shape, dtype=F32):
        return self.es.enter_context(self.nc.psum_tensor(name, list(shape), dtype))

    @staticmethod
    def _tkey(tok):
        return ("c", tok[1]) if tok[0] == "c" else ("d", tok[1].name)

    @staticmethod
    def _tval(tok):
        return tok[2]

    def _need(self, stream, tok, waits):
        if tok is None:
            return
        if tok[0] == "c" and tok[1] == "pe" and stream == "pe":
            return
        k = self._tkey(tok)
        if self.seen[stream].get(k, 0) >= self._tval(tok):
            return
        cur = waits.get(k)
        if cur is None or self._tval(cur) < self._tval(tok):
            waits[k] = tok

    def _deps(self, stream, reads, writes, waits, is_dma=False):
        for k in reads:
            self._need(stream, self.lastw.get(k), waits)
        for k in writes:
            t = self.lastw.get(k)
            if is_dma or not (t is not None and t[0] == "c" and t[1] == stream):
                self._need(stream, t, waits)
            for t in self.readers.get(k, ()):
                if is_dma or not (t[0] == "c" and t[1] == stream):
                    self._need(stream, t, waits)

    def _commit(self, stream, tok, reads, writes, waits):
        for k, t in waits.items():
            self.seen[stream][k] = self._tval(t)
            if t[0] == "c":
                self.waited[t[1]].add(t[2])
        for k in writes:
            self.lastw[k] = tok
            self.readers[k] = []
        for k in reads:
            if k in writes:
                continue
            lst = self.readers.setdefault(k, [])
            lst.append(tok)
            if len(lst) > 16:
                best = {}
                for t in lst:
                    kk = self._tkey(t)
                    b = best.get(kk)
                    if b is None or self._tval(b) < self._tval(t):
                        best[kk] = t
                self.readers[k] = list(best.values())

    def capture(self, f):
        prev = self._cap
        self._cap = []
        f()
        lst = self._cap
        self._cap = prev
        return lst

    def interleave(self, lists):
        lists = [list(l) for l in lists if l]
        idx = [0] * len(lists)
        while True:
            done = True
            for k, l in enumerate(lists):
                if idx[k] < len(l):
                    done = False
                    kind, a = l[idx[k]]
                    idx[k] += 1
                    if kind == "op":
                        self._op2(*a)
                    else:
                        self._dma2(*a)
            if done:
                break

    def op(self, eng, fn, reads=(), writes=()):
        if self.limit is not None and self.nops >= self.limit:
            return None
        rec = _Rec()
        fn(rec)
        name, args, kwargs = rec.call
        if self._cap is not None:
            self._cap.append(("op", (eng, name, args, kwargs, tuple(reads), tuple(writes))))
            return None
        return self._op2(eng, name, args, kwargs, reads, writes)

    def _op2(self, eng, name, args, kwargs, reads, writes):
        if os.environ.get("OPTRACE"):
            def _d(a):
                try:
                    return "%s%s" % (tuple(a.shape), "" )
                except Exception:
                    return str(a)[:30]
            print("OP", self.nops, eng, name, [_d(a) for a in args], {k: _d(v) for k, v in kwargs.items()}, flush=True)
        fn = lambda e, name=name, args=args, kwargs=kwargs: getattr(e, name)(*args, **kwargs)
        waits = {}
        self._deps(eng, reads, writes, waits)
        self.nop_eng[eng] += 1
        tok = ("c", eng, self.nop_eng[eng])
        self._commit(eng, tok, reads, writes, waits)
        self.streams[eng].append([list(waits.values()), fn, tok])
        self.nops += 1
        return tok

    def dma(self, q, out, in_, reads=(), writes=(), **kw):
        if self.limit is not None and self.nops >= self.limit:
            return None
        if self._cap is not None:
            self._cap.append(("dma", (q, out, in_, tuple(reads), tuple(writes), kw)))
            return None
        return self._dma2(q, out, in_, reads, writes, kw)

    def _dma2(self, q, out, in_, reads, writes, kw):
        stream = ENG_OF[q]
        waits = {}
        i = self.dnext[q]
        self.dnext[q] = (i + 1) % len(self.dsem[q])
        sem = self.dsem[q][i]
        if self.dcnt[q][i] > 0:
            self._need(stream, ("d", sem, 16 * self.dcnt[q][i], q), waits)
        self._deps(stream, reads, writes, waits, is_dma=True)
        self.dcnt[q][i] += 1
        tok = ("d", sem, 16 * self.dcnt[q][i], q)
        self._commit(stream, tok, reads, writes, waits)
        fn = lambda e, out=out, in_=in_, kw=kw: e.dma_start(out=out, in_=in_, **kw)
        self.streams[stream].append([list(waits.values()), fn, tok])
        self.nops += 1
        return tok

    def barrier(self):
        toks = []
        for q in QUEUES:
            for i, sem in enumerate(self.dsem[q]):
                if self.dcnt[q][i]:
                    toks.append(("d", sem, 16 * self.dcnt[q][i], q))
        for e in COMPUTE:
            if self.nop_eng[e]:
                toks.append(("c", e, self.nop_eng[e]))
        for stream in self.streams:
            waits = {}
            for t in toks:
                if t[0] == "c" and t[1] == stream:
                    continue
                self._need(stream, t, waits)
            for k, t in waits.items():
                self.seen[stream][k] = self._tval(t)
                if t[0] == "c":
                    self.waited[t[1]].add(t[2])
            if waits:
                self.streams[stream].append([list(waits.values()), None, None])
        self.lastw = {}
        self.readers = {}

    def emit(self):
        nc = self.nc
        streams = self.streams
        rank = {}
        for e in COMPUTE:
            rank[e] = {idx: r + 1 for r, idx in enumerate(sorted(self.waited[e]))}

        def run(eng_obj, lst):
            for waits, fn, tok in lst:
                for t in waits:
                    if t[0] == "c":
                        eng_obj.wait_ge(self.sem[t[1]], rank[t[1]][t[2]])
                    else:
                        eng_obj.wait_ge(t[1], t[2])
                if fn is not None:
                    ins = fn(eng_obj)
                    if tok[0] == "d":
                        ins.then_inc(tok[1], 16)
                    elif tok[2] in rank[tok[1]]:
                        ins.then_inc(self.sem[tok[1]], 1)

        with nc.Block() as block:
            @block.tensor
            def _(e):
                run(e, streams["pe"])

            @block.scalar
            def _(e):
                run(e, streams["act"])

            @block.vector
            def _(e):
                run(e, streams["dve"])

            @block.gpsimd
            def _(e):
                run(e, streams["pool"])

            @block.sync
            def _(e):
                run(e, streams["sp"])

    def close(self):
        self.es.close()


class _Rec:
    def __init__(self):
        self.call = None

    def __getattr__(self, name):
        def f(*args, **kwargs):
            self.call = (name, args, kwargs)
            return self
        return f


class Arena:
    def __init__(self, P, words):
        self.t = P.sbuf("arena", [128, words], F32)
        self.words = words
        self.off = 0
        self.n = 0

    def reset(self):
        self.off = 0

    def f32(self, n):
        assert self.off + n <= self.words, ("arena overflow", self.off, n, self.words)
        ap = self.t[:, self.off:self.off + n]
        self.off += n
        return ap

    def bf16(self, n):
        w = (n + 1) // 2
        return self.f32(w).bitcast(BF16)[:, 0:n]


class Rot:
    def __init__(self, items):
        self.items = items
        self.i = 0

    def next(self):
        it = self.items[self.i % len(self.items)]
        self.i += 1
        return it


def build(LL, LC, depth, debug=False):
    TT = LL + LC
    NT = TT // 128
    NCH = TT // 64
    NC8 = TT // 8
    LC8 = LC // 8
    ROWS = LL // 64
    RB = ROWS // 8
    nc = bass.Bass("TRN2", target_bir_lowering=False)
    P = Prog(nc)

    def din(name, shape, dt=F32):
        return nc.dram_tensor(name, list(shape), dt, kind="ExternalInput").ap()

    dkind = "ExternalOutput" if debug else "Internal"

    def dscr(name, shape, dt=F32):
        return nc.dram_tensor(name, list(shape), dt, kind=dkind).ap()

    xin = din("xin", [TT, D])
    cvec = din("cvec", [128, KT, 2])
    w_mod = din("w_mod", [depth, D, 6 * D])
    b_mod = din("b_mod", [depth, 6 * D])
    norm1 = din("norm1", [depth, D])
    norm2 = din("norm2", [depth, D])
    w_in = din("w_in", [depth, D, 2336])
    gla_up_w = din("gla_up_w", [depth, 2, 16, 256])
    pp = din("pp", [depth, 128, 64])
    lru_wa = din("lru_wa", [depth, 2, 4, 64, 64])
    lru_wx = din("lru_wx", [depth, 2, 4, 64, 64])
    s5p = din("s5p", [depth, 128, 3, 16])
    s5b = din("s5b", [depth, 128, 2, 16, 16])
    s5c = din("s5c", [depth, 128, 2, 16, 16])
    s5_d = din("s5_d", [depth, 256])
    s5_glu_w = din("s5_glu_w", [depth, 256, 256])
    w_out = din("w_out", [depth, D, D])
    w_ff1 = din("w_ff1", [depth, D, 4 * D])
    w_ff2 = din("w_ff2", [depth, 4 * D, D])
    final_norm = din("final_norm", [1, D])
    tau_in = din("tau", [1, 2 * NC8])

    out_d = nc.dram_tensor("out", [LL, D], F32, kind="ExternalOutput").ap()

    mraw = dscr("mraw", [depth, 2, 6 * D])
    gsc = dscr("gsc", [depth, 2, 2, D])
    qT = dscr("qT", [256, TT]); kT = dscr("kT", [256, TT]); ggT = dscr("ggT", [512, TT])
    lrT = [dscr("lrT0", [16, TT]), dscr("lrT1", [16, TT])]
    lxT = dscr("lxT", [256, TT]); lgT = dscr("lgT", [256, TT])
    v_tok = dscr("v_tok", [TT, 512], BF16)
    su_tok = dscr("su_tok", [TT, 256])
    oT = dscr("oT", [512, TT])
    lruT = dscr("lruT", [256, TT])
    s5y = dscr("s5y", [TT, 256])
    x1 = dscr("x1", [TT, D])
    xres = dscr("xres", [TT, D])
    dbg = {}

    ident_f = P.sbuf("ident_f", [128, 128], F32)
    ident_b = P.sbuf("ident_b", [128, 128], BF16)
    ones_f = P.sbuf("ones_f", [128, 128], F32)
    ones_b = P.sbuf("ones_b", [128, 128], BF16)
    maskF = P.sbuf("maskF", [128, 128], F32)
    maskB = P.sbuf("maskB", [128, 128], F32)
    m8F = P.sbuf("m8F", [128, 128], F32)
    m8B = P.sbuf("m8B", [128, 128], F32)
    jidx = P.sbuf("jidx", [128, 2, 8, 9], F32)
    ppsb = P.sbuf("ppsb", [128, 64], F32)
    AW = 51500
    A = Arena(P, AW)
    PS = [P.psum("ps%d" % i, [128, 512], F32) for i in range(8)]

    def setup_consts():
        P.op("pool", lambda e: e.memset(ident_f[:], 0.0), writes=["ident_f"])
        P.op("pool", lambda e: e.affine_select(out=ident_f[:], in_=ident_f[:], pattern=[[-1, 128]], compare_op=ALU.not_equal,
                                               fill=1.0, base=0, channel_multiplier=1), reads=["ident_f"], writes=["ident_f"])
        P.op("dve", lambda e: e.tensor_copy(out=ident_b[:], in_=ident_f[:]), reads=["ident_f"], writes=["ident_b"])
        P.op("dve", lambda e: e.memset(ones_f[:], 1.0), writes=["ones_f"])
        P.op("dve", lambda e: e.memset(ones_b[:], 1.0), writes=["ones_b"])
        P.op("pool", lambda e: e.affine_select(out=maskF[:], in_=ones_f[:], pattern=[[1, 128]], compare_op=ALU.is_ge,
                                               fill=0.0, base=0, channel_multiplier=-1), reads=["ones_f"], writes=["maskF"])
        P.op("pool", lambda e: e.memset(maskF[0:64, 64:128], 0.0), reads=["maskF"], writes=["maskF"])
        P.op("pool", lambda e: e.affine_select(out=maskB[:], in_=ones_f[:], pattern=[[-1, 128]], compare_op=ALU.is_ge,
                                               fill=0.0, base=0, channel_multiplier=1), reads=["ones_f"], writes=["maskB"])
        P.op("pool", lambda e: e.memset(maskB[64:128, 0:64], 0.0), reads=["maskB"], writes=["maskB"])
        P.op("pool", lambda e: e.memset(m8F[:], 1.0), writes=["m8F"])
        P.op("pool", lambda e: e.memset(m8B[:], 1.0), writes=["m8B"])
        P.op("pool", lambda e: e.affine_select(out=m8F[:].rearrange("p (r j h) -> p r j h", r=1, j=8), in_=m8F[:].rearrange("p (r j h) -> p r j h", r=1, j=8),
                                               pattern=[[0, 1], [16, 8], [0, 16]], compare_op=ALU.is_ge, fill=0.0, base=15, channel_multiplier=-1),
             reads=["m8F"], writes=["m8F"])
        P.op("pool", lambda e: e.affine_select(out=m8B[:].rearrange("p (r j h) -> p r j h", r=1, j=8), in_=m8B[:].rearrange("p (r j h) -> p r j h", r=1, j=8),
                                               pattern=[[0, 1], [-16, 8], [0, 16]], compare_op=ALU.is_ge, fill=0.0, base=0, channel_multiplier=1),
             reads=["m8B"], writes=["m8B"])
        for j in range(9):
            P.op("dve", lambda e, j=j: e.memset(jidx[:, 0, :, j:j + 1], float(j)), writes=["jidx"])
            P.op("dve", lambda e, j=j: e.memset(jidx[:, 1, :, j:j + 1], float(7 - j) if j < 8 else 8.0), writes=["jidx"])

    PP_UPB = 0
    PP_GNORM = 4
    PP_CONVW = 8
    PP_CONVB = 24
    PP_BA = 28
    PP_BX = 32
    PP_LAM = 36
    PP_GLUB = 40
    PP_SGN = 42

    def modulation(l):
        A.reset()
        cs_raw = A.f32(16); cs = A.f32(16)
        msb = A.f32(6 * D)
        bm = A.f32(6 * D)
        n12 = A.f32(2 * D)
        gt = A.f32(2 * D)
        wblk = [A.f32(KT * 512), A.f32(KT * 512)]
        P.dma("sp", cs_raw, cvec.rearrange("p k w -> p (k w)"), writes=["cs_raw"])
        P.op("act", lambda e: e.activation(out=cs, in_=cs_raw, func=AF.Silu), reads=["cs_raw"], writes=["cs"])
        P.dma("sp", bm[0:2, :], b_mod[l:l + 1, :].partition_broadcast(2), writes=["bm"])
        P.dma("sp", n12[0:2, 0:D], norm1[l:l + 1, :].partition_broadcast(2), writes=["n12a"])
        P.dma("sp", n12[0:2, D:2 * D], norm2[l:l + 1, :].partition_broadcast(2), writes=["n12b"])
        wv = w_mod[l].rearrange("(kt p) n -> p kt n", p=128)
        cs3 = cs.rearrange("p (k w) -> p k w", w=2)
        for j in range(12):
            wb = wblk[j % 2]
            wb3 = wb.rearrange("p (k n) -> p k n", k=KT)
            P.dma("sp", wb3, wv[:, :, j * 512:(j + 1) * 512], writes=["wblk%d" % (j % 2)])
            ps = PS[j % 2]
            for kt in range(KT):
                P.op("pe", lambda e, ps=ps, kt=kt, wb3=wb3: e.matmul(ps[0:2, :], lhsT=cs3[:, kt, :], rhs=wb3[:, kt, :], start=(kt == 0), stop=(kt == KT - 1)),
                     reads=["cs", "wblk%d" % (j % 2)], writes=["ps%d" % (j % 2)])
            P.op("dve", lambda e, ps=ps, j=j: e.tensor_tensor(out=msb[0:2, j * 512:(j + 1) * 512], in0=ps[0:2, :], in1=bm[0:2, j * 512:(j + 1) * 512], op=ALU.add),
                 reads=["bm"], writes=["ps%d" % (j % 2), "msb"])
        P.op("dve", lambda e: e.scalar_tensor_tensor(out=gt[0:2, 0:D], in0=msb[0:2, D:2 * D], scalar=1.0, in1=n12[0:2, 0:D], op0=ALU.add, op1=ALU.mult),
             reads=["msb", "n12a"], writes=["gt"])
        P.op("dve", lambda e: e.scalar_tensor_tensor(out=gt[0:2, D:2 * D], in0=msb[0:2, 4 * D:5 * D], scalar=1.0, in1=n12[0:2, D:2 * D], op0=ALU.add, op1=ALU.mult),
             reads=["msb", "n12b", "gt"], writes=["gt"])
        P.dma("sp", mraw[l], msb[0:2, :], reads=["msb"], writes=["mraw"])
        P.dma("sp", gsc[l].rearrange("w g d -> w (g d)"), gt[0:2, :], reads=["gt"], writes=["gsc"])

    def bc_load(dst, src_row):
        return src_row.to_broadcast([128, src_row.shape[-1]])

    def token_blocks(last_skip_ctx=False):
        blks = []
        if not last_skip_ctx:
            t = 0
            while t < LC // 128:
                n = min(4, LC // 128 - t)
                blks.append((t, n, 1))
                t += n
        t = LC // 128
        while t < NT:
            n = min(4, NT - t)
            blks.append((t, n, 0))
            t += n
        return blks

    def rmsnorm_mod(xt_ap, Gbc, Sbc, hb_out, sfx, junk, ss, rs, hf):
        P.op("act", lambda e: e.activation(out=junk, in_=xt_ap, func=AF.Square, accum_out=ss), reads=["xt" + sfx], writes=["junk", "ss" + sfx])
        P.op("act", lambda e: e.activation(out=rs, in_=ss, func=AF.Sqrt, scale=1.0 / D, bias=EPS), reads=["ss" + sfx], writes=["rs" + sfx])
        P.op("dve", lambda e: e.reciprocal(out=rs, in_=rs), reads=["rs" + sfx], writes=["rs" + sfx])
        P.op("dve", lambda e: e.scalar_tensor_tensor(out=hf, in0=xt_ap, scalar=rs, in1=Gbc, op0=ALU.mult, op1=ALU.mult),
             reads=["xt" + sfx, "rs" + sfx, "bc"], writes=["hf"])
        P.op("pool", lambda e: e.tensor_tensor(out=hb_out, in0=hf, in1=Sbc, op=ALU.add), reads=["hf", "bc"], writes=["hb" + sfx])

    def transpose_to(hb, hT3, j, sfx, psrot):
        pi = psrot.next()
        pst = PS[pi][:].bitcast(BF16)
        for kt in range(KT):
            P.op("pe", lambda e, kt=kt, pst=pst: e.transpose(pst[:, kt * 128:(kt + 1) * 128], hb[:, kt * 128:(kt + 1) * 128], ident_b[:]),
                 reads=["hb" + sfx, "ident_b"], writes=["ps%d" % pi])
        P.op("act", lambda e, pst=pst: e.activation(out=hT3[:, :, j * 128:(j + 1) * 128], in_=pst.rearrange("p (k t) -> p k t", k=KT), func=AF.Identity),
             reads=[], writes=["ps%d" % pi, "hT"])

    def phaseA(l):
        A.reset()
        xsrc = xin if l == 0 else xres
        win = A.bf16(KT * 2336).rearrange("p (k n) -> p k n", k=KT)
        wv = w_in[l].rearrange("(kt p) n -> p kt n", p=128)
        for c0 in range(0, 2336, 512):
            c1 = min(2336, c0 + 512)
            P.dma("poolq", win[:, :, c0:c1], wv[:, :, c0:c1], writes=["win"])
        bcs = {}
        for w in (0, 1):
            G = A.f32(D); S = A.f32(D)
            P.dma("sp", G, gsc[l, w, 0:1, :].partition_broadcast(128), reads=["gsc"], writes=["bc"])
            P.dma("sp", S, mraw[l, w:w + 1, 0:D].partition_broadcast(128), reads=["mraw"], writes=["bc"])
            bcs[w] = (G, S)
        NS = 8
        xts = [A.f32(D) for _ in range(2)]
        hbs = [A.bf16(D) for _ in range(NS)]
        hfs = [A.f32(D), A.f32(D)]
        sss = [A.f32(1) for _ in range(NS)]; rss = [A.f32(1) for _ in range(NS)]
        hTs = [A.bf16(KT * 512).rearrange("p (k t) -> p k t", k=KT) for _ in range(2)]
        stg = [A.f32(512) for _ in range(3)]
        vst = [A.bf16(512) for _ in range(2)]
        sst = [A.f32(256) for _ in range(2)]
        psT = Rot([0, 1]); psM = Rot([2, 3, 4, 5, 6, 7])
        stgR = Rot([0, 1, 2]); vR = Rot([0, 1]); sR = Rot([0, 1])
        FM = [("q", 0, qT, 0, 128), ("q", 128, qT, 128, 128), ("k", 256, kT, 0, 128), ("k", 384, kT, 128, 128)]
        for i in range(4):
            FM.append(("gg", 1024 + 128 * i, ggT, 128 * i, 128))
        FM.append(("lr0", 1536, lrT[0], 0, 16)); FM.append(("lr1", 1552, lrT[1], 0, 16))
        for i in range(2):
            FM.append(("lx", 1568 + 128 * i, lxT, 128 * i, 128))
        for i in range(2):
            FM.append(("lg", 1824 + 128 * i, lgT, 128 * i, 128))
        blocks = token_blocks()
        slot_ctr = [0]
        blk_slots = {}

        def norms(bi):
            (t0, n, w) = blocks[bi]
            G, S = bcs[w]
            sl = []
            for j in range(n):
                ti = t0 + j
                s = slot_ctr[0] % NS; slot_ctr[0] += 1
                xs = s % 2
                sl.append(s)
                kx, kh = "xt%d" % xs, "hb%d" % s
                P.dma("sp", xts[xs], xsrc[ti * 128:(ti + 1) * 128, :], reads=["xsrc"], writes=[kx])
                P.op("act", lambda e, xs=xs, s=s: e.activation(out=hbs[s], in_=xts[xs], func=AF.Square, accum_out=sss[s]), reads=[kx], writes=[kh, "ss%d" % s])
                P.op("act", lambda e, s=s: e.activation(out=rss[s], in_=sss[s], func=AF.Sqrt, scale=1.0 / D, bias=EPS), reads=["ss%d" % s], writes=["rs%d" % s])
                P.op("dve", lambda e, s=s: e.reciprocal(out=rss[s], in_=rss[s]), reads=["rs%d" % s], writes=["rs%d" % s])
                P.op("dve", lambda e, xs=xs, s=s, G=G: e.scalar_tensor_tensor(out=hfs[xs], in0=xts[xs], scalar=rss[s], in1=G, op0=ALU.mult, op1=ALU.mult),
                     reads=[kx, "rs%d" % s, "bc"], writes=["hf%d" % xs])
                P.op("pool", lambda e, xs=xs, s=s, S=S: e.tensor_tensor(out=hbs[s], in0=hfs[xs], in1=S, op=ALU.add), reads=["hf%d" % xs, "bc"], writes=[kh])
            blk_slots[bi] = sl

        def transposes(bi):
            hT = hTs[bi % 2]
            for j, s in enumerate(blk_slots[bi]):
                pi = psT.next()
                pst = PS[pi][:].bitcast(BF16)
                for kt in range(KT):
                    P.op("pe", lambda e, kt=kt, pst=pst, s=s: e.transpose(pst[:, kt * 128:(kt + 1) * 128], hbs[s][:, kt * 128:(kt + 1) * 128], ident_b[:]),
                         reads=["hb%d" % s, "ident_b"], writes=["ps%d" % pi])
                P.op("act", lambda e, pst=pst, j=j, hT=hT: e.activation(out=hT[:, :, j * 128:(j + 1) * 128], in_=pst.rearrange("p (k t) -> p k t", k=KT), func=AF.Identity),
                     reads=[], writes=["ps%d" % pi, "hT%d" % (bi % 2)])

        def fm_part(bi):
            (t0, n, w) = blocks[bi]
            hT = hTs[bi % 2]; kT_ = "hT%d" % (bi % 2)
            ntok = n * 128; tok0 = t0 * 128
            for (nm, c0, dst, r0, m) in FM:
                pi = psM.next()
                for kt in range(KT):
                    P.op("pe", lambda e, pi=pi, kt=kt, c0=c0, m=m, hT=hT, ntok=ntok: e.matmul(PS[pi][0:m, 0:ntok], lhsT=win[:, kt, c0:c0 + m], rhs=hT[:, kt, 0:ntok],
                                                                                           start=(kt == 0), stop=(kt == KT - 1)),
                         reads=["win", kT_], writes=["ps%d" % pi])
                si = stgR.next()
                P.op("act", lambda e, pi=pi, si=si, m=m, ntok=ntok: e.activation(out=stg[si][0:m, 0:ntok], in_=PS[pi][0:m, 0:ntok], func=AF.Identity),
                     reads=[], writes=["ps%d" % pi, "stg%d" % si])
                P.dma("sp", dst[r0:r0 + m, tok0:tok0 + ntok], stg[si][0:m, 0:ntok], reads=["stg%d" % si], writes=["zT_%s_%d" % (nm, r0)])

        def tm_part(bi):
            (t0, n, w) = blocks[bi]
            hT = hTs[bi % 2]; kT_ = "hT%d" % (bi % 2)
            for j in range(n):
                ti = t0 + j
                pi = psM.next()
                for kt in range(KT):
                    P.op("pe", lambda e, pi=pi, kt=kt, j=j, hT=hT: e.matmul(PS[pi][:, :], lhsT=hT[:, kt, j * 128:(j + 1) * 128], rhs=win[:, kt, 512:1024],
                                                                         start=(kt == 0), stop=(kt == KT - 1)),
                         reads=["win", kT_], writes=["ps%d" % pi])
                vi = vR.next()
                P.op("dve", lambda e, pi=pi, vi=vi: e.tensor_copy(out=vst[vi], in_=PS[pi][:, :]), reads=[], writes=["ps%d" % pi, "vst%d" % vi])
                P.dma("sp", v_tok[ti * 128:(ti + 1) * 128, :], vst[vi], reads=["vst%d" % vi], writes=["v_tok%d" % ti])
                pi = psM.next()
                for kt in range(KT):
                    P.op("pe", lambda e, pi=pi, kt=kt, j=j, hT=hT: e.matmul(PS[pi][:, 0:256], lhsT=hT[:, kt, j * 128:(j + 1) * 128], rhs=win[:, kt, 2080:2336],
                                                                         start=(kt == 0), stop=(kt == KT - 1)),
                         reads=["win", kT_], writes=["ps%d" % pi])
                si = sR.next()
                P.op("dve", lambda e, pi=pi, si=si: e.tensor_copy(out=sst[si], in_=PS[pi][:, 0:256]), reads=[], writes=["ps%d" % pi, "sst%d" % si])
                P.dma("sp", su_tok[ti * 128:(ti + 1) * 128, :], sst[si], reads=["sst%d" % si], writes=["su_tok%d" % ti])

        norms(0)
        transposes(0)
        for bi in range(len(blocks)):
            if bi + 1 < len(blocks):
                norms(bi + 1)
            fm_part(bi)
            if bi + 1 < len(blocks):
                transposes(bi + 1)
            tm_part(bi)

    def load_pp(l):
        P.dma("sp", ppsb[:], pp[l], writes=["ppsb"])

    def gla(l):
        for hp in range(2):
            A.reset()
            sm0 = A.bf16(TT); sm1 = A.bf16(TT)
            P.op("pool", lambda e: e.memset(sm0, 1.0), writes=["sm"])
            P.op("pool", lambda e: e.memset(sm0.rearrange("p (c j) -> p c j", j=64)[:, :, 0:1], 0.0), reads=["sm"], writes=["sm"])
            P.op("pool", lambda e: e.memset(sm1, 1.0), reads=["sm"], writes=["sm"])
            P.op("pool", lambda e: e.memset(sm1.rearrange("p (c j) -> p c j", j=64)[:, :, 63:64], 0.0), reads=["sm"], writes=["sm"])
            vt = A.bf16(NT * 256).rearrange("p (t c) -> p t c", t=NT)
            vsrc = v_tok.rearrange("(t p) c -> p t c", p=128)
            for t0_ in range(0, NT, 8):
                t1_ = min(NT, t0_ + 8)
                P.dma("sp", vt[:, t0_:t1_, :], vsrc[:, t0_:t1_, hp * 256:(hp + 1) * 256], reads=["v_tok"], writes=["vt"])
            oacc = A.f32(2 * TT).rearrange("p (h t) -> p h t", h=2)
            P.op("pool", lambda e: e.memset(oacc, 0.0), writes=["oacc%d_%d" % (hh, t) for hh in range(2) for t in range(NT)])
            lrsb = A.f32(TT); Bp = A.f32(TT); Bc = A.f32(TT); qk = A.f32(TT)
            qd = [A.bf16(TT), A.bf16(TT)]; ki = [A.bf16(TT), A.bf16(TT)]
            kiT = [A.bf16(NT * 128).rearrange("p (t c) -> p t c", t=NT) for _ in range(2)]
            upw = A.f32(256); nb = A.f32(1)
            gam = [A.f32(NCH), A.f32(NCH)]
            S = [A.f32(128), A.f32(128)]; Sb = [A.bf16(128), A.bf16(128)]; tmp = [A.f32(128), A.f32(128)]
            sT = [[A.bf16(128), A.bf16(128)], [A.bf16(128), A.bf16(128)]]
            for d in range(2):
                ds_ = str(d)
                sm = sm0 if d == 0 else sm1
                P.dma("sp", lrsb[0:16, :], lrT[d], reads=["zT_lr%d" % d], writes=["lrsb"])
                P.dma("sp", upw[0:16, :], gla_up_w[l, d], writes=["upw"])
                P.op("dve", lambda e, d=d: e.tensor_scalar(out=nb, in0=ppsb[:, PP_UPB + d * 2 + hp:PP_UPB + d * 2 + hp + 1], scalar1=-1.0, scalar2=None, op0=ALU.mult),
                     reads=["ppsb"], writes=["nb"])
                psr = Rot([0, 1])
                for b0 in range(0, TT, 512):
                    n = min(512, TT - b0)
                    pi = psr.next()
                    P.op("pe", lambda e, pi=pi, b0=b0, n=n: e.matmul(PS[pi][:, 0:n], lhsT=upw[0:16, hp * 128:(hp + 1) * 128], rhs=lrsb[0:16, b0:b0 + n], start=True, stop=True),
                         reads=["upw", "lrsb"], writes=["ps%d" % pi])
                    P.op("act", lambda e, pi=pi, b0=b0, n=n: e.activation(out=Bc[:, b0:b0 + n], in_=PS[pi][:, 0:n], func=AF.Exp, scale=-1.0, bias=nb),
                         reads=["nb"], writes=["ps%d" % pi, "Bc"])
                P.op("act", lambda e: e.activation(out=Bp, in_=Bc, func=AF.Ln, bias=1.0, scale=1.0), reads=["Bc"], writes=["Bp"])
                if d == 0:
                    P.op("dve", lambda e, sm=sm: e.tensor_tensor_scan(out=Bc, data0=sm, data1=Bp, initial=0.0, op0=ALU.mult, op1=ALU.add),
                         reads=["Bp", "sm"], writes=["Bc"])
                else:
                    P.op("dve", lambda e, sm=sm: e.tensor_tensor_scan(out=Bc[:, ::-1], data0=sm[:, ::-1], data1=Bp[:, ::-1], initial=0.0, op0=ALU.mult, op1=ALU.add),
                         reads=["Bp", "sm"], writes=["Bc"])
                Bc3 = Bc.rearrange("p (c j) -> p c j", j=64)
                endj = 63 if d == 0 else 0
                P.op("act", lambda e, endj=endj, d=d: e.activation(out=gam[d], in_=Bc3[:, :, endj], func=AF.Exp, scale=-1.0 / 16.0), reads=["Bc"], writes=["gam" + ds_])
                P.dma("sp", qk, qT[hp * 128:(hp + 1) * 128, :], reads=["zT_q"], writes=["qk"])
                P.op("act", lambda e: e.activation(out=Bp, in_=Bc, func=AF.Exp, scale=-1.0 / 16.0), reads=["Bc"], writes=["Bp"])
                P.op("dve", lambda e, d=d: e.scalar_tensor_tensor(out=qd[d], in0=qk, scalar=0.125, in1=Bp, op0=ALU.mult, op1=ALU.mult), reads=["qk", "Bp"], writes=["qd" + ds_])
                P.dma("sp", qk, kT[hp * 128:(hp + 1) * 128, :], reads=["zT_k"], writes=["qk"])
                P.op("act", lambda e: e.activation(out=Bp, in_=Bc, func=AF.Exp, scale=1.0 / 16.0), reads=["Bc"], writes=["Bp"])
                P.op("dve", lambda e, d=d: e.tensor_tensor(out=ki[d], in0=qk, in1=Bp, op=ALU.mult), reads=["qk", "Bp"], writes=["ki" + ds_])
                psr = Rot([0, 1])
                for t in range(NT):
                    pi = psr.next()
                    pst = PS[pi][:].bitcast(BF16)
                    P.op("pe", lambda e, pst=pst, t=t, d=d: e.transpose(pst[:, 0:128], ki[d][:, t * 128:(t + 1) * 128], ident_b[:]), reads=["ki" + ds_, "ident_b"], writes=["ps%d" % pi])
                    P.op("act", lambda e, pst=pst, t=t, d=d: e.activation(out=kiT[d][:, t, :], in_=pst[:, 0:128], func=AF.Identity), reads=[], writes=["ps%d" % pi, "kiT" + ds_])
                P.op("dve", lambda e, d=d: e.memset(S[d], 0.0), writes=["S" + ds_])
                P.op("dve", lambda e, d=d: e.memset(Sb[d], 0.0), writes=["Sb" + ds_])

            def chunk_loop(d):
                ds_ = str(d)
                mask = maskF if d == 0 else maskB
                ctx_t = list(range(LC // 128)); lat_t = list(range(LC // 128, NT))
                order = ctx_t + lat_t if d == 0 else ctx_t[::-1] + lat_t[::-1]
                corder = (0, 1) if d == 0 else (1, 0)
                pS = d * 4 + 0; pO = (d * 4 + 1, d * 4 + 2); pD = d * 4 + 3
                for t in order:
                    for hh in range(2):
                        pr = slice(hh * 64, (hh + 1) * 64)
                        P.op("pe", lambda e, pr=pr, t=t: e.matmul(PS[pS][:, 0:128], lhsT=ki[d][pr, t * 128:(t + 1) * 128], rhs=qd[d][pr, t * 128:(t + 1) * 128], start=True, stop=True),
                             reads=["ki" + ds_, "qd" + ds_], writes=["ps%d" % pS])
                        P.op("dve", lambda e, hh=hh: e.tensor_tensor(out=sT[d][hh], in0=PS[pS][:, 0:128], in1=mask[:], op=ALU.mult),
                             reads=["mask"], writes=["ps%d" % pS, "sT%s_%d" % (ds_, hh)])
                        P.op("pe", lambda e, hh=hh, t=t: e.matmul(PS[pO[hh]][:, 0:128], lhsT=vt[:, t, hh * 128:(hh + 1) * 128], rhs=sT[d][hh], start=True, stop=True),
                             reads=["vt", "sT%s_%d" % (ds_, hh)], writes=["ps%d" % pO[hh]])
                    for ci, cc in enumerate(corder):
                        ch = t * 2 + cc
                        cols = slice(t * 128 + cc * 64, t * 128 + cc * 64 + 64)
                        for hh in range(2):
                            pr = slice(hh * 64, (hh + 1) * 64)
                            P.op("pe", lambda e, hh=hh, pr=pr, cols=cols, cc=cc: e.matmul(PS[pO[hh]][:, cc * 64:(cc + 1) * 64], lhsT=Sb[d][pr, :], rhs=qd[d][pr, cols],
                                                                                      start=False, stop=False, skip_group_check=True),
                                 reads=["Sb" + ds_, "qd" + ds_], writes=["ps%d" % pO[hh]])
                        jr = slice(cc * 64, (cc + 1) * 64)
                        for hh in range(2):
                            pr = slice(hh * 64, (hh + 1) * 64)
                            P.op("pe", lambda e, hh=hh, pr=pr, jr=jr, t=t: e.matmul(PS[pD][pr, 0:128], lhsT=kiT[d][jr, t, hh * 64:(hh + 1) * 64], rhs=vt[jr, t, hh * 128:(hh + 1) * 128],
                                                                                 start=True, stop=True),
                                 reads=["kiT" + ds_, "vt"], writes=["ps%d" % pD])
                        P.op("dve", lambda e: e.tensor_tensor(out=tmp[d], in0=PS[pD][:, 0:128], in1=S[d], op=ALU.add), reads=["S" + ds_], writes=["ps%d" % pD, "tmp" + ds_])
                        P.op("dve", lambda e, ch=ch: e.tensor_scalar(out=S[d], in0=tmp[d], scalar1=gam[d][:, ch:ch + 1], scalar2=None, op0=ALU.mult), reads=["tmp" + ds_, "gam" + ds_], writes=["S" + ds_])
                        P.op("act", lambda e, ch=ch: e.activation(out=Sb[d], in_=tmp[d], func=AF.Identity, scale=gam[d][:, ch:ch + 1]), reads=["tmp" + ds_, "gam" + ds_], writes=["Sb" + ds_])
                    for hh in range(2):
                        ko = "oacc%d_%d" % (hh, t)
                        P.op("dve", lambda e, hh=hh, t=t: e.tensor_tensor(out=oacc[:, hh, t * 128:(t + 1) * 128], in0=PS[pO[hh]][:, 0:128], in1=oacc[:, hh, t * 128:(t + 1) * 128], op=ALU.add),
                             reads=[ko], writes=["ps%d" % pO[hh], ko])

            P.interleave([P.capture(lambda: chunk_loop(0)), P.capture(lambda: chunk_loop(1))])
            for hh in range(2):
                P.dma("sp", oT[(hp * 2 + hh) * 128:(hp * 2 + hh + 1) * 128, :], oacc[:, hh, :],
                      reads=["oacc%d_%d" % (hh, t) for t in range(NT)], writes=["oT"])
            P.barrier()

    def lru(l):
        A.reset()
        cst = A.f32(4); cst2 = A.f32(4)
        P.op("act", lambda e: e.activation(out=cst, in_=ppsb[:, PP_LAM:PP_LAM + 4], func=AF.Exp, scale=-1.0), reads=["ppsb"], writes=["cst"])
        P.op("act", lambda e: e.activation(out=cst2, in_=cst, func=AF.Ln, bias=1.0, scale=1.0), reads=["cst"], writes=["cst2"])
        P.op("dve", lambda e: e.tensor_scalar(out=cst, in0=cst2, scalar1=-8.0, scalar2=None, op0=ALU.mult), reads=["cst2"], writes=["cst"])
        x = A.f32(TT); hs = A.f32(TT); th = A.f32(TT)
        xc = [A.f32(TT), A.f32(TT)]; r = [A.f32(TT), A.f32(TT)]; ig = [A.f32(TT), A.f32(TT)]; a = [A.f32(TT), A.f32(TT)]
        Wa = [A.f32(128), A.f32(128)]; Wx = [A.f32(128), A.f32(128)]
        segs = [(0, LC), (LC, TT)]
        for ct in range(2):
            P.dma("sp", x, lxT[ct * 128:(ct + 1) * 128, :], reads=["zT_lx"], writes=["x"])

            def body(d):
                ds_ = str(d)
                col = d * 2 + ct
                kxc, kr, kig, ka, kWa, kWx = "xc" + ds_, "r" + ds_, "ig" + ds_, "a" + ds_, "Wa" + ds_, "Wx" + ds_
                P.op("pool", lambda e: e.memset(Wa[d], 0.0), writes=[kWa])
                P.op("pool", lambda e: e.memset(Wx[d], 0.0), writes=[kWx])
                for bb in range(2):
                    P.dma("sp", Wa[d][bb * 64:(bb + 1) * 64, bb * 64:(bb + 1) * 64], lru_wa[l, d, ct * 2 + bb], writes=[kWa], reads=[kWa])
                    P.dma("sp", Wx[d][bb * 64:(bb + 1) * 64, bb * 64:(bb + 1) * 64], lru_wx[l, d, ct * 2 + bb], writes=[kWx], reads=[kWx])
                wcol = lambda k: ppsb[:, PP_CONVW + (d * 4 + k) * 2 + ct:PP_CONVW + (d * 4 + k) * 2 + ct + 1]
                bcol = ppsb[:, PP_CONVB + col:PP_CONVB + col + 1]
                P.op("dve", lambda e: e.tensor_scalar(out=xc[d], in0=x, scalar1=wcol(3), scalar2=bcol, op0=ALU.mult, op1=ALU.add), reads=["x", "ppsb"], writes=[kxc])
                for (s0, s1) in segs:
                    for sh in (1, 2, 3):
                        k = 3 - sh
                        if d == 0:
                            o_ap, i_ap = xc[d][:, s0 + sh:s1], x[:, s0:s1 - sh]
                        else:
                            o_ap, i_ap = xc[d][:, s0:s1 - sh], x[:, s0 + sh:s1]
                        P.op("dve", lambda e, o_ap=o_ap, i_ap=i_ap, k=k: e.scalar_tensor_tensor(out=o_ap, in0=i_ap, scalar=wcol(k), in1=o_ap, op0=ALU.mult, op1=ALU.add),
                             reads=["x", "ppsb", kxc], writes=[kxc])
                psr = Rot([d * 4 + 0, d * 4 + 1, d * 4 + 2, d * 4 + 3])
                for (Wm, dst, bc0, nm, kW) in ((Wa[d], r[d], PP_BA, kr, kWa), (Wx[d], ig[d], PP_BX, kig, kWx)):
                    for b0 in range(0, TT, 512):
                        n = min(512, TT - b0)
                        pi = psr.next()
                        P.op("pe", lambda e, pi=pi, b0=b0, n=n, Wm=Wm: e.matmul(PS[pi][:, 0:n], lhsT=Wm, rhs=xc[d][:, b0:b0 + n], start=True, stop=True),
                             reads=[kW, kxc], writes=["ps%d" % pi])
                        P.op("act", lambda e, pi=pi, b0=b0, n=n, dst=dst, bc0=bc0: e.activation(out=dst[:, b0:b0 + n], in_=PS[pi][:, 0:n], func=AF.Sigmoid,
                                                                                              bias=ppsb[:, bc0 + col:bc0 + col + 1]),
                             reads=["ppsb"], writes=["ps%d" % pi, nm])
                P.op("act", lambda e: e.activation(out=a[d], in_=r[d], func=AF.Exp, scale=cst[:, col:col + 1]), reads=[kr, "cst"], writes=[ka])
                P.op("pool", lambda e: e.tensor_tensor(out=r[d], in0=a[d], in1=a[d], op=ALU.mult), reads=[ka], writes=[kr])
                P.op("act", lambda e: e.activation(out=r[d], in_=r[d], func=AF.Sqrt, scale=-1.0, bias=1.0), reads=[kr], writes=[kr])
                P.op("dve", lambda e: e.tensor_tensor(out=ig[d], in0=ig[d], in1=r[d], op=ALU.mult), reads=[kig, kr], writes=[kig])
                P.op("dve", lambda e: e.tensor_tensor(out=xc[d], in0=xc[d], in1=ig[d], op=ALU.mult), reads=[kig, kxc], writes=[kxc])
                if d == 0:
                    P.op("dve", lambda e: e.tensor_tensor_scan(out=hs, data0=a[d], data1=xc[d], initial=0.0, op0=ALU.mult, op1=ALU.add), reads=[ka, kxc], writes=["hs"])
                else:
                    P.op("dve", lambda e: e.tensor_tensor_scan(out=th[:, 0:LC][:, ::-1], data0=a[d][:, 0:LC][:, ::-1], data1=xc[d][:, 0:LC][:, ::-1], initial=0.0,
                                                               op0=ALU.mult, op1=ALU.add), reads=[ka, kxc], writes=["th"])
                    P.op("dve", lambda e: e.tensor_tensor_scan(out=th[:, LC:TT][:, ::-1], data0=a[d][:, LC:TT][:, ::-1], data1=xc[d][:, LC:TT][:, ::-1], initial=th[:, 0:1],
                                                               op0=ALU.mult, op1=ALU.add), reads=[ka, kxc, "th"], writes=["th"])

            P.interleave([P.capture(lambda: body(0)), P.capture(lambda: body(1))])
            P.op("dve", lambda e: e.tensor_tensor(out=hs, in0=hs, in1=th, op=ALU.add), reads=["hs", "th"], writes=["hs"])
            P.dma("sp", lruT[ct * 128:(ct + 1) * 128, :], hs, reads=["hs"], writes=["lruT"])

    def s5(l):
        A.reset()
        NG = 16
        prm = A.f32(3 * NG).rearrange("p (a g) -> p a g", a=3)
        Bsb = A.f32(2 * NG * 16).rearrange("p (a g h) -> p a g h", a=2, g=NG)
        Csb = A.f32(2 * NG * 16).rearrange("p (a g h) -> p a g h", a=2, g=NG)
        P.dma("sp", prm, s5p[l], writes=["prm"])
        P.dma("sp", Bsb, s5b[l], writes=["Bsb"])
        P.dma("sp", Csb, s5c[l], writes=["Csb"])
        tauf = A.f32(2 * NC8)
        tau = tauf.rearrange("p (d c) -> p d c", d=2)
        P.dma("sp", tauf, tau_in.partition_broadcast(128), writes=["tau"])
        dt = A.f32(NG); lrdt = A.f32(NG); th = A.f32(NG); u8 = A.f32(NG); rho8 = A.f32(NG)
        t1 = A.f32(NG); t2 = A.f32(NG); t3 = A.f32(NG); den = A.f32(NG); cr = A.f32(NG); ci = A.f32(NG)
        J = jidx[:].rearrange("p d g j -> p (d g) j")
        mg = A.f32(NG * 9).rearrange("p (g j) -> p g j", j=9)
        xa = A.f32(NG * 9).rearrange("p (g j) -> p g j", j=9)
        xr = A.f32(NG * 9).rearrange("p (g j) -> p g j", j=9)
        sn = A.f32(NG * 9).rearrange("p (g j) -> p g j", j=9)
        cs_ = A.f32(NG * 9).rearrange("p (g j) -> p g j", j=9)
        ar = A.f32(NG * 9).rearrange("p (g j) -> p g j", j=9)
        ai = A.f32(NG * 9).rearrange("p (g j) -> p g j", j=9)
        mr = A.f32(NG * 8).rearrange("p (g j) -> p g j", j=8)
        mi = A.f32(NG * 8).rearrange("p (g j) -> p g j", j=8)
        br_ = A.f32(NG * 8).rearrange("p (g j) -> p g j", j=8)
        bi_ = A.f32(NG * 8).rearrange("p (g j) -> p g j", j=8)
        w1 = A.f32(NG * 9).rearrange("p (g j) -> p g j", j=9)
        K = ["prm", "s5t"]

        def op(eng, fn):
            P.op(eng, fn, reads=K, writes=["s5t"])

        def bg(v, n):
            return v.unsqueeze(2).to_broadcast([128, NG, n])

        P._cap = []
        op("act", lambda e: e.activation(out=dt, in_=prm[:, 2, :], func=AF.Exp))
        op("dve", lambda e: e.tensor_tensor(out=lrdt, in0=prm[:, 0, :], in1=dt, op=ALU.mult))
        op("dve", lambda e: e.tensor_tensor(out=th, in0=prm[:, 1, :], in1=dt, op=ALU.mult))
        op("dve", lambda e: e.tensor_scalar(out=th, in0=th, scalar1=1.0 / TWO_PI, scalar2=None, op0=ALU.mult))
        op("dve", lambda e: e.tensor_tensor(out=mg, in0=J, in1=bg(lrdt, 9), op=ALU.mult))
        op("act", lambda e: e.activation(out=mg, in_=mg, func=AF.Exp))
        op("dve", lambda e: e.tensor_tensor(out=xa, in0=J, in1=bg(th, 9), op=ALU.mult))
        op("dve", lambda e: e.tensor_scalar(out=xr, in0=xa, scalar1=MAGIC, scalar2=-MAGIC, op0=ALU.add, op1=ALU.add))
        op("dve", lambda e: e.tensor_tensor(out=w1, in0=xa, in1=xr, op=ALU.subtract))
        op("act", lambda e: e.activation(out=sn, in_=w1, func=AF.Sin, scale=TWO_PI))
        op("dve", lambda e: e.tensor_scalar(out=xa, in0=xa, scalar1=0.25, scalar2=None, op0=ALU.add))
        op("dve", lambda e: e.tensor_scalar(out=xr, in0=xa, scalar1=MAGIC, scalar2=-MAGIC, op0=ALU.add, op1=ALU.add))
        op("dve", lambda e: e.tensor_tensor(out=w1, in0=xa, in1=xr, op=ALU.subtract))
        op("act", lambda e: e.activation(out=cs_, in_=w1, func=AF.Sin, scale=TWO_PI))
        op("dve", lambda e: e.tensor_tensor(out=ar, in0=mg, in1=cs_, op=ALU.mult))
        op("dve", lambda e: e.tensor_tensor(out=ai, in0=mg, in1=sn, op=ALU.mult))
        op("dve", lambda e: e.tensor_tensor(out=w1, in0=mg, in1=mg, op=ALU.mult))
        op("dve", lambda e: e.reciprocal(out=w1, in_=w1))
        op("dve", lambda e: e.tensor_tensor(out=mr, in0=ar[:, :, 0:8], in1=w1[:, :, 0:8], op=ALU.mult))
        op("dve", lambda e: e.scalar_tensor_tensor(out=mi, in0=ai[:, :, 0:8], scalar=-1.0, in1=w1[:, :, 0:8], op0=ALU.mult, op1=ALU.mult))
        a1r = A.f32(NG); a1i = A.f32(NG)
        op("dve", lambda e: e.tensor_copy(out=a1r[:, 0:8], in_=ar[:, 0:8, 1]))
        op("dve", lambda e: e.tensor_copy(out=a1r[:, 8:16], in_=ar[:, 8:16, 6]))
        op("dve", lambda e: e.tensor_copy(out=a1i[:, 0:8], in_=ai[:, 0:8, 1]))
        op("dve", lambda e: e.tensor_copy(out=a1i[:, 8:16], in_=ai[:, 8:16, 6]))
        lr_ = prm[:, 0, :]; li_ = prm[:, 1, :]
        op("dve", lambda e: e.tensor_tensor(out=den, in0=lr_, in1=lr_, op=ALU.mult))
        op("dve", lambda e: e.tensor_tensor(out=t1, in0=li_, in1=li_, op=ALU.mult))
        op("dve", lambda e: e.tensor_tensor(out=den, in0=den, in1=t1, op=ALU.add))
        op("dve", lambda e: e.reciprocal(out=den, in_=den))
        op("dve", lambda e: e.tensor_scalar(out=t1, in0=a1r, scalar1=-1.0, scalar2=None, op0=ALU.add))
        op("dve", lambda e: e.tensor_tensor(out=t2, in0=t1, in1=lr_, op=ALU.mult))
        op("dve", lambda e: e.tensor_tensor(out=t3, in0=a1i, in1=li_, op=ALU.mult))
        op("dve", lambda e: e.tensor_tensor(out=t2, in0=t2, in1=t3, op=ALU.add))
        op("dve", lambda e: e.tensor_tensor(out=cr, in0=t2, in1=den, op=ALU.mult))
        op("dve", lambda e: e.tensor_tensor(out=t2, in0=a1i, in1=lr_, op=ALU.mult))
        op("dve", lambda e: e.tensor_tensor(out=t3, in0=t1, in1=li_, op=ALU.mult))
        op("dve", lambda e: e.tensor_tensor(out=t2, in0=t2, in1=t3, op=ALU.subtract))
        op("dve", lambda e: e.tensor_tensor(out=ci, in0=t2, in1=den, op=ALU.mult))
        w8a = A.f32(NG * 8).rearrange("p (g j) -> p g j", j=8)
        op("dve", lambda e: e.tensor_tensor(out=br_, in0=mr, in1=bg(cr, 8), op=ALU.mult))
        op("dve", lambda e: e.tensor_tensor(out=w8a, in0=mi, in1=bg(ci, 8), op=ALU.mult))
        op("dve", lambda e: e.tensor_tensor(out=br_, in0=br_, in1=w8a, op=ALU.subtract))
        op("dve", lambda e: e.tensor_tensor(out=bi_, in0=mr, in1=bg(ci, 8), op=ALU.mult))
        op("dve", lambda e: e.tensor_tensor(out=w8a, in0=mi, in1=bg(cr, 8), op=ALU.mult))
        op("dve", lambda e: e.tensor_tensor(out=bi_, in0=bi_, in1=w8a, op=ALU.add))
        op("dve", lambda e: e.tensor_copy(out=rho8, in_=mg[:, :, 8]))
        op("dve", lambda e: e.tensor_scalar(out=u8, in0=th, scalar1=8.0, scalar2=None, op0=ALU.mult))
        op("dve", lambda e: e.tensor_scalar(out=t1, in0=u8, scalar1=MAGIC, scalar2=-MAGIC, op0=ALU.add, op1=ALU.add))
        op("dve", lambda e: e.tensor_tensor(out=u8, in0=u8, in1=t1, op=ALU.subtract))
        SZ = NG * 8 * 16
        Btr = A.f32(SZ).rearrange("p (g j h) -> p g j h", g=NG, j=8)
        Bti = A.f32(SZ).rearrange("p (g j h) -> p g j h", g=NG, j=8)
        Ctr = A.f32(SZ).rearrange("p (g j h) -> p g j h", g=NG, j=8)
        Cti = A.f32(SZ).rearrange("p (g j h) -> p g j h", g=NG, j=8)
        regB = A.f32(4 * 2048)
        wk = regB[:, 0:2048].rearrange("p (g j h) -> p g j h", g=NG, j=8)

        def bj(v):
            return v.unsqueeze(3).to_broadcast([128, NG, 8, 16])

        def bh(v):
            return v.unsqueeze(2).to_broadcast([128, NG, 8, 16])

        Br, Bi = Bsb[:, 0], Bsb[:, 1]
        Cr, Ci = Csb[:, 0], Csb[:, 1]
        KB = ["s5t", "Bsb", "Csb", "s5m"]

        def opb(fn):
            P.op("dve", fn, reads=KB, writes=["s5m"])

        opb(lambda e: e.tensor_tensor(out=Btr, in0=bj(br_), in1=bh(Br), op=ALU.mult))
        opb(lambda e: e.tensor_tensor(out=wk, in0=bj(bi_), in1=bh(Bi), op=ALU.mult))
        opb(lambda e: e.tensor_tensor(out=Btr, in0=Btr, in1=wk, op=ALU.subtract))
        opb(lambda e: e.tensor_tensor(out=Bti, in0=bj(br_), in1=bh(Bi), op=ALU.mult))
        opb(lambda e: e.tensor_tensor(out=wk, in0=bj(bi_), in1=bh(Br), op=ALU.mult))
        opb(lambda e: e.tensor_tensor(out=Bti, in0=Bti, in1=wk, op=ALU.add))
        opb(lambda e: e.tensor_tensor(out=Ctr, in0=bj(ar[:, :, 0:8]), in1=bh(Cr), op=ALU.mult))
        opb(lambda e: e.tensor_tensor(out=wk, in0=bj(ai[:, :, 0:8]), in1=bh(Ci), op=ALU.mult))
        opb(lambda e: e.tensor_tensor(out=Ctr, in0=Ctr, in1=wk, op=ALU.subtract))
        opb(lambda e: e.tensor_tensor(out=Cti, in0=bj(ai[:, :, 0:8]), in1=bh(Cr), op=ALU.mult))
        opb(lambda e: e.tensor_tensor(out=wk, in0=bj(ar[:, :, 0:8]), in1=bh(Ci), op=ALU.mult))
        opb(lambda e: e.tensor_tensor(out=Cti, in0=Cti, in1=wk, op=ALU.add))
        opb(lambda e: e.tensor_scalar(out=Cti, in0=Cti, scalar1=-1.0, scalar2=None, op0=ALU.mult))
        NCT = (NC8 + 127) // 128
        U8 = A.f32(16 * NC8).rearrange("p (g c) -> p g c", g=16)
        cst_ = [regB[:, 2048:4096], regB[:, 4096:6144]]
        Ug = regB[:, 6144:8192]
        Yst = cst_

        def chunk_tiles():
            tiles = []
            c = 0
            while c < NC8:
                n = min(128, NC8 - c)
                tiles.append((c, n))
                c += n
            return tiles

        def chunk_dram(base, c0, n):
            pieces = []
            c = c0
            while c < c0 + n:
                if c < LC8:
                    m = min(c0 + n, LC8) - c
                    ap = base[c * 8:(c + m) * 8, :].rearrange("(c i) h -> c i h", i=8)
                    pieces.append((c - c0, m, ap))
                    c += m
                else:
                    cl = c - LC8
                    col, rb = cl // RB, cl % RB
                    m = min(RB - rb, c0 + n - c)
                    lat = base[LC:TT, :].rearrange("(rb i w) h -> w rb i h", i=8, w=64)
                    ap = lat[col, rb:rb + m, :, :]
                    pieces.append((c - c0, m, ap))
                    c += m
            return pieces

        cap_prep = P._cap
        P._cap = []
        cap_u8 = P._cap
        psr = Rot([0, 1, 2, 3])
        for ti, (c0, n) in enumerate(chunk_tiles()):
            cs = cst_[ti % 2]
            cs3 = cs.rearrange("p (i h) -> p i h", i=8)
            for (p0, m, ap) in chunk_dram(su_tok, c0, n):
                P.dma("sp", cs3[p0:p0 + m, :, :], ap, reads=["su_tok"], writes=["cst%d" % (ti % 2)])
            P.op("dve", lambda e, n=n, cs=cs: e.tensor_copy(out=Ug[0:n].rearrange("p (g i h) -> p g i h", g=16, i=8),
                                                          in_=cs[0:n].rearrange("p (i g h) -> p g i h", i=8, g=16)),
                 reads=["cst%d" % (ti % 2)], writes=["Ug"])
            for g in range(16):
                pi = psr.next()
                P.op("pe", lambda e, pi=pi, g=g, n=n: e.transpose(PS[pi][:, 0:n], Ug[0:n, g * 128:(g + 1) * 128], ident_f[0:n, 0:n]),
                     reads=["Ug", "ident_f"], writes=["ps%d" % pi])
                P.op("act", lambda e, pi=pi, g=g, n=n, c0=c0: e.activation(out=U8[:, g, c0:c0 + n], in_=PS[pi][:, 0:n], func=AF.Identity),
                     reads=[], writes=["ps%d" % pi, "U8"])
        P._cap = None
        P.interleave([cap_prep, cap_u8])
        P.barrier()
        halves = [(0, min(512, NC8))] + ([(512, NC8)] if NC8 > 512 else [])

        def carve(base):
            o = [0]

            def take(n):
                ap = base[:, o[0]:o[0] + n]
                o[0] += n
                return ap
            d_ = {}
            d_["BtT"] = take(256).rearrange("p (a s) -> p a s", a=2)
            d_["M8"] = take(256).rearrange("p (g c) -> p g c", g=2)
            for nm in ("Zr", "Zi", "Wr", "Wi", "Or", "Oi", "Xr", "Xi", "tA", "tB", "Cn", "Sn"):
                d_[nm] = take(NC8)
            return d_

        SETW = 512 + 12 * NC8
        sets = [carve(A.f32(SETW)), carve(regB)]
        Yall = A.f32(16 * NC8).rearrange("p (g c) -> p g c", g=16)
        psr = Rot([0, 1, 2, 3, 4, 5, 6, 7])

        def front(it):
            d, gp = divmod(it, 8)
            dg = it
            par = str(it % 2)
            S_ = sets[it % 2]
            BtT, M8, Zr, Zi, Cn, Sn = S_["BtT"], S_["M8"], S_["Zr"], S_["Zi"], S_["Cn"], S_["Sn"]
            xx, rr = S_["Wr"], S_["Wi"]
            m8 = m8F if d == 0 else m8B
            for a_, Bt in enumerate((Btr, Bti)):
                pi = psr.next()
                P.op("pe", lambda e, pi=pi, Bt=Bt: e.transpose(PS[pi][:, 0:128], Bt[:, dg].rearrange("p j h -> p (j h)"), ident_f[:]),
                     reads=["s5m", "ident_f"], writes=["ps%d" % pi])
                P.op("act", lambda e, pi=pi, a_=a_: e.activation(out=BtT[:, a_, :], in_=PS[pi][:, 0:128], func=AF.Identity), reads=[], writes=["ps%d" % pi, "BtT" + par])
            for gm in range(2):
                pi = psr.next()
                pr = slice(gm * 64, (gm + 1) * 64)
                for a_, (Bt, Ct) in enumerate(((Btr, Ctr), (Bti, Cti))):
                    P.op("pe", lambda e, pi=pi, gm=gm, pr=pr, Bt=Bt, Ct=Ct, a_=a_: e.matmul(PS[pi][:, 0:128], lhsT=Bt[pr, dg].rearrange("p j h -> p (j h)"),
                                                                                         rhs=Ct[pr, dg].rearrange("p j h -> p (j h)"), start=(a_ == 0), stop=(a_ == 1)),
                         reads=["s5m"], writes=["ps%d" % pi])
                P.op("dve", lambda e, pi=pi, m8=m8, gm=gm: e.tensor_tensor(out=M8[:, gm, :], in0=PS[pi][:, 0:128], in1=m8[:, 0:128], op=ALU.mult),
                     reads=["m8"], writes=["ps%d" % pi, "M8" + par])
            for (Zt, a_) in ((Zr, 0), (Zi, 1)):
                for (h0, h1) in halves:
                    pi = psr.next()
                    for gm in range(2):
                        g = gp * 2 + gm
                        P.op("pe", lambda e, pi=pi, gm=gm, g=g, a_=a_, h0=h0, h1=h1: e.matmul(PS[pi][gm * 64:(gm + 1) * 64, 0:h1 - h0], lhsT=BtT[:, a_, gm * 64:(gm + 1) * 64],
                                                                                           rhs=U8[:, g, h0:h1], start=True, stop=True),
                             reads=["BtT" + par, "U8"], writes=["ps%d" % pi])
                    P.op("act", lambda e, pi=pi, Zt=Zt, h0=h0, h1=h1: e.activation(out=Zt[:, h0:h1], in_=PS[pi][:, 0:h1 - h0], func=AF.Identity),
                         reads=[], writes=["ps%d" % pi, "Z" + par])
            ucol = u8[:, dg:dg + 1]
            kx, kr = "Wr" + par, "Wi" + par
            P.op("dve", lambda e, ucol=ucol: e.tensor_scalar(out=xx, in0=tau[:, d, :], scalar1=ucol, scalar2=None, op0=ALU.mult), reads=["tau", "s5t"], writes=[kx])
            P.op("dve", lambda e: e.tensor_scalar(out=rr, in0=xx, scalar1=MAGIC, scalar2=-MAGIC, op0=ALU.add, op1=ALU.add), reads=[kx], writes=[kr])
            P.op("dve", lambda e: e.tensor_tensor(out=rr, in0=xx, in1=rr, op=ALU.subtract), reads=[kx, kr], writes=[kr])
            P.op("act", lambda e: e.activation(out=Sn, in_=rr, func=AF.Sin, scale=TWO_PI), reads=[kr], writes=["Sn" + par])
            P.op("dve", lambda e: e.tensor_scalar(out=xx, in0=xx, scalar1=0.25, scalar2=None, op0=ALU.add), reads=[kx], writes=[kx])
            P.op("dve", lambda e: e.tensor_scalar(out=rr, in0=xx, scalar1=MAGIC, scalar2=-MAGIC, op0=ALU.add, op1=ALU.add), reads=[kx, "Sn" + par], writes=[kr])
            P.op("dve", lambda e: e.tensor_tensor(out=rr, in0=xx, in1=rr, op=ALU.subtract), reads=[kx, kr], writes=[kr])
            P.op("act", lambda e: e.activation(out=Cn, in_=rr, func=AF.Sin, scale=TWO_PI), reads=[kr], writes=["Cn" + par])

        def back(it):
            d, gp = divmod(it, 8)
            dg = it
            par = str(it % 2)
            S_ = sets[it % 2]
            M8, Zr, Zi, Wr, Wi, Or, Oi = S_["M8"], S_["Zr"], S_["Zi"], S_["Wr"], S_["Wi"], S_["Or"], S_["Oi"]
            Xr, Xi, tA, tB, Cn, Sn = S_["Xr"], S_["Xi"], S_["tA"], S_["tB"], S_["Cn"], S_["Sn"]
            kC, kS, kZ, kWr, kWi, kA, kB, kX = "Cn" + par, "Sn" + par, "Z" + par, "Wr" + par, "Wi" + par, "tA" + par, "tB" + par, "X" + par
            P.op("dve", lambda e: e.tensor_tensor(out=Wr, in0=Cn, in1=Zr, op=ALU.mult), reads=[kC, kZ], writes=[kWr])
            P.op("dve", lambda e: e.tensor_tensor(out=Wi, in0=Cn, in1=Zi, op=ALU.mult), reads=[kC, kZ], writes=[kWi])
            P.op("dve", lambda e: e.tensor_tensor(out=tA, in0=Sn, in1=Zi, op=ALU.mult), reads=[kS, kZ], writes=[kA])
            P.op("dve", lambda e: e.tensor_tensor(out=tB, in0=Sn, in1=Zr, op=ALU.mult), reads=[kS, kZ], writes=[kB])
            P.op("dve", lambda e: e.tensor_tensor(out=Wr, in0=Wr, in1=tA, op=ALU.add), reads=[kWr, kA], writes=[kWr])
            P.op("dve", lambda e: e.tensor_tensor(out=Wi, in0=Wi, in1=tB, op=ALU.subtract), reads=[kWi, kB], writes=[kWi])
            rcol = rho8[:, dg:dg + 1]
            for (Wt, Ot, nm) in ((Wr, Or, "Or" + par), (Wi, Oi, "Oi" + par)):
                if d == 0:
                    P.op("dve", lambda e, Wt=Wt, Ot=Ot, rcol=rcol: e.tensor_tensor_scan(out=Ot, data0=Wt, data1=rcol.to_broadcast([128, NC8]), initial=0.0, op0=ALU.add, op1=ALU.mult),
                         reads=[kWr, kWi, "s5t"], writes=[nm])
                else:
                    P.op("dve", lambda e, Wt=Wt, Ot=Ot, rcol=rcol: e.tensor_tensor_scan(out=Ot[:, 0:LC8][:, ::-1], data0=Wt[:, 0:LC8][:, ::-1], data1=rcol.to_broadcast([128, LC8]),
                                                                                     initial=0.0, op0=ALU.add, op1=ALU.mult),
                         reads=[kWr, kWi, "s5t"], writes=[nm])
                    P.op("dve", lambda e, Wt=Wt, Ot=Ot, rcol=rcol: e.tensor_tensor_scan(out=Ot[:, LC8:NC8][:, ::-1], data0=Wt[:, LC8:NC8][:, ::-1], data1=rcol.to_broadcast([128, NC8 - LC8]),
                                                                                     initial=Ot[:, 0:1], op0=ALU.add, op1=ALU.mult),
                         reads=[kWr, kWi, "s5t", nm], writes=[nm])
            if d == 0:
                sh = [(slice(1, NC8), slice(0, NC8 - 1))]
                zero_cols = [0]
                carry = None
            else:
                sh = [(slice(0, LC8 - 1), slice(1, LC8)), (slice(LC8, NC8 - 1), slice(LC8 + 1, NC8))]
                zero_cols = [LC8 - 1]
                carry = (NC8 - 1, 0)
            RK = ["Or" + par, "Oi" + par, kC, kS, kX, kA, kB]
            pairs = list(sh)
            if carry is not None:
                dc, sc = carry
                pairs.append((slice(dc, dc + 1), slice(sc, sc + 1)))
            kXr, kXi, kOr, kOi = "Xr" + par, "Xi" + par, "Or" + par, "Oi" + par
            for (do, so) in pairs:
                P.op("dve", lambda e, do=do, so=so: e.tensor_tensor(out=Xr[:, do], in0=Cn[:, do], in1=Or[:, so], op=ALU.mult), reads=[kC, kOr, kX], writes=[kXr])
                P.op("dve", lambda e, do=do, so=so: e.tensor_tensor(out=Xi[:, do], in0=Cn[:, do], in1=Oi[:, so], op=ALU.mult), reads=[kC, kOi, kX], writes=[kXi])
                P.op("dve", lambda e, do=do, so=so: e.tensor_tensor(out=tA[:, do], in0=Sn[:, do], in1=Oi[:, so], op=ALU.mult), reads=[kS, kOi], writes=[kA])
                P.op("dve", lambda e, do=do, so=so: e.tensor_tensor(out=tB[:, do], in0=Sn[:, do], in1=Or[:, so], op=ALU.mult), reads=[kS, kOr], writes=[kB])
                P.op("dve", lambda e, do=do, so=so: e.tensor_tensor(out=Xr[:, do], in0=Xr[:, do], in1=tA[:, do], op=ALU.subtract), reads=[kXr, kA], writes=[kXr])
                P.op("dve", lambda e, do=do, so=so: e.tensor_tensor(out=Xi[:, do], in0=Xi[:, do], in1=tB[:, do], op=ALU.add), reads=[kXi, kB], writes=[kXi])
            for zc in zero_cols:
                P.op("dve", lambda e, zc=zc: e.memset(Xr[:, zc:zc + 1], 0.0), reads=[kX], writes=[kXr])
                P.op("dve", lambda e, zc=zc: e.memset(Xi[:, zc:zc + 1], 0.0), reads=[kX], writes=[kXi])
            for gm in range(2):
                g = gp * 2 + gm
                pr = slice(gm * 64, (gm + 1) * 64)
                for (h0, h1) in halves:
                    pi = psr.next()
                    P.op("pe", lambda e, pi=pi, gm=gm, g=g, h0=h0, h1=h1: e.matmul(PS[pi][:, 0:h1 - h0], lhsT=M8[:, gm, :], rhs=U8[:, g, h0:h1], start=True, stop=False),
                         reads=["M8" + par, "U8"], writes=["ps%d" % pi])
                    P.op("pe", lambda e, pi=pi, pr=pr, h0=h0, h1=h1: e.matmul(PS[pi][:, 0:h1 - h0], lhsT=Ctr[pr, dg].rearrange("p j h -> p (j h)"), rhs=Xr[pr, h0:h1], start=False, stop=False),
                         reads=["s5m", kXr], writes=["ps%d" % pi, kX])
                    P.op("pe", lambda e, pi=pi, pr=pr, h0=h0, h1=h1: e.matmul(PS[pi][:, 0:h1 - h0], lhsT=Cti[pr, dg].rearrange("p j h -> p (j h)"), rhs=Xi[pr, h0:h1], start=False, stop=True),
                         reads=["s5m", kXi], writes=["ps%d" % pi, kX])
                    if d == 0:
                        P.op("act", lambda e, pi=pi, g=g, h0=h0, h1=h1: e.activation(out=Yall[:, g, h0:h1], in_=PS[pi][:, 0:h1 - h0], func=AF.Identity),
                             reads=[], writes=["ps%d" % pi, "Yall"])
                    else:
                        P.op("dve", lambda e, pi=pi, g=g, h0=h0, h1=h1: e.tensor_tensor(out=Yall[:, g, h0:h1], in0=PS[pi][:, 0:h1 - h0], in1=Yall[:, g, h0:h1], op=ALU.add),
                             reads=["Yall"], writes=["ps%d" % pi, "Yall"])

        front(0)
        for it in range(16):
            if it + 1 < 16:
                P.interleave([P.capture(lambda: back(it)), P.capture(lambda: front(it + 1))])
            else:
                back(it)
        P.barrier()
        psr = Rot([0, 1, 2, 3])
        for ti, (c0, n) in enumerate(chunk_tiles()):
            ys = Yst[ti % 2]
            ys3 = ys.rearrange("p (i h) -> p i h", i=8)
            for g in range(16):
                pi = psr.next()
                P.op("pe", lambda e, pi=pi, g=g, n=n, c0=c0: e.transpose(PS[pi][0:n, 0:128], Yall[:, g, c0:c0 + n], ident_f[:]),
                     reads=["Yall", "ident_f"], writes=["ps%d" % pi])
                P.op("act", lambda e, pi=pi, g=g, n=n, ys3=ys3: e.activation(out=ys3[0:n, :, g * 16:(g + 1) * 16], in_=PS[pi][0:n, 0:128].rearrange("p (j h) -> p j h", j=8), func=AF.Identity),
                     reads=[], writes=["ps%d" % pi, "yst%d" % (ti % 2)])
            for (p0, m, ap) in chunk_dram(s5y, c0, n):
                P.dma("sp", ap, ys3[p0:p0 + m, :, :], reads=["yst%d" % (ti % 2)], writes=["s5y"])

    def phaseC1(l, last):
        A.reset()
        xsrc = xin if l == 0 else xres
        w1p = A.bf16(KT * 4 * D).rearrange("p (k n) -> p k n", k=KT)
        w2p = A.bf16(32 * D).rearrange("p (k n) -> p k n", k=32)
        w1v = w_ff1[l].rearrange("(kt p) n -> p kt n", p=128)
        w2v = w_ff2[l].rearrange("(kt p) n -> p kt n", p=128)
        pre = []
        for c0 in range(0, 4 * D, 512):
            pre.append((w1p[:, :, c0:c0 + 512], w1v[:, :, c0:c0 + 512], "w1"))
        for k0 in range(0, 32, 4):
            for c0 in range(0, D, 512):
                pre.append((w2p[:, k0:k0 + 4, c0:c0 + 512], w2v[:, k0:k0 + 4, c0:c0 + 512], "w2"))
        wo = A.bf16(KT * D).rearrange("p (k n) -> p k n", k=KT)
        wv = w_out[l].rearrange("(kt p) n -> p kt n", p=128)
        for c0 in range(0, D, 512):
            P.dma("poolq", wo[:, :, c0:c0 + 512], wv[:, :, c0:c0 + 512], writes=["wo"])
        glu = A.bf16(2 * 256).rearrange("p (k n) -> p k n", k=2)
        P.dma("poolq", glu, s5_glu_w[l].rearrange("(kt p) n -> p kt n", p=128), writes=["glu"])
        g1t = A.f32(D)
        g1bc = {0: g1t, 1: g1t}
        cur_w = [None]

        def load_g1(w):
            if cur_w[0] == w:
                return
            cur_w[0] = w
            P.dma("sp", g1t, mraw[l, w:w + 1, 2 * D:3 * D].partition_broadcast(128), reads=["mraw"], writes=["bc"])

        dbc = A.f32(256)
        P.dma("sp", dbc, s5_d[l:l + 1, :].partition_broadcast(128), writes=["bcd"])
        NB = 256
        o4 = A.f32(4 * NB).rearrange("p (h t) -> p h t", h=4)
        g4 = A.f32(4 * NB).rearrange("p (h t) -> p h t", h=4)
        sq = A.bf16(4 * NB).rearrange("p (h t) -> p h t", h=4)
        rn = A.f32(4 * NB).rearrange("p (h t) -> p h t", h=4)
        lh = A.f32(2 * NB).rearrange("p (h t) -> p h t", h=2)
        lg = A.f32(2 * NB).rearrange("p (h t) -> p h t", h=2)
        cat = A.bf16(KT * NB).rearrange("p (k t) -> p k t", k=KT)
        ysb = A.f32(2 * 256).rearrange("p (j c) -> p j c", j=2)
        usb = A.f32(2 * 256).rearrange("p (j c) -> p j c", j=2)
        sb16 = A.bf16(2 * 256).rearrange("p (j c) -> p j c", j=2)
        sTt = A.bf16(2 * NB).rearrange("p (k t) -> p k t", k=2)
        gsig = A.f32(2 * NB).rearrange("p (k t) -> p k t", k=2)
        xt = [A.f32(D), A.f32(D)]
        xo = [A.f32(D), A.f32(D)]
        t0 = 0 if not last else LC
        psr = Rot([0, 1, 2, 3, 4, 5, 6, 7])
        while t0 < TT:
            w = 1 if t0 < LC else 0
            n = min(NB, (LC if w == 1 else TT) - t0)
            tk = slice(t0, t0 + n)
            load_g1(w)
            for _ in range(3):
                if pre:
                    o_, i_, k_ = pre.pop(0)
                    P.dma("poolq", o_, i_, writes=[k_])
            P.dma("sp", o4[:, :, 0:n], oT.rearrange("(h p) t -> p h t", p=128)[:, :, tk], reads=["oT"], writes=["o4"])
            P.dma("sp", g4[:, :, 0:n], ggT.rearrange("(h p) t -> p h t", p=128)[:, :, tk], reads=["zT_gg"], writes=["g4"])
            P.op("pool", lambda e, n=n: e.tensor_tensor(out=sq[:, :, 0:n], in0=o4[:, :, 0:n], in1=o4[:, :, 0:n], op=ALU.mult), reads=["o4"], writes=["sq"])
            P.op("act", lambda e, n=n: e.activation(out=g4[:, :, 0:n], in_=g4[:, :, 0:n], func=AF.Silu), reads=["g4"], writes=["g4"])
            for h in range(4):
                pi = psr.next()
                P.op("pe", lambda e, pi=pi, h=h, n=n: e.matmul(PS[pi][:, 0:n], lhsT=ones_b[:], rhs=sq[:, h, 0:n], start=True, stop=True), reads=["sq", "ones_b"], writes=["ps%d" % pi])
                P.op("act", lambda e, pi=pi, h=h, n=n: e.activation(out=rn[:, h, 0:n], in_=PS[pi][:, 0:n], func=AF.Sqrt, scale=1.0 / 128.0, bias=EPS), reads=[], writes=["ps%d" % pi, "rn"])
            P.op("dve", lambda e, n=n: e.reciprocal(out=rn[:, :, 0:n], in_=rn[:, :, 0:n]), reads=["rn"], writes=["rn"])
            P.op("dve", lambda e, n=n: e.tensor_tensor(out=o4[:, :, 0:n], in0=o4[:, :, 0:n], in1=rn[:, :, 0:n], op=ALU.mult), reads=["o4", "rn"], writes=["o4"])
            for h in range(4):
                P.op("dve", lambda e, h=h, n=n: e.scalar_tensor_tensor(out=cat[:, h, 0:n], in0=o4[:, h, 0:n], scalar=ppsb[:, PP_GNORM + h:PP_GNORM + h + 1], in1=g4[:, h, 0:n],
                                                                       op0=ALU.mult, op1=ALU.mult), reads=["o4", "g4", "ppsb"], writes=["cat"])
            P.dma("sp", lh[:, :, 0:n], lruT.rearrange("(h p) t -> p h t", p=128)[:, :, tk], reads=["lruT"], writes=["lh"])
            P.dma("sp", lg[:, :, 0:n], lgT.rearrange("(h p) t -> p h t", p=128)[:, :, tk], reads=["zT_lg"], writes=["lg"])
            P.op("act", lambda e, n=n: e.activation(out=lg[:, :, 0:n], in_=lg[:, :, 0:n], func=AF.Gelu), reads=["lg"], writes=["lg"])
            P.op("dve", lambda e, n=n: e.tensor_tensor(out=cat[:, 4:6, 0:n], in0=lh[:, :, 0:n], in1=lg[:, :, 0:n], op=ALU.mult), reads=["lh", "lg"], writes=["cat"])
            nj = n // 128
            P.dma("sp", ysb[:, 0:nj, :], s5y[tk, :].rearrange("(j p) c -> p j c", p=128), reads=["s5y"], writes=["ysb"])
            P.dma("sp", usb[:, 0:nj, :], su_tok[tk, :].rearrange("(j p) c -> p j c", p=128), reads=["su_tok"], writes=["usb"])
            P.op("dve", lambda e, nj=nj: e.tensor_tensor(out=usb[:, 0:nj, :], in0=usb[:, 0:nj, :], in1=dbc.unsqueeze(1).to_broadcast([128, nj, 256]), op=ALU.mult), reads=["usb", "bcd"], writes=["usb"])
            P.op("dve", lambda e, nj=nj: e.tensor_tensor(out=ysb[:, 0:nj, :], in0=ysb[:, 0:nj, :], in1=usb[:, 0:nj, :], op=ALU.add), reads=["usb", "ysb"], writes=["ysb"])
            P.op("act", lambda e, nj=nj: e.activation(out=sb16[:, 0:nj, :], in_=ysb[:, 0:nj, :], func=AF.Gelu), reads=["ysb"], writes=["sb16"])
            for j in range(nj):
                pi = psr.next()
                pst = PS[pi][:].bitcast(BF16)
                for k in range(2):
                    P.op("pe", lambda e, pst=pst, j=j, k=k: e.transpose(pst[:, k * 128:(k + 1) * 128], sb16[:, j, k * 128:(k + 1) * 128], ident_b[:]), reads=["sb16", "ident_b"], writes=["ps%d" % pi])
                P.op("act", lambda e, pst=pst, j=j: e.activation(out=sTt[:, :, j * 128:(j + 1) * 128], in_=pst[:, 0:256].rearrange("p (k t) -> p k t", k=2), func=AF.Identity),
                     reads=[], writes=["ps%d" % pi, "sTt"])
            for ko in range(2):
                pi = psr.next()
                for ki_ in range(2):
                    P.op("pe", lambda e, pi=pi, ko=ko, ki_=ki_, n=n: e.matmul(PS[pi][:, 0:n], lhsT=glu[:, ki_, ko * 128:(ko + 1) * 128], rhs=sTt[:, ki_, 0:n], start=(ki_ == 0), stop=(ki_ == 1)),
                         reads=["glu", "sTt"], writes=["ps%d" % pi])
                P.op("act", lambda e, pi=pi, ko=ko, n=n: e.activation(out=gsig[:, ko, 0:n], in_=PS[pi][:, 0:n], func=AF.Sigmoid, bias=ppsb[:, PP_GLUB + ko:PP_GLUB + ko + 1]),
                     reads=["ppsb"], writes=["ps%d" % pi, "gsig"])
            P.op("dve", lambda e, n=n: e.tensor_tensor(out=cat[:, 6:8, 0:n], in0=sTt[:, :, 0:n], in1=gsig[:, :, 0:n], op=ALU.mult), reads=["sTt", "gsig"], writes=["cat"])
            for j in range(nj):
                ti0 = t0 + j * 128
                xs = (ti0 // 128) % 2
                P.dma("sp", xt[xs], xsrc[ti0:ti0 + 128, :], reads=["xsrc"], writes=["xtc%d" % xs])
                for hf_ in range(2):
                    pi = psr.next()
                    for kt in range(KT):
                        P.op("pe", lambda e, pi=pi, kt=kt, j=j, hf_=hf_: e.matmul(PS[pi][:, :], lhsT=cat[:, kt, j * 128:(j + 1) * 128], rhs=wo[:, kt, hf_ * 512:(hf_ + 1) * 512],
                                                                               start=(kt == 0), stop=(kt == KT - 1)),
                             reads=["cat", "wo"], writes=["ps%d" % pi])
                    P.op("dve", lambda e, pi=pi, xs=xs, hf_=hf_, w=w: e.tensor_tensor(out=xo[xs][:, hf_ * 512:(hf_ + 1) * 512], in0=PS[pi][:, :], in1=g1bc[w][:, hf_ * 512:(hf_ + 1) * 512], op=ALU.mult),
                         reads=["bc"], writes=["ps%d" % pi, "xo%d" % xs])
                P.op("pool", lambda e, xs=xs: e.tensor_tensor(out=xo[xs], in0=xo[xs], in1=xt[xs], op=ALU.add), reads=["xtc%d" % xs, "xo%d" % xs], writes=["xo%d" % xs])
                P.dma("sp", x1[ti0:ti0 + 128, :], xo[xs], reads=["xo%d" % xs], writes=["x1"])
            t0 += n
        while pre:
            o_, i_, k_ = pre.pop(0)
            P.dma("poolq", o_, i_, writes=[k_])

    def phaseC2(l, last):
        A.reset()
        w1 = A.bf16(KT * 4 * D).rearrange("p (k n) -> p k n", k=KT)
        w2 = A.bf16(32 * D).rearrange("p (k n) -> p k n", k=32)
        G = A.f32(D); S = A.f32(D)
        g2s = {0: A.f32(D)}
        if not last:
            g2s[1] = A.f32(D)
        for w_, t_ in g2s.items():
            P.dma("sp", t_, mraw[l, w_:w_ + 1, 5 * D:6 * D].partition_broadcast(128), reads=["mraw"], writes=["bcg2"])
        cur_w = [None]

        def load_bc(w):
            if cur_w[0] == w:
                return
            cur_w[0] = w
            P.dma("sp", G, gsc[l, w, 1:2, :].partition_broadcast(128), reads=["gsc"], writes=["bcG"])
            P.dma("sp", S, mraw[l, w:w + 1, 3 * D:4 * D].partition_broadcast(128), reads=["mraw"], writes=["bcG"])

        if last:
            fn = A.f32(D)
            P.dma("sp", fn, final_norm.partition_broadcast(128), writes=["bcf"])
        NB = 256
        xts = [A.f32(D) for _ in range(4)]
        hbs = [A.bf16(D) for _ in range(4)]
        sss = [A.f32(1) for _ in range(4)]; rss = [A.f32(1) for _ in range(4)]
        fss = [A.f32(1) for _ in range(2)]; frs = [A.f32(1) for _ in range(2)]
        hTs = [A.bf16(KT * NB).rearrange("p (k t) -> p k t", k=KT) for _ in range(2)]
        uT = A.bf16(32 * NB).rearrange("p (k t) -> p k t", k=32)
        rl = [A.bf16(NB), A.bf16(NB)]
        yo = [A.f32(D), A.f32(D)]
        psT = Rot([0, 1]); psM = Rot([2, 3, 4, 5, 6, 7]); rlR = Rot([0, 1])
        blocks = []
        t0 = 0 if not last else LC
        while t0 < TT:
            w = 1 if t0 < LC else 0
            n = min(NB, (LC if w == 1 else TT) - t0)
            blocks.append((t0, n, w))
            t0 += n

        def slot(bi, j):
            return (bi % 2) * 2 + j

        def norms(bi):
            (t0, n, w) = blocks[bi]
            load_bc(w)
            for j in range(n // 128):
                s = slot(bi, j)
                ti0 = t0 + j * 128
                kx, kh = "xt%d" % s, "hb%d" % s
                P.dma("sp", xts[s], x1[ti0:ti0 + 128, :], reads=["x1"], writes=[kx])
                P.op("act", lambda e, s=s: e.activation(out=hbs[s], in_=xts[s], func=AF.Square, accum_out=sss[s]), reads=[kx], writes=[kh, "ss%d" % s])
                P.op("act", lambda e, s=s: e.activation(out=rss[s], in_=sss[s], func=AF.Sqrt, scale=1.0 / D, bias=EPS), reads=["ss%d" % s], writes=["rs%d" % s])
                P.op("dve", lambda e, s=s: e.reciprocal(out=rss[s], in_=rss[s]), reads=["rs%d" % s], writes=["rs%d" % s])
                P.op("dve", lambda e, s=s: e.scalar_tensor_tensor(out=hbs[s], in0=xts[s], scalar=rss[s], in1=G, op0=ALU.mult, op1=ALU.mult),
                     reads=[kx, "rs%d" % s, "bcG"], writes=[kh])
                P.op("pool", lambda e, s=s: e.tensor_tensor(out=hbs[s], in0=hbs[s], in1=S, op=ALU.add), reads=[kh, "bcG"], writes=[kh])

        def transposes(bi):
            (t0, n, w) = blocks[bi]
            hT = hTs[bi % 2]
            for j in range(n // 128):
                s = slot(bi, j)
                pi = psT.next()
                pst = PS[pi][:].bitcast(BF16)
                for kt in range(KT):
                    P.op("pe", lambda e, kt=kt, pst=pst, s=s: e.transpose(pst[:, kt * 128:(kt + 1) * 128], hbs[s][:, kt * 128:(kt + 1) * 128], ident_b[:]),
                         reads=["hb%d" % s, "ident_b"], writes=["ps%d" % pi])
                P.op("act", lambda e, pst=pst, j=j, hT=hT: e.activation(out=hT[:, :, j * 128:(j + 1) * 128], in_=pst.rearrange("p (k t) -> p k t", k=KT), func=AF.Identity),
                     reads=[], writes=["ps%d" % pi, "hT%d" % (bi % 2)])

        def ff1(bi):
            (t0, n, w) = blocks[bi]
            hT = hTs[bi % 2]; kT_ = "hT%d" % (bi % 2)
            for ft in range(32):
                pi = psM.next()
                for kt in range(KT):
                    P.op("pe", lambda e, pi=pi, kt=kt, ft=ft, n=n, hT=hT: e.matmul(PS[pi][:, 0:n], lhsT=w1[:, kt, ft * 128:(ft + 1) * 128], rhs=hT[:, kt, 0:n], start=(kt == 0), stop=(kt == KT - 1)),
                         reads=["w1", kT_], writes=["ps%d" % pi])
                ri = rlR.next()
                P.op("act", lambda e, pi=pi, ri=ri, n=n: e.activation(out=rl[ri][:, 0:n], in_=PS[pi][:, 0:n], func=AF.Relu), reads=[], writes=["ps%d" % pi, "rl%d" % ri])
                P.op("dve", lambda e, ri=ri, ft=ft, n=n: e.tensor_tensor(out=uT[:, ft, 0:n], in0=rl[ri][:, 0:n], in1=rl[ri][:, 0:n], op=ALU.mult), reads=["rl%d" % ri], writes=["uT"])

        def ff2(bi):
            (t0, n, w) = blocks[bi]
            for j in range(n // 128):
                ti0 = t0 + j * 128
                s = slot(bi, j)
                y = j % 2
                for hf_ in range(2):
                    pi = psM.next()
                    for ft in range(32):
                        P.op("pe", lambda e, pi=pi, ft=ft, j=j, hf_=hf_: e.matmul(PS[pi][:, :], lhsT=uT[:, ft, j * 128:(j + 1) * 128], rhs=w2[:, ft, hf_ * 512:(hf_ + 1) * 512],
                                                                               start=(ft == 0), stop=(ft == 31)),
                             reads=["w2", "uT"], writes=["ps%d" % pi])
                    P.op("dve", lambda e, pi=pi, y=y, hf_=hf_: e.tensor_tensor(out=yo[y][:, hf_ * 512:(hf_ + 1) * 512], in0=PS[pi][:, :], in1=g2s[w][:, hf_ * 512:(hf_ + 1) * 512], op=ALU.mult),
                         reads=["bcg2"], writes=["ps%d" % pi, "yo%d" % y])
                P.op("pool", lambda e, y=y, s=s: e.tensor_tensor(out=yo[y], in0=yo[y], in1=xts[s], op=ALU.add), reads=["xt%d" % s, "yo%d" % y], writes=["yo%d" % y])
                if not last:
                    P.dma("sp", xres[ti0:ti0 + 128, :], yo[y], reads=["yo%d" % y], writes=["xres%d" % ti0])
                else:
                    kh = "hb%d" % s
                    P.op("act", lambda e, y=y, s=s: e.activation(out=hbs[s], in_=yo[y], func=AF.Square, accum_out=fss[y]), reads=["yo%d" % y], writes=[kh, "fss%d" % y])
                    P.op("act", lambda e, y=y: e.activation(out=frs[y], in_=fss[y], func=AF.Sqrt, scale=1.0 / D, bias=EPS), reads=["fss%d" % y], writes=["frs%d" % y])
                    P.op("dve", lambda e, y=y: e.reciprocal(out=frs[y], in_=frs[y]), reads=["frs%d" % y], writes=["frs%d" % y])
                    P.op("dve", lambda e, y=y: e.scalar_tensor_tensor(out=yo[y], in0=yo[y], scalar=frs[y], in1=fn, op0=ALU.mult, op1=ALU.mult),
                         reads=["yo%d" % y, "frs%d" % y, "bcf"], writes=["yo%d" % y])
                    P.dma("sp", out_d[ti0 - LC:ti0 - LC + 128, :], yo[y], reads=["yo%d" % y], writes=["out%d" % ti0])

        norms(0)
        transposes(0)
        for bi in range(len(blocks)):
            if bi + 1 < len(blocks):
                norms(bi + 1)
            ff1(bi)
            if bi + 1 < len(blocks):
                transposes(bi + 1)
            ff2(bi)

    setup_consts()
    stages = build.stages if hasattr(build, "stages") else None
    for l in range(depth):
        last = (l == depth - 1)
        if stages == "C":
            break
        modulation(l)
        load_pp(l)
        P.barrier()
        if stages == "M":
            break
        phaseA(l)
        P.barrier()
        if stages is not None and "A" == stages:
            break
        print("nops before gla", P.nops, flush=True)
        gla(l)
        P.barrier()
        print("nops before lru", P.nops, flush=True)
        lru(l)
        P.barrier()
        print("nops before s5", P.nops, flush=True)
        s5(l)
        P.barrier()
        print("nops after s5", P.nops, flush=True)
        if stages is not None and "B" == stages:
            break
        phaseC1(l, last)
        P.barrier()
        phaseC2(l, last)
        P.barrier()
    P.barrier()
    print("nops", P.nops, flush=True)
    P.emit()
    P.close()
    return nc


def prep_inputs(inp, b, LL, LC, depth):
    f = lambda a: np.ascontiguousarray(np.asarray(a, dtype=np.float32))
    TT = LL + LC
    NC8, LC8 = TT // 8, LC // 8
    m = {}
    m["xin"] = f(np.concatenate([inp["ctx"][b], inp["x"][b]], axis=0))
    cv = np.stack([np.asarray(inp["c"][b]).reshape(KT, 128).T, np.asarray(inp["c_ctx"]).reshape(KT, 128).T], axis=-1)
    m["cvec"] = f(cv)
    for k in ("w_mod", "b_mod", "norm1", "norm2", "w_in", "gla_up_w", "lru_wa", "lru_wx", "s5_d", "s5_glu_w", "w_out", "w_ff1", "w_ff2"):
        m[k] = f(inp[k])
    m["final_norm"] = f(np.asarray(inp["final_norm"]).reshape(1, D))
    pp = np.zeros((depth, 128, 64), np.float32)
    for l in range(depth):
        for d in range(2):
            for hp in range(2):
                pp[l, :, 0 + d * 2 + hp] = inp["gla_up_b"][l, d, hp * 128:(hp + 1) * 128]
            for ct in range(2):
                sl = slice(ct * 128, (ct + 1) * 128)
                for k in range(4):
                    pp[l, :, 8 + (d * 4 + k) * 2 + ct] = inp["lru_conv_w"][l, d, k, sl]
                pp[l, :, 24 + d * 2 + ct] = inp["lru_conv_b"][l, d, sl]
                pp[l, :, 28 + d * 2 + ct] = inp["lru_ba"][l, d, sl]
                pp[l, :, 32 + d * 2 + ct] = inp["lru_bx"][l, d, sl]
                pp[l, :, 36 + d * 2 + ct] = inp["lru_lambda"][l, d, sl]
        for h in range(4):
            pp[l, :, 4 + h] = inp["gla_norm"][l, h * 128:(h + 1) * 128]
        for ct in range(2):
            pp[l, :, 40 + ct] = inp["s5_glu_b"][l, ct * 128:(ct + 1) * 128]
    m["pp"] = pp
    s5p = np.zeros((depth, 128, 3, 16), np.float32)
    s5b = np.zeros((depth, 128, 2, 16, 16), np.float32)
    s5c = np.zeros((depth, 128, 2, 16, 16), np.float32)
    for l in range(depth):
        for d in range(2):
            for gp in range(8):
                for gm in range(2):
                    g = gp * 2 + gm
                    ps_ = slice(gm * 64, (gm + 1) * 64)
                    s5p[l, ps_, 0, d * 8 + gp] = inp["s5_lam_re"][l, d, g]
                    s5p[l, ps_, 1, d * 8 + gp] = inp["s5_lam_im"][l, d, g]
                    s5p[l, ps_, 2, d * 8 + gp] = inp["s5_log_dt"][l, d, g]
                    s5b[l, ps_, 0, d * 8 + gp, :] = inp["s5_b_re"][l, d, g]
                    s5b[l, ps_, 1, d * 8 + gp, :] = inp["s5_b_im"][l, d, g]
                    s5c[l, ps_, 0, d * 8 + gp, :] = np.asarray(inp["s5_c_re"][l, d, g]).T
                    s5c[l, ps_, 1, d * 8 + gp, :] = np.asarray(inp["s5_c_im"][l, d, g]).T
    m["s5p"], m["s5b"], m["s5c"] = s5p, s5b, s5c
    tau = np.zeros((2, NC8), np.float32)
    tau[0] = np.arange(NC8)
    tau[1, :LC8] = LC8 - 1 - np.arange(LC8)
    tau[1, LC8:] = LC8 + (NC8 - 1 - np.arange(LC8, NC8))
    m["tau"] = tau.reshape(1, 2 * NC8)
    return m


_CACHE = {}


def kernel(**inputs):
    LL, LC, depth = 4096, 256, 4
    B = inputs["x"].shape[0]
    key = (LL, LC, depth)
    if key not in _CACHE:
        _CACHE[key] = build(LL, LC, depth)
    nc = _CACHE[key]
    in_maps = [prep_inputs(inputs, b, LL, LC, depth) for b in range(B)]
    res = run_bass_kernel_spmd(nc, in_maps, core_ids=list(range(B)))
    return np.stack([np.asarray(r["out"], dtype=np.float32) for r in res.results], axis=0)
```

```python
import math
import os
from contextlib import ExitStack

import numpy as np
import concourse.bass as bass
import concourse.mybir as mybir
from concourse.bass_utils import run_bass_kernel_spmd

F32 = mybir.dt.float32
BF16 = mybir.dt.bfloat16
ALU = mybir.AluOpType
AF = mybir.ActivationFunctionType

D = 1024
KT = 8
EPS = 1e-6
MAGIC = 12582912.0
TWO_PI = 2.0 * math.pi

COMPUTE = ("pe", "act", "dve", "pool")
QUEUES = ("sp", "poolq")
ENG_OF = {"pe": "pe", "act": "act", "dve": "dve", "pool": "pool", "sp": "sp", "poolq": "pool"}


class Prog:
    def __init__(self, nc, ndma=8):
        import os
        self.nc = nc
        self.es = ExitStack()
        self.streams = {e: [] for e in ("pe", "act", "dve", "pool", "sp")}
        self.sem = {}
        self.nop_eng = {}
        for e in COMPUTE:
            self.sem[e] = self.es.enter_context(nc.semaphore("s_" + e))
            self.nop_eng[e] = 0
        self.dsem, self.dcnt, self.dnext = {}, {}, {}
        for q in QUEUES:
            self.dsem[q] = [self.es.enter_context(nc.semaphore("d_%s%d" % (q, i))) for i in range(ndma)]
            self.dcnt[q] = [0] * ndma
            self.dnext[q] = 0
        self.seen = {e: {} for e in self.streams}
        self.lastw = {}
        self.readers = {}
        self.nops = 0
        self.waited = {e: set() for e in COMPUTE}
        self._cap = None
        self.limit = int(os.environ["OPLIMIT"]) if os.environ.get("OPLIMIT") else None

    def sbuf(self, name, shape, dtype=F32):
        return self.es.enter_context(self.nc.sbuf_tensor(name, list(shape), dtype))

    def psum(self, name, shape, dtype=F32):
        return self.es.enter_context(self.nc.psum_tensor(name, list(shape), dtype))

    @staticmethod
    def _tkey(tok):
        return ("c", tok[1]) if tok[0] == "c" else ("d", tok[1].name)

    @staticmethod
    def _tval(tok):
        return tok[2]

    def _need(self, stream, tok, waits):
        if tok is None:
            return
        if tok[0] == "c" and tok[1] == "pe" and stream == "pe":
            return
        k = self._tkey(tok)
        if self.seen[stream].get(k, 0) >= self._tval(tok):
            return
        cur = waits.get(k)
        if cur is None or self._tval(cur) < self._tval(tok):
            waits[k] = tok

    def _deps(self, stream, reads, writes, waits, is_dma=False):
        for k in reads:
            self._need(stream, self.lastw.get(k), waits)
        for k in writes:
            t = self.lastw.get(k)
            if is_dma or not (t is not None and t[0] == "c" and t[1] == stream):
                self._need(stream, t, waits)
            for t in self.readers.get(k, ()):
                if is_dma or not (t[0] == "c" and t[1] == stream):
                    self._need(stream, t, waits)

    def _commit(self, stream, tok, reads, writes, waits):
        for k, t in waits.items():
            self.seen[stream][k] = self._tval(t)
            if t[0] == "c":
                self.waited[t[1]].add(t[2])
        for k in writes:
            self.lastw[k] = tok
            self.readers[k] = []
        for k in reads:
            if k in writes:
                continue
            lst = self.readers.setdefault(k, [])
            lst.append(tok)
            if len(lst) > 16:
                best = {}
                for t in lst:
                    kk = self._tkey(t)
                    b = best.get(kk)
                    if b is None or self._tval(b) < self._tval(t):
                        best[kk] = t
                self.readers[k] = list(best.values())

    def capture(self, f):
        prev = self._cap
        self._cap = []
        f()
        lst = self._cap
        self._cap = prev
        return lst

    def interleave(self, lists):
        lists = [list(l) for l in lists if l]
        idx = [0] * len(lists)
        while True:
            done = True
            for k, l in enumerate(lists):
                if idx[k] < len(l):
                    done = False
                    kind, a = l[idx[k]]
                    idx[k] += 1
                    if kind == "op":
                        self._op2(*a)
                    else:
                        self._dma2(*a)
            if done:
                break

    def op(self, eng, fn, reads=(), writes=()):
        if self.limit is not None and self.nops >= self.limit:
            return None
        rec = _Rec()
        fn(rec)
        name, args, kwargs = rec.call
        if self._cap is not None:
            self._cap.append(("op", (eng, name, args, kwargs, tuple(reads), tuple(writes))))
            return None
        return self._op2(eng, name, args, kwargs, reads, writes)

    def _op2(self, eng, name, args, kwargs, reads, writes):
        if os.environ.get("OPTRACE"):
            def _d(a):
                try:
                    return "%s%s" % (tuple(a.shape), "" )
                except Exception:
                    return str(a)[:30]
            print("OP", self.nops, eng, name, [_d(a) for a in args], {k: _d(v) for k, v in kwargs.items()}, flush=True)
        fn = lambda e, name=name, args=args, kwargs=kwargs: getattr(e, name)(*args, **kwargs)
        waits = {}
        self._deps(eng, reads, writes, waits)
        self.nop_eng[eng] += 1
        tok = ("c", eng, self.nop_eng[eng])
        self._commit(eng, tok, reads, writes, waits)
        self.streams[eng].append([list(waits.values()), fn, tok])
        self.nops += 1
        return tok

    def dma(self, q, out, in_, reads=(), writes=(), **kw):
        if self.limit is not None and self.nops >= self.limit:
            return None
        if self._cap is not None:
            self._cap.append(("dma", (q, out, in_, tuple(reads), tuple(writes), kw)))
            return None
        return self._dma2(q, out, in_, reads, writes, kw)

    def _dma2(self, q, out, in_, reads, writes, kw):
        stream = ENG_OF[q]
        waits = {}
        i = self.dnext[q]
        self.dnext[q] = (i + 1) % len(self.dsem[q])
        sem = self.dsem[q][i]
        if self.dcnt[q][i] > 0:
            self._need(stream, ("d", sem, 16 * self.dcnt[q][i], q), waits)
        self._deps(stream, reads, writes, waits, is_dma=True)
        self.dcnt[q][i] += 1
        tok = ("d", sem, 16 * self.dcnt[q][i], q)
        self._commit(stream, tok, reads, writes, waits)
        fn = lambda e, out=out, in_=in_, kw=kw: e.dma_start(out=out, in_=in_, **kw)
        self.streams[stream].append([list(waits.values()), fn, tok])
        self.nops += 1
        return tok

    def barrier(self):
        toks = []
        for q in QUEUES:
            for i, sem in enumerate(self.dsem[q]):
                if self.dcnt[q][i]:
                    toks.append(("d", sem, 16 * self.dcnt[q][i], q))
        for e in COMPUTE:
            if self.nop_eng[e]:
                toks.append(("c", e, self.nop_eng[e]))
        for stream in self.streams:
            waits = {}
            for t in toks:
                if t[0] == "c" and t[1] == stream:
                    continue
                self._need(stream, t, waits)
            for k, t in waits.items():
                self.seen[stream][k] = self._tval(t)
                if t[0] == "c":
                    self.waited[t[1]].add(t[2])
            if waits:
                self.streams[stream].append([list(waits.values()), None, None])
        self.lastw = {}
        self.readers = {}

    def emit(self):
        nc = self.nc
        streams = self.streams
        rank = {}
        for e in COMPUTE:
            rank[e] = {idx: r + 1 for r, idx in enumerate(sorted(self.waited[e]))}

        def run(eng_obj, lst):
            for waits, fn, tok in lst:
                for t in waits:
                    if t[0] == "c":
                        eng_obj.wait_ge(self.sem[t[1]], rank[t[1]][t[2]])
                    else:
                        eng_obj.wait_ge(t[1], t[2])
                if fn is not None:
                    ins = fn(eng_obj)
                    if tok[0] == "d":
                        ins.then_inc(tok[1], 16)
                    elif tok[2] in rank[tok[1]]:
                        ins.then_inc(self.sem[tok[1]], 1)

        with nc.Block() as block:
            @block.tensor
            def _(e):
                run(e, streams["pe"])

            @block.scalar
            def _(e):
                run(e, streams["act"])

            @block.vector
            def _(e):
                run(e, streams["dve"])

            @block.gpsimd
            def _(e):
                run(e, streams["pool"])

            @block.sync
            def _(e):
                run(e, streams["sp"])

    def close(self):
        self.es.close()


class _Rec:
    def __init__(self):
        self.call = None

    def __getattr__(self, name):
        def f(*args, **kwargs):
            self.call = (name, args, kwargs)
            return self
        return f


class Arena:
    def __init__(self, P, words):
        self.t = P.sbuf("arena", [128, words], F32)
        self.words = words
        self.off = 0
        self.n = 0

    def reset(self):
        self.off = 0

    def f32(self, n):
        assert self.off + n <= self.words, ("arena overflow", self.off, n, self.words)
        ap = self.t[:, self.off:self.off + n]
        self.off += n
        return ap

    def bf16(self, n):
        w = (n + 1) // 2
        return self.f32(w).bitcast(BF16)[:, 0:n]


class Rot:
    def __init__(self, items):
        self.items = items
        self.i = 0

    def next(self):
        it = self.items[self.i % len(self.items)]
        self.i += 1
        return it


def build(LL, LC, depth, debug=False):
    TT = LL + LC
    NT = TT // 128
    NCH = TT // 64
    NC8 = TT // 8
    LC8 = LC // 8
    ROWS = LL // 64
    RB = ROWS // 8
    nc = bass.Bass("TRN2", target_bir_lowering=False)
    P = Prog(nc)

    def din(name, shape, dt=F32):
        return nc.dram_tensor(name, list(shape), dt, kind="ExternalInput").ap()

    dkind = "ExternalOutput" if debug else "Internal"

    def dscr(name, shape, dt=F32):
        return nc.dram_tensor(name, list(shape), dt, kind=dkind).ap()

    xin = din("xin", [TT, D])
    cvec = din("cvec", [128, KT, 2])
    w_mod = din("w_mod", [depth, D, 6 * D])
    b_mod = din("b_mod", [depth, 6 * D])
    norm1 = din("norm1", [depth, D])
    norm2 = din("norm2", [depth, D])
    w_in = din("w_in", [depth, D, 2336])
    gla_up_w = din("gla_up_w", [depth, 2, 16, 256])
    pp = din("pp", [depth, 128, 64])
    lru_wa = din("lru_wa", [depth, 2, 4, 64, 64])
    lru_wx = din("lru_wx", [depth, 2, 4, 64, 64])
    s5p = din("s5p", [depth, 128, 3, 16])
    s5b = din("s5b", [depth, 128, 2, 16, 16])
    s5c = din("s5c", [depth, 128, 2, 16, 16])
    s5_d = din("s5_d", [depth, 256])
    s5_glu_w = din("s5_glu_w", [depth, 256, 256])
    w_out = din("w_out", [depth, D, D])
    w_ff1 = din("w_ff1", [depth, D, 4 * D])
    w_ff2 = din("w_ff2", [depth, 4 * D, D])
    final_norm = din("final_norm", [1, D])
    tau_in = din("tau", [1, 2 * NC8])

    out_d = nc.dram_tensor("out", [LL, D], F32, kind="ExternalOutput").ap()

    mraw = dscr("mraw", [depth, 2, 6 * D])
    gsc = dscr("gsc", [depth, 2, 2, D])
    qT = dscr("qT", [256, TT]); kT = dscr("kT", [256, TT]); ggT = dscr("ggT", [512, TT])
    lrT = [dscr("lrT0", [16, TT]), dscr("lrT1", [16, TT])]
    lxT = dscr("lxT", [256, TT]); lgT = dscr("lgT", [256, TT])
    v_tok = dscr("v_tok", [TT, 512], BF16)
    su_tok = dscr("su_tok", [TT, 256])
    oT = dscr("oT", [512, TT])
    lruT = dscr("lruT", [256, TT])
    s5y = dscr("s5y", [TT, 256])
    x1 = dscr("x1", [TT, D])
    xres = dscr("xres", [TT, D])
    dbg = {}

    ident_f = P.sbuf("ident_f", [128, 128], F32)
    ident_b = P.sbuf("ident_b", [128, 128], BF16)
    ones_f = P.sbuf("ones_f", [128, 128], F32)
    ones_b = P.sbuf("ones_b", [128, 128], BF16)
    maskF = P.sbuf("maskF", [128, 128], F32)
    maskB = P.sbuf("maskB", [128, 128], F32)
    m8F = P.sbuf("m8F", [128, 128], F32)
    m8B = P.sbuf("m8B", [128, 128], F32)
    jidx = P.sbuf("jidx", [128, 2, 8, 9], F32)
    ppsb = P.sbuf("ppsb", [128, 64], F32)
    AW = 51500
    A = Arena(P, AW)
    PS = [P.psum("ps%d" % i, [128, 512], F32) for i in range(8)]

    def setup_consts():
        P.op("pool", lambda e: e.memset(ident_f[:], 0.0), writes=["ident_f"])
        P.op("pool", lambda e: e.affine_select(out=ident_f[:], in_=ident_f[:], pattern=[[-1, 128]], compare_op=ALU.not_equal,
                                               fill=1.0, base=0, channel_multiplier=1), reads=["ident_f"], writes=["ident_f"])
        P.op("dve", lambda e: e.tensor_copy(out=ident_b[:], in_=ident_f[:]), reads=["ident_f"], writes=["ident_b"])
        P.op("dve", lambda e: e.memset(ones_f[:], 1.0), writes=["ones_f"])
        P.op("dve", lambda e: e.memset(ones_b[:], 1.0), writes=["ones_b"])
        P.op("pool", lambda e: e.affine_select(out=maskF[:], in_=ones_f[:], pattern=[[1, 128]], compare_op=ALU.is_ge,
                                               fill=0.0, base=0, channel_multiplier=-1), reads=["ones_f"], writes=["maskF"])
        P.op("pool", lambda e: e.memset(maskF[0:64, 64:128], 0.0), reads=["maskF"], writes=["maskF"])
        P.op("pool", lambda e: e.affine_select(out=maskB[:], in_=ones_f[:], pattern=[[-1, 128]], compare_op=ALU.is_ge,
                                               fill=0.0, base=0, channel_multiplier=1), reads=["ones_f"], writes=["maskB"])
        P.op("pool", lambda e: e.memset(maskB[64:128, 0:64], 0.0), reads=["maskB"], writes=["maskB"])
        P.op("pool", lambda e: e.memset(m8F[:], 1.0), writes=["m8F"])
        P.op("pool", lambda e: e.memset(m8B[:], 1.0), writes=["m8B"])
        P.op("pool", lambda e: e.affine_select(out=m8F[:].rearrange("p (r j h) -> p r j h", r=1, j=8), in_=m8F[:].rearrange("p (r j h) -> p r j h", r=1, j=8),
                                               pattern=[[0, 1], [16, 8], [0, 16]], compare_op=ALU.is_ge, fill=0.0, base=15, channel_multiplier=-1),
             reads=["m8F"], writes=["m8F"])
        P.op("pool", lambda e: e.affine_select(out=m8B[:].rearrange("p (r j h) -> p r j h", r=1, j=8), in_=m8B[:].rearrange("p (r j h) -> p r j h", r=1, j=8),
                                               pattern=[[0, 1], [-16, 8], [0, 16]], compare_op=ALU.is_ge, fill=0.0, base=0, channel_multiplier=1),
             reads=["m8B"], writes=["m8B"])
        for j in range(9):
            P.op("dve", lambda e, j=j: e.memset(jidx[:, 0, :, j:j + 1], float(j)), writes=["jidx"])
            P.op("dve", lambda e, j=j: e.memset(jidx[:, 1, :, j:j + 1], float(7 - j) if j < 8 else 8.0), writes=["jidx"])

    PP_UPB = 0
    PP_GNORM = 4
    PP_CONVW = 8
    PP_CONVB = 24
    PP_BA = 28
    PP_BX = 32
    PP_LAM = 36
    PP_GLUB = 40
    PP_SGN = 42

    def modulation(l):
        A.reset()
        win_p = A.bf16(KT * 2336).rearrange("p (k n) -> p k n", k=KT)
        wv_p = w_in[l].rearrange("(kt p) n -> p kt n", p=128)
        for c0 in range(0, 2336, 512):
            c1 = min(2336, c0 + 512)
            P.dma("poolq", win_p[:, :, c0:c1], wv_p[:, :, c0:c1], writes=["win"])
        cs_raw = A.f32(16); cs = A.f32(16)
        msb = A.f32(6 * D)
        bm = A.f32(6 * D)
        n12 = A.f32(2 * D)
        gt = A.f32(2 * D)
        wblk = [A.f32(KT * 512), A.f32(KT * 512)]
        P.dma("sp", cs_raw, cvec.rearrange("p k w -> p (k w)"), writes=["cs_raw"])
        P.op("act", lambda e: e.activation(out=cs, in_=cs_raw, func=AF.Silu), reads=["cs_raw"], writes=["cs"])
        P.dma("sp", bm[0:2, :], b_mod[l:l + 1, :].partition_broadcast(2), writes=["bm"])
        P.dma("sp", n12[0:2, 0:D], norm1[l:l + 1, :].partition_broadcast(2), writes=["n12a"])
        P.dma("sp", n12[0:2, D:2 * D], norm2[l:l + 1, :].partition_broadcast(2), writes=["n12b"])
        wv = w_mod[l].rearrange("(kt p) n -> p kt n", p=128)
        cs3 = cs.rearrange("p (k w) -> p k w", w=2)
        for j in range(12):
            wb = wblk[j % 2]
            wb3 = wb.rearrange("p (k n) -> p k n", k=KT)
            P.dma("sp", wb3, wv[:, :, j * 512:(j + 1) * 512], writes=["wblk%d" % (j % 2)])
            ps = PS[j % 2]
            for kt in range(KT):
                P.op("pe", lambda e, ps=ps, kt=kt, wb3=wb3: e.matmul(ps[0:2, :], lhsT=cs3[:, kt, :], rhs=wb3[:, kt, :], start=(kt == 0), stop=(kt == KT - 1)),
                     reads=["cs", "wblk%d" % (j % 2)], writes=["ps%d" % (j % 2)])
            P.op("dve", lambda e, ps=ps, j=j: e.tensor_tensor(out=msb[0:2, j * 512:(j + 1) * 512], in0=ps[0:2, :], in1=bm[0:2, j * 512:(j + 1) * 512], op=ALU.add),
                 reads=["bm"], writes=["ps%d" % (j % 2), "msb"])
        P.op("dve", lambda e: e.scalar_tensor_tensor(out=gt[0:2, 0:D], in0=msb[0:2, D:2 * D], scalar=1.0, in1=n12[0:2, 0:D], op0=ALU.add, op1=ALU.mult),
             reads=["msb", "n12a"], writes=["gt"])
        P.op("dve", lambda e: e.scalar_tensor_tensor(out=gt[0:2, D:2 * D], in0=msb[0:2, 4 * D:5 * D], scalar=1.0, in1=n12[0:2, D:2 * D], op0=ALU.add, op1=ALU.mult),
             reads=["msb", "n12b", "gt"], writes=["gt"])
        P.dma("sp", mraw[l], msb[0:2, :], reads=["msb"], writes=["mraw"])
        P.dma("sp", gsc[l].rearrange("w g d -> w (g d)"), gt[0:2, :], reads=["gt"], writes=["gsc"])

    def bc_load(dst, src_row):
        return src_row.to_broadcast([128, src_row.shape[-1]])

    def token_blocks(last_skip_ctx=False):
        blks = []
        if not last_skip_ctx:
            t = 0
            while t < LC // 128:
                n = min(4, LC // 128 - t)
                blks.append((t, n, 1))
                t += n
        t = LC // 128
        while t < NT:
            n = min(4, NT - t)
            blks.append((t, n, 0))
            t += n
        return blks

    def rmsnorm_mod(xt_ap, Gbc, Sbc, hb_out, sfx, junk, ss, rs, hf):
        P.op("act", lambda e: e.activation(out=junk, in_=xt_ap, func=AF.Square, accum_out=ss), reads=["xt" + sfx], writes=["junk", "ss" + sfx])
        P.op("act", lambda e: e.activation(out=rs, in_=ss, func=AF.Sqrt, scale=1.0 / D, bias=EPS), reads=["ss" + sfx], writes=["rs" + sfx])
        P.op("dve", lambda e: e.reciprocal(out=rs, in_=rs), reads=["rs" + sfx], writes=["rs" + sfx])
        P.op("dve", lambda e: e.scalar_tensor_tensor(out=hf, in0=xt_ap, scalar=rs, in1=Gbc, op0=ALU.mult, op1=ALU.mult),
             reads=["xt" + sfx, "rs" + sfx, "bc"], writes=["hf"])
        P.op("pool", lambda e: e.tensor_tensor(out=hb_out, in0=hf, in1=Sbc, op=ALU.add), reads=["hf", "bc"], writes=["hb" + sfx])

    def transpose_to(hb, hT3, j, sfx, psrot):
        pi = psrot.next()
        pst = PS[pi][:].bitcast(BF16)
        for kt in range(KT):
            P.op("pe", lambda e, kt=kt, pst=pst: e.transpose(pst[:, kt * 128:(kt + 1) * 128], hb[:, kt * 128:(kt + 1) * 128], ident_b[:]),
                 reads=["hb" + sfx, "ident_b"], writes=["ps%d" % pi])
        P.op("act", lambda e, pst=pst: e.activation(out=hT3[:, :, j * 128:(j + 1) * 128], in_=pst.rearrange("p (k t) -> p k t", k=KT), func=AF.Identity),
             reads=[], writes=["ps%d" % pi, "hT"])

    def phaseA(l):
        A.reset()
        xsrc = xin if l == 0 else xres
        win = A.bf16(KT * 2336).rearrange("p (k n) -> p k n", k=KT)
        bcs = {}
        for w in (0, 1):
            G = A.f32(D); S = A.f32(D)
            P.dma("sp", G, gsc[l, w, 0:1, :].partition_broadcast(128), reads=["gsc"], writes=["bc"])
            P.dma("sp", S, mraw[l, w:w + 1, 0:D].partition_broadcast(128), reads=["mraw"], writes=["bc"])
            bcs[w] = (G, S)
        NS = 8
        xts = [A.f32(D) for _ in range(2)]
        hbs = [A.bf16(D) for _ in range(NS)]
        hfs = [A.f32(D), A.f32(D)]
        sss = [A.f32(1) for _ in range(NS)]; rss = [A.f32(1) for _ in range(NS)]
        hTs = [A.bf16(KT * 512).rearrange("p (k t) -> p k t", k=KT) for _ in range(2)]
        stg = [A.f32(512) for _ in range(3)]
        vst = [A.bf16(512) for _ in range(2)]
        sst = [A.f32(256) for _ in range(2)]
        psT = Rot([0, 1]); psM = Rot([2, 3, 4, 5, 6, 7])
        stgR = Rot([0, 1, 2]); vR = Rot([0, 1]); sR = Rot([0, 1])
        FM = [("q", 0, qT, 0, 128), ("q", 128, qT, 128, 128), ("k", 256, kT, 0, 128), ("k", 384, kT, 128, 128)]
        for i in range(4):
            FM.append(("gg", 1024 + 128 * i, ggT, 128 * i, 128))
        FM.append(("lr0", 1536, lrT[0], 0, 16)); FM.append(("lr1", 1552, lrT[1], 0, 16))
        for i in range(2):
            FM.append(("lx", 1568 + 128 * i, lxT, 128 * i, 128))
        for i in range(2):
            FM.append(("lg", 1824 + 128 * i, lgT, 128 * i, 128))
        blocks = token_blocks()
        slot_ctr = [0]
        blk_slots = {}

        def norms(bi):
            (t0, n, w) = blocks[bi]
            G, S = bcs[w]
            sl = []
            for j in range(n):
                ti = t0 + j
                s = slot_ctr[0] % NS; slot_ctr[0] += 1
                xs = s % 2
                sl.append(s)
                kx, kh = "xt%d" % xs, "hb%d" % s
                P.dma("sp", xts[xs], xsrc[ti * 128:(ti + 1) * 128, :], reads=["xsrc"], writes=[kx])
                P.op("act", lambda e, xs=xs, s=s: e.activation(out=hbs[s], in_=xts[xs], func=AF.Square, accum_out=sss[s]), reads=[kx], writes=[kh, "ss%d" % s])
                P.op("act", lambda e, s=s: e.activation(out=rss[s], in_=sss[s], func=AF.Sqrt, scale=1.0 / D, bias=EPS), reads=["ss%d" % s], writes=["rs%d" % s])
                P.op("dve", lambda e, s=s: e.reciprocal(out=rss[s], in_=rss[s]), reads=["rs%d" % s], writes=["rs%d" % s])
                P.op("dve", lambda e, xs=xs, s=s, G=G: e.scalar_tensor_tensor(out=hfs[xs], in0=xts[xs], scalar=rss[s], in1=G, op0=ALU.mult, op1=ALU.mult),
                     reads=[kx, "rs%d" % s, "bc"], writes=["hf%d" % xs])
                P.op("pool", lambda e, xs=xs, s=s, S=S: e.tensor_tensor(out=hbs[s], in0=hfs[xs], in1=S, op=ALU.add), reads=["hf%d" % xs, "bc"], writes=[kh])
            blk_slots[bi] = sl

        def transposes(bi):
            hT = hTs[bi % 2]
            for j, s in enumerate(blk_slots[bi]):
                pi = psT.next()
                pst = PS[pi][:].bitcast(BF16)
                for kt in range(KT):
                    P.op("pe", lambda e, kt=kt, pst=pst, s=s: e.transpose(pst[:, kt * 128:(kt + 1) * 128], hbs[s][:, kt * 128:(kt + 1) * 128], ident_b[:]),
                         reads=["hb%d" % s, "ident_b"], writes=["ps%d" % pi])
                P.op("act", lambda e, pst=pst, j=j, hT=hT: e.activation(out=hT[:, :, j * 128:(j + 1) * 128], in_=pst.rearrange("p (k t) -> p k t", k=KT), func=AF.Identity),
                     reads=[], writes=["ps%d" % pi, "hT%d" % (bi % 2)])

        def fm_part(bi):
            (t0, n, w) = blocks[bi]
            hT = hTs[bi % 2]; kT_ = "hT%d" % (bi % 2)
            ntok = n * 128; tok0 = t0 * 128
            for (nm, c0, dst, r0, m) in FM:
                pi = psM.next()
                for kt in range(KT):
                    P.op("pe", lambda e, pi=pi, kt=kt, c0=c0, m=m, hT=hT, ntok=ntok: e.matmul(PS[pi][0:m, 0:ntok], lhsT=win[:, kt, c0:c0 + m], rhs=hT[:, kt, 0:ntok],
                                                                                           start=(kt == 0), stop=(kt == KT - 1)),
                         reads=["win", kT_], writes=["ps%d" % pi])
                si = stgR.next()
                P.op("act", lambda e, pi=pi, si=si, m=m, ntok=ntok: e.activation(out=stg[si][0:m, 0:ntok], in_=PS[pi][0:m, 0:ntok], func=AF.Identity),
                     reads=[], writes=["ps%d" % pi, "stg%d" % si])
                P.dma("sp", dst[r0:r0 + m, tok0:tok0 + ntok], stg[si][0:m, 0:ntok], reads=["stg%d" % si], writes=["zT_%s_%d" % (nm, r0)])

        def tm_part(bi):
            (t0, n, w) = blocks[bi]
            hT = hTs[bi % 2]; kT_ = "hT%d" % (bi % 2)
            for j in range(n):
                ti = t0 + j
                pi = psM.next()
                for kt in range(KT):
                    P.op("pe", lambda e, pi=pi, kt=kt, j=j, hT=hT: e.matmul(PS[pi][:, :], lhsT=hT[:, kt, j * 128:(j + 1) * 128], rhs=win[:, kt, 512:1024],
                                                                         start=(kt == 0), stop=(kt == KT - 1)),
                         reads=["win", kT_], writes=["ps%d" % pi])
                vi = vR.next()
                P.op("dve", lambda e, pi=pi, vi=vi: e.tensor_copy(out=vst[vi], in_=PS[pi][:, :]), reads=[], writes=["ps%d" % pi, "vst%d" % vi])
                P.dma("sp", v_tok[ti * 128:(ti + 1) * 128, :], vst[vi], reads=["vst%d" % vi], writes=["v_tok%d" % ti])
                pi = psM.next()
                for kt in range(KT):
                    P.op("pe", lambda e, pi=pi, kt=kt, j=j, hT=hT: e.matmul(PS[pi][:, 0:256], lhsT=hT[:, kt, j * 128:(j + 1) * 128], rhs=win[:, kt, 2080:2336],
                                                                         start=(kt == 0), stop=(kt == KT - 1)),
                         reads=["win", kT_], writes=["ps%d" % pi])
                si = sR.next()
                P.op("dve", lambda e, pi=pi, si=si: e.tensor_copy(out=sst[si], in_=PS[pi][:, 0:256]), reads=[], writes=["ps%d" % pi, "sst%d" % si])
                P.dma("sp", su_tok[ti * 128:(ti + 1) * 128, :], sst[si], reads=["sst%d" % si], writes=["su_tok%d" % ti])

        norms(0)
        transposes(0)
        for bi in range(len(blocks)):
            if bi + 1 < len(blocks):
                norms(bi + 1)
            fm_part(bi)
            if bi + 1 < len(blocks):
                transposes(bi + 1)
            tm_part(bi)

    def load_pp(l):
        P.dma("sp", ppsb[:], pp[l], writes=["ppsb"])

    def gla(l):
        for hp in range(2):
            A.reset()
            sm0 = A.bf16(TT); sm1 = A.bf16(TT)
            P.op("pool", lambda e: e.memset(sm0, 1.0), writes=["sm"])
            P.op("pool", lambda e: e.memset(sm0.rearrange("p (c j) -> p c j", j=64)[:, :, 0:1], 0.0), reads=["sm"], writes=["sm"])
            P.op("pool", lambda e: e.memset(sm1, 1.0), reads=["sm"], writes=["sm"])
            P.op("pool", lambda e: e.memset(sm1.rearrange("p (c j) -> p c j", j=64)[:, :, 63:64], 0.0), reads=["sm"], writes=["sm"])
            vt = A.bf16(NT * 256).rearrange("p (t c) -> p t c", t=NT)
            vsrc = v_tok.rearrange("(t p) c -> p t c", p=128)
            for t0_ in range(0, NT, 8):
                t1_ = min(NT, t0_ + 8)
                P.dma("sp", vt[:, t0_:t1_, :], vsrc[:, t0_:t1_, hp * 256:(hp + 1) * 256], reads=["v_tok"], writes=["vt"])
            oacc = A.f32(2 * TT).rearrange("p (h t) -> p h t", h=2)
            P.op("pool", lambda e: e.memset(oacc, 0.0), writes=["oacc%d_%d" % (hh, t) for hh in range(2) for t in range(NT)])
            lrsb = A.f32(TT); Bp = A.f32(TT); Bc = A.f32(TT); qk = A.f32(TT)
            qd = [A.bf16(TT), A.bf16(TT)]; ki = [A.bf16(TT), A.bf16(TT)]
            kiT = [A.bf16(NT * 128).rearrange("p (t c) -> p t c", t=NT) for _ in range(2)]
            upw = A.f32(256); nb = A.f32(1)
            gam = [A.f32(NCH), A.f32(NCH)]
            S = [A.f32(128), A.f32(128)]; Sb = [A.bf16(128), A.bf16(128)]; tmp = [A.f32(128), A.f32(128)]
            sT = [[A.bf16(128), A.bf16(128)], [A.bf16(128), A.bf16(128)]]
            for d in range(2):
                ds_ = str(d)
                sm = sm0 if d == 0 else sm1
                P.dma("sp", lrsb[0:16, :], lrT[d], reads=["zT_lr%d" % d], writes=["lrsb"])
                P.dma("sp", upw[0:16, :], gla_up_w[l, d], writes=["upw"])
                P.op("dve", lambda e, d=d: e.tensor_scalar(out=nb, in0=ppsb[:, PP_UPB + d * 2 + hp:PP_UPB + d * 2 + hp + 1], scalar1=-1.0, scalar2=None, op0=ALU.mult),
                     reads=["ppsb"], writes=["nb"])
                psr = Rot([0, 1])
                for b0 in range(0, TT, 512):
                    n = min(512, TT - b0)
                    pi = psr.next()
                    P.op("pe", lambda e, pi=pi, b0=b0, n=n: e.matmul(PS[pi][:, 0:n], lhsT=upw[0:16, hp * 128:(hp + 1) * 128], rhs=lrsb[0:16, b0:b0 + n], start=True, stop=True),
                         reads=["upw", "lrsb"], writes=["ps%d" % pi])
                    P.op("act", lambda e, pi=pi, b0=b0, n=n: e.activation(out=Bc[:, b0:b0 + n], in_=PS[pi][:, 0:n], func=AF.Exp, scale=-1.0, bias=nb),
                         reads=["nb"], writes=["ps%d" % pi, "Bc"])
                P.op("act", lambda e: e.activation(out=Bp, in_=Bc, func=AF.Ln, bias=1.0, scale=1.0), reads=["Bc"], writes=["Bp"])
                if d == 0:
                    P.op("dve", lambda e, sm=sm: e.tensor_tensor_scan(out=Bc, data0=sm, data1=Bp, initial=0.0, op0=ALU.mult, op1=ALU.add),
                         reads=["Bp", "sm"], writes=["Bc"])
                else:
                    P.op("dve", lambda e, sm=sm: e.tensor_tensor_scan(out=Bc[:, ::-1], data0=sm[:, ::-1], data1=Bp[:, ::-1], initial=0.0, op0=ALU.mult, op1=ALU.add),
                         reads=["Bp", "sm"], writes=["Bc"])
                Bc3 = Bc.rearrange("p (c j) -> p c j", j=64)
                endj = 63 if d == 0 else 0
                P.op("act", lambda e, endj=endj, d=d: e.activation(out=gam[d], in_=Bc3[:, :, endj], func=AF.Exp, scale=-1.0 / 16.0), reads=["Bc"], writes=["gam" + ds_])
                P.dma("sp", qk, qT[hp * 128:(hp + 1) * 128, :], reads=["zT_q"], writes=["qk"])
                P.op("act", lambda e: e.activation(out=Bp, in_=Bc, func=AF.Exp, scale=-1.0 / 16.0), reads=["Bc"], writes=["Bp"])
                P.op("dve", lambda e, d=d: e.scalar_tensor_tensor(out=qd[d], in0=qk, scalar=0.125, in1=Bp, op0=ALU.mult, op1=ALU.mult), reads=["qk", "Bp"], writes=["qd" + ds_])
                P.dma("sp", qk, kT[hp * 128:(hp + 1) * 128, :], reads=["zT_k"], writes=["qk"])
                P.op("act", lambda e: e.activation(out=Bp, in_=Bc, func=AF.Exp, scale=1.0 / 16.0), reads=["Bc"], writes=["Bp"])
                P.op("dve", lambda e, d=d: e.tensor_tensor(out=ki[d], in0=qk, in1=Bp, op=ALU.mult), reads=["qk", "Bp"], writes=["ki" + ds_])
                psr = Rot([0, 1])
                for t in range(NT):
                    pi = psr.next()
                    pst = PS[pi][:].bitcast(BF16)
                    P.op("pe", lambda e, pst=pst, t=t, d=d: e.transpose(pst[:, 0:128], ki[d][:, t * 128:(t + 1) * 128], ident_b[:]), reads=["ki" + ds_, "ident_b"], writes=["ps%d" % pi])
                    P.op("act", lambda e, pst=pst, t=t, d=d: e.activation(out=kiT[d][:, t, :], in_=pst[:, 0:128], func=AF.Identity), reads=[], writes=["ps%d" % pi, "kiT" + ds_])
                P.op("dve", lambda e, d=d: e.memset(S[d], 0.0), writes=["S" + ds_])
                P.op("dve", lambda e, d=d: e.memset(Sb[d], 0.0), writes=["Sb" + ds_])

            def chunk_loop(d):
                ds_ = str(d)
                mask = maskF if d == 0 else maskB
                ctx_t = list(range(LC // 128)); lat_t = list(range(LC // 128, NT))
                order = ctx_t + lat_t if d == 0 else ctx_t[::-1] + lat_t[::-1]
                corder = (0, 1) if d == 0 else (1, 0)
                pS = d * 4 + 0; pO = (d * 4 + 1, d * 4 + 2); pD = d * 4 + 3
                for t in order:
                    for hh in range(2):
                        pr = slice(hh * 64, (hh + 1) * 64)
                        P.op("pe", lambda e, pr=pr, t=t: e.matmul(PS[pS][:, 0:128], lhsT=ki[d][pr, t * 128:(t + 1) * 128], rhs=qd[d][pr, t * 128:(t + 1) * 128], start=True, stop=True),
                             reads=["ki" + ds_, "qd" + ds_], writes=["ps%d" % pS])
                        P.op("dve", lambda e, hh=hh: e.tensor_tensor(out=sT[d][hh], in0=PS[pS][:, 0:128], in1=mask[:], op=ALU.mult),
                             reads=["mask"], writes=["ps%d" % pS, "sT%s_%d" % (ds_, hh)])
                        P.op("pe", lambda e, hh=hh, t=t: e.matmul(PS[pO[hh]][:, 0:128], lhsT=vt[:, t, hh * 128:(hh + 1) * 128], rhs=sT[d][hh], start=True, stop=True),
                             reads=["vt", "sT%s_%d" % (ds_, hh)], writes=["ps%d" % pO[hh]])
                    for ci, cc in enumerate(corder):
                        ch = t * 2 + cc
                        cols = slice(t * 128 + cc * 64, t * 128 + cc * 64 + 64)
                        for hh in range(2):
                            pr = slice(hh * 64, (hh + 1) * 64)
                            P.op("pe", lambda e, hh=hh, pr=pr, cols=cols, cc=cc: e.matmul(PS[pO[hh]][:, cc * 64:(cc + 1) * 64], lhsT=Sb[d][pr, :], rhs=qd[d][pr, cols],
                                                                                      start=False, stop=False, skip_group_check=True),
                                 reads=["Sb" + ds_, "qd" + ds_], writes=["ps%d" % pO[hh]])
                        jr = slice(cc * 64, (cc + 1) * 64)
                        for hh in range(2):
                            pr = slice(hh * 64, (hh + 1) * 64)
                            P.op("pe", lambda e, hh=hh, pr=pr, jr=jr, t=t: e.matmul(PS[pD][pr, 0:128], lhsT=kiT[d][jr, t, hh * 64:(hh + 1) * 64], rhs=vt[jr, t, hh * 128:(hh + 1) * 128],
                                                                                 start=True, stop=True),
                                 reads=["kiT" + ds_, "vt"], writes=["ps%d" % pD])
                        P.op("dve", lambda e: e.tensor_tensor(out=tmp[d], in0=PS[pD][:, 0:128], in1=S[d], op=ALU.add), reads=["S" + ds_], writes=["ps%d" % pD, "tmp" + ds_])
                        P.op("dve", lambda e, ch=ch: e.tensor_scalar(out=S[d], in0=tmp[d], scalar1=gam[d][:, ch:ch + 1], scalar2=None, op0=ALU.mult), reads=["tmp" + ds_, "gam" + ds_], writes=["S" + ds_])
                        P.op("act", lambda e, ch=ch: e.activation(out=Sb[d], in_=tmp[d], func=AF.Identity, scale=gam[d][:, ch:ch + 1]), reads=["tmp" + ds_, "gam" + ds_], writes=["Sb" + ds_])
                    for hh in range(2):
                        ko = "oacc%d_%d" % (hh, t)
                        P.op("dve", lambda e, hh=hh, t=t: e.tensor_tensor(out=oacc[:, hh, t * 128:(t + 1) * 128], in0=PS[pO[hh]][:, 0:128], in1=oacc[:, hh, t * 128:(t + 1) * 128], op=ALU.add),
                             reads=[ko], writes=["ps%d" % pO[hh], ko])

            P.interleave([P.capture(lambda: chunk_loop(0)), P.capture(lambda: chunk_loop(1))])
            for hh in range(2):
                P.dma("sp", oT[(hp * 2 + hh) * 128:(hp * 2 + hh + 1) * 128, :], oacc[:, hh, :],
                      reads=["oacc%d_%d" % (hh, t) for t in range(NT)], writes=["oT"])
            P.barrier()

    def lru(l):
        A.reset()
        cst = A.f32(4); cst2 = A.f32(4)
        P.op("act", lambda e: e.activation(out=cst, in_=ppsb[:, PP_LAM:PP_LAM + 4], func=AF.Exp, scale=-1.0), reads=["ppsb"], writes=["cst"])
        P.op("act", lambda e: e.activation(out=cst2, in_=cst, func=AF.Ln, bias=1.0, scale=1.0), reads=["cst"], writes=["cst2"])
        P.op("dve", lambda e: e.tensor_scalar(out=cst, in0=cst2, scalar1=-8.0, scalar2=None, op0=ALU.mult), reads=["cst2"], writes=["cst"])
        x = A.f32(TT); hs = A.f32(TT); th = A.f32(TT)
        xc = [A.f32(TT), A.f32(TT)]; r = [A.f32(TT), A.f32(TT)]; ig = [A.f32(TT), A.f32(TT)]; a = [A.f32(TT), A.f32(TT)]
        Wa = [A.f32(128), A.f32(128)]; Wx = [A.f32(128), A.f32(128)]
        segs = [(0, LC), (LC, TT)]
        for ct in range(2):
            P.dma("sp", x, lxT[ct * 128:(ct + 1) * 128, :], reads=["zT_lx"], writes=["x"])

            def body(d):
                ds_ = str(d)
                col = d * 2 + ct
                kxc, kr, kig, ka, kWa, kWx = "xc" + ds_, "r" + ds_, "ig" + ds_, "a" + ds_, "Wa" + ds_, "Wx" + ds_
                P.op("pool", lambda e: e.memset(Wa[d], 0.0), writes=[kWa])
                P.op("pool", lambda e: e.memset(Wx[d], 0.0), writes=[kWx])
                for bb in range(2):
                    P.dma("sp", Wa[d][bb * 64:(bb + 1) * 64, bb * 64:(bb + 1) * 64], lru_wa[l, d, ct * 2 + bb], writes=[kWa], reads=[kWa])
                    P.dma("sp", Wx[d][bb * 64:(bb + 1) * 64, bb * 64:(bb + 1) * 64], lru_wx[l, d, ct * 2 + bb], writes=[kWx], reads=[kWx])
                wcol = lambda k: ppsb[:, PP_CONVW + (d * 4 + k) * 2 + ct:PP_CONVW + (d * 4 + k) * 2 + ct + 1]
                bcol = ppsb[:, PP_CONVB + col:PP_CONVB + col + 1]
                P.op("dve", lambda e: e.tensor_scalar(out=xc[d], in0=x, scalar1=wcol(3), scalar2=bcol, op0=ALU.mult, op1=ALU.add), reads=["x", "ppsb"], writes=[kxc])
                for (s0, s1) in segs:
                    for sh in (1, 2, 3):
                        k = 3 - sh
                        if d == 0:
                            o_ap, i_ap = xc[d][:, s0 + sh:s1], x[:, s0:s1 - sh]
                        else:
                            o_ap, i_ap = xc[d][:, s0:s1 - sh], x[:, s0 + sh:s1]
                        P.op("dve", lambda e, o_ap=o_ap, i_ap=i_ap, k=k: e.scalar_tensor_tensor(out=o_ap, in0=i_ap, scalar=wcol(k), in1=o_ap, op0=ALU.mult, op1=ALU.add),
                             reads=["x", "ppsb", kxc], writes=[kxc])
                psr = Rot([d * 4 + 0, d * 4 + 1, d * 4 + 2, d * 4 + 3])
                for (Wm, dst, bc0, nm, kW) in ((Wa[d], r[d], PP_BA, kr, kWa), (Wx[d], ig[d], PP_BX, kig, kWx)):
                    for b0 in range(0, TT, 512):
                        n = min(512, TT - b0)
                        pi = psr.next()
                        P.op("pe", lambda e, pi=pi, b0=b0, n=n, Wm=Wm: e.matmul(PS[pi][:, 0:n], lhsT=Wm, rhs=xc[d][:, b0:b0 + n], start=True, stop=True),
                             reads=[kW, kxc], writes=["ps%d" % pi])
                        P.op("act", lambda e, pi=pi, b0=b0, n=n, dst=dst, bc0=bc0: e.activation(out=dst[:, b0:b0 + n], in_=PS[pi][:, 0:n], func=AF.Sigmoid,
                                                                                              bias=ppsb[:, bc0 + col:bc0 + col + 1]),
                             reads=["ppsb"], writes=["ps%d" % pi, nm])
                P.op("act", lambda e: e.activation(out=a[d], in_=r[d], func=AF.Exp, scale=cst[:, col:col + 1]), reads=[kr, "cst"], writes=[ka])
                P.op("pool", lambda e: e.tensor_tensor(out=r[d], in0=a[d], in1=a[d], op=ALU.mult), reads=[ka], writes=[kr])
                P.op("act", lambda e: e.activation(out=r[d], in_=r[d], func=AF.Sqrt, scale=-1.0, bias=1.0), reads=[kr], writes=[kr])
                P.op("dve", lambda e: e.tensor_tensor(out=ig[d], in0=ig[d], in1=r[d], op=ALU.mult), reads=[kig, kr], writes=[kig])
                P.op("dve", lambda e: e.tensor_tensor(out=xc[d], in0=xc[d], in1=ig[d], op=ALU.mult), reads=[kig, kxc], writes=[kxc])
                if d == 0:
                    P.op("dve", lambda e: e.tensor_tensor_scan(out=hs, data0=a[d], data1=xc[d], initial=0.0, op0=ALU.mult, op1=ALU.add), reads=[ka, kxc], writes=["hs"])
                else:
                    P.op("dve", lambda e: e.tensor_tensor_scan(out=th[:, 0:LC][:, ::-1], data0=a[d][:, 0:LC][:, ::-1], data1=xc[d][:, 0:LC][:, ::-1], initial=0.0,
                                                               op0=ALU.mult, op1=ALU.add), reads=[ka, kxc], writes=["th"])
                    P.op("dve", lambda e: e.tensor_tensor_scan(out=th[:, LC:TT][:, ::-1], data0=a[d][:, LC:TT][:, ::-1], data1=xc[d][:, LC:TT][:, ::-1], initial=th[:, 0:1],
                                                               op0=ALU.mult, op1=ALU.add), reads=[ka, kxc, "th"], writes=["th"])

            P.interleave([P.capture(lambda: body(0)), P.capture(lambda: body(1))])
            P.op("dve", lambda e: e.tensor_tensor(out=hs, in0=hs, in1=th, op=ALU.add), reads=["hs", "th"], writes=["hs"])
            P.dma("sp", lruT[ct * 128:(ct + 1) * 128, :], hs, reads=["hs"], writes=["lruT"])

    def s5(l):
        A.reset()
        NG = 16
        prm = A.f32(3 * NG).rearrange("p (a g) -> p a g", a=3)
        Bsb = A.f32(2 * NG * 16).rearrange("p (a g h) -> p a g h", a=2, g=NG)
        Csb = A.f32(2 * NG * 16).rearrange("p (a g h) -> p a g h", a=2, g=NG)
        P.dma("sp", prm, s5p[l], writes=["prm"])
        P.dma("sp", Bsb, s5b[l], writes=["Bsb"])
        P.dma("sp", Csb, s5c[l], writes=["Csb"])
        tauf = A.f32(2 * NC8)
        tau = tauf.rearrange("p (d c) -> p d c", d=2)
        P.dma("sp", tauf, tau_in.partition_broadcast(128), writes=["tau"])
        dt = A.f32(NG); lrdt = A.f32(NG); th = A.f32(NG); u8 = A.f32(NG); rho8 = A.f32(NG)
        t1 = A.f32(NG); t2 = A.f32(NG); t3 = A.f32(NG); den = A.f32(NG); cr = A.f32(NG); ci = A.f32(NG)
        J = jidx[:].rearrange("p d g j -> p (d g) j")
        mg = A.f32(NG * 9).rearrange("p (g j) -> p g j", j=9)
        xa = A.f32(NG * 9).rearrange("p (g j) -> p g j", j=9)
        xr = A.f32(NG * 9).rearrange("p (g j) -> p g j", j=9)
        sn = A.f32(NG * 9).rearrange("p (g j) -> p g j", j=9)
        cs_ = A.f32(NG * 9).rearrange("p (g j) -> p g j", j=9)
        ar = A.f32(NG * 9).rearrange("p (g j) -> p g j", j=9)
        ai = A.f32(NG * 9).rearrange("p (g j) -> p g j", j=9)
        mr = A.f32(NG * 8).rearrange("p (g j) -> p g j", j=8)
        mi = A.f32(NG * 8).rearrange("p (g j) -> p g j", j=8)
        br_ = A.f32(NG * 8).rearrange("p (g j) -> p g j", j=8)
        bi_ = A.f32(NG * 8).rearrange("p (g j) -> p g j", j=8)
        w1 = A.f32(NG * 9).rearrange("p (g j) -> p g j", j=9)
        K = ["prm", "s5t"]

        def op(eng, fn):
            P.op(eng, fn, reads=K, writes=["s5t"])

        def bg(v, n):
            return v.unsqueeze(2).to_broadcast([128, NG, n])

        P._cap = []
        op("act", lambda e: e.activation(out=dt, in_=prm[:, 2, :], func=AF.Exp))
        op("dve", lambda e: e.tensor_tensor(out=lrdt, in0=prm[:, 0, :], in1=dt, op=ALU.mult))
        op("dve", lambda e: e.tensor_tensor(out=th, in0=prm[:, 1, :], in1=dt, op=ALU.mult))
        op("dve", lambda e: e.tensor_scalar(out=th, in0=th, scalar1=1.0 / TWO_PI, scalar2=None, op0=ALU.mult))
        op("dve", lambda e: e.tensor_tensor(out=mg, in0=J, in1=bg(lrdt, 9), op=ALU.mult))
        op("act", lambda e: e.activation(out=mg, in_=mg, func=AF.Exp))
        op("dve", lambda e: e.tensor_tensor(out=xa, in0=J, in1=bg(th, 9), op=ALU.mult))
        op("dve", lambda e: e.tensor_scalar(out=xr, in0=xa, scalar1=MAGIC, scalar2=-MAGIC, op0=ALU.add, op1=ALU.add))
        op("dve", lambda e: e.tensor_tensor(out=w1, in0=xa, in1=xr, op=ALU.subtract))
        op("act", lambda e: e.activation(out=sn, in_=w1, func=AF.Sin, scale=TWO_PI))
        op("dve", lambda e: e.tensor_scalar(out=xa, in0=xa, scalar1=0.25, scalar2=None, op0=ALU.add))
        op("dve", lambda e: e.tensor_scalar(out=xr, in0=xa, scalar1=MAGIC, scalar2=-MAGIC, op0=ALU.add, op1=ALU.add))
        op("dve", lambda e: e.tensor_tensor(out=w1, in0=xa, in1=xr, op=ALU.subtract))
        op("act", lambda e: e.activation(out=cs_, in_=w1, func=AF.Sin, scale=TWO_PI))
        op("dve", lambda e: e.tensor_tensor(out=ar, in0=mg, in1=cs_, op=ALU.mult))
        op("dve", lambda e: e.tensor_tensor(out=ai, in0=mg, in1=sn, op=ALU.mult))
        op("dve", lambda e: e.tensor_tensor(out=w1, in0=mg, in1=mg, op=ALU.mult))
        op("dve", lambda e: e.reciprocal(out=w1, in_=w1))
        op("dve", lambda e: e.tensor_tensor(out=mr, in0=ar[:, :, 0:8], in1=w1[:, :, 0:8], op=ALU.mult))
        op("dve", lambda e: e.scalar_tensor_tensor(out=mi, in0=ai[:, :, 0:8], scalar=-1.0, in1=w1[:, :, 0:8], op0=ALU.mult, op1=ALU.mult))
        a1r = A.f32(NG); a1i = A.f32(NG)
        op("dve", lambda e: e.tensor_copy(out=a1r[:, 0:8], in_=ar[:, 0:8, 1]))
        op("dve", lambda e: e.tensor_copy(out=a1r[:, 8:16], in_=ar[:, 8:16, 6]))
        op("dve", lambda e: e.tensor_copy(out=a1i[:, 0:8], in_=ai[:, 0:8, 1]))
        op("dve", lambda e: e.tensor_copy(out=a1i[:, 8:16], in_=ai[:, 8:16, 6]))
        lr_ = prm[:, 0, :]; li_ = prm[:, 1, :]
        op("dve", lambda e: e.tensor_tensor(out=den, in0=lr_, in1=lr_, op=ALU.mult))
        op("dve", lambda e: e.tensor_tensor(out=t1, in0=li_, in1=li_, op=ALU.mult))
        op("dve", lambda e: e.tensor_tensor(out=den, in0=den, in1=t1, op=ALU.add))
        op("dve", lambda e: e.reciprocal(out=den, in_=den))
        op("dve", lambda e: e.tensor_scalar(out=t1, in0=a1r, scalar1=-1.0, scalar2=None, op0=ALU.add))
        op("dve", lambda e: e.tensor_tensor(out=t2, in0=t1, in1=lr_, op=ALU.mult))
        op("dve", lambda e: e.tensor_tensor(out=t3, in0=a1i, in1=li_, op=ALU.mult))
        op("dve", lambda e: e.tensor_tensor(out=t2, in0=t2, in1=t3, op=ALU.add))
        op("dve", lambda e: e.tensor_tensor(out=cr, in0=t2, in1=den, op=ALU.mult))
        op("dve", lambda e: e.tensor_tensor(out=t2, in0=a1i, in1=lr_, op=ALU.mult))
        op("dve", lambda e: e.tensor_tensor(out=t3, in0=t1, in1=li_, op=ALU.mult))
        op("dve", lambda e: e.tensor_tensor(out=t2, in0=t2, in1=t3, op=ALU.subtract))
        op("dve", lambda e: e.tensor_tensor(out=ci, in0=t2, in1=den, op=ALU.mult))
        w8a = A.f32(NG * 8).rearrange("p (g j) -> p g j", j=8)
        op("dve", lambda e: e.tensor_tensor(out=br_, in0=mr, in1=bg(cr, 8), op=ALU.mult))
        op("dve", lambda e: e.tensor_tensor(out=w8a, in0=mi, in1=bg(ci, 8), op=ALU.mult))
        op("dve", lambda e: e.tensor_tensor(out=br_, in0=br_, in1=w8a, op=ALU.subtract))
        op("dve", lambda e: e.tensor_tensor(out=bi_, in0=mr, in1=bg(ci, 8), op=ALU.mult))
        op("dve", lambda e: e.tensor_tensor(out=w8a, in0=mi, in1=bg(cr, 8), op=ALU.mult))
        op("dve", lambda e: e.tensor_tensor(out=bi_, in0=bi_, in1=w8a, op=ALU.add))
        op("dve", lambda e: e.tensor_copy(out=rho8, in_=mg[:, :, 8]))
        op("dve", lambda e: e.tensor_scalar(out=u8, in0=th, scalar1=8.0, scalar2=None, op0=ALU.mult))
        op("dve", lambda e: e.tensor_scalar(out=t1, in0=u8, scalar1=MAGIC, scalar2=-MAGIC, op0=ALU.add, op1=ALU.add))
        op("dve", lambda e: e.tensor_tensor(out=u8, in0=u8, in1=t1, op=ALU.subtract))
        SZ = NG * 8 * 16
        Btr = A.f32(SZ).rearrange("p (g j h) -> p g j h", g=NG, j=8)
        Bti = A.f32(SZ).rearrange("p (g j h) -> p g j h", g=NG, j=8)
        Ctr = A.f32(SZ).rearrange("p (g j h) -> p g j h", g=NG, j=8)
        Cti = A.f32(SZ).rearrange("p (g j h) -> p g j h", g=NG, j=8)
        regB = A.f32(4 * 2048)
        wk = regB[:, 0:2048].rearrange("p (g j h) -> p g j h", g=NG, j=8)

        def bj(v):
            return v.unsqueeze(3).to_broadcast([128, NG, 8, 16])

        def bh(v):
            return v.unsqueeze(2).to_broadcast([128, NG, 8, 16])

        Br, Bi = Bsb[:, 0], Bsb[:, 1]
        Cr, Ci = Csb[:, 0], Csb[:, 1]
        KB = ["s5t", "Bsb", "Csb", "s5m"]

        def opb(fn):
            P.op("dve", fn, reads=KB, writes=["s5m"])

        opb(lambda e: e.tensor_tensor(out=Btr, in0=bj(br_), in1=bh(Br), op=ALU.mult))
        opb(lambda e: e.tensor_tensor(out=wk, in0=bj(bi_), in1=bh(Bi), op=ALU.mult))
        opb(lambda e: e.tensor_tensor(out=Btr, in0=Btr, in1=wk, op=ALU.subtract))
        opb(lambda e: e.tensor_tensor(out=Bti, in0=bj(br_), in1=bh(Bi), op=ALU.mult))
        opb(lambda e: e.tensor_tensor(out=wk, in0=bj(bi_), in1=bh(Br), op=ALU.mult))
        opb(lambda e: e.tensor_tensor(out=Bti, in0=Bti, in1=wk, op=ALU.add))
        opb(lambda e: e.tensor_tensor(out=Ctr, in0=bj(ar[:, :, 0:8]), in1=bh(Cr), op=ALU.mult))
        opb(lambda e: e.tensor_tensor(out=wk, in0=bj(ai[:, :, 0:8]), in1=bh(Ci), op=ALU.mult))
        opb(lambda e: e.tensor_tensor(out=Ctr, in0=Ctr, in1=wk, op=ALU.subtract))
        opb(lambda e: e.tensor_tensor(out=Cti, in0=bj(ai[:, :, 0:8]), in1=bh(Cr), op=ALU.mult))
        opb(lambda e: e.tensor_tensor(out=wk, in0=bj(ar[:, :, 0:8]), in1=bh(Ci), op=ALU.mult))
        opb(lambda e: e.tensor_tensor(out=Cti, in0=Cti, in1=wk, op=ALU.add))
        opb(lambda e: e.tensor_scalar(out=Cti, in0=Cti, scalar1=-1.0, scalar2=None, op0=ALU.mult))
        NCT = (NC8 + 127) // 128
        U8 = A.f32(16 * NC8).rearrange("p (g c) -> p g c", g=16)
        cst_ = [regB[:, 2048:4096], regB[:, 4096:6144]]
        Ug = regB[:, 6144:8192]
        Yst = cst_

        def chunk_tiles():
            tiles = []
            c = 0
            while c < NC8:
                n = min(128, NC8 - c)
                tiles.append((c, n))
                c += n
            return tiles

        def chunk_dram(base, c0, n):
            pieces = []
            c = c0
            while c < c0 + n:
                if c < LC8:
                    m = min(c0 + n, LC8) - c
                    ap = base[c * 8:(c + m) * 8, :].rearrange("(c i) h -> c i h", i=8)
                    pieces.append((c - c0, m, ap))
                    c += m
                else:
                    cl = c - LC8
                    col, rb = cl // RB, cl % RB
                    m = min(RB - rb, c0 + n - c)
                    lat = base[LC:TT, :].rearrange("(rb i w) h -> w rb i h", i=8, w=64)
                    ap = lat[col, rb:rb + m, :, :]
                    pieces.append((c - c0, m, ap))
                    c += m
            return pieces

        cap_prep = P._cap
        P._cap = []
        cap_u8 = P._cap
        psr = Rot([0, 1, 2, 3])
        for ti, (c0, n) in enumerate(chunk_tiles()):
            cs = cst_[ti % 2]
            cs3 = cs.rearrange("p (i h) -> p i h", i=8)
            for (p0, m, ap) in chunk_dram(su_tok, c0, n):
                P.dma("sp", cs3[p0:p0 + m, :, :], ap, reads=["su_tok"], writes=["cst%d" % (ti % 2)])
            P.op("dve", lambda e, n=n, cs=cs: e.tensor_copy(out=Ug[0:n].rearrange("p (g i h) -> p g i h", g=16, i=8),
                                                          in_=cs[0:n].rearrange("p (i g h) -> p g i h", i=8, g=16)),
                 reads=["cst%d" % (ti % 2)], writes=["Ug"])
            for g in range(16):
                pi = psr.next()
                P.op("pe", lambda e, pi=pi, g=g, n=n: e.transpose(PS[pi][:, 0:n], Ug[0:n, g * 128:(g + 1) * 128], ident_f[0:n, 0:n]),
                     reads=["Ug", "ident_f"], writes=["ps%d" % pi])
                P.op("act", lambda e, pi=pi, g=g, n=n, c0=c0: e.activation(out=U8[:, g, c0:c0 + n], in_=PS[pi][:, 0:n], func=AF.Identity),
                     reads=[], writes=["ps%d" % pi, "U8"])
        P._cap = None
        P.interleave([cap_prep, cap_u8])
        P.barrier()
        halves = [(0, min(512, NC8))] + ([(512, NC8)] if NC8 > 512 else [])

        def carve(base):
            o = [0]

            def take(n):
                ap = base[:, o[0]:o[0] + n]
                o[0] += n
                return ap
            d_ = {}
            d_["BtT"] = take(256).rearrange("p (a s) -> p a s", a=2)
            d_["M8"] = take(256).rearrange("p (g c) -> p g c", g=2)
            for nm in ("Zr", "Zi", "Wr", "Wi", "Or", "Oi", "Xr", "Xi", "tA", "tB", "Cn", "Sn"):
                d_[nm] = take(NC8)
            return d_

        SETW = 512 + 12 * NC8
        sets = [carve(A.f32(SETW)), carve(regB)]
        Yall = A.f32(16 * NC8).rearrange("p (g c) -> p g c", g=16)
        psr = Rot([0, 1, 2, 3, 4, 5, 6, 7])

        def front(it):
            d, gp = divmod(it, 8)
            dg = it
            par = str(it % 2)
            S_ = sets[it % 2]
            BtT, M8, Zr, Zi, Cn, Sn = S_["BtT"], S_["M8"], S_["Zr"], S_["Zi"], S_["Cn"], S_["Sn"]
            xx, rr = S_["Wr"], S_["Wi"]
            m8 = m8F if d == 0 else m8B
            for a_, Bt in enumerate((Btr, Bti)):
                pi = psr.next()
                P.op("pe", lambda e, pi=pi, Bt=Bt: e.transpose(PS[pi][:, 0:128], Bt[:, dg].rearrange("p j h -> p (j h)"), ident_f[:]),
                     reads=["s5m", "ident_f"], writes=["ps%d" % pi])
                P.op("act", lambda e, pi=pi, a_=a_: e.activation(out=BtT[:, a_, :], in_=PS[pi][:, 0:128], func=AF.Identity), reads=[], writes=["ps%d" % pi, "BtT" + par])
            for gm in range(2):
                pi = psr.next()
                pr = slice(gm * 64, (gm + 1) * 64)
                for a_, (Bt, Ct) in enumerate(((Btr, Ctr), (Bti, Cti))):
                    P.op("pe", lambda e, pi=pi, gm=gm, pr=pr, Bt=Bt, Ct=Ct, a_=a_: e.matmul(PS[pi][:, 0:128], lhsT=Bt[pr, dg].rearrange("p j h -> p (j h)"),
                                                                                         rhs=Ct[pr, dg].rearrange("p j h -> p (j h)"), start=(a_ == 0), stop=(a_ == 1)),
                         reads=["s5m"], writes=["ps%d" % pi])
                P.op("dve", lambda e, pi=pi, m8=m8, gm=gm: e.tensor_tensor(out=M8[:, gm, :], in0=PS[pi][:, 0:128], in1=m8[:, 0:128], op=ALU.mult),
                     reads=["m8"], writes=["ps%d" % pi, "M8" + par])
            for (Zt, a_) in ((Zr, 0), (Zi, 1)):
                for (h0, h1) in halves:
                    pi = psr.next()
                    for gm in range(2):
                        g = gp * 2 + gm
                        P.op("pe", lambda e, pi=pi, gm=gm, g=g, a_=a_, h0=h0, h1=h1: e.matmul(PS[pi][gm * 64:(gm + 1) * 64, 0:h1 - h0], lhsT=BtT[:, a_, gm * 64:(gm + 1) * 64],
                                                                                           rhs=U8[:, g, h0:h1], start=True, stop=True),
                             reads=["BtT" + par, "U8"], writes=["ps%d" % pi])
                    P.op("act", lambda e, pi=pi, Zt=Zt, h0=h0, h1=h1: e.activation(out=Zt[:, h0:h1], in_=PS[pi][:, 0:h1 - h0], func=AF.Identity),
                         reads=[], writes=["ps%d" % pi, "Z" + par])
            ucol = u8[:, dg:dg + 1]
            kx, kr = "Wr" + par, "Wi" + par
            P.op("dve", lambda e, ucol=ucol: e.tensor_scalar(out=xx, in0=tau[:, d, :], scalar1=ucol, scalar2=None, op0=ALU.mult), reads=["tau", "s5t"], writes=[kx])
            P.op("dve", lambda e: e.tensor_scalar(out=rr, in0=xx, scalar1=MAGIC, scalar2=-MAGIC, op0=ALU.add, op1=ALU.add), reads=[kx], writes=[kr])
            P.op("dve", lambda e: e.tensor_tensor(out=rr, in0=xx, in1=rr, op=ALU.subtract), reads=[kx, kr], writes=[kr])
            P.op("act", lambda e: e.activation(out=Sn, in_=rr, func=AF.Sin, scale=TWO_PI), reads=[kr], writes=["Sn" + par])
            P.op("dve", lambda e: e.tensor_scalar(out=xx, in0=xx, scalar1=0.25, scalar2=None, op0=ALU.add), reads=[kx], writes=[kx])
            P.op("dve", lambda e: e.tensor_scalar(out=rr, in0=xx, scalar1=MAGIC, scalar2=-MAGIC, op0=ALU.add, op1=ALU.add), reads=[kx, "Sn" + par], writes=[kr])
            P.op("dve", lambda e: e.tensor_tensor(out=rr, in0=xx, in1=rr, op=ALU.subtract), reads=[kx, kr], writes=[kr])
            P.op("act", lambda e: e.activation(out=Cn, in_=rr, func=AF.Sin, scale=TWO_PI), reads=[kr], writes=["Cn" + par])

        def back(it):
            d, gp = divmod(it, 8)
            dg = it
            par = str(it % 2)
            S_ = sets[it % 2]
            M8, Zr, Zi, Wr, Wi, Or, Oi = S_["M8"], S_["Zr"], S_["Zi"], S_["Wr"], S_["Wi"], S_["Or"], S_["Oi"]
            Xr, Xi, tA, tB, Cn, Sn = S_["Xr"], S_["Xi"], S_["tA"], S_["tB"], S_["Cn"], S_["Sn"]
            kC, kS, kZ, kWr, kWi, kA, kB, kX = "Cn" + par, "Sn" + par, "Z" + par, "Wr" + par, "Wi" + par, "tA" + par, "tB" + par, "X" + par
            P.op("dve", lambda e: e.tensor_tensor(out=Wr, in0=Cn, in1=Zr, op=ALU.mult), reads=[kC, kZ], writes=[kWr])
            P.op("pool", lambda e: e.tensor_tensor(out=Wi, in0=Cn, in1=Zi, op=ALU.mult), reads=[kC, kZ], writes=[kWi])
            P.op("dve", lambda e: e.tensor_tensor(out=tA, in0=Sn, in1=Zi, op=ALU.mult), reads=[kS, kZ], writes=[kA])
            P.op("pool", lambda e: e.tensor_tensor(out=tB, in0=Sn, in1=Zr, op=ALU.mult), reads=[kS, kZ], writes=[kB])
            P.op("dve", lambda e: e.tensor_tensor(out=Wr, in0=Wr, in1=tA, op=ALU.add), reads=[kWr, kA], writes=[kWr])
            P.op("pool", lambda e: e.tensor_tensor(out=Wi, in0=Wi, in1=tB, op=ALU.subtract), reads=[kWi, kB], writes=[kWi])
            rcol = rho8[:, dg:dg + 1]
            for (Wt, Ot, nm) in ((Wr, Or, "Or" + par), (Wi, Oi, "Oi" + par)):
                if d == 0:
                    P.op("dve", lambda e, Wt=Wt, Ot=Ot, rcol=rcol: e.tensor_tensor_scan(out=Ot, data0=Wt, data1=rcol.to_broadcast([128, NC8]), initial=0.0, op0=ALU.add, op1=ALU.mult),
                         reads=[kWr, kWi, "s5t"], writes=[nm])
                else:
                    P.op("dve", lambda e, Wt=Wt, Ot=Ot, rcol=rcol: e.tensor_tensor_scan(out=Ot[:, 0:LC8][:, ::-1], data0=Wt[:, 0:LC8][:, ::-1], data1=rcol.to_broadcast([128, LC8]),
                                                                                     initial=0.0, op0=ALU.add, op1=ALU.mult),
                         reads=[kWr, kWi, "s5t"], writes=[nm])
                    P.op("dve", lambda e, Wt=Wt, Ot=Ot, rcol=rcol: e.tensor_tensor_scan(out=Ot[:, LC8:NC8][:, ::-1], data0=Wt[:, LC8:NC8][:, ::-1], data1=rcol.to_broadcast([128, NC8 - LC8]),
                                                                                     initial=Ot[:, 0:1], op0=ALU.add, op1=ALU.mult),
                         reads=[kWr, kWi, "s5t", nm], writes=[nm])
            if d == 0:
                sh = [(slice(1, NC8), slice(0, NC8 - 1))]
                zero_cols = [0]
                carry = None
            else:
                sh = [(slice(0, LC8 - 1), slice(1, LC8)), (slice(LC8, NC8 - 1), slice(LC8 + 1, NC8))]
                zero_cols = [LC8 - 1]
                carry = (NC8 - 1, 0)
            RK = ["Or" + par, "Oi" + par, kC, kS, kX, kA, kB]
            pairs = list(sh)
            if carry is not None:
                dc, sc = carry
                pairs.append((slice(dc, dc + 1), slice(sc, sc + 1)))
            kXr, kXi, kOr, kOi = "Xr" + par, "Xi" + par, "Or" + par, "Oi" + par
            for (do, so) in pairs:
                P.op("dve", lambda e, do=do, so=so: e.tensor_tensor(out=Xr[:, do], in0=Cn[:, do], in1=Or[:, so], op=ALU.mult), reads=[kC, kOr, kX], writes=[kXr])
                P.op("pool", lambda e, do=do, so=so: e.tensor_tensor(out=Xi[:, do], in0=Cn[:, do], in1=Oi[:, so], op=ALU.mult), reads=[kC, kOi, kX], writes=[kXi])
                P.op("dve", lambda e, do=do, so=so: e.tensor_tensor(out=tA[:, do], in0=Sn[:, do], in1=Oi[:, so], op=ALU.mult), reads=[kS, kOi], writes=[kA])
                P.op("pool", lambda e, do=do, so=so: e.tensor_tensor(out=tB[:, do], in0=Sn[:, do], in1=Or[:, so], op=ALU.mult), reads=[kS, kOr], writes=[kB])
                P.op("dve", lambda e, do=do, so=so: e.tensor_tensor(out=Xr[:, do], in0=Xr[:, do], in1=tA[:, do], op=ALU.subtract), reads=[kXr, kA], writes=[kXr])
                P.op("pool", lambda e, do=do, so=so: e.tensor_tensor(out=Xi[:, do], in0=Xi[:, do], in1=tB[:, do], op=ALU.add), reads=[kXi, kB], writes=[kXi])
            for zc in zero_cols:
                P.op("dve", lambda e, zc=zc: e.memset(Xr[:, zc:zc + 1], 0.0), reads=[kX], writes=[kXr])
                P.op("dve", lambda e, zc=zc: e.memset(Xi[:, zc:zc + 1], 0.0), reads=[kX], writes=[kXi])
            for gm in range(2):
                g = gp * 2 + gm
                pr = slice(gm * 64, (gm + 1) * 64)
                for (h0, h1) in halves:
                    pi = psr.next()
                    P.op("pe", lambda e, pi=pi, gm=gm, g=g, h0=h0, h1=h1: e.matmul(PS[pi][:, 0:h1 - h0], lhsT=M8[:, gm, :], rhs=U8[:, g, h0:h1], start=True, stop=False),
                         reads=["M8" + par, "U8"], writes=["ps%d" % pi])
                    P.op("pe", lambda e, pi=pi, pr=pr, h0=h0, h1=h1: e.matmul(PS[pi][:, 0:h1 - h0], lhsT=Ctr[pr, dg].rearrange("p j h -> p (j h)"), rhs=Xr[pr, h0:h1], start=False, stop=False),
                         reads=["s5m", kXr], writes=["ps%d" % pi, kX])
                    P.op("pe", lambda e, pi=pi, pr=pr, h0=h0, h1=h1: e.matmul(PS[pi][:, 0:h1 - h0], lhsT=Cti[pr, dg].rearrange("p j h -> p (j h)"), rhs=Xi[pr, h0:h1], start=False, stop=True),
                         reads=["s5m", kXi], writes=["ps%d" % pi, kX])
                    if d == 0:
                        P.op("act", lambda e, pi=pi, g=g, h0=h0, h1=h1: e.activation(out=Yall[:, g, h0:h1], in_=PS[pi][:, 0:h1 - h0], func=AF.Identity),
                             reads=[], writes=["ps%d" % pi, "Yall"])
                    else:
                        P.op("dve", lambda e, pi=pi, g=g, h0=h0, h1=h1: e.tensor_tensor(out=Yall[:, g, h0:h1], in0=PS[pi][:, 0:h1 - h0], in1=Yall[:, g, h0:h1], op=ALU.add),
                             reads=["Yall"], writes=["ps%d" % pi, "Yall"])

        front(0)
        for it in range(16):
            if it + 1 < 16:
                P.interleave([P.capture(lambda: back(it)), P.capture(lambda: front(it + 1))])
            else:
                back(it)
        P.barrier()
        psr = Rot([0, 1, 2, 3])
        for ti, (c0, n) in enumerate(chunk_tiles()):
            ys = Yst[ti % 2]
            ys3 = ys.rearrange("p (i h) -> p i h", i=8)
            for g in range(16):
                pi = psr.next()
                P.op("pe", lambda e, pi=pi, g=g, n=n, c0=c0: e.transpose(PS[pi][0:n, 0:128], Yall[:, g, c0:c0 + n], ident_f[:]),
                     reads=["Yall", "ident_f"], writes=["ps%d" % pi])
                P.op("act", lambda e, pi=pi, g=g, n=n, ys3=ys3: e.activation(out=ys3[0:n, :, g * 16:(g + 1) * 16], in_=PS[pi][0:n, 0:128].rearrange("p (j h) -> p j h", j=8), func=AF.Identity),
                     reads=[], writes=["ps%d" % pi, "yst%d" % (ti % 2)])
            for (p0, m, ap) in chunk_dram(s5y, c0, n):
                P.dma("sp", ap, ys3[p0:p0 + m, :, :], reads=["yst%d" % (ti % 2)], writes=["s5y"])

    def phaseC1(l, last):
        A.reset()
        xsrc = xin if l == 0 else xres
        w1p = A.bf16(KT * 4 * D).rearrange("p (k n) -> p k n", k=KT)
        w2p = A.bf16(32 * D).rearrange("p (k n) -> p k n", k=32)
        w1v = w_ff1[l].rearrange("(kt p) n -> p kt n", p=128)
        w2v = w_ff2[l].rearrange("(kt p) n -> p kt n", p=128)
        pre = []
        for c0 in range(0, 4 * D, 512):
            pre.append((w1p[:, :, c0:c0 + 512], w1v[:, :, c0:c0 + 512], "w1"))
        for k0 in range(0, 32, 4):
            for c0 in range(0, D, 512):
                pre.append((w2p[:, k0:k0 + 4, c0:c0 + 512], w2v[:, k0:k0 + 4, c0:c0 + 512], "w2"))
        wo = A.bf16(KT * D).rearrange("p (k n) -> p k n", k=KT)
        wv = w_out[l].rearrange("(kt p) n -> p kt n", p=128)
        for c0 in range(0, D, 512):
            P.dma("poolq", wo[:, :, c0:c0 + 512], wv[:, :, c0:c0 + 512], writes=["wo"])
        glu = A.bf16(2 * 256).rearrange("p (k n) -> p k n", k=2)
        P.dma("poolq", glu, s5_glu_w[l].rearrange("(kt p) n -> p kt n", p=128), writes=["glu"])
        g1t = A.f32(D)
        g1bc = {0: g1t, 1: g1t}
        cur_w = [None]

        def load_g1(w):
            if cur_w[0] == w:
                return
            cur_w[0] = w
            P.dma("sp", g1t, mraw[l, w:w + 1, 2 * D:3 * D].partition_broadcast(128), reads=["mraw"], writes=["bc"])

        dbc = A.f32(256)
        P.dma("sp", dbc, s5_d[l:l + 1, :].partition_broadcast(128), writes=["bcd"])
        NB = 256
        o4 = A.f32(4 * NB).rearrange("p (h t) -> p h t", h=4)
        g4 = A.f32(4 * NB).rearrange("p (h t) -> p h t", h=4)
        sq = A.bf16(4 * NB).rearrange("p (h t) -> p h t", h=4)
        rn = A.f32(4 * NB).rearrange("p (h t) -> p h t", h=4)
        lh = A.f32(2 * NB).rearrange("p (h t) -> p h t", h=2)
        lg = A.f32(2 * NB).rearrange("p (h t) -> p h t", h=2)
        cat = A.bf16(KT * NB).rearrange("p (k t) -> p k t", k=KT)
        ysb = A.f32(2 * 256).rearrange("p (j c) -> p j c", j=2)
        usb = A.f32(2 * 256).rearrange("p (j c) -> p j c", j=2)
        sb16 = A.bf16(2 * 256).rearrange("p (j c) -> p j c", j=2)
        sTt = A.bf16(2 * NB).rearrange("p (k t) -> p k t", k=2)
        gsig = A.f32(2 * NB).rearrange("p (k t) -> p k t", k=2)
        xt = [A.f32(D), A.f32(D)]
        xo = [A.f32(D), A.f32(D)]
        t0 = 0 if not last else LC
        psr = Rot([0, 1, 2, 3, 4, 5, 6, 7])
        while t0 < TT:
            w = 1 if t0 < LC else 0
            n = min(NB, (LC if w == 1 else TT) - t0)
            tk = slice(t0, t0 + n)
            load_g1(w)
            for _ in range(3):
                if pre:
                    o_, i_, k_ = pre.pop(0)
                    P.dma("poolq", o_, i_, writes=[k_])
            P.dma("sp", o4[:, :, 0:n], oT.rearrange("(h p) t -> p h t", p=128)[:, :, tk], reads=["oT"], writes=["o4"])
            P.dma("sp", g4[:, :, 0:n], ggT.rearrange("(h p) t -> p h t", p=128)[:, :, tk], reads=["zT_gg"], writes=["g4"])
            P.op("pool", lambda e, n=n: e.tensor_tensor(out=sq[:, :, 0:n], in0=o4[:, :, 0:n], in1=o4[:, :, 0:n], op=ALU.mult), reads=["o4"], writes=["sq"])
            P.op("act", lambda e, n=n: e.activation(out=g4[:, :, 0:n], in_=g4[:, :, 0:n], func=AF.Silu), reads=["g4"], writes=["g4"])
            for h in range(4):
                pi = psr.next()
                P.op("pe", lambda e, pi=pi, h=h, n=n: e.matmul(PS[pi][:, 0:n], lhsT=ones_b[:], rhs=sq[:, h, 0:n], start=True, stop=True), reads=["sq", "ones_b"], writes=["ps%d" % pi])
                P.op("act", lambda e, pi=pi, h=h, n=n: e.activation(out=rn[:, h, 0:n], in_=PS[pi][:, 0:n], func=AF.Sqrt, scale=1.0 / 128.0, bias=EPS), reads=[], writes=["ps%d" % pi, "rn"])
            P.op("dve", lambda e, n=n: e.reciprocal(out=rn[:, :, 0:n], in_=rn[:, :, 0:n]), reads=["rn"], writes=["rn"])
            P.op("dve", lambda e, n=n: e.tensor_tensor(out=o4[:, :, 0:n], in0=o4[:, :, 0:n], in1=rn[:, :, 0:n], op=ALU.mult), reads=["o4", "rn"], writes=["o4"])
            for h in range(4):
                P.op("dve", lambda e, h=h, n=n: e.scalar_tensor_tensor(out=cat[:, h, 0:n], in0=o4[:, h, 0:n], scalar=ppsb[:, PP_GNORM + h:PP_GNORM + h + 1], in1=g4[:, h, 0:n],
                                                                       op0=ALU.mult, op1=ALU.mult), reads=["o4", "g4", "ppsb"], writes=["cat"])
            P.dma("sp", lh[:, :, 0:n], lruT.rearrange("(h p) t -> p h t", p=128)[:, :, tk], reads=["lruT"], writes=["lh"])
            P.dma("sp", lg[:, :, 0:n], lgT.rearrange("(h p) t -> p h t", p=128)[:, :, tk], reads=["zT_lg"], writes=["lg"])
            P.op("act", lambda e, n=n: e.activation(out=lg[:, :, 0:n], in_=lg[:, :, 0:n], func=AF.Gelu), reads=["lg"], writes=["lg"])
            P.op("dve", lambda e, n=n: e.tensor_tensor(out=cat[:, 4:6, 0:n], in0=lh[:, :, 0:n], in1=lg[:, :, 0:n], op=ALU.mult), reads=["lh", "lg"], writes=["cat"])
            nj = n // 128
            P.dma("sp", ysb[:, 0:nj, :], s5y[tk, :].rearrange("(j p) c -> p j c", p=128), reads=["s5y"], writes=["ysb"])
            P.dma("sp", usb[:, 0:nj, :], su_tok[tk, :].rearrange("(j p) c -> p j c", p=128), reads=["su_tok"], writes=["usb"])
            P.op("dve", lambda e, nj=nj: e.tensor_tensor(out=usb[:, 0:nj, :], in0=usb[:, 0:nj, :], in1=dbc.unsqueeze(1).to_broadcast([128, nj, 256]), op=ALU.mult), reads=["usb", "bcd"], writes=["usb"])
            P.op("dve", lambda e, nj=nj: e.tensor_tensor(out=ysb[:, 0:nj, :], in0=ysb[:, 0:nj, :], in1=usb[:, 0:nj, :], op=ALU.add), reads=["usb", "ysb"], writes=["ysb"])
            P.op("act", lambda e, nj=nj: e.activation(out=sb16[:, 0:nj, :], in_=ysb[:, 0:nj, :], func=AF.Gelu), reads=["ysb"], writes=["sb16"])
            for j in range(nj):
                pi = psr.next()
                pst = PS[pi][:].bitcast(BF16)
                for k in range(2):
                    P.op("pe", lambda e, pst=pst, j=j, k=k: e.transpose(pst[:, k * 128:(k + 1) * 128], sb16[:, j, k * 128:(k + 1) * 128], ident_b[:]), reads=["sb16", "ident_b"], writes=["ps%d" % pi])
                P.op("act", lambda e, pst=pst, j=j: e.activation(out=sTt[:, :, j * 128:(j + 1) * 128], in_=pst[:, 0:256].rearrange("p (k t) -> p k t", k=2), func=AF.Identity),
                     reads=[], writes=["ps%d" % pi, "sTt"])
            for ko in range(2):
                pi = psr.next()
                for ki_ in range(2):
                    P.op("pe", lambda e, pi=pi, ko=ko, ki_=ki_, n=n: e.matmul(PS[pi][:, 0:n], lhsT=glu[:, ki_, ko * 128:(ko + 1) * 128], rhs=sTt[:, ki_, 0:n], start=(ki_ == 0), stop=(ki_ == 1)),
                         reads=["glu", "sTt"], writes=["ps%d" % pi])
                P.op("act", lambda e, pi=pi, ko=ko, n=n: e.activation(out=gsig[:, ko, 0:n], in_=PS[pi][:, 0:n], func=AF.Sigmoid, bias=ppsb[:, PP_GLUB + ko:PP_GLUB + ko + 1]),
                     reads=["ppsb"], writes=["ps%d" % pi, "gsig"])
            P.op("dve", lambda e, n=n: e.tensor_tensor(out=cat[:, 6:8, 0:n], in0=sTt[:, :, 0:n], in1=gsig[:, :, 0:n], op=ALU.mult), reads=["sTt", "gsig"], writes=["cat"])
            for j in range(nj):
                ti0 = t0 + j * 128
                xs = (ti0 // 128) % 2
                P.dma("sp", xt[xs], xsrc[ti0:ti0 + 128, :], reads=["xsrc"], writes=["xtc%d" % xs])
                for hf_ in range(2):
                    pi = psr.next()
                    for kt in range(KT):
                        P.op("pe", lambda e, pi=pi, kt=kt, j=j, hf_=hf_: e.matmul(PS[pi][:, :], lhsT=cat[:, kt, j * 128:(j + 1) * 128], rhs=wo[:, kt, hf_ * 512:(hf_ + 1) * 512],
                                                                               start=(kt == 0), stop=(kt == KT - 1)),
                             reads=["cat", "wo"], writes=["ps%d" % pi])
                    P.op("dve", lambda e, pi=pi, xs=xs, hf_=hf_, w=w: e.tensor_tensor(out=xo[xs][:, hf_ * 512:(hf_ + 1) * 512], in0=PS[pi][:, :], in1=g1bc[w][:, hf_ * 512:(hf_ + 1) * 512], op=ALU.mult),
                         reads=["bc"], writes=["ps%d" % pi, "xo%d" % xs])
                P.op("pool", lambda e, xs=xs: e.tensor_tensor(out=xo[xs], in0=xo[xs], in1=xt[xs], op=ALU.add), reads=["xtc%d" % xs, "xo%d" % xs], writes=["xo%d" % xs])
                P.dma("sp", x1[ti0:ti0 + 128, :], xo[xs], reads=["xo%d" % xs], writes=["x1"])
            t0 += n
        while pre:
            o_, i_, k_ = pre.pop(0)
            P.dma("poolq", o_, i_, writes=[k_])

    def phaseC2(l, last):
        A.reset()
        w1 = A.bf16(KT * 4 * D).rearrange("p (k n) -> p k n", k=KT)
        w2 = A.bf16(32 * D).rearrange("p (k n) -> p k n", k=32)
        G = A.f32(D); S = A.f32(D)
        g2s = {0: A.f32(D)}
        if not last:
            g2s[1] = A.f32(D)
        for w_, t_ in g2s.items():
            P.dma("sp", t_, mraw[l, w_:w_ + 1, 5 * D:6 * D].partition_broadcast(128), reads=["mraw"], writes=["bcg2"])
        cur_w = [None]

        def load_bc(w):
            if cur_w[0] == w:
                return
            cur_w[0] = w
            P.dma("sp", G, gsc[l, w, 1:2, :].partition_broadcast(128), reads=["gsc"], writes=["bcG"])
            P.dma("sp", S, mraw[l, w:w + 1, 3 * D:4 * D].partition_broadcast(128), reads=["mraw"], writes=["bcG"])

        if last:
            fn = A.f32(D)
            P.dma("sp", fn, final_norm.partition_broadcast(128), writes=["bcf"])
        NB = 256
        xts = [A.f32(D) for _ in range(4)]
        hbs = [A.bf16(D) for _ in range(4)]
        sss = [A.f32(1) for _ in range(4)]; rss = [A.f32(1) for _ in range(4)]
        fss = [A.f32(1) for _ in range(2)]; frs = [A.f32(1) for _ in range(2)]
        hTs = [A.bf16(KT * NB).rearrange("p (k t) -> p k t", k=KT) for _ in range(2)]
        uT = A.bf16(32 * NB).rearrange("p (k t) -> p k t", k=32)
        rl = [A.bf16(NB), A.bf16(NB)]
        yo = [A.f32(D), A.f32(D)]
        psT = Rot([0, 1]); psM = Rot([2, 3, 4, 5, 6, 7]); rlR = Rot([0, 1])
        blocks = []
        t0 = 0 if not last else LC
        while t0 < TT:
            w = 1 if t0 < LC else 0
            n = min(NB, (LC if w == 1 else TT) - t0)
            blocks.append((t0, n, w))
            t0 += n

        def slot(bi, j):
            return (bi % 2) * 2 + j

        def norms(bi):
            (t0, n, w) = blocks[bi]
            load_bc(w)
            for j in range(n // 128):
                s = slot(bi, j)
                ti0 = t0 + j * 128
                kx, kh = "xt%d" % s, "hb%d" % s
                P.dma("sp", xts[s], x1[ti0:ti0 + 128, :], reads=["x1"], writes=[kx])
                P.op("act", lambda e, s=s: e.activation(out=hbs[s], in_=xts[s], func=AF.Square, accum_out=sss[s]), reads=[kx], writes=[kh, "ss%d" % s])
                P.op("act", lambda e, s=s: e.activation(out=rss[s], in_=sss[s], func=AF.Sqrt, scale=1.0 / D, bias=EPS), reads=["ss%d" % s], writes=["rs%d" % s])
                P.op("dve", lambda e, s=s: e.reciprocal(out=rss[s], in_=rss[s]), reads=["rs%d" % s], writes=["rs%d" % s])
                P.op("dve", lambda e, s=s: e.scalar_tensor_tensor(out=hbs[s], in0=xts[s], scalar=rss[s], in1=G, op0=ALU.mult, op1=ALU.mult),
                     reads=[kx, "rs%d" % s, "bcG"], writes=[kh])
                P.op("pool", lambda e, s=s: e.tensor_tensor(out=hbs[s], in0=hbs[s], in1=S, op=ALU.add), reads=[kh, "bcG"], writes=[kh])

        def transposes(bi):
            (t0, n, w) = blocks[bi]
            hT = hTs[bi % 2]
            for j in range(n // 128):
                s = slot(bi, j)
                pi = psT.next()
                pst = PS[pi][:].bitcast(BF16)
                for kt in range(KT):
                    P.op("pe", lambda e, kt=kt, pst=pst, s=s: e.transpose(pst[:, kt * 128:(kt + 1) * 128], hbs[s][:, kt * 128:(kt + 1) * 128], ident_b[:]),
                         reads=["hb%d" % s, "ident_b"], writes=["ps%d" % pi])
                P.op("act", lambda e, pst=pst, j=j, hT=hT: e.activation(out=hT[:, :, j * 128:(j + 1) * 128], in_=pst.rearrange("p (k t) -> p k t", k=KT), func=AF.Identity),
                     reads=[], writes=["ps%d" % pi, "hT%d" % (bi % 2)])

        def ff1(bi):
            (t0, n, w) = blocks[bi]
            hT = hTs[bi % 2]; kT_ = "hT%d" % (bi % 2)
            for ft in range(32):
                pi = psM.next()
                for kt in range(KT):
                    P.op("pe", lambda e, pi=pi, kt=kt, ft=ft, n=n, hT=hT: e.matmul(PS[pi][:, 0:n], lhsT=w1[:, kt, ft * 128:(ft + 1) * 128], rhs=hT[:, kt, 0:n], start=(kt == 0), stop=(kt == KT - 1)),
                         reads=["w1", kT_], writes=["ps%d" % pi])
                ri = rlR.next()
                P.op("act", lambda e, pi=pi, ri=ri, n=n: e.activation(out=rl[ri][:, 0:n], in_=PS[pi][:, 0:n], func=AF.Relu), reads=[], writes=["ps%d" % pi, "rl%d" % ri])
                P.op("dve", lambda e, ri=ri, ft=ft, n=n: e.tensor_tensor(out=uT[:, ft, 0:n], in0=rl[ri][:, 0:n], in1=rl[ri][:, 0:n], op=ALU.mult), reads=["rl%d" % ri], writes=["uT"])

        def ff2(bi):
            (t0, n, w) = blocks[bi]
            for j in range(n // 128):
                ti0 = t0 + j * 128
                s = slot(bi, j)
                y = j % 2
                for hf_ in range(2):
                    pi = psM.next()
                    for ft in range(32):
                        P.op("pe", lambda e, pi=pi, ft=ft, j=j, hf_=hf_: e.matmul(PS[pi][:, :], lhsT=uT[:, ft, j * 128:(j + 1) * 128], rhs=w2[:, ft, hf_ * 512:(hf_ + 1) * 512],
                                                                               start=(ft == 0), stop=(ft == 31)),
                             reads=["w2", "uT"], writes=["ps%d" % pi])
                    P.op("dve", lambda e, pi=pi, y=y, hf_=hf_: e.tensor_tensor(out=yo[y][:, hf_ * 512:(hf_ + 1) * 512], in0=PS[pi][:, :], in1=g2s[w][:, hf_ * 512:(hf_ + 1) * 512], op=ALU.mult),
                         reads=["bcg2"], writes=["ps%d" % pi, "yo%d" % y])
                P.op("pool", lambda e, y=y, s=s: e.tensor_tensor(out=yo[y], in0=yo[y], in1=xts[s], op=ALU.add), reads=["xt%d" % s, "yo%d" % y], writes=["yo%d" % y])
                if not last:
                    P.dma("sp", xres[ti0:ti0 + 128, :], yo[y], reads=["yo%d" % y], writes=["xres%d" % ti0])
                else:
                    kh = "hb%d" % s
                    P.op("act", lambda e, y=y, s=s: e.activation(out=hbs[s], in_=yo[y], func=AF.Square, accum_out=fss[y]), reads=["yo%d" % y], writes=[kh, "fss%d" % y])
                    P.op("act", lambda e, y=y: e.activation(out=frs[y], in_=fss[y], func=AF.Sqrt, scale=1.0 / D, bias=EPS), reads=["fss%d" % y], writes=["frs%d" % y])
                    P.op("dve", lambda e, y=y: e.reciprocal(out=frs[y], in_=frs[y]), reads=["frs%d" % y], writes=["frs%d" % y])
                    P.op("dve", lambda e, y=y: e.scalar_tensor_tensor(out=yo[y], in0=yo[y], scalar=frs[y], in1=fn, op0=ALU.mult, op1=ALU.mult),
                         reads=["yo%d" % y, "frs%d" % y, "bcf"], writes=["yo%d" % y])
                    P.dma("sp", out_d[ti0 - LC:ti0 - LC + 128, :], yo[y], reads=["yo%d" % y], writes=["out%d" % ti0])

        norms(0)
        transposes(0)
        for bi in range(len(blocks)):
            if bi + 1 < len(blocks):
                norms(bi + 1)
            ff1(bi)
            if bi + 1 < len(blocks):
                transposes(bi + 1)
            ff2(bi)

    setup_consts()
    stages = build.stages if hasattr(build, "stages") else None
    for l in range(depth):
        last = (l == depth - 1)
        if stages == "C":
            break
        modulation(l)
        load_pp(l)
        P.barrier()
        if stages == "M":
            break
        phaseA(l)
        P.barrier()
        if stages is not None and "A" == stages:
            break
        print("nops before gla", P.nops, flush=True)
        gla(l)
        P.barrier()
        print("nops before lru", P.nops, flush=True)
        lru(l)
        P.barrier()
        print("nops before s5", P.nops, flush=True)
        s5(l)
        P.barrier()
        print("nops after s5", P.nops, flush=True)
        if stages is not None and "B" == stages:
            break
        phaseC1(l, last)
        P.barrier()
        phaseC2(l, last)
        P.barrier()
    P.barrier()
    print("nops", P.nops, flush=True)
    P.emit()
    P.close()
    return nc


def prep_inputs(inp, b, LL, LC, depth):
    f = lambda a: np.ascontiguousarray(np.asarray(a, dtype=np.float32))
    TT = LL + LC
    NC8, LC8 = TT // 8, LC // 8
    m = {}
    m["xin"] = f(np.concatenate([inp["ctx"][b], inp["x"][b]], axis=0))
    cv = np.stack([np.asarray(inp["c"][b]).reshape(KT, 128).T, np.asarray(inp["c_ctx"]).reshape(KT, 128).T], axis=-1)
    m["cvec"] = f(cv)
    for k in ("w_mod", "b_mod", "norm1", "norm2", "w_in", "gla_up_w", "lru_wa", "lru_wx", "s5_d", "s5_glu_w", "w_out", "w_ff1", "w_ff2"):
        m[k] = f(inp[k])
    m["final_norm"] = f(np.asarray(inp["final_norm"]).reshape(1, D))
    pp = np.zeros((depth, 128, 64), np.float32)
    for l in range(depth):
        for d in range(2):
            for hp in range(2):
                pp[l, :, 0 + d * 2 + hp] = inp["gla_up_b"][l, d, hp * 128:(hp + 1) * 128]
            for ct in range(2):
                sl = slice(ct * 128, (ct + 1) * 128)
                for k in range(4):
                    pp[l, :, 8 + (d * 4 + k) * 2 + ct] = inp["lru_conv_w"][l, d, k, sl]
                pp[l, :, 24 + d * 2 + ct] = inp["lru_conv_b"][l, d, sl]
                pp[l, :, 28 + d * 2 + ct] = inp["lru_ba"][l, d, sl]
                pp[l, :, 32 + d * 2 + ct] = inp["lru_bx"][l, d, sl]
                pp[l, :, 36 + d * 2 + ct] = inp["lru_lambda"][l, d, sl]
        for h in range(4):
            pp[l, :, 4 + h] = inp["gla_norm"][l, h * 128:(h + 1) * 128]
        for ct in range(2):
            pp[l, :, 40 + ct] = inp["s5_glu_b"][l, ct * 128:(ct + 1) * 128]
    m["pp"] = pp
    s5p = np.zeros((depth, 128, 3, 16), np.float32)
    s5b = np.zeros((depth, 128, 2, 16, 16), np.float32)
    s5c = np.zeros((depth, 128, 2, 16, 16), np.float32)
    for l in range(depth):
        for d in range(2):
            for gp in range(8):
                for gm in range(2):
                    g = gp * 2 + gm
                    ps_ = slice(gm * 64, (gm + 1) * 64)
                    s5p[l, ps_, 0, d * 8 + gp] = inp["s5_lam_re"][l, d, g]
                    s5p[l, ps_, 1, d * 8 + gp] = inp["s5_lam_im"][l, d, g]
                    s5p[l, ps_, 2, d * 8 + gp] = inp["s5_log_dt"][l, d, g]
                    s5b[l, ps_, 0, d * 8 + gp, :] = inp["s5_b_re"][l, d, g]
                    s5b[l, ps_, 1, d * 8 + gp, :] = inp["s5_b_im"][l, d, g]
                    s5c[l, ps_, 0, d * 8 + gp, :] = np.asarray(inp["s5_c_re"][l, d, g]).T
                    s5c[l, ps_, 1, d * 8 + gp, :] = np.asarray(inp["s5_c_im"][l, d, g]).T
    m["s5p"], m["s5b"], m["s5c"] = s5p, s5b, s5c
    tau = np.zeros((2, NC8), np.float32)
    tau[0] = np.arange(NC8)
    tau[1, :LC8] = LC8 - 1 - np.arange(LC8)
    tau[1, LC8:] = LC8 + (NC8 - 1 - np.arange(LC8, NC8))
    m["tau"] = tau.reshape(1, 2 * NC8)
    return m


_CACHE = {}


def kernel(**inputs):
    LL, LC, depth = 4096, 256, 4
    B = inputs["x"].shape[0]
    key = (LL, LC, depth)
    if key not in _CACHE:
        _CACHE[key] = build(LL, LC, depth)
    nc = _CACHE[key]
    in_maps = [prep_inputs(inputs, b, LL, LC, depth) for b in range(B)]
    res = run_bass_kernel_spmd(nc, in_maps, core_ids=list(range(B)))
    return np.stack([np.asarray(r["out"], dtype=np.float32) for r in res.results], axis=0)
```
